# Optimizing a Trainium2 kernel written in Bass

```python
import math
import jax, jax.numpy as jnp
from jax import lax
import numpy as np


D_MODEL = 2048
BATCH = 2
SEQ = 4096
DEPTH = 4

HEAD_DIM = 64
RWKV_HEADS = 16
RWKV_DIM = RWKV_HEADS * HEAD_DIM
DECAY_LORA = 96
AAA_LORA = 96
GATE_LORA = 256
ATTN_GROUPS = ((128, 1), (512, 4), (2048, 16))
N_GROUPS = len(ATTN_GROUPS)
ATTN_SLOTS = 8
ATTN_HEADS = N_GROUPS * ATTN_SLOTS
ATTN_DIM = ATTN_HEADS * HEAD_DIM
ATTN_OUT_DIM = ATTN_SLOTS * HEAD_DIM
N_BUCKETS = 32
MAX_DISTANCE = 1024
D_FF = 4 * D_MODEL
N_BRANCHES = 2
RWKV_COLS = 3 * RWKV_DIM + GATE_LORA + 2 * DECAY_LORA + 2 * AAA_LORA
N_IN_COLS = RWKV_COLS + 3 * ATTN_DIM + N_BRANCHES * D_MODEL
RMS_EPS = 1e-6
GN_EPS = 64e-5
L2_EPS = 1e-12

kernel_name = 'hybrid_rwkv7_dilated_attn_encoder'


def rmsnorm(x, g):
    xf = x.astype(jnp.float32)
    y = xf * lax.rsqrt(jnp.mean(xf * xf, axis=-1, keepdims=True) + RMS_EPS)
    return (y * g.astype(jnp.float32)).astype(x.dtype)


def t5_bucket(rel):
    nb = N_BUCKETS // 2
    max_exact = nb // 2
    ret = jnp.where(rel > 0, nb, 0)
    n = jnp.abs(rel)
    nf = jnp.maximum(n, 1).astype(jnp.float32)
    large = max_exact + (jnp.log(nf / max_exact) / math.log(MAX_DISTANCE / max_exact)
                         * (nb - max_exact)).astype(jnp.int32)
    large = jnp.minimum(large, nb - 1)
    return ret + jnp.where(n < max_exact, n, large)


def dilated_window_attention(q, k, v, bias_table, dilation, half):
    B, S, H, Dh = q.shape
    L = S // dilation
    Q = half
    nb = -(-L // Q)
    Lp = nb * Q

    def to_sub(t):
        t = t.reshape(B, L, dilation, H, Dh).transpose(0, 2, 1, 3, 4)
        return jnp.pad(t, ((0, 0), (0, 0), (0, Lp - L), (0, 0), (0, 0)))

    def windows(t):
        t = jnp.pad(to_sub(t), ((0, 0), (0, 0), (Q, Q), (0, 0), (0, 0)))
        t = t.reshape(B, dilation, nb + 2, Q, H, Dh)
        return jnp.concatenate([t[:, :, :-2], t[:, :, 1:-1], t[:, :, 2:]], axis=3)

    qs = to_sub(q).reshape(B, dilation, nb, Q, H, Dh)
    kw = windows(k)
    vw = windows(v)
    rel = jnp.arange(3 * Q)[None, :] - Q - jnp.arange(Q)[:, None]
    band = jnp.abs(rel) <= half
    key_idx = jnp.arange(nb)[:, None] * Q - Q + jnp.arange(3 * Q)[None, :]
    valid = (key_idx >= 0) & (key_idx < L)
    mask = band[None] & valid[:, None, :]
    bias = bias_table[t5_bucket(rel * dilation)].transpose(2, 0, 1)
    logits = jnp.einsum('brnqhd,brnkhd->brnhqk', qs, kw) * (Dh ** -0.5) + bias
    logits = jnp.where(mask[None, None, :, None], logits, -jnp.inf)
    m = jnp.max(logits, axis=-1, keepdims=True)
    p = jnp.exp(logits - m)
    denom = jnp.sum(p, axis=-1, keepdims=True)
    o = jnp.einsum('brnhqk,brnkhd->brnqhd', p / denom, vw)
    lse = (m + jnp.log(denom))[..., 0].transpose(0, 1, 2, 4, 3)

    def from_sub(t):
        t = t.reshape((B, dilation, Lp) + t.shape[4:])[:, :, :L]
        t = jnp.moveaxis(t, 1, 2)
        return t.reshape((B, S) + t.shape[3:])

    return from_sub(o), from_sub(lse)


def attention_branch(q, k, v, rel_bias):
    B, S, _ = q.shape
    heads = lambda t: t.astype(jnp.float32).reshape(B, S, ATTN_HEADS, HEAD_DIM)
    q, k, v = heads(q), heads(k), heads(v)
    table = rel_bias.astype(jnp.float32)
    outs, lses = [], []
    for gi, (window, dilation) in enumerate(ATTN_GROUPS):
        hs = slice(gi * ATTN_SLOTS, (gi + 1) * ATTN_SLOTS)
        o, lse = dilated_window_attention(q[:, :, hs], k[:, :, hs], v[:, :, hs],
                                          table[:, hs], dilation, window // (2 * dilation))
        outs.append(o)
        lses.append(lse)
    wts = jax.nn.softmax(jnp.stack(lses, axis=0), axis=0)
    o = jnp.sum(wts[..., None] * jnp.stack(outs, axis=0), axis=0)
    return o.reshape(B, S, ATTN_OUT_DIM)


def rwkv7_scan(r, w, k, v, kk, a, reverse):
    B, S, H, N = r.shape

    def step(state, inp):
        r_t, w_t, k_t, v_t, kk_t, a_t = inp
        sk = jnp.einsum('bhij,bhj->bhi', state, kk_t)
        state = (state * w_t[:, :, None, :] - sk[..., :, None] * (kk_t * a_t)[:, :, None, :]
                 + v_t[..., :, None] * k_t[:, :, None, :])
        return state, jnp.einsum('bhij,bhj->bhi', state, r_t)

    xs = tuple(jnp.moveaxis(t, 1, 0) for t in (r, w, k, v, kk, a))
    s0 = jnp.zeros((B, H, N, N), jnp.float32)
    _, y = lax.scan(step, s0, xs, reverse=reverse)
    return jnp.moveaxis(y, 0, 1)


def rwkv_branch(slab, mu, w0, w_up, a0, a_up, g_up, k_k, k_a, r_k, gn_w, gn_b):
    f32 = lambda t: t.astype(jnp.float32)
    slab = f32(slab)
    B, S, _ = slab.shape
    prev = jnp.pad(slab[:, :-1], ((0, 0), (1, 0), (0, 0)))
    nxt = jnp.pad(slab[:, 1:], ((0, 0), (0, 1), (0, 0)))
    p = slab + f32(mu) * (0.5 * (prev + nxt) - slab)
    splits = np.cumsum([RWKV_DIM, RWKV_DIM, RWKV_DIM, GATE_LORA, DECAY_LORA, DECAY_LORA, AAA_LORA])
    r, k, v, gd, wdf, wdb, adf, adb = jnp.split(p, splits, axis=-1)
    heads = lambda t: t.reshape(B, S, RWKV_HEADS, HEAD_DIM)
    g = jnp.matmul(jax.nn.sigmoid(gd), f32(g_up))
    kk = heads(k * f32(k_k))
    kk = kk / jnp.maximum(jnp.sqrt(jnp.sum(kk * kk, axis=-1, keepdims=True)), L2_EPS)
    rh, vh = heads(r), heads(v)
    y = jnp.zeros_like(rh)
    bonus = jnp.zeros_like(rh)
    for direction, (wd, ad) in enumerate(((wdf, adf), (wdb, adb))):
        wl = -jax.nn.softplus(-(f32(w0[direction]) + jnp.matmul(jnp.tanh(wd), f32(w_up[direction])))) - 0.5
        decay = jnp.exp(-jnp.exp(wl))
        a = jax.nn.sigmoid(f32(a0[direction]) + jnp.matmul(ad, f32(a_up[direction])))
        kd = heads(k * (1.0 + (a - 1.0) * f32(k_a)))
        y = y + rwkv7_scan(rh, heads(decay), kd, vh, kk, heads(a), reverse=(direction == 1))
        bonus = bonus + jnp.sum(rh * kd * f32(r_k), axis=-1, keepdims=True) * vh
    mean = jnp.mean(y, axis=-1, keepdims=True)
    var = jnp.mean(jnp.square(y - mean), axis=-1, keepdims=True)
    yn = ((y - mean) * lax.rsqrt(var + GN_EPS)).reshape(B, S, RWKV_DIM) * f32(gn_w) + f32(gn_b)
    return (yn + bonus.reshape(B, S, RWKV_DIM)) * g


def setup_inputs(seed: int = 0) -> dict:
    key = jax.random.key(seed)
    ks = jax.random.split(key, 24)
    nrm = lambda kk_, shape, scale: scale * jax.random.normal(kk_, shape, jnp.float32)
    x = nrm(ks[0], (BATCH, SEQ, D_MODEL), 1.0)
    norm1_g = 1.0 + nrm(ks[1], (DEPTH, D_MODEL), 0.02)
    w_in = nrm(ks[2], (DEPTH, D_MODEL, N_IN_COLS), D_MODEL ** -0.5)
    tshift_mu = jax.random.uniform(ks[3], (DEPTH, RWKV_COLS), jnp.float32, 0.2, 0.8)
    w0 = jax.random.uniform(ks[4], (DEPTH, 2, RWKV_DIM), jnp.float32, -6.5, -0.5)
    w_lora_up = nrm(ks[5], (DEPTH, 2, DECAY_LORA, RWKV_DIM), 0.5 * DECAY_LORA ** -0.5)
    a0 = nrm(ks[6], (DEPTH, 2, RWKV_DIM), 0.5)
    a_lora_up = nrm(ks[7], (DEPTH, 2, AAA_LORA, RWKV_DIM), 0.5 * AAA_LORA ** -0.5)
    g_lora_up = nrm(ks[8], (DEPTH, GATE_LORA, RWKV_DIM), GATE_LORA ** -0.5)
    k_k = 0.85 + nrm(ks[9], (DEPTH, RWKV_DIM), 0.05)
    k_a = 1.0 + nrm(ks[10], (DEPTH, RWKV_DIM), 0.05)
    r_k = nrm(ks[11], (DEPTH, RWKV_HEADS, HEAD_DIM), 0.1)
    gn_w = 1.0 + nrm(ks[12], (DEPTH, RWKV_DIM), 0.02)
    gn_b = nrm(ks[13], (DEPTH, RWKV_DIM), 0.02)
    rel_bias = nrm(ks[14], (N_BUCKETS, ATTN_HEADS), 0.5)
    w_branch_rwkv = nrm(ks[15], (DEPTH, RWKV_DIM, D_MODEL), RWKV_DIM ** -0.5)
    w_branch_attn = nrm(ks[16], (DEPTH, ATTN_OUT_DIM, D_MODEL), ATTN_OUT_DIM ** -0.5)
    w_out = nrm(ks[17], (DEPTH, D_MODEL, D_MODEL), D_MODEL ** -0.5)
    norm2_g = 1.0 + nrm(ks[18], (DEPTH, D_MODEL), 0.02)
    w_mlp_in = nrm(ks[19], (DEPTH, D_MODEL, D_FF), D_MODEL ** -0.5)
    w_mlp_out = nrm(ks[20], (DEPTH, D_FF, D_MODEL), D_FF ** -0.5)
    final_g = 1.0 + nrm(ks[21], (D_MODEL,), 0.02)
    return {'x': x, 'norm1_g': norm1_g, 'w_in': w_in, 'tshift_mu': tshift_mu, 'w0': w0,
            'w_lora_up': w_lora_up, 'a0': a0, 'a_lora_up': a_lora_up, 'g_lora_up': g_lora_up,
            'k_k': k_k, 'k_a': k_a, 'r_k': r_k, 'gn_w': gn_w, 'gn_b': gn_b, 'rel_bias': rel_bias,
            'w_branch_rwkv': w_branch_rwkv, 'w_branch_attn': w_branch_attn, 'w_out': w_out,
            'norm2_g': norm2_g, 'w_mlp_in': w_mlp_in, 'w_mlp_out': w_mlp_out, 'final_g': final_g}


def reference(x, norm1_g, w_in, tshift_mu, w0, w_lora_up, a0, a_lora_up, g_lora_up, k_k, k_a,
              r_k, gn_w, gn_b, rel_bias, w_branch_rwkv, w_branch_attn, w_out, norm2_g,
              w_mlp_in, w_mlp_out, final_g):
    dtype = x.dtype
    cuts = [RWKV_COLS, RWKV_COLS + ATTN_DIM, RWKV_COLS + 2 * ATTN_DIM, RWKV_COLS + 3 * ATTN_DIM]
    h = x
    for l in range(DEPTH):
        u = rmsnorm(h, norm1_g[l])
        proj = jnp.matmul(u, w_in[l])
        slab, q, k, v, gates = jnp.split(proj, cuts, axis=-1)
        o_rwkv = rwkv_branch(slab, tshift_mu[l], w0[l], w_lora_up[l], a0[l], a_lora_up[l],
                             g_lora_up[l], k_k[l], k_a[l], r_k[l], gn_w[l], gn_b[l]).astype(dtype)
        o_attn = attention_branch(q, k, v, rel_bias).astype(dtype)
        gate = jax.nn.sigmoid(gates.astype(jnp.float32)).astype(dtype)
        g_rwkv, g_attn = jnp.split(gate, 2, axis=-1)
        merged = (g_rwkv * jnp.matmul(o_rwkv, w_branch_rwkv[l])
                  + g_attn * jnp.matmul(o_attn, w_branch_attn[l]))
        h = h + jnp.matmul(merged, w_out[l])
        u = rmsnorm(h, norm2_g[l])
        h = h + jnp.matmul(jnp.square(jax.nn.relu(jnp.matmul(u, w_mlp_in[l]))), w_mlp_out[l])
    return rmsnorm(h, final_g)
```

```python
import contextlib
import numpy as np
import ml_dtypes
import concourse.bass as bass
import concourse.mybir as mybir
from concourse.bass_utils import run_bass_kernel_spmd

F32 = mybir.dt.float32
BF16 = mybir.dt.bfloat16
AF = mybir.ActivationFunctionType
ALU = mybir.AluOpType
NCORES = 8


class Sched:
    ENGS = ("pe", "act", "dve", "pool", "sp")

    def __init__(self, nc):
        self.nc = nc
        self.ops = []
        self.last_w = {}
        self.readers = {}
        self.chan_cnt = {}
        self.chan_order = []
        self.bar = {}

    def op(self, eng, fn, reads=(), writes=(), chan=None):
        idx = len(self.ops)
        deps = set()
        for r in reads:
            if r in self.last_w:
                deps.add(self.last_w[r])
        for w in writes:
            if w in self.last_w:
                deps.add(self.last_w[w])
            deps.update(self.readers.get(w, ()))
        if eng in self.bar:
            deps.update(self.bar.pop(eng))
        deps.discard(idx)
        cdeps = []
        odeps = []
        for d in deps:
            o = self.ops[d]
            if o["chan"] is not None:
                cdeps.append((o["chan"], self.chan_cnt[o["chan"]] * 16))
            else:
                odeps.append(d)
        if chan is not None:
            if chan not in self.chan_cnt:
                self.chan_cnt[chan] = 0
                self.chan_order.append(chan)
            self.chan_cnt[chan] += 1
        self.ops.append(dict(eng=eng, fn=fn, odeps=odeps, cdeps=cdeps, chan=chan, waited=False))
        for r in reads:
            self.readers.setdefault(r, []).append(idx)
        for w in writes:
            self.last_w[w] = idx
            self.readers[w] = []
        return idx

    def barrier(self):
        last = {}
        for i, o in enumerate(self.ops):
            last[(o["eng"], o["chan"])] = i
        deps = set(last.values())
        for e in self.ENGS:
            self.bar[e] = set(deps)

    def run(self):
        nc = self.nc
        ops = self.ops
        for i, o in enumerate(ops):
            for d in o["odeps"]:
                p = ops[d]
                if p["eng"] == "pe" and o["eng"] == "pe":
                    continue
                p["waited"] = True
        cnt = {e: 0 for e in self.ENGS}
        for o in ops:
            if o["chan"] is None and o["waited"]:
                cnt[o["eng"]] += 1
                o["val"] = cnt[o["eng"]]
        import contextlib
        with contextlib.ExitStack() as st:
            esem = {e: st.enter_context(nc.semaphore("s_" + e)) for e in self.ENGS}
            csem = {c: st.enter_context(nc.semaphore("c_%d" % i)) for i, c in enumerate(self.chan_order)}
            block = st.enter_context(nc.Block())
            final_c = {c: self.chan_cnt[c] * 16 for c in self.chan_order}

            def emit(ename):
                def body(eng):
                    waited = {}
                    for o in ops:
                        if o["eng"] != ename:
                            continue
                        need = {}
                        for d in o["odeps"]:
                            p = ops[d]
                            if p["eng"] == "pe" and ename == "pe":
                                continue
                            k = ("e", p["eng"])
                            need[k] = max(need.get(k, 0), p["val"])
                        for c, v in o["cdeps"]:
                            k = ("c", c)
                            need[k] = max(need.get(k, 0), v)
                        for k, v in need.items():
                            if waited.get(k, 0) >= v:
                                continue
                            waited[k] = v
                            eng.wait_ge(esem[k[1]] if k[0] == "e" else csem[k[1]], v)
                        ins = o["fn"](eng)
                        if o["chan"] is not None:
                            ins.then_inc(csem[o["chan"]], 16)
                        elif o["waited"]:
                            ins.then_inc(esem[ename], 1)
                    if ename == "sp":
                        for c in self.chan_order:
                            eng.wait_ge(csem[c], final_c[c])
                        for e in self.ENGS:
                            if e != "sp" and cnt[e] > 0:
                                eng.wait_ge(esem[e], cnt[e])
                return body

            block.sync(emit("sp"))
            block.scalar(emit("act"))
            block.vector(emit("dve"))
            block.gpsimd(emit("pool"))
            block.tensor(emit("pe"))
        return cnt, final_c
class WRing:
    def __init__(self, S, st, nc, n=3, name="wr"):
        self.S = S
        self.n = n
        self.slots = [st.enter_context(nc.sbuf_tensor("%s%d" % (name, i), [128, 16, 512], BF16)) for i in range(n)]
        self.i = 0
        self.name = name

    def load(self, W, k0, nk, c0, ncols):
        s = self.i % self.n
        self.i += 1
        slot = self.slots[s]
        src = W[k0 * 128:(k0 + nk) * 128, c0:c0 + ncols].rearrange("(k p) c -> p k c", p=128)
        step = 4
        for ka in range(0, nk, step):
            kb = min(nk, ka + step)
            self.S.op("pool", lambda e, ka=ka, kb=kb, slot=slot, src=src: e.dma_start(out=slot[:, ka:kb, 0:ncols], in_=src[:, ka:kb, :]),
                      writes=[(self.name, s)], chan=(self.name, s))
        return slot, (self.name, s)


def emit_rmsnorm(S, nc, h, u, g, sq, rs, ones, pn, hres, ures, gres, out_fp32_inplace=False):
    for k in range(16):
        b = k % 2
        S.op("act", lambda e, k=k, b=b: e.activation(out=sq[b][:], in_=h[:, k, :], func=AF.Square),
             reads=[hres], writes=[("sq", b)])
        S.op("pe", lambda e, k=k, b=b: e.matmul(pn[:], lhsT=ones[:], rhs=sq[b][:], start=(k == 0), stop=(k == 15)),
             reads=[("sq", b), "ones"], writes=["pn"])
    S.op("act", lambda e: e.activation(out=rs[:], in_=pn[:], func=AF.Sqrt, scale=1.0 / 2048, bias=1e-6),
         reads=["pn"], writes=["rs"])
    S.op("dve", lambda e: e.reciprocal(out=rs[:], in_=rs[:]), reads=["rs"], writes=["rs"])
    for k in range(16):
        S.op("dve", lambda e, k=k: e.scalar_tensor_tensor(out=u[:, k, :], in0=h[:, k, :], scalar=g[:, k:k + 1], in1=rs[:], op0=ALU.mult, op1=ALU.mult),
             reads=[hres, gres, "rs"], writes=[ures])


def build_c(last):
    nc = bass.Bass("TRN2", target_bir_lowering=False)
    di = lambda name, shape, dt=F32: nc.dram_tensor(name, shape, dt, kind="ExternalInput").ap()
    hT = di("hT", [2048, 1024]); uT = di("uT", [2048, 1024], BF16); oT = di("oT", [1536, 1024], BF16)
    wg = di("wg", [2048, 4096]); wbr = di("wbr", [1024, 2048]); wba = di("wba", [512, 2048]); wout = di("wout", [2048, 2048])
    w1 = di("w1", [2048, 8192]); w2 = di("w2", [8192, 2048]); g2d = di("g2", [128, 16]); gnd = di("gn", [128, 16])
    ident = di("ident", [128, 128])
    if last:
        outd = nc.dram_tensor("out", [1024, 2048], F32, kind="ExternalOutput").ap()
    else:
        hTo = nc.dram_tensor("hTo", [2048, 1024], F32, kind="ExternalOutput").ap()
        uTo = nc.dram_tensor("uTo", [2048, 1024], BF16, kind="ExternalOutput").ap()
    with contextlib.ExitStack() as st:
        sb = lambda name, shape, dt: st.enter_context(nc.sbuf_tensor(name, shape, dt))
        ps = lambda name, shape, dt: st.enter_context(nc.psum_tensor(name, shape, dt))
        S = Sched(nc)
        h = sb("h", [128, 16, 512], F32); u = sb("u", [128, 16, 512], BF16); o = sb("o", [128, 12, 512], BF16)
        mg = sb("mg", [128, 16, 512], BF16); hid = sb("hid", [128, 32, 512], BF16)
        gt = [sb("gt%d" % i, [128, 4, 512], BF16) for i in range(2)]
        t1 = sb("t1", [128, 4, 512], F32); t2 = sb("t2", [128, 512], F32)
        rl = [sb("rl%d" % i, [128, 512], F32) for i in range(2)]
        sq = [sb("sq%d" % i, [128, 512], F32) for i in range(2)]
        rs = sb("rs", [128, 512], F32)
        g2 = sb("g2s", [128, 16], F32); gn = sb("gns", [128, 16], F32)
        ones = sb("ones", [128, 128], F32); idt = sb("idt", [128, 128], F32)
        if last:
            ot = [sb("ot%d" % i, [128, 2048], F32) for i in range(2)]
        ring = WRing(S, st, nc, 3)
        pA = [ps("pA%d" % i, [128, 512], F32) for i in range(4)]
        pB = [ps("pB%d" % i, [128, 512], F32) for i in range(3)]
        pn = ps("pn", [128, 512], F32)
        S.op("sp", lambda e: e.dma_start(out=g2[:], in_=g2d), writes=["g2"], chan="g2")
        S.op("sp", lambda e: e.dma_start(out=gn[:], in_=gnd), writes=["gn"], chan="gn")
        S.op("sp", lambda e: e.dma_start(out=idt[:], in_=ident), writes=["idt"], chan="idt")
        S.op("dve", lambda e: e.memset(ones[:], 1.0), writes=["ones"])
        hTr = hT.rearrange("(k p) t -> p k t", p=128); uTr = uT.rearrange("(k p) t -> p k t", p=128)
        oTr = oT.rearrange("(k p) t -> p k t", p=128)

        PBR = [("pB", 0), ("pB", 1), ("pB", 2), "pn"]
        PAR = [("pA", j) for j in range(4)]

        def mm_group(pbanks, pres, slot, sres, nk, rhs, rres, first=True, lastk=True, ncol=4):
            for j in range(ncol):
                for k in range(nk):
                    S.op("pe", lambda e, j=j, k=k: e.matmul(pbanks[j][:], lhsT=slot[:, k, j * 128:(j + 1) * 128], rhs=rhs[:, k, :],
                                                            start=(first and k == 0), stop=(lastk and k == nk - 1)),
                         reads=[sres, rres], writes=[pres[j]])

        for tb in range(2):
            tsl = slice(tb * 512, (tb + 1) * 512)
            S.op("sp", lambda e, tsl=tsl: e.dma_start(out=h[:], in_=hTr[:, :, tsl]), writes=["h"], chan="h")
            S.op("sp", lambda e, tsl=tsl: e.dma_start(out=u[:], in_=uTr[:, :, tsl]), writes=["u"], chan="u")
            S.op("sp", lambda e, tsl=tsl: e.dma_start(out=o[:], in_=oTr[:, :, tsl]), writes=["o"], chan="o")
            for cg in range(4):
                slot, sres = ring.load(wg, 0, 16, cg * 512, 512)
                mm_group(pA, PAR, slot, sres, 16, u, "u")
                for j in range(4):
                    S.op("act", lambda e, j=j: e.activation(out=gt[0][:, j, :], in_=pA[j][:], func=AF.Sigmoid),
                         reads=[("pA", j)], writes=[("gt0", j)])
                slot, sres = ring.load(wbr, 0, 8, cg * 512, 512)
                pBx = [pB[0], pB[1], pB[2], pn]
                mm_group(pBx, PBR, slot, sres, 8, o, "o")
                for j in range(4):
                    S.op("dve", lambda e, j=j, pBx=pBx: e.tensor_tensor(out=t1[:, j, :], in0=pBx[j][:], in1=gt[0][:, j, :], op=ALU.mult),
                         reads=[PBR[j], ("gt0", j)], writes=[("t1", j)])
                slot, sres = ring.load(wg, 0, 16, 2048 + cg * 512, 512)
                mm_group(pA, PAR, slot, sres, 16, u, "u")
                for j in range(4):
                    S.op("act", lambda e, j=j: e.activation(out=gt[1][:, j, :], in_=pA[j][:], func=AF.Sigmoid),
                         reads=[("pA", j)], writes=[("gt1", j)])
                slot, sres = ring.load(wba, 0, 4, cg * 512, 512)
                osub = o[:, 8:12, :]
                mm_group(pBx, PBR, slot, sres, 4, osub, "o")
                for j in range(4):
                    S.op("dve", lambda e, j=j, pBx=pBx: e.tensor_tensor(out=t2[:], in0=pBx[j][:], in1=gt[1][:, j, :], op=ALU.mult),
                         reads=[PBR[j], ("gt1", j)], writes=["t2"])
                    S.op("pool", lambda e, j=j, cg=cg: e.tensor_tensor(out=mg[:, cg * 4 + j, :], in0=t1[:, j, :], in1=t2[:], op=ALU.add),
                         reads=[("t1", j), "t2"], writes=["mg"])
            for cg in range(4):
                slot, sres = ring.load(wout, 0, 16, cg * 512, 512)
                mm_group(pA, PAR, slot, sres, 16, mg, "mg")
                for j in range(4):
                    S.op("dve", lambda e, j=j, cg=cg: e.tensor_tensor(out=h[:, cg * 4 + j, :], in0=h[:, cg * 4 + j, :], in1=pA[j][:], op=ALU.add),
                         reads=[("pA", j), "h"], writes=["h"])
            emit_rmsnorm(S, nc, h, u, g2, sq, rs, ones, pn, "h", "u", "g2")
            for half in range(2):
                for cg in range(8):
                    slot, sres = ring.load(w1, 0, 16, (half * 8 + cg) * 512, 512)
                    mm_group(pA, PAR, slot, sres, 16, u, "u")
                    for j in range(4):
                        b = j % 2
                        S.op("act", lambda e, j=j, b=b: e.activation(out=rl[b][:], in_=pA[j][:], func=AF.Relu),
                             reads=[("pA", j)], writes=[("rl", b)])
                        S.op("pool", lambda e, j=j, b=b, cg=cg: e.tensor_tensor(out=hid[:, cg * 4 + j, :], in0=rl[b][:], in1=rl[b][:], op=ALU.mult),
                             reads=[("rl", b)], writes=["hid"])
                for og in range(4):
                    pBx = [pB[0], pB[1], pB[2], pn]
                    for kh in range(2):
                        slot, sres = ring.load(w2, half * 32 + kh * 16, 16, og * 512, 512)
                        mm_group(pBx, PBR, slot, sres, 16, hid[:, kh * 16:(kh + 1) * 16, :], "hid", first=(kh == 0), lastk=(kh == 1))
                    for j in range(4):
                        S.op("dve", lambda e, j=j, og=og, pBx=pBx: e.tensor_tensor(out=h[:, og * 4 + j, :], in0=h[:, og * 4 + j, :], in1=pBx[j][:], op=ALU.add),
                             reads=[PBR[j], "h"], writes=["h"])
            if not last:
                emit_rmsnorm(S, nc, h, u, gn, sq, rs, ones, pn, "h", "u", "gn")
                S.op("sp", lambda e, tsl=tsl: e.dma_start(out=hTo.rearrange("(k p) t -> p k t", p=128)[:, :, tsl], in_=h[:]), reads=["h"], chan="oh")
                S.op("sp", lambda e, tsl=tsl: e.dma_start(out=uTo.rearrange("(k p) t -> p k t", p=128)[:, :, tsl], in_=u[:]), reads=["u"], chan="ou")
            else:
                for k in range(16):
                    b = k % 2
                    S.op("act", lambda e, k=k, b=b: e.activation(out=sq[b][:], in_=h[:, k, :], func=AF.Square), reads=["h"], writes=[("sq", b)])
                    S.op("pe", lambda e, k=k, b=b: e.matmul(pn[:], lhsT=ones[:], rhs=sq[b][:], start=(k == 0), stop=(k == 15)),
                         reads=[("sq", b), "ones"], writes=["pn"])
                S.op("act", lambda e: e.activation(out=rs[:], in_=pn[:], func=AF.Sqrt, scale=1.0 / 2048, bias=1e-6), reads=["pn"], writes=["rs"])
                S.op("dve", lambda e: e.reciprocal(out=rs[:], in_=rs[:]), reads=["rs"], writes=["rs"])
                for k in range(16):
                    S.op("dve", lambda e, k=k: e.scalar_tensor_tensor(out=h[:, k, :], in0=h[:, k, :], scalar=gn[:, k:k + 1], in1=rs[:], op0=ALU.mult, op1=ALU.mult),
                         reads=["h", "gn", "rs"], writes=["h"])
                for tt in range(4):
                    ob = tt % 2
                    for kg in range(4):
                        for j in range(4):
                            k = kg * 4 + j
                            S.op("pe", lambda e, k=k, j=j, kg=kg, tt=tt: e.transpose(out=pA[kg][:, j * 128:(j + 1) * 128], in_=h[:, k, tt * 128:(tt + 1) * 128], identity=idt[:]),
                                 reads=["h", "idt"], writes=[("pA", kg)])
                        S.op("act" if kg % 2 else "dve",
                             (lambda e, kg=kg, ob=ob: e.copy(out=ot[ob][:, kg * 512:(kg + 1) * 512], in_=pA[kg][:])) if kg % 2 else
                             (lambda e, kg=kg, ob=ob: e.tensor_copy(out=ot[ob][:, kg * 512:(kg + 1) * 512], in_=pA[kg][:])),
                             reads=[("pA", kg)], writes=[("ot", ob)])
                    r0 = tb * 512 + tt * 128
                    S.op("sp", lambda e, r0=r0, ob=ob: e.dma_start(out=outd[r0:r0 + 128, :], in_=ot[ob][:]), reads=[("ot", ob)], chan=("oo", ob))
        S.run()
    return nc
GROUPS = ((128, 1), (512, 4), (2048, 16))
S_LEN = 4096


def t5_bucket_np(rel):
    nb = 16
    max_exact = 8
    ret = np.where(rel > 0, nb, 0)
    n = np.abs(rel)
    nf = np.maximum(n, 1).astype(np.float32)
    large = max_exact + (np.log(nf / np.float32(max_exact)) / np.float32(np.log(1024 / max_exact)) * np.float32(nb - max_exact)).astype(np.int32)
    large = np.minimum(large, nb - 1)
    return ret + np.where(n < max_exact, n, large)


def bias_index_tables():
    kap = np.arange(128)[:, None]
    qi = np.arange(128)[None, :]
    out = []
    for (window, d) in GROUPS:
        da = kap - 64 - qi
        db = kap + 64 - qi
        delta = np.concatenate([da, db], axis=1)
        out.append((t5_bucket_np(delta * d), np.abs(delta) <= 64))
    return out


def emit_attention(S, nc, st, uTb, Wa, biasd, identd, oT_dst, sbufs):
    sb, ps = sbufs["sb"], sbufs["ps"]
    wa, ub, Qz0, Qz1, Qz2, K01, K2, V01, V2, Vt, E, Pf, Pm, accn, accd, onesk, idb, ot = [sbufs[k] for k in
        ("wa", "ub", "Qz0", "Qz1", "Qz2", "K01", "K2", "V01", "V2", "Vt", "E", "Pf", "Pm", "accn", "accd", "onesk", "idb", "ot")]
    pj, pS, pOD, pT = sbufs["pj"], sbufs["pS"], sbufs["pOD"], sbufs["pT"]
    uTr = uTb.rearrange("(k p) t -> p k t", p=128)
    for tblk in range(8):
        b = tblk % 2
        S.op("sp", lambda e, tblk=tblk, b=b: e.dma_start(out=ub[b][:], in_=uTr[:, :, tblk * 512:(tblk + 1) * 512]), writes=[("ub", b)], chan=("ub", b))
        for ct in range(5):
            M = 64 if ct == 3 else 128
            c0 = ct * 128 if ct < 4 else 448
            pb = (tblk * 5 + ct) % 2
            for k in range(16):
                S.op("pe", lambda e, k=k, M=M, c0=c0, pb=pb, b=b: e.matmul(pj[pb][0:M, :], lhsT=wa[:, k, c0:c0 + M], rhs=ub[b][:, k, :], start=(k == 0), stop=(k == 15)),
                     reads=[("ub", b), "wa"], writes=[("pj", pb)])
            t0 = tblk * 512
            if ct == 0:
                S.op("act", lambda e, pb=pb, t0=t0: e.copy(out=Qz0[0:64, t0:t0 + 512], in_=pj[pb][0:64, :]), reads=[("pj", pb)], writes=["Qz0"])
                S.op("dve", lambda e, pb=pb, t0=t0: e.tensor_copy(out=Qz1[64:128, :].rearrange("p (r i) -> p r i", r=4)[:, :, t0 // 4:t0 // 4 + 128],
                                                                   in_=pj[pb][64:128, :].rearrange("p (i r) -> p r i", r=4)), reads=[("pj", pb)], writes=["Qz1"])
            elif ct == 1:
                S.op("act", lambda e, pb=pb, t0=t0: e.copy(out=K01[0:64, 64 + t0:64 + t0 + 512], in_=pj[pb][0:64, :]), reads=[("pj", pb)], writes=["K01"])
                S.op("dve", lambda e, pb=pb, t0=t0: e.tensor_copy(out=K01[64:128, :].rearrange("p (r i) -> p r i", r=4)[:, :, 64 + t0 // 4:64 + t0 // 4 + 128],
                                                                   in_=pj[pb][64:128, :].rearrange("p (i r) -> p r i", r=4)), reads=[("pj", pb)], writes=["K01"])
            elif ct == 2:
                S.op("act", lambda e, pb=pb, t0=t0: e.copy(out=Qz2[0:64, :].rearrange("p (r i) -> p r i", r=16)[:, :, t0 // 16:t0 // 16 + 32],
                                                            in_=pj[pb][0:64, :].rearrange("p (i r) -> p r i", r=16)), reads=[("pj", pb)], writes=["Qz2"])
                S.op("dve", lambda e, pb=pb, t0=t0: e.tensor_copy(out=V2[64:128, :].rearrange("p (r i) -> p r i", r=16)[:, :, 64 + t0 // 16:64 + t0 // 16 + 32],
                                                                   in_=pj[pb][64:128, :].rearrange("p (i r) -> p r i", r=16)), reads=[("pj", pb)], writes=["V2"])
            elif ct == 3:
                S.op("act", lambda e, pb=pb, t0=t0: e.copy(out=K2[0:64, :].rearrange("p (r i) -> p r i", r=16)[:, :, 64 + t0 // 16:64 + t0 // 16 + 32],
                                                            in_=pj[pb][0:64, :].rearrange("p (i r) -> p r i", r=16)), reads=[("pj", pb)], writes=["K2"])
            else:
                S.op("act", lambda e, pb=pb, t0=t0: e.copy(out=V01[0:64, 64 + t0:64 + t0 + 512], in_=pj[pb][0:64, :]), reads=[("pj", pb)], writes=["V01"])
                S.op("dve", lambda e, pb=pb, t0=t0: e.tensor_copy(out=V01[64:128, :].rearrange("p (r i) -> p r i", r=4)[:, :, 64 + t0 // 4:64 + t0 // 4 + 128],
                                                                   in_=pj[pb][64:128, :].rearrange("p (i r) -> p r i", r=4)), reads=[("pj", pb)], writes=["V01"])
    vsrc = [V01[:, m * 128:(m + 1) * 128] for m in range(36)] + [V2[:, m * 128:(m + 1) * 128] for m in range(48)]
    for t0 in range(0, 84, 8):
        n = min(8, 84 - t0)
        pb = (t0 // 8) % 2
        for j in range(n):
            S.op("pe", lambda e, src=vsrc[t0 + j], j=j, pb=pb: e.transpose(out=pT[pb][:, j * 128:(j + 1) * 128], in_=src, identity=idb[:]),
                 reads=["V01", "V2", "idb"], writes=[("pT", pb)])
        S.op("dve" if pb else "act",
             (lambda e, t0=t0, n=n, pb=pb: e.tensor_copy(out=Vt[:, t0:t0 + n, :], in_=pT[pb][:, 0:n * 128].rearrange("p (j c) -> p j c", c=128))) if pb else
             (lambda e, t0=t0, n=n, pb=pb: e.copy(out=Vt[:, t0:t0 + n, :], in_=pT[pb][:, 0:n * 128].rearrange("p (j c) -> p j c", c=128))),
             reads=[("pT", pb)], writes=["Vt"])
    Qsrc = [lambda r: Qz0[:, :], lambda r: Qz1[:, :].rearrange("p (r i) -> p r i", r=4)[:, r, :],
            lambda r: Qz2[:, :].rearrange("p (r i) -> p r i", r=16)[:, r, :]]
    Ksrc = [lambda r: K01[:, 0:4224], lambda r: K01[:, :].rearrange("p (r i) -> p r i", r=4)[:, r, :],
            lambda r: K2[:, :].rearrange("p (r i) -> p r i", r=16)[:, r, :]]
    vt_of = [lambda r, m: (m, 0), lambda r, m: (r * 9 + m, 64), lambda r, m: (36 + r * 3 + m, 64)]
    qb = 0
    for g, (window, d) in enumerate(GROUPS):
        L = S_LEN // d
        nt = L // 128 + 1
        nq = L // 128
        for r in range(d):
            q_ap, k_ap = Qsrc[g](r), Ksrc[g](r)
            for m in range(nq):
                sbk = qb % 2
                S.op("pe", lambda e, k_ap=k_ap, q_ap=q_ap, m=m, sbk=sbk: e.matmul(pS[sbk][:, 0:128], lhsT=k_ap[:, m * 128:m * 128 + 128], rhs=q_ap[:, m * 128:m * 128 + 128], start=True, stop=True),
                     reads=["Qz0", "Qz1", "Qz2", "K01", "K2"], writes=[("pS", sbk)])
                S.op("pe", lambda e, k_ap=k_ap, q_ap=q_ap, m=m, sbk=sbk: e.matmul(pS[sbk][:, 128:256], lhsT=k_ap[:, m * 128 + 128:m * 128 + 256], rhs=q_ap[:, m * 128:m * 128 + 128], start=True, stop=True),
                     reads=["Qz0", "Qz1", "Qz2", "K01", "K2"], writes=[("pS", sbk)])
                S.op("act", lambda e, sbk=sbk: e.activation(out=Pf[sbk][:], in_=pS[sbk][:, 0:256], func=AF.Exp, scale=0.125), reads=[("pS", sbk)], writes=[("Pf", sbk)])
                S.op("dve", lambda e, sbk=sbk, g=g: e.tensor_tensor(out=Pm[sbk][:], in0=Pf[sbk][:], in1=E[:, g, :], op=ALU.mult), reads=[("Pf", sbk), "E"], writes=[("Pm", sbk)])
                half = m % 2
                ob = (qb // 2) % 2
                for (pp, pname, lh, coff) in ((pOD, "pOD", None, 0), (pOD, "pOD", "ones", 256)):
                    for kt in range(2):
                        if lh is None:
                            tix, c0v = vt_of[g](r, m + kt)
                            lhs = Vt[:, tix, c0v:c0v + 64]
                        else:
                            var = 1 if (m == 0 and kt == 0) else (2 if (m == nq - 1 and kt == 1) else 0)
                            lhs = onesk[:, var, :]
                        S.op("pe", lambda e, pp=pp, lhs=lhs, kt=kt, sbk=sbk, ob=ob, half=half, coff=coff: e.matmul(pp[ob][0:64, coff + half * 128:coff + (half + 1) * 128], lhsT=lhs, rhs=Pm[sbk][:, kt * 128:(kt + 1) * 128], start=(kt == 0), stop=(kt == 1)),
                             reads=[("Pm", sbk), "Vt", "onesk"], writes=[(pname, ob)])
                if half == 1:
                    m0 = m - 1
                    dst = lambda acc, d=d, r=r, m0=m0: acc[:, :].rearrange("p (i r) -> p r i", r=d)[:, r, m0 * 128:m0 * 128 + 256]
                    if g == 0:
                        S.op("act", lambda e, ob=ob, dst=dst: e.copy(out=dst(accn), in_=pOD[ob][0:64, 0:256]), reads=[("pOD", ob)], writes=["accn"])
                        S.op("act", lambda e, ob=ob, dst=dst: e.copy(out=dst(accd), in_=pOD[ob][0:64, 256:512]), reads=[("pOD", ob)], writes=["accd"])
                    else:
                        S.op("dve", lambda e, ob=ob, dst=dst: e.tensor_tensor(out=dst(accn), in0=dst(accn), in1=pOD[ob][0:64, 0:256], op=ALU.add), reads=[("pOD", ob), "accn"], writes=["accn"])
                        S.op("dve", lambda e, ob=ob, dst=dst: e.tensor_tensor(out=dst(accd), in0=dst(accd), in1=pOD[ob][0:64, 256:512], op=ALU.add), reads=[("pOD", ob), "accd"], writes=["accd"])
                qb += 1
    S.op("dve", lambda e: e.reciprocal(out=accd[:], in_=accd[:]), reads=["accd"], writes=["accd"])
    S.op("dve", lambda e: e.tensor_tensor(out=ot[:], in0=accn[:], in1=accd[:], op=ALU.mult), reads=["accn", "accd"], writes=["ot"])
    S.op("sp", lambda e: e.dma_start(out=oT_dst, in_=ot[:]), reads=["ot"], chan="oattn")


def alloc_attention(nc, st):
    sb = lambda name, shape, dt: st.enter_context(nc.sbuf_tensor(name, shape, dt))
    ps = lambda name, shape, dt: st.enter_context(nc.psum_tensor(name, shape, dt))
    d = dict(sb=sb, ps=ps)
    d["wa"] = sb("wa_sb", [128, 16, 576], BF16)
    d["ub"] = [sb("ub%d" % i, [128, 16, 512], BF16) for i in range(2)]
    d["Qz0"] = sb("Qz0", [128, 4096], BF16)
    d["Qz1"] = sb("Qz1", [128, 4096], BF16)
    d["Qz2"] = sb("Qz2", [128, 4096], BF16)
    d["K01"] = sb("K01", [128, 4608], BF16)
    d["K2"] = sb("K2", [128, 6144], BF16)
    d["V01"] = sb("V01", [128, 4608], BF16)
    d["V2"] = sb("V2", [128, 6144], BF16)
    d["Vt"] = sb("Vt", [128, 84, 128], BF16)
    d["E"] = sb("E", [128, 3, 256], F32)
    d["Pf"] = [sb("Pf%d" % i, [128, 256], F32) for i in range(2)]
    d["Pm"] = [sb("Pm%d" % i, [128, 256], BF16) for i in range(2)]
    d["accn"] = sb("accn", [64, 4096], F32)
    d["accd"] = sb("accd", [64, 4096], F32)
    d["onesk"] = sb("onesk", [128, 3, 64], BF16)
    d["idb"] = sb("idb", [128, 128], BF16)
    d["ot"] = sb("ot", [64, 4096], BF16)
    d["pj"] = [ps("pj%d" % i, [128, 512], F32) for i in range(2)]
    d["pS"] = [ps("pS%d" % i, [128, 512], F32) for i in range(2)]
    d["pOD"] = [ps("pOD%d" % i, [128, 512], F32) for i in range(2)]
    d["pT"] = [ps("pT%d" % i, [128, 1024], BF16) for i in range(2)]
    return d


def build_b_attn():
    nc = bass.Bass("TRN2", target_bir_lowering=False)
    di = lambda name, shape, dt=F32: nc.dram_tensor(name, shape, dt, kind="ExternalInput").ap()
    uT = di("uT", [2048, 8192], BF16)
    wad = di("wa", [2048, 576])
    biasd = di("biasT", [128, 3, 256])
    identd = di("identb", [128, 128], BF16)
    oT = nc.dram_tensor("oT", [64, 8192], BF16, kind="ExternalOutput").ap()
    with contextlib.ExitStack() as st:
        S = Sched(nc)
        A = alloc_attention(nc, st)
        emit_attn_setup(S, nc, A, wad, biasd, identd)
        for b in range(2):
            emit_attention(S, nc, st, uT[:, b * 4096:(b + 1) * 4096], wad, biasd, identd, oT[:, b * 4096:(b + 1) * 4096], A)
        S.run()
    return nc


def emit_attn_setup(S, nc, A, wad, biasd, identd):
    wa, E, onesk, idb = A["wa"], A["E"], A["onesk"], A["idb"]
    src = wad.rearrange("(k p) c -> p k c", p=128)
    for ka in range(0, 16, 4):
        S.op("pool", lambda e, ka=ka: e.dma_start(out=wa[:, ka:ka + 4, :], in_=src[:, ka:ka + 4, :]), writes=["wa"], chan="wa")
    S.op("sp", lambda e: e.dma_start(out=E[:], in_=biasd), writes=["E"], chan="E")
    S.op("sp", lambda e: e.dma_start(out=idb[:], in_=identd), writes=["idb"], chan="idb")
    S.op("act", lambda e: e.activation(out=E[:], in_=E[:], func=AF.Exp), reads=["E"], writes=["E"])
    S.op("dve", lambda e: e.memset(onesk[:], 1.0), writes=["onesk"])
    S.op("dve", lambda e: e.memset(onesk[0:64, 1, :], 0.0), writes=["onesk"])
    S.op("dve", lambda e: e.memset(onesk[64:128, 2, :], 0.0), writes=["onesk"])
    for name in ("K01", "V01", "K2", "V2", "Qz0", "Qz1", "Qz2"):
        S.op("pool", lambda e, name=name: e.memset(A[name][:], 0.0), writes=[name])
R1_OUT = ("p_r", "p_k", "p_v", "kk", "a_f", "a_b", "l_f", "l_b")


def build_r1():
    nc = bass.Bass("TRN2", target_bir_lowering=False)
    di = lambda name, shape, dt=F32: nc.dram_tensor(name, shape, dt, kind="ExternalInput").ap()
    uT = di("uT", [2048, 8192], BF16)
    wrd = di("wr", [2048, 1024])
    pard = di("par", [128, 32])
    gupd = di("gup", [256, 128]); wupd = di("wup", [2, 96, 128]); aupd = di("aup", [2, 96, 128])
    bonesd = di("bones", [128, 128])
    outs = {n: nc.dram_tensor(n, [128, 8192], F32, kind="ExternalOutput").ap() for n in R1_OUT}
    g_out = nc.dram_tensor("g", [128, 8192], BF16, kind="ExternalOutput").ap()
    with contextlib.ExitStack() as st:
        sb = lambda name, shape, dt: st.enter_context(nc.sbuf_tensor(name, shape, dt))
        ps = lambda name, shape, dt: st.enter_context(nc.psum_tensor(name, shape, dt))
        S = Sched(nc)
        wr = sb("wr_sb", [128, 16, 1024], BF16)
        ub = [sb("ub%d" % i, [128, 16, 512], BF16) for i in range(2)]
        raw = [sb("raw%d" % i, [128, 4098], BF16) for i in range(9)]
        par = sb("par_sb", [128, 32], F32)
        gup = sb("gup_sb", [128, 2, 128], BF16); wup = sb("wup_sb", [96, 2, 128], BF16); aup = sb("aup_sb", [96, 2, 128], BF16)
        bones = sb("bones_sb", [128, 128], F32)
        t1 = sb("t1", [128, 4096], F32); t2 = sb("t2", [128, 4096], F32); t3 = sb("t3", [128, 4096], F32)
        nl = sb("nl", [128, 2, 4096], BF16)
        pj = [ps("pj%d" % i, [128, 512], F32) for i in range(4)]
        pq = [ps("pq%d" % i, [128, 512], F32) for i in range(4)]
        src = wrd.rearrange("(k p) c -> p k c", p=128)
        for ka in range(0, 16, 2):
            S.op("pool", lambda e, ka=ka: e.dma_start(out=wr[:, ka:ka + 2, :], in_=src[:, ka:ka + 2, :]), writes=["wr"], chan="wr")
        S.op("sp", lambda e: e.dma_start(out=par[:], in_=pard), writes=["par"], chan="par")
        S.op("sp", lambda e: e.dma_start(out=bones[:], in_=bonesd), writes=["bones"], chan="bones")
        S.op("pool", lambda e: e.dma_start(out=gup[:], in_=gupd.rearrange("(k p) c -> p k c", p=128)), writes=["gup"], chan="gup")
        S.op("pool", lambda e: e.dma_start(out=wup[:], in_=wupd.rearrange("d p c -> p d c")), writes=["wup"], chan="wup")
        S.op("pool", lambda e: e.dma_start(out=aup[:], in_=aupd.rearrange("d p c -> p d c")), writes=["aup"], chan="aup")
        S.op("dve", lambda e: e.tensor_scalar(out=par[:, 9:18], in0=par[:, 0:9], scalar1=-1.0, scalar2=1.0, op0=ALU.mult, op1=ALU.add), reads=["par"], writes=["par"])
        S.op("dve", lambda e: e.tensor_scalar(out=par[:, 18:27], in0=par[:, 0:9], scalar1=0.5, scalar2=None, op0=ALU.mult), reads=["par"], writes=["par"])
        for i in range(9):
            S.op("pool", lambda e, i=i: e.memset(raw[i][:], 0.0), writes=[("raw", i)])
        rows = [128] * 5 + [96] * 4
        uTr = uT.rearrange("(k p) t -> p k t", p=128)
        for b in range(2):
            tok0 = b * 4096
            for tblk in range(8):
                bb = tblk % 2
                S.op("sp", lambda e, tblk=tblk, bb=bb, tok0=tok0: e.dma_start(out=ub[bb][:], in_=uTr[:, :, tok0 + tblk * 512:tok0 + (tblk + 1) * 512]), writes=[("ub", bb)], chan=("ub", bb))
                for ct in range(9):
                    M = rows[ct]
                    c0 = ct * 128 if ct < 5 else 640 + (ct - 5) * 96
                    pb = ct % 4
                    for k in range(16):
                        S.op("pe", lambda e, k=k, M=M, c0=c0, pb=pb, bb=bb: e.matmul(pj[pb][0:M, :], lhsT=wr[:, k, c0:c0 + M], rhs=ub[bb][:, k, :], start=(k == 0), stop=(k == 15)),
                             reads=[("ub", bb), "wr"], writes=[("pj", pb)])
                    S.op("act" if ct % 2 else "dve",
                         (lambda e, ct=ct, M=M, pb=pb, tblk=tblk: e.copy(out=raw[ct][0:M, 1 + tblk * 512:1 + (tblk + 1) * 512], in_=pj[pb][0:M, :])) if ct % 2 else
                         (lambda e, ct=ct, M=M, pb=pb, tblk=tblk: e.tensor_copy(out=raw[ct][0:M, 1 + tblk * 512:1 + (tblk + 1) * 512], in_=pj[pb][0:M, :])),
                         reads=[("pj", pb)], writes=[("raw", ct)])
            tsl = slice(tok0, tok0 + 4096)

            def shift(ct, dst):
                M = rows[ct]
                S.op("dve", lambda e: e.tensor_tensor(out=t1[0:M, :], in0=raw[ct][0:M, 0:4096], in1=raw[ct][0:M, 2:4098], op=ALU.add), reads=[("raw", ct)], writes=["t1"])
                S.op("act", lambda e: e.activation(out=t2[0:M, :], in_=raw[ct][0:M, 1:4097], func=AF.Copy, scale=par[0:M, 9 + ct:10 + ct]), reads=[("raw", ct), "par"], writes=["t2"])
                S.op("dve", lambda e: e.scalar_tensor_tensor(out=dst[0:M, :], in0=t1[0:M, :], scalar=par[0:M, 18 + ct:19 + ct], in1=t2[0:M, :], op0=ALU.mult, op1=ALU.add),
                     reads=["t1", "t2", "par"], writes=["t3"])

            def store(name, srct, res, tsl=tsl):
                S.op("sp", lambda e: e.dma_start(out=outs[name][:, tsl], in_=srct[:]), reads=[res], chan="o_" + name)

            shift(0, t3); store("p_r", t3, "t3")
            shift(2, t3); store("p_v", t3, "t3")
            shift(1, t3); store("p_k", t3, "t3")
            S.op("dve", lambda e: e.tensor_scalar(out=t1[:], in0=t3[:], scalar1=par[:, 27:28], scalar2=None, op0=ALU.mult), reads=["t3", "par"], writes=["t1"])
            S.op("act", lambda e: e.activation(out=t2[:], in_=t1[:], func=AF.Square), reads=["t1"], writes=["t2"])
            for blk in range(8):
                pb = blk % 4
                bs = slice(blk * 512, (blk + 1) * 512)
                S.op("pe", lambda e, pb=pb, bs=bs: e.matmul(pq[pb][:], lhsT=bones[:], rhs=t2[:, bs], start=True, stop=True), reads=["t2", "bones"], writes=[("pq", pb)])
                S.op("act", lambda e, pb=pb, bs=bs: e.activation(out=t3[:, bs], in_=pq[pb][:], func=AF.Sqrt), reads=[("pq", pb)], writes=["t3"])
            S.op("dve", lambda e: e.tensor_scalar(out=t3[:], in0=t3[:], scalar1=1e-12, scalar2=None, op0=ALU.max), reads=["t3"], writes=["t3"])
            S.op("dve", lambda e: e.reciprocal(out=t3[:], in_=t3[:]), reads=["t3"], writes=["t3"])
            S.op("dve", lambda e: e.tensor_tensor(out=t3[:], in0=t3[:], in1=t1[:], op=ALU.mult), reads=["t3", "t1"], writes=["t3"])
            store("kk", t3, "t3")
            for j in range(2):
                shift(3 + j, t3)
                S.op("act", lambda e, j=j: e.activation(out=nl[:, j, :], in_=t3[:], func=AF.Sigmoid), reads=["t3"], writes=["nl"])
            for blk in range(8):
                pb = blk % 4
                bs = slice(blk * 512, (blk + 1) * 512)
                for j in range(2):
                    S.op("pe", lambda e, pb=pb, bs=bs, j=j: e.matmul(pq[pb][:], lhsT=gup[:, j, :], rhs=nl[:, j, bs], start=(j == 0), stop=(j == 1)), reads=["nl", "gup"], writes=[("pq", pb)])
                S.op("act", lambda e, pb=pb, blk=blk: e.copy(out=raw[3][:, 1 + blk * 512:1 + (blk + 1) * 512], in_=pq[pb][:]), reads=[("pq", pb)], writes=[("raw", 3)])
            S.op("sp", lambda e, tsl=tsl: e.dma_start(out=g_out[:, tsl], in_=raw[3][:, 1:4097]), reads=[("raw", 3)], chan="o_g")
            for d in range(2):
                shift(5 + d, t3)
                S.op("act", lambda e: e.activation(out=nl[0:96, 0, :], in_=t3[0:96, :], func=AF.Tanh), reads=["t3"], writes=["nl"])
                for blk in range(8):
                    pb = blk % 4
                    bs = slice(blk * 512, (blk + 1) * 512)
                    S.op("pe", lambda e, pb=pb, bs=bs, d=d: e.matmul(pq[pb][:], lhsT=wup[:, d, :], rhs=nl[0:96, 0, bs], start=True, stop=True), reads=["nl", "wup"], writes=[("pq", pb)])
                    S.op("act", lambda e, pb=pb, bs=bs, d=d: e.activation(out=t1[:, bs], in_=pq[pb][:], func=AF.Sigmoid, bias=par[:, 28 + d:29 + d]), reads=[("pq", pb), "par"], writes=["t1"])
                S.op("dve", lambda e: e.tensor_scalar(out=t1[:], in0=t1[:], scalar1=-0.6065306597126334, scalar2=None, op0=ALU.mult), reads=["t1"], writes=["t1"])
                store("l_f" if d == 0 else "l_b", t1, "t1")
                shift(7 + d, t3)
                S.op("act", lambda e: e.copy(out=nl[0:96, 1, :], in_=t3[0:96, :]), reads=["t3"], writes=["nl"])
                for blk in range(8):
                    pb = blk % 4
                    bs = slice(blk * 512, (blk + 1) * 512)
                    S.op("pe", lambda e, pb=pb, bs=bs, d=d: e.matmul(pq[pb][:], lhsT=aup[:, d, :], rhs=nl[0:96, 1, bs], start=True, stop=True), reads=["nl", "aup"], writes=[("pq", pb)])
                    S.op("act", lambda e, pb=pb, bs=bs, d=d: e.activation(out=t2[:, bs], in_=pq[pb][:], func=AF.Sigmoid, bias=par[:, 30 + d:31 + d]), reads=[("pq", pb), "par"], writes=["t2"])
                store("a_f" if d == 0 else "a_b", t2, "t2")
        S.run()
    return nc


def r1_inputs(inp, l, c):
    hs = [2 * c, 2 * c + 1]
    ch = np.concatenate([np.arange(h * 64, h * 64 + 64) for h in hs])
    cols = np.concatenate([ch, 1024 + ch, 2048 + ch, np.arange(3072, 3712)])
    wr = np.ascontiguousarray(inp["w_in"][l][:, cols])
    mu = inp["tshift_mu"][l][cols]
    par = np.zeros((128, 32), np.float32)
    for ct in range(5):
        par[:, ct] = mu[ct * 128:(ct + 1) * 128]
    for ct in range(5, 9):
        par[:96, ct] = mu[640 + (ct - 5) * 96:640 + (ct - 4) * 96]
    par[:, 27] = inp["k_k"][l][ch]
    par[:, 28] = inp["w0"][l][0][ch]; par[:, 29] = inp["w0"][l][1][ch]
    par[:, 30] = inp["a0"][l][0][ch]; par[:, 31] = inp["a0"][l][1][ch]
    bones = np.kron(np.eye(2, dtype=np.float32), np.ones((64, 64), np.float32))
    return dict(wr=wr, par=par, gup=np.ascontiguousarray(inp["g_lora_up"][l][:, ch]), wup=np.ascontiguousarray(inp["w_lora_up"][l][:, :, ch]),
                aup=np.ascontiguousarray(inp["a_lora_up"][l][:, :, ch]), bones=bones)
R2_IN = ("p_r", "p_k", "p_v", "kk", "a_f", "a_b", "l_f", "l_b")


def r2_consts():
    p = np.arange(128)
    same = (p[:, None] // 64) == (p[None, :] // 64)
    s = p[:, None] % 64
    t = p[None, :] % 64
    masks = np.zeros((128, 2, 3, 512), np.float32)
    for d in range(2):
        strict = ((s < t) if d == 0 else (s > t)) & same
        incl = ((s <= t) if d == 0 else (s >= t)) & same
        a = np.concatenate([-strict.astype(np.float32), -incl.astype(np.float32)], axis=1)
        masks[:, d, 0] = np.tile(a, (1, 2))
        masks[:, d, 1] = np.tile(-a, (1, 2))
        masks[:, d, 2] = np.tile(-strict.T.astype(np.float32), (1, 4))
    ident4 = np.tile(np.eye(128, dtype=np.float32), (1, 4))
    lvl = np.zeros((128, 6, 512), np.float32)
    for i, sz in enumerate((1, 2, 4, 8, 16, 32)):
        m = ((p[:, None] // (2 * sz)) == (p[None, :] // (2 * sz))) & ((p[:, None] // sz) != (p[None, :] // sz))
        lvl[:, i] = np.tile(m.astype(np.float32), (1, 4))
    m01 = np.ones((128, 512), np.float32)
    m01[:, ::64] = 0.0
    bones = np.kron(np.eye(2, dtype=np.float32), np.ones((64, 64), np.float32))
    return dict(masks=masks, lvl=lvl, ident4=ident4, m01=m01, bones=bones, identb=np.eye(128).astype(ml_dtypes.bfloat16))


def build_r2():
    nc = bass.Bass("TRN2", target_bir_lowering=False)
    di = lambda name, shape, dt=F32: nc.dram_tensor(name, shape, dt, kind="ExternalInput").ap()
    X = {n: di(n, [128, 8192]) for n in R2_IN}
    gd = di("g", [128, 8192], BF16)
    par2d = di("par2", [128, 8])
    masksd = di("masks", [128, 2, 3, 512]); lvld = di("lvl", [128, 6, 512]); ident4d = di("ident4", [128, 512]); m01d = di("m01", [128, 512])
    bonesd = di("bones", [128, 128]); identbd = di("identb", [128, 128], BF16)
    oT = nc.dram_tensor("oT", [128, 8192], BF16, kind="ExternalOutput").ap()
    with contextlib.ExitStack() as st:
        sb = lambda name, shape, dt: st.enter_context(nc.sbuf_tensor(name, shape, dt))
        ps = lambda name, shape, dt: st.enter_context(nc.psum_tensor(name, shape, dt))
        S = Sched(nc)
        par2 = sb("par2s", [128, 8], F32); masks = sb("maskss", [128, 2, 3, 512], F32); lvl = sb("lvls", [128, 6, 512], F32); ident4 = sb("ident4s", [128, 512], F32)
        m01 = sb("m01s", [128, 512], F32); bones = sb("boness", [128, 128], F32); idb = sb("idbs", [128, 128], BF16)
        NB = 2
        inb = [{n: sb("in_%s%d" % (n, i), [128, 512], F32) for n in ("p_r", "p_k", "p_v", "kk", "a", "l")} for i in range(NB)]
        tmp = {n: sb("tmp_" + n, [128, 512], F32) for n in ("f", "kd", "ka", "Lc", "Linc", "w1", "w2")}
        ltot = sb("ltot", [128, 8], F32); wtot = [sb("wtot%d" % i, [128, 8], F32) for i in range(NB)]
        BDn = ("KR", "BB", "KT", "BH", "KH", "VV")
        BD = [{n: sb("bd_%s%d" % (n, i), [128, 8, 256 if n == "KR" else 128], BF16) for n in BDn} for i in range(NB)]
        AB2 = [sb("AB2_%d" % i, [128, 8, 256], BF16) for i in range(NB)]
        AK2 = [sb("AK2_%d" % i, [128, 8, 256], BF16) for i in range(NB)]
        Pb = [sb("Pb%d" % i, [128, 8, 128], BF16) for i in range(2)]
        Qb = [sb("Qb%d" % i, [128, 8, 128], BF16) for i in range(2)]
        MT = [sb("MT%d" % i, [128, 8, 128], BF16) for i in range(NB)]
        Wt = sb("Wt", [128, 8, 128], BF16); Q0b = sb("Q0b", [128, 8, 128], BF16)
        TT = [{n: sb("tt_%s%d" % (n, i), [128, 8, 128], BF16) for n in ("BH", "KH", "VV")} for i in range(NB)]
        T = sb("T", [128, 128], F32); Tb = sb("Tb", [128, 128], BF16)
        Xs = sb("Xs", [128, 128], BF16); Us = sb("Us", [128, 128], BF16)
        y = sb("y", [128, 4096], F32)
        ob = {n: sb("ob_" + n, [128, 512], F32) for n in ("p_r", "p_k", "p_v", "a_f", "a_b", "t1", "t2", "t3")}
        ogb = sb("ogb", [128, 512], BF16); oo = sb("oo", [128, 512], BF16)
        pa = ps("pa", [128, 512], F32); pk = ps("pk", [128, 512], F32)
        pi = [ps("pi%d" % i, [128, 512], F32) for i in range(2)]
        ptr = ps("ptr", [128, 1024], BF16)
        pS = ps("pS", [128, 512], F32)
        pY = [ps("pY%d" % i, [128, 512], F32) for i in range(2)]
        for (t_, d_, nm) in ((par2, par2d, "par2"), (masks, masksd, "masks"), (lvl, lvld, "lvl"), (ident4, ident4d, "ident4"), (m01, m01d, "m01"), (bones, bonesd, "bones"), (idb, identbd, "idb")):
            S.op("sp", lambda e, t_=t_, d_=d_: e.dma_start(out=t_[:], in_=d_), writes=[nm], chan=nm)
        S.op("dve", lambda e: e.tensor_scalar(out=par2[:, 1:2], in0=par2[:, 0:1], scalar1=-1.0, scalar2=1.0, op0=ALU.mult, op1=ALU.add), reads=["par2"], writes=["par2"])
        S.op("dve", lambda e: e.tensor_scalar(out=par2[:, 5:6], in0=par2[:, 0:1], scalar1=-2.0, scalar2=2.0, op0=ALU.mult, op1=ALU.add), reads=["par2"], writes=["par2"])
        for i in range(NB):
            for n in BDn:
                S.op("pool", lambda e, i=i, n=n: e.memset(BD[i][n][:], 0.0), writes=[("bd", i)])

        v3 = lambda ap: ap.rearrange("p (c t) -> p c t", t=64)

        def prep_group(b, d, gi, sl):
            tok = b * 4096 + gi * 512
            I = inb[sl]
            for n in ("p_r", "p_k", "p_v", "kk"):
                S.op("sp", lambda e, n=n: e.dma_start(out=I[n][:], in_=X[n][:, tok:tok + 512]), writes=[("in", sl)], chan=("in", sl, n))
            sfx = "_f" if d == 0 else "_b"
            S.op("sp", lambda e: e.dma_start(out=I["a"][:], in_=X["a" + sfx][:, tok:tok + 512]), writes=[("in", sl)], chan=("in", sl, "a"))
            S.op("sp", lambda e: e.dma_start(out=I["l"][:], in_=X["l" + sfx][:, tok:tok + 512]), writes=[("in", sl)], chan=("in", sl, "l"))
            R = [("in", sl)]
            f, kd, ka, Lc, Linc, w1, w2 = [tmp[n] for n in ("f", "kd", "ka", "Lc", "Linc", "w1", "w2")]
            S.op("dve", lambda e: e.tensor_scalar(out=f[:], in0=I["a"][:], scalar1=par2[:, 0:1], scalar2=par2[:, 1:2], op0=ALU.mult, op1=ALU.add), reads=R + ["par2"], writes=["f"])
            S.op("dve", lambda e: e.tensor_tensor(out=kd[:], in0=I["p_k"][:], in1=f[:], op=ALU.mult), reads=R + ["f"], writes=["kd"])
            S.op("pool", lambda e: e.tensor_tensor(out=ka[:], in0=I["kk"][:], in1=I["a"][:], op=ALU.mult), reads=R, writes=["ka"])
            S.op("dve", lambda e: e.tensor_tensor_scan(out=Lc[:], data0=m01[:], data1=I["l"][:], initial=0.0, op0=ALU.mult, op1=ALU.add), reads=R + ["m01"], writes=["Lc"])
            S.op("dve", lambda e: e.tensor_copy(out=ltot[:], in_=v3(Lc[:])[:, :, 63]), reads=["Lc"], writes=["ltot"])
            lt_b = ltot[:].unsqueeze(2).to_broadcast([128, 8, 64])
            if d == 0:
                LI = Lc
                lres = "Lc"
            else:
                S.op("dve", lambda e: e.tensor_tensor(out=v3(Linc[:]), in0=lt_b, in1=v3(Lc[:]), op=ALU.subtract), reads=["Lc", "ltot"], writes=["Linc"])
                S.op("dve", lambda e: e.tensor_tensor(out=Linc[:], in0=Linc[:], in1=I["l"][:], op=ALU.add), reads=["Linc"] + R, writes=["Linc"])
                LI = Linc
                lres = "Linc"
            S.op("act", lambda e: e.activation(out=wtot[sl][:], in_=ltot[:], func=AF.Exp), reads=["ltot"], writes=[("wtot", sl)])
            Bd = BD[sl]

            def bd_write(name, c0, in0, in1, neg=False):
                for hh in range(2):
                    prt = slice(hh * 64, hh * 64 + 64)
                    o = Bd[name][prt, :, c0 + hh * 64:c0 + hh * 64 + 64]
                    if neg:
                        S.op("dve", lambda e, o=o, prt=prt: e.scalar_tensor_tensor(out=o, in0=v3(in0[prt, :]), scalar=-1.0, in1=v3(in1[prt, :]), op0=ALU.mult, op1=ALU.mult),
                             reads=R + ["ka", "kd", "w1", "w2"], writes=[("bd", sl)])
                    elif in1 is None:
                        S.op("pool", lambda e, o=o, prt=prt: e.tensor_copy(out=o, in_=v3(in0[prt, :])), reads=R, writes=[("bd", sl)])
                    else:
                        S.op("pool" if hh else "dve", lambda e, o=o, prt=prt: e.tensor_tensor(out=o, in0=v3(in0[prt, :]), in1=v3(in1[prt, :]), op=ALU.mult),
                             reads=R + ["ka", "kd", "w1", "w2"], writes=[("bd", sl)])

            S.op("act", lambda e: e.activation(out=w1[:], in_=LI[:], func=AF.Exp), reads=[lres], writes=["w1"])
            bd_write("KR", 128, I["p_r"], w1)
            S.op("dve", lambda e: e.tensor_tensor(out=w2[:], in0=LI[:], in1=I["l"][:], op=ALU.subtract), reads=[lres] + R, writes=["w2"])
            S.op("act", lambda e: e.activation(out=w2[:], in_=w2[:], func=AF.Exp), reads=["w2"], writes=["w2"])
            bd_write("KR", 0, I["kk"], w2)
            S.op("act", lambda e: e.activation(out=w1[:], in_=LI[:], func=AF.Exp, scale=-1.0), reads=[lres], writes=["w1"])
            bd_write("BB", 0, ka, w1)
            bd_write("KT", 0, kd, w1)
            S.op("dve", lambda e: e.tensor_tensor(out=v3(w2[:]), in0=lt_b, in1=v3(LI[:]), op=ALU.subtract), reads=[lres, "ltot"], writes=["w2"])
            S.op("act", lambda e: e.activation(out=w2[:], in_=w2[:], func=AF.Exp), reads=["w2"], writes=["w2"])
            bd_write("BH", 0, ka, w2, neg=True)
            bd_write("KH", 0, kd, w2)
            bd_write("VV", 0, I["p_v"], None)
            BR = [("bd", sl)]
            for c in range(8):
                o2 = (c % 2) * 256
                S.op("pe", lambda e, c=c, o2=o2: e.matmul(pa[:, o2:o2 + 256], lhsT=Bd["BB"][:, c, :], rhs=Bd["KR"][:, c, :], start=True, stop=True), reads=BR, writes=["pa"])
                S.op("pe", lambda e, c=c, o2=o2: e.matmul(pk[:, o2:o2 + 256], lhsT=Bd["KT"][:, c, :], rhs=Bd["KR"][:, c, :], start=True, stop=True), reads=BR, writes=["pk"])
                if c % 2 == 1:
                    S.op("dve", lambda e, c=c: e.tensor_tensor(out=AB2[sl][:, c - 1:c + 1, :], in0=pa[:].rearrange("p (c x) -> p c x", c=2), in1=masks[:, d, 0, :].rearrange("p (c x) -> p c x", c=2), op=ALU.mult),
                         reads=["pa", "masks"], writes=[("AB2", sl)])
                    S.op("dve", lambda e, c=c: e.tensor_tensor(out=AK2[sl][:, c - 1:c + 1, :], in0=pk[:].rearrange("p (c x) -> p c x", c=2), in1=masks[:, d, 1, :].rearrange("p (c x) -> p c x", c=2), op=ALU.mult),
                         reads=["pk", "masks"], writes=[("AK2", sl)])
            for c in range(8):
                pb = c // 4
                o4 = (c % 4) * 128
                S.op("pe", lambda e, c=c, pb=pb, o4=o4: e.matmul(pi[pb][:, o4:o4 + 128], lhsT=Bd["KR"][:, c, 0:128], rhs=Bd["BB"][:, c, :], start=True, stop=True), reads=BR, writes=[("pi", pb)])
            for pb in range(2):
                S.op("dve", lambda e, pb=pb: e.tensor_tensor(out=Q0b[:, pb * 4:pb * 4 + 4, :], in0=pi[pb][:].rearrange("p (c x) -> p c x", c=4), in1=masks[:, d, 2, :].rearrange("p (c x) -> p c x", c=4), op=ALU.mult),
                     reads=[("pi", pb), "masks"], writes=["Q0b"])
            W = MT[sl]
            NOs, NOTs, T1, T1p = Pb[0], Pb[1], Qb[0], Qb[1]
            c4 = lambda ap: ap.rearrange("p (c x) -> p c x", c=4)

            def masked(dst, dres, src, sres, lev):
                for hf in range(2):
                    S.op("pool", lambda e, hf=hf: e.tensor_tensor(out=dst[:, hf * 4:hf * 4 + 4, :], in0=src[:, hf * 4:hf * 4 + 4, :], in1=c4(lvl[:, lev, :]), op=ALU.mult),
                         reads=[sres, "lvl"], writes=[dres])

            masked(NOs, "NOs", AB2[sl][:, :, 0:128], ("AB2", sl), 0)
            masked(NOTs, "NOTs", Q0b, "Q0b", 0)
            for hf in range(2):
                S.op("dve", lambda e, hf=hf: e.tensor_tensor(out=W[:, hf * 4:hf * 4 + 4, :], in0=NOs[:, hf * 4:hf * 4 + 4, :], in1=c4(ident4[:]), op=ALU.add), reads=["NOs", "ident4"], writes=[("MT", sl)])
                S.op("dve", lambda e, hf=hf: e.tensor_tensor(out=Wt[:, hf * 4:hf * 4 + 4, :], in0=NOTs[:, hf * 4:hf * 4 + 4, :], in1=c4(ident4[:]), op=ALU.add), reads=["NOTs", "ident4"], writes=["Wt"])
            WR = ("MT", sl)

            def mm8(lhs, lres, rhs, rres):
                for c in range(8):
                    pb = c // 4
                    o4 = (c % 4) * 128
                    S.op("pe", lambda e, c=c, pb=pb, o4=o4: e.matmul(pi[pb][:, o4:o4 + 128], lhsT=lhs[:, c, :], rhs=rhs[:, c, :], start=True, stop=True),
                         reads=[lres, rres], writes=[("pi", pb)])

            for lev in range(1, 6):
                masked(NOs, "NOs", AB2[sl][:, :, 0:128], ("AB2", sl), lev)
                masked(NOTs, "NOTs", Q0b, "Q0b", lev)
                mm8(NOTs, "NOTs", W, WR)
                for pb in range(2):
                    S.op("act", lambda e, pb=pb: e.copy(out=T1[:, pb * 4:pb * 4 + 4, :], in_=c4(pi[pb][:])), reads=[("pi", pb)], writes=["T1"])
                mm8(NOs, "NOs", Wt, "Wt")
                for pb in range(2):
                    S.op("act", lambda e, pb=pb: e.copy(out=T1p[:, pb * 4:pb * 4 + 4, :], in_=c4(pi[pb][:])), reads=[("pi", pb)], writes=["T1p"])
                mm8(Wt, "Wt", T1, "T1")
                for pb in range(2):
                    S.op("dve", lambda e, pb=pb: e.tensor_copy(out=T1[:, pb * 4:pb * 4 + 4, :], in_=c4(pi[pb][:])), reads=[("pi", pb)], writes=["T1"])
                mm8(W, WR, T1p, "T1p")
                for pb in range(2):
                    S.op("dve", lambda e, pb=pb: e.tensor_tensor(out=Wt[:, pb * 4:pb * 4 + 4, :], in0=Wt[:, pb * 4:pb * 4 + 4, :], in1=c4(pi[pb][:]), op=ALU.add), reads=[("pi", pb), "Wt"], writes=["Wt"])
                for hf in range(2):
                    S.op("pool", lambda e, hf=hf: e.tensor_tensor(out=W[:, hf * 4:hf * 4 + 4, :], in0=W[:, hf * 4:hf * 4 + 4, :], in1=T1[:, hf * 4:hf * 4 + 4, :], op=ALU.add), reads=["T1", WR], writes=[WR])
            for n in ("BH", "KH", "VV"):
                for c in range(8):
                    S.op("pe", lambda e, c=c, n=n: e.transpose(out=ptr[:, c * 128:(c + 1) * 128], in_=Bd[n][:, c, :], identity=idb[:]), reads=BR + ["idb"], writes=["ptr"])
                S.op("act", lambda e, n=n: e.copy(out=TT[sl][n][:], in_=ptr[:].rearrange("p (c x) -> p c x", c=8)), reads=["ptr"], writes=[("TT", sl)])

        def scan_group(b, d, gi, sl):
            Bd = BD[sl]
            order = range(8) if d == 0 else range(7, -1, -1)
            for idx, c in enumerate(order):
                yb = idx // 4
                yo = (idx % 4) * 128
                S.op("pe", lambda e, c=c: e.matmul(pS[:, 0:128], lhsT=Bd["KR"][:, c, 0:128], rhs=Tb[:], start=True, stop=False), reads=[("bd", sl), "Tb"], writes=["pS0"])
                S.op("pe", lambda e, c=c: e.matmul(pS[:, 0:128], lhsT=AK2[sl][:, c, 0:128], rhs=TT[sl]["VV"][:, c, :], start=False, stop=True), reads=[("AK2", sl), ("TT", sl)], writes=["pS0"])
                S.op("act", lambda e: e.copy(out=Xs[:], in_=pS[:, 0:128]), reads=["pS0"], writes=["Xs"])
                S.op("pe", lambda e, c=c: e.matmul(pS[:, 128:256], lhsT=MT[sl][:, c, :], rhs=Xs[:], start=True, stop=True), reads=[("MT", sl), "Xs"], writes=["pS1"])
                S.op("dve", lambda e: e.tensor_copy(out=Us[:], in_=pS[:, 128:256]), reads=["pS1"], writes=["Us"])
                S.op("pe", lambda e, c=c: e.matmul(pS[:, 256:384], lhsT=TT[sl]["BH"][:, c, :], rhs=Us[:], start=True, stop=False), reads=[("TT", sl), "Us"], writes=["pS2"])
                S.op("pe", lambda e, c=c: e.matmul(pS[:, 256:384], lhsT=TT[sl]["KH"][:, c, :], rhs=TT[sl]["VV"][:, c, :], start=False, stop=True), reads=[("TT", sl)], writes=["pS2"])
                S.op("pe", lambda e, c=c, yb=yb, yo=yo: e.matmul(pY[yb][:, yo:yo + 128], lhsT=Tb[:], rhs=Bd["KR"][:, c, 128:256], start=True, stop=False), reads=[("bd", sl), "Tb"], writes=[("pY", yb)])
                S.op("pe", lambda e, c=c, yb=yb, yo=yo: e.matmul(pY[yb][:, yo:yo + 128], lhsT=Us[:], rhs=AB2[sl][:, c, 128:256], start=False, stop=False), reads=[("AB2", sl), "Us"], writes=[("pY", yb)])
                S.op("pe", lambda e, c=c, yb=yb, yo=yo: e.matmul(pY[yb][:, yo:yo + 128], lhsT=TT[sl]["VV"][:, c, :], rhs=AK2[sl][:, c, 128:256], start=False, stop=True), reads=[("AK2", sl), ("TT", sl)], writes=[("pY", yb)])
                S.op("dve", lambda e, c=c: e.scalar_tensor_tensor(out=T[:], in0=T[:], scalar=wtot[sl][:, c:c + 1], in1=pS[:, 256:384], op0=ALU.mult, op1=ALU.add), reads=["pS2", "T", ("wtot", sl)], writes=["T"])
                S.op("act", lambda e: e.copy(out=Tb[:], in_=T[:]), reads=["T"], writes=["Tb"])
                if idx % 4 == 3:
                    cs = sorted(list(order)[idx - 3:idx + 1])
                    c_lo = cs[0]
                    for hh in range(2):
                        prt = slice(hh * 64, hh * 64 + 64)
                        src = pY[yb][prt, :].rearrange("p (c x) -> p c x", c=4)[:, :, hh * 64:hh * 64 + 64]
                        if d == 1:
                            dsts = [(y[prt, gi * 512 + (c_lo + 3 - j) * 64:gi * 512 + (c_lo + 4 - j) * 64], pY[yb][prt, j * 128 + hh * 64:j * 128 + hh * 64 + 64]) for j in range(4)]
                            for (dd, ss) in dsts:
                                S.op("dve", lambda e, dd=dd, ss=ss: e.tensor_tensor(out=dd, in0=dd, in1=ss, op=ALU.add), reads=[("pY", yb), "y"], writes=["y"])
                        else:
                            dd = y[prt, gi * 512 + c_lo * 64:gi * 512 + (c_lo + 4) * 64].rearrange("p (c x) -> p c x", c=4)
                            S.op("act", lambda e, dd=dd, src=src: e.copy(out=dd, in_=src), reads=[("pY", yb)], writes=["y"])

        def out_block(b, blk):
            tok = b * 4096 + blk * 512
            for n in ("p_r", "p_k", "p_v", "a_f", "a_b"):
                S.op("sp", lambda e, n=n: e.dma_start(out=ob[n][:], in_=X[n][:, tok:tok + 512]), writes=[("ob", n)], chan=("ob", n))
            S.op("sp", lambda e: e.dma_start(out=ogb[:], in_=gd[:, tok:tok + 512]), writes=["ogb"], chan="ogb")
            t1, t2, t3 = ob["t1"], ob["t2"], ob["t3"]
            ys = y[:, blk * 512:(blk + 1) * 512]
            S.op("pe", lambda e: e.matmul(pa[:], lhsT=bones[:], rhs=ys, start=True, stop=True), reads=["y", "bones"], writes=["pa"])
            S.op("dve", lambda e: e.scalar_tensor_tensor(out=t1[:], in0=pa[:], scalar=-1.0 / 64, in1=ys, op0=ALU.mult, op1=ALU.add), reads=["pa", "y"], writes=["t1"])
            S.op("act", lambda e: e.activation(out=t2[:], in_=t1[:], func=AF.Square), reads=["t1"], writes=["t2"])
            S.op("pe", lambda e: e.matmul(pk[:], lhsT=bones[:], rhs=t2[:], start=True, stop=True), reads=["t2", "bones"], writes=["pk"])
            S.op("act", lambda e: e.activation(out=t2[:], in_=pk[:], func=AF.Sqrt, scale=1.0 / 64, bias=64e-5), reads=["pk"], writes=["t2"])
            S.op("dve", lambda e: e.reciprocal(out=t2[:], in_=t2[:]), reads=["t2"], writes=["t2"])
            S.op("dve", lambda e: e.tensor_tensor(out=t1[:], in0=t1[:], in1=t2[:], op=ALU.mult), reads=["t1", "t2"], writes=["t1"])
            S.op("dve", lambda e: e.tensor_scalar(out=t1[:], in0=t1[:], scalar1=par2[:, 3:4], scalar2=par2[:, 4:5], op0=ALU.mult, op1=ALU.add), reads=["t1", "par2"], writes=["t1"])
            S.op("pool", lambda e: e.tensor_tensor(out=t2[:], in0=ob["a_f"][:], in1=ob["a_b"][:], op=ALU.add), reads=[("ob", "a_f"), ("ob", "a_b")], writes=["t2"])
            S.op("dve", lambda e: e.tensor_scalar(out=t2[:], in0=t2[:], scalar1=par2[:, 0:1], scalar2=par2[:, 5:6], op0=ALU.mult, op1=ALU.add), reads=["t2", "par2"], writes=["t2"])
            S.op("pool", lambda e: e.tensor_tensor(out=t2[:], in0=t2[:], in1=ob["p_k"][:], op=ALU.mult), reads=["t2", ("ob", "p_k")], writes=["t2"])
            S.op("dve", lambda e: e.scalar_tensor_tensor(out=t3[:], in0=t2[:], scalar=par2[:, 2:3], in1=ob["p_r"][:], op0=ALU.mult, op1=ALU.mult), reads=["t2", "par2", ("ob", "p_r")], writes=["t3"])
            S.op("pe", lambda e: e.matmul(pa[:], lhsT=bones[:], rhs=t3[:], start=True, stop=True), reads=["t3", "bones"], writes=["pa"])
            S.op("dve", lambda e: e.tensor_tensor(out=t3[:], in0=pa[:], in1=ob["p_v"][:], op=ALU.mult), reads=["pa", ("ob", "p_v")], writes=["t3"])
            S.op("pool", lambda e: e.tensor_tensor(out=t3[:], in0=t3[:], in1=t1[:], op=ALU.add), reads=["t3", "t1"], writes=["t3"])
            S.op("dve", lambda e: e.tensor_tensor(out=oo[:], in0=t3[:], in1=ogb[:], op=ALU.mult), reads=["t3", "ogb"], writes=["oo"])
            S.op("sp", lambda e: e.dma_start(out=oT[:, tok:tok + 512], in_=oo[:]), reads=["oo"], chan="oo")

        for b in range(2):
            for d in range(2):
                S.op("dve", lambda e: e.memset(T[:], 0.0), writes=["T"])
                S.op("pool", lambda e: e.memset(Tb[:], 0.0), writes=["Tb"])
                gorder = list(range(8)) if d == 0 else list(range(7, -1, -1))
                prep_group(b, d, gorder[0], 0)
                for j, gi in enumerate(gorder):
                    if j + 1 < 8:
                        prep_group(b, d, gorder[j + 1], (j + 1) % 2)
                    scan_group(b, d, gi, j % 2)
            for blk in range(8):
                out_block(b, blk)
        S.run()
    return nc


def r2_inputs(inp, l, c):
    hs = [2 * c, 2 * c + 1]
    ch = np.concatenate([np.arange(h * 64, h * 64 + 64) for h in hs])
    par2 = np.zeros((128, 8), np.float32)
    par2[:, 0] = inp["k_a"][l][ch]
    par2[:, 2] = inp["r_k"][l].reshape(-1)[ch]
    par2[:, 3] = inp["gn_w"][l][ch]
    par2[:, 4] = inp["gn_b"][l][ch]
    return dict(par2=par2)
def build_p0():
    nc = bass.Bass("TRN2", target_bir_lowering=False)
    x = nc.dram_tensor("x", [1024, 2048], F32, kind="ExternalInput").ap()
    g1 = nc.dram_tensor("g1", [128, 16], F32, kind="ExternalInput").ap()
    ident = nc.dram_tensor("ident", [128, 128], F32, kind="ExternalInput").ap()
    hT = nc.dram_tensor("hT", [2048, 1024], F32, kind="ExternalOutput").ap()
    uT = nc.dram_tensor("uT", [2048, 1024], BF16, kind="ExternalOutput").ap()
    with contextlib.ExitStack() as st:
        sb = lambda name, shape, dt: st.enter_context(nc.sbuf_tensor(name, shape, dt))
        ps = lambda name, shape, dt: st.enter_context(nc.psum_tensor(name, shape, dt))
        xt = [sb("xt%d" % i, [128, 2048], F32) for i in range(2)]
        h = sb("h", [128, 16, 1024], F32)
        u = sb("u", [128, 16, 1024], BF16)
        sq = [sb("sq%d" % i, [128, 512], F32) for i in range(2)]
        rs = sb("rs", [128, 512], F32)
        g = sb("g", [128, 16], F32)
        idt = sb("idt", [128, 128], F32)
        ones = sb("ones", [128, 128], F32)
        pt = [ps("pt%d" % i, [128, 512], F32) for i in range(4)]
        pn = ps("pn", [128, 512], F32)
        S = Sched(nc)
        S.op("sp", lambda e: e.dma_start(out=g[:], in_=g1), writes=["g"], chan="g")
        S.op("sp", lambda e: e.dma_start(out=idt[:], in_=ident), writes=["idt"], chan="idt")
        S.op("dve", lambda e: e.memset(ones[:], 1.0), writes=["ones"])
        for tt in range(8):
            b = tt % 2
            S.op("sp", lambda e, tt=tt, b=b: e.dma_start(out=xt[b][:], in_=x[tt * 128:(tt + 1) * 128, :]), writes=[("xt", b)], chan=("xt", b))
            for kg in range(4):
                pb = kg
                for j in range(4):
                    k = kg * 4 + j
                    S.op("pe", lambda e, k=k, j=j, pb=pb, b=b: e.transpose(out=pt[pb][:, j * 128:(j + 1) * 128], in_=xt[b][:, k * 128:(k + 1) * 128], identity=idt[:]),
                         reads=[("xt", b), "idt"], writes=[("pt", pb)])
                if kg % 2:
                    f = lambda e, kg=kg, pb=pb, tt=tt: e.copy(out=h[:, kg * 4:(kg + 1) * 4, tt * 128:(tt + 1) * 128], in_=pt[pb][:].rearrange("p (j t) -> p j t", j=4))
                else:
                    f = lambda e, kg=kg, pb=pb, tt=tt: e.tensor_copy(out=h[:, kg * 4:(kg + 1) * 4, tt * 128:(tt + 1) * 128], in_=pt[pb][:].rearrange("p (j t) -> p j t", j=4))
                S.op("act" if kg % 2 else "dve", f, reads=[("pt", pb)], writes=[("h", tt // 4)])
        for tb in range(2):
            tsl = slice(tb * 512, (tb + 1) * 512)
            emit_rmsnorm(S, nc, h[:, :, tsl], u[:, :, tsl], g, sq, rs, ones, pn, ("h", tb), ("u", tb), "g")
        S.op("sp", lambda e: e.dma_start(out=hT.rearrange("(k p) t -> p k t", p=128), in_=h[:]), reads=[("h", 0), ("h", 1)], chan="oh")
        S.op("sp", lambda e: e.dma_start(out=uT.rearrange("(k p) t -> p k t", p=128), in_=u[:]), reads=[("u", 0), ("u", 1)], chan="ou")
        S.run()
    return nc


_NC_CACHE = {}


def _prog(name, fn):
    if name not in _NC_CACHE:
        _NC_CACHE[name] = fn()
    return _NC_CACHE[name]


def _run(nc, in_maps):
    res = run_bass_kernel_spmd(nc, in_maps, core_ids=list(range(NCORES)))
    return res.results


def kernel(x, norm1_g, w_in, tshift_mu, w0, w_lora_up, a0, a_lora_up, g_lora_up, k_k, k_a, r_k, gn_w, gn_b, rel_bias,
           w_branch_rwkv, w_branch_attn, w_out, norm2_g, w_mlp_in, w_mlp_out, final_g):
    inp = dict(x=x, norm1_g=norm1_g, w_in=w_in, tshift_mu=tshift_mu, w0=w0, w_lora_up=w_lora_up, a0=a0, a_lora_up=a_lora_up,
               g_lora_up=g_lora_up, k_k=k_k, k_a=k_a, r_k=r_k, gn_w=gn_w, gn_b=gn_b, rel_bias=rel_bias, w_branch_rwkv=w_branch_rwkv,
               w_branch_attn=w_branch_attn, w_out=w_out, norm2_g=norm2_g, w_mlp_in=w_mlp_in, w_mlp_out=w_mlp_out, final_g=final_g)
    inp = {k: np.asarray(v, dtype=np.float32) for k, v in inp.items()}
    bf = ml_dtypes.bfloat16
    depth = inp["w_in"].shape[0]
    vec = lambda v: np.ascontiguousarray(v.reshape(16, 128).T)
    xs = inp["x"].reshape(8192, 2048)
    eye = np.eye(128, dtype=np.float32)
    r = _run(_prog("p0", build_p0), [dict(x=np.ascontiguousarray(xs[c * 1024:(c + 1) * 1024]), g1=vec(inp["norm1_g"][0]), ident=eye) for c in range(NCORES)])
    hT = [np.asarray(r[c]["hT"]) for c in range(NCORES)]
    uT = [np.asarray(r[c]["uT"]) for c in range(NCORES)]
    tabs = bias_index_tables()
    consts2 = r2_consts()
    out = None
    for l in range(depth):
        uT_all = np.ascontiguousarray(np.concatenate(uT, axis=1))
        r1 = _run(_prog("r1", build_r1), [dict(r1_inputs(inp, l, c), uT=uT_all) for c in range(NCORES)])
        in2 = []
        for c in range(NCORES):
            m = dict(consts2, **r2_inputs(inp, l, c))
            for n in R2_IN:
                m[n] = np.asarray(r1[c][n])
            m["g"] = np.asarray(r1[c]["g"])
            in2.append(m)
        r2 = _run(_prog("r2", build_r2), in2)
        ina = []
        for c in range(NCORES):
            heads = [g * 8 + c for g in range(3)]
            qc = lambda h: np.arange(3712 + h * 64, 3712 + h * 64 + 64)
            kc = lambda h: np.arange(3712 + 1536 + h * 64, 3712 + 1536 + h * 64 + 64)
            vc = lambda h: np.arange(3712 + 3072 + h * 64, 3712 + 3072 + h * 64 + 64)
            cols = np.concatenate([qc(heads[0]), qc(heads[1]), kc(heads[0]), kc(heads[1]), qc(heads[2]), vc(heads[2]), kc(heads[2]), vc(heads[0]), vc(heads[1])])
            bias = np.stack([np.where(m, inp["rel_bias"][:, heads[g]][idx], np.float32(-30000.0)) for g, (idx, m) in enumerate(tabs)], axis=1).astype(np.float32)
            ina.append(dict(uT=uT_all, wa=np.ascontiguousarray(inp["w_in"][l][:, cols]), biasT=np.ascontiguousarray(bias), identb=np.eye(128).astype(bf)))
        ra = _run(_prog("battn", build_b_attn), ina)
        o_all = np.concatenate([np.asarray(r2[c]["oT"]) for c in range(NCORES)] + [np.asarray(ra[c]["oT"]) for c in range(NCORES)], axis=0)
        last = (l == depth - 1)
        common = dict(wg=np.ascontiguousarray(inp["w_in"][l][:, 8320:]), wbr=inp["w_branch_rwkv"][l], wba=inp["w_branch_attn"][l], wout=inp["w_out"][l],
                      w1=inp["w_mlp_in"][l], w2=inp["w_mlp_out"][l], g2=vec(inp["norm2_g"][l]),
                      gn=vec(inp["final_g"] if last else inp["norm1_g"][l + 1]), ident=eye)
        inc = [dict(common, hT=hT[c], uT=uT[c], oT=np.ascontiguousarray(o_all[:, c * 1024:(c + 1) * 1024])) for c in range(NCORES)]
        rc = _run(_prog("c_last" if last else "c", lambda: build_c(last)), inc)
        if last:
            out = np.concatenate([np.asarray(rc[c]["out"]) for c in range(NCORES)], axis=0)
        else:
            hT = [np.asarray(rc[c]["hTo"]) for c in range(NCORES)]
            uT = [np.asarray(rc[c]["uTo"]) for c in range(NCORES)]
    return out.reshape(inp["x"].shape).astype(np.float32)
```

```python
import contextlib
import numpy as np
import ml_dtypes
import concourse.bass as bass
import concourse.mybir as mybir
from concourse.bass_utils import run_bass_kernel_spmd

F32 = mybir.dt.float32
BF16 = mybir.dt.bfloat16
AF = mybir.ActivationFunctionType
ALU = mybir.AluOpType
NCORES = 8


class Sched:
    ENGS = ("pe", "act", "dve", "pool", "sp")

    def __init__(self, nc):
        self.nc = nc
        self.ops = []
        self.last_w = {}
        self.readers = {}
        self.chan_cnt = {}
        self.chan_order = []
        self.bar = {}

    def op(self, eng, fn, reads=(), writes=(), chan=None, inc=16):
        idx = len(self.ops)
        deps = set()
        for r in reads:
            if r in self.last_w:
                deps.add(self.last_w[r])
        for w in writes:
            if w in self.last_w:
                deps.add(self.last_w[w])
            deps.update(self.readers.get(w, ()))
        if eng in self.bar:
            deps.update(self.bar.pop(eng))
        deps.discard(idx)
        cdeps = []
        odeps = []
        for d in deps:
            o = self.ops[d]
            if o["chan"] is not None:
                cdeps.append((o["chan"], self.chan_cnt[o["chan"]]))
            else:
                odeps.append(d)
        if chan is not None:
            if chan not in self.chan_cnt:
                self.chan_cnt[chan] = 0
                self.chan_order.append(chan)
            self.chan_cnt[chan] += inc
        self.ops.append(dict(eng=eng, fn=fn, odeps=odeps, cdeps=cdeps, chan=chan, waited=False, inc=inc))
        for r in reads:
            self.readers.setdefault(r, []).append(idx)
        for w in writes:
            self.last_w[w] = idx
            self.readers[w] = []
        return idx

    def barrier(self):
        last = {}
        for i, o in enumerate(self.ops):
            last[(o["eng"], o["chan"])] = i
        deps = set(last.values())
        for e in self.ENGS:
            self.bar[e] = set(deps)

    def run(self):
        nc = self.nc
        ops = self.ops
        for i, o in enumerate(ops):
            for d in o["odeps"]:
                p = ops[d]
                if p["eng"] == "pe" and o["eng"] == "pe":
                    continue
                p["waited"] = True
        cnt = {e: 0 for e in self.ENGS}
        for o in ops:
            if o["chan"] is None and o["waited"]:
                cnt[o["eng"]] += 1
                o["val"] = cnt[o["eng"]]
        import contextlib
        with contextlib.ExitStack() as st:
            esem = {e: st.enter_context(nc.semaphore("s_" + e)) for e in self.ENGS}
            csem = {c: st.enter_context(nc.semaphore("c_%d" % i)) for i, c in enumerate(self.chan_order)}
            block = st.enter_context(nc.Block())
            final_c = {c: self.chan_cnt[c] for c in self.chan_order}

            def emit(ename):
                def body(eng):
                    waited = {}
                    for o in ops:
                        if o["eng"] != ename:
                            continue
                        need = {}
                        for d in o["odeps"]:
                            p = ops[d]
                            if p["eng"] == "pe" and ename == "pe":
                                continue
                            k = ("e", p["eng"])
                            need[k] = max(need.get(k, 0), p["val"])
                        for c, v in o["cdeps"]:
                            k = ("c", c)
                            need[k] = max(need.get(k, 0), v)
                        for k, v in need.items():
                            if waited.get(k, 0) >= v:
                                continue
                            waited[k] = v
                            eng.wait_ge(esem[k[1]] if k[0] == "e" else csem[k[1]], v)
                        ins = o["fn"](eng)
                        if o["chan"] is not None:
                            ins.then_inc(csem[o["chan"]], o["inc"])
                        elif o["waited"]:
                            ins.then_inc(esem[ename], 1)
                    if ename == "sp":
                        for c in self.chan_order:
                            eng.wait_ge(csem[c], final_c[c])
                        for e in self.ENGS:
                            if e != "sp" and cnt[e] > 0:
                                eng.wait_ge(esem[e], cnt[e])
                return body

            block.sync(emit("sp"))
            block.scalar(emit("act"))
            block.vector(emit("dve"))
            block.gpsimd(emit("pool"))
            block.tensor(emit("pe"))
        return cnt, final_c
class WRing:
    def __init__(self, S, st, nc, n=2, name="wr"):
        self.S = S
        self.n = n
        self.slots = [st.enter_context(nc.sbuf_tensor("%s%d" % (name, i), [128, 16, 512], BF16)) for i in range(n)]
        self.stg = [st.enter_context(nc.sbuf_tensor("%sstg%d" % (name, i), [128, 8, 512], F32)) for i in range(3)]
        self.i = 0
        self.j = 0
        self.name = name

    def load(self, W, k0, nk, c0, ncols):
        s = self.i % self.n
        self.i += 1
        slot = self.slots[s]
        src = W[k0 * 128:(k0 + nk) * 128, c0:c0 + ncols].rearrange("(k p) c -> p k c", p=128)
        for ka in range(0, nk, 8):
            kb = min(nk, ka + 8)
            j = self.j % 3
            eng = ("dve", "act", "pool")[self.j % 3]
            self.j += 1
            stg = self.stg[j]
            self.S.op("sp", lambda e, ka=ka, kb=kb, stg=stg, src=src: e.dma_start(out=stg[:, 0:kb - ka, 0:ncols], in_=src[:, ka:kb, :]),
                      writes=[(self.name + "stg", j)], chan=(self.name + "stg", j))
            if eng == "act":
                f = lambda e, ka=ka, kb=kb, stg=stg, slot=slot: e.copy(out=slot[:, ka:kb, 0:ncols], in_=stg[:, 0:kb - ka, 0:ncols])
            else:
                f = lambda e, ka=ka, kb=kb, stg=stg, slot=slot: e.tensor_copy(out=slot[:, ka:kb, 0:ncols], in_=stg[:, 0:kb - ka, 0:ncols])
            self.S.op(eng, f, reads=[(self.name + "stg", j)], writes=[(self.name, s)])
        return slot, (self.name, s)


def emit_rmsnorm(S, nc, h, u, g, sq, rs, ones, pn, hres, ures, gres, out_fp32_inplace=False):
    for k in range(16):
        b = k % 2
        S.op("act", lambda e, k=k, b=b: e.activation(out=sq[b][:], in_=h[:, k, :], func=AF.Square),
             reads=[hres], writes=[("sq", b)])
        S.op("pe", lambda e, k=k, b=b: e.matmul(pn[:], lhsT=ones[:], rhs=sq[b][:], start=(k == 0), stop=(k == 15)),
             reads=[("sq", b), "ones"], writes=["pn"])
    S.op("act", lambda e: e.activation(out=rs[:], in_=pn[:], func=AF.Sqrt, scale=1.0 / 2048, bias=1e-6),
         reads=["pn"], writes=["rs"])
    S.op("dve", lambda e: e.reciprocal(out=rs[:], in_=rs[:]), reads=["rs"], writes=["rs"])
    for k in range(16):
        S.op("dve", lambda e, k=k: e.scalar_tensor_tensor(out=u[:, k, :], in0=h[:, k, :], scalar=g[:, k:k + 1], in1=rs[:], op0=ALU.mult, op1=ALU.mult),
             reads=[hres, gres, "rs"], writes=[ures])


def build_c(last):
    nc = bass.Bass("TRN2", target_bir_lowering=False)
    di = lambda name, shape, dt=F32: nc.dram_tensor(name, shape, dt, kind="ExternalInput").ap()
    hT = di("hT", [2048, 1024]); uT = di("uT", [2048, 1024], BF16); oT = di("oT", [1536, 1024], BF16)
    wg = di("wg", [2048, 4096]); wbr = di("wbr", [1024, 2048]); wba = di("wba", [512, 2048]); wout = di("wout", [2048, 2048])
    w1 = di("w1", [2048, 8192]); w2 = di("w2", [8192, 2048]); g2d = di("g2", [128, 16]); gnd = di("gn", [128, 16])
    ident = di("ident", [128, 128])
    if last:
        outd = nc.dram_tensor("out", [1024, 2048], F32, kind="ExternalOutput").ap()
    else:
        hTo = nc.dram_tensor("hTo", [2048, 1024], F32, kind="ExternalOutput").ap()
        uTo = nc.dram_tensor("uTo", [2048, 1024], BF16, kind="ExternalOutput").ap()
    with contextlib.ExitStack() as st:
        sb = lambda name, shape, dt: st.enter_context(nc.sbuf_tensor(name, shape, dt))
        ps = lambda name, shape, dt: st.enter_context(nc.psum_tensor(name, shape, dt))
        S = Sched(nc)
        h = sb("h", [128, 16, 512], F32); u = sb("u", [128, 16, 512], BF16); o = sb("o", [128, 12, 512], BF16)
        mg = sb("mg", [128, 16, 512], BF16); hid = sb("hid", [128, 16, 512], BF16)
        gt = [sb("gt%d" % i, [128, 4, 512], BF16) for i in range(2)]
        t1 = sb("t1", [128, 4, 512], F32); t2 = sb("t2", [128, 512], F32)
        rl = [sb("rl%d" % i, [128, 512], F32) for i in range(2)]
        sq = [sb("sq%d" % i, [128, 512], F32) for i in range(2)]
        rs = sb("rs", [128, 512], F32)
        g2 = sb("g2s", [128, 16], F32); gn = sb("gns", [128, 16], F32)
        ones = sb("ones", [128, 128], F32); idt = sb("idt", [128, 128], F32)
        ring = WRing(S, st, nc, 2)
        pA = [ps("pA%d" % i, [128, 512], F32) for i in range(4)]
        pB = [ps("pB%d" % i, [128, 512], F32) for i in range(3)]
        pn = ps("pn", [128, 512], F32)
        S.op("sp", lambda e: e.dma_start(out=g2[:], in_=g2d), writes=["g2"], chan="g2")
        S.op("sp", lambda e: e.dma_start(out=gn[:], in_=gnd), writes=["gn"], chan="gn")
        S.op("sp", lambda e: e.dma_start(out=idt[:], in_=ident), writes=["idt"], chan="idt")
        S.op("dve", lambda e: e.memset(ones[:], 1.0), writes=["ones"])
        hTr = hT.rearrange("(k p) t -> p k t", p=128); uTr = uT.rearrange("(k p) t -> p k t", p=128)
        oTr = oT.rearrange("(k p) t -> p k t", p=128)

        PBR = [("pB", 0), ("pB", 1), ("pB", 2), "pn"]
        PAR = [("pA", j) for j in range(4)]

        def mm_group(pbanks, pres, slot, sres, nk, rhs, rres, first=True, lastk=True, ncol=4):
            for j in range(ncol):
                for k in range(nk):
                    S.op("pe", lambda e, j=j, k=k: e.matmul(pbanks[j][:], lhsT=slot[:, k, j * 128:(j + 1) * 128], rhs=rhs[:, k, :],
                                                            start=(first and k == 0), stop=(lastk and k == nk - 1)),
                         reads=[sres, rres], writes=[pres[j]])

        for tb in range(2):
            tsl = slice(tb * 512, (tb + 1) * 512)
            S.op("sp", lambda e, tsl=tsl: e.dma_start(out=h[:], in_=hTr[:, :, tsl]), writes=["h"], chan="h")
            S.op("sp", lambda e, tsl=tsl: e.dma_start(out=u[:], in_=uTr[:, :, tsl]), writes=["u"], chan="u")
            S.op("sp", lambda e, tsl=tsl: e.dma_start(out=o[:], in_=oTr[:, :, tsl]), writes=["o"], chan="o")
            for cg in range(4):
                slot, sres = ring.load(wg, 0, 16, cg * 512, 512)
                mm_group(pA, PAR, slot, sres, 16, u, "u")
                for j in range(4):
                    S.op("act", lambda e, j=j: e.activation(out=gt[0][:, j, :], in_=pA[j][:], func=AF.Sigmoid),
                         reads=[("pA", j)], writes=[("gt0", j)])
                slot, sres = ring.load(wbr, 0, 8, cg * 512, 512)
                pBx = [pB[0], pB[1], pB[2], pn]
                mm_group(pBx, PBR, slot, sres, 8, o, "o")
                for j in range(4):
                    S.op("dve", lambda e, j=j, pBx=pBx: e.tensor_tensor(out=t1[:, j, :], in0=pBx[j][:], in1=gt[0][:, j, :], op=ALU.mult),
                         reads=[PBR[j], ("gt0", j)], writes=[("t1", j)])
                slot, sres = ring.load(wg, 0, 16, 2048 + cg * 512, 512)
                mm_group(pA, PAR, slot, sres, 16, u, "u")
                for j in range(4):
                    S.op("act", lambda e, j=j: e.activation(out=gt[1][:, j, :], in_=pA[j][:], func=AF.Sigmoid),
                         reads=[("pA", j)], writes=[("gt1", j)])
                slot, sres = ring.load(wba, 0, 4, cg * 512, 512)
                osub = o[:, 8:12, :]
                mm_group(pBx, PBR, slot, sres, 4, osub, "o")
                for j in range(4):
                    S.op("dve", lambda e, j=j, pBx=pBx: e.tensor_tensor(out=t2[:], in0=pBx[j][:], in1=gt[1][:, j, :], op=ALU.mult),
                         reads=[PBR[j], ("gt1", j)], writes=["t2"])
                    S.op("pool", lambda e, j=j, cg=cg: e.tensor_tensor(out=mg[:, cg * 4 + j, :], in0=t1[:, j, :], in1=t2[:], op=ALU.add),
                         reads=[("t1", j), "t2"], writes=["mg"])
            for cg in range(4):
                slot, sres = ring.load(wout, 0, 16, cg * 512, 512)
                mm_group(pA, PAR, slot, sres, 16, mg, "mg")
                for j in range(4):
                    S.op("dve", lambda e, j=j, cg=cg: e.tensor_tensor(out=h[:, cg * 4 + j, :], in0=h[:, cg * 4 + j, :], in1=pA[j][:], op=ALU.add),
                         reads=[("pA", j), "h"], writes=["h"])
            emit_rmsnorm(S, nc, h, u, g2, sq, rs, ones, pn, "h", "u", "g2")
            for qt in range(4):
                for cg in range(4):
                    slot, sres = ring.load(w1, 0, 16, (qt * 4 + cg) * 512, 512)
                    mm_group(pA, PAR, slot, sres, 16, u, "u")
                    for j in range(4):
                        b = j % 2
                        S.op("act", lambda e, j=j, b=b: e.activation(out=rl[b][:], in_=pA[j][:], func=AF.Relu),
                             reads=[("pA", j)], writes=[("rl", b)])
                        S.op("pool", lambda e, j=j, b=b, cg=cg: e.tensor_tensor(out=hid[:, cg * 4 + j, :], in0=rl[b][:], in1=rl[b][:], op=ALU.mult),
                             reads=[("rl", b)], writes=["hid"])
                for og in range(4):
                    pBx = [pB[0], pB[1], pB[2], pn]
                    slot, sres = ring.load(w2, qt * 16, 16, og * 512, 512)
                    mm_group(pBx, PBR, slot, sres, 16, hid, "hid")
                    for j in range(4):
                        S.op("dve", lambda e, j=j, og=og, pBx=pBx: e.tensor_tensor(out=h[:, og * 4 + j, :], in0=h[:, og * 4 + j, :], in1=pBx[j][:], op=ALU.add),
                             reads=[PBR[j], "h"], writes=["h"])
            if not last:
                emit_rmsnorm(S, nc, h, u, gn, sq, rs, ones, pn, "h", "u", "gn")
                S.op("sp", lambda e, tsl=tsl: e.dma_start(out=hTo.rearrange("(k p) t -> p k t", p=128)[:, :, tsl], in_=h[:]), reads=["h"], chan="oh")
                S.op("sp", lambda e, tsl=tsl: e.dma_start(out=uTo.rearrange("(k p) t -> p k t", p=128)[:, :, tsl], in_=u[:]), reads=["u"], chan="ou")
            else:
                for k in range(16):
                    b = k % 2
                    S.op("act", lambda e, k=k, b=b: e.activation(out=sq[b][:], in_=h[:, k, :], func=AF.Square), reads=["h"], writes=[("sq", b)])
                    S.op("pe", lambda e, k=k, b=b: e.matmul(pn[:], lhsT=ones[:], rhs=sq[b][:], start=(k == 0), stop=(k == 15)),
                         reads=[("sq", b), "ones"], writes=["pn"])
                S.op("act", lambda e: e.activation(out=rs[:], in_=pn[:], func=AF.Sqrt, scale=1.0 / 2048, bias=1e-6), reads=["pn"], writes=["rs"])
                S.op("dve", lambda e: e.reciprocal(out=rs[:], in_=rs[:]), reads=["rs"], writes=["rs"])
                for k in range(16):
                    S.op("dve", lambda e, k=k: e.scalar_tensor_tensor(out=h[:, k, :], in0=h[:, k, :], scalar=gn[:, k:k + 1], in1=rs[:], op0=ALU.mult, op1=ALU.mult),
                         reads=["h", "gn", "rs"], writes=["h"])
                otv = t1[:].rearrange("p a b -> p (a b)")
                T1R = [("t1", j) for j in range(4)]
                for tt in range(4):
                    for kg in range(4):
                        for j in range(4):
                            k = kg * 4 + j
                            S.op("pe", lambda e, k=k, j=j, kg=kg, tt=tt: e.transpose(out=pA[kg][:, j * 128:(j + 1) * 128], in_=h[:, k, tt * 128:(tt + 1) * 128], identity=idt[:]),
                                 reads=["h", "idt"], writes=[("pA", kg)])
                        S.op("act" if kg % 2 else "dve",
                             (lambda e, kg=kg: e.copy(out=otv[:, kg * 512:(kg + 1) * 512], in_=pA[kg][:])) if kg % 2 else
                             (lambda e, kg=kg: e.tensor_copy(out=otv[:, kg * 512:(kg + 1) * 512], in_=pA[kg][:])),
                             reads=[("pA", kg)], writes=[("t1", kg)])
                    r0 = tb * 512 + tt * 128
                    S.op("sp", lambda e, r0=r0: e.dma_start(out=outd[r0:r0 + 128, :], in_=otv), reads=T1R, chan="oo")
        S.run()
    return nc
GROUPS = ((128, 1), (512, 4), (2048, 16))
S_LEN = 4096


def t5_bucket_np(rel):
    nb = 16
    max_exact = 8
    ret = np.where(rel > 0, nb, 0)
    n = np.abs(rel)
    nf = np.maximum(n, 1).astype(np.float32)
    large = max_exact + (np.log(nf / np.float32(max_exact)) / np.float32(np.log(1024 / max_exact)) * np.float32(nb - max_exact)).astype(np.int32)
    large = np.minimum(large, nb - 1)
    return ret + np.where(n < max_exact, n, large)


def bias_index_tables():
    kap = np.arange(128)[:, None]
    qi = np.arange(128)[None, :]
    out = []
    for (window, d) in GROUPS:
        da = kap - 64 - qi
        db = kap + 64 - qi
        delta = np.concatenate([da, db], axis=1)
        out.append((t5_bucket_np(delta * d), np.abs(delta) <= 64))
    return out


def emit_attention(S, nc, st, uTb, Wa, biasd, identd, oT_dst, sbufs):
    sb, ps = sbufs["sb"], sbufs["ps"]
    wa, ub, Qz0, Qz1, Qz2, K01, K2, V01, V2, Vt, E, Pf, Pm, accn, accd, onesk, idb, ot = [sbufs[k] for k in
        ("wa", "ub", "Qz0", "Qz1", "Qz2", "K01", "K2", "V01", "V2", "Vt", "E", "Pf", "Pm", "accn", "accd", "onesk", "idb", "ot")]
    pj, pS, pOD, pT = sbufs["pj"], sbufs["pS"], sbufs["pOD"], sbufs["pT"]
    uTr = uTb.rearrange("(k p) t -> p k t", p=128)
    for tblk in range(8):
        b = tblk % 2
        S.op("sp", lambda e, tblk=tblk, b=b: e.dma_start(out=ub[b][:], in_=uTr[:, :, tblk * 512:(tblk + 1) * 512]), writes=[("ub", b)], chan=("ub", b))
        for ct in range(5):
            M = 64 if ct == 3 else 128
            c0 = ct * 128 if ct < 4 else 448
            pb = (tblk * 5 + ct) % 2
            for k in range(16):
                S.op("pe", lambda e, k=k, M=M, c0=c0, pb=pb, b=b: e.matmul(pj[pb][0:M, :], lhsT=wa[:, k, c0:c0 + M], rhs=ub[b][:, k, :], start=(k == 0), stop=(k == 15)),
                     reads=[("ub", b), "wa"], writes=[("pj", pb)])
            t0 = tblk * 512
            if ct == 0:
                S.op("act", lambda e, pb=pb, t0=t0: e.copy(out=Qz0[0:64, t0:t0 + 512], in_=pj[pb][0:64, :]), reads=[("pj", pb)], writes=["Qz0"])
                S.op("dve", lambda e, pb=pb, t0=t0: e.tensor_copy(out=Qz1[64:128, :].rearrange("p (r i) -> p r i", r=4)[:, :, t0 // 4:t0 // 4 + 128],
                                                                   in_=pj[pb][64:128, :].rearrange("p (i r) -> p r i", r=4)), reads=[("pj", pb)], writes=["Qz1"])
            elif ct == 1:
                S.op("act", lambda e, pb=pb, t0=t0: e.copy(out=K01[0:64, 64 + t0:64 + t0 + 512], in_=pj[pb][0:64, :]), reads=[("pj", pb)], writes=["K01"])
                S.op("dve", lambda e, pb=pb, t0=t0: e.tensor_copy(out=K01[64:128, :].rearrange("p (r i) -> p r i", r=4)[:, :, 64 + t0 // 4:64 + t0 // 4 + 128],
                                                                   in_=pj[pb][64:128, :].rearrange("p (i r) -> p r i", r=4)), reads=[("pj", pb)], writes=["K01"])
            elif ct == 2:
                S.op("act", lambda e, pb=pb, t0=t0: e.copy(out=Qz2[0:64, :].rearrange("p (r i) -> p r i", r=16)[:, :, t0 // 16:t0 // 16 + 32],
                                                            in_=pj[pb][0:64, :].rearrange("p (i r) -> p r i", r=16)), reads=[("pj", pb)], writes=["Qz2"])
                S.op("dve", lambda e, pb=pb, t0=t0: e.tensor_copy(out=V2[64:128, :].rearrange("p (r i) -> p r i", r=16)[:, :, 64 + t0 // 16:64 + t0 // 16 + 32],
                                                                   in_=pj[pb][64:128, :].rearrange("p (i r) -> p r i", r=16)), reads=[("pj", pb)], writes=["V2"])
            elif ct == 3:
                S.op("act", lambda e, pb=pb, t0=t0: e.copy(out=K2[0:64, :].rearrange("p (r i) -> p r i", r=16)[:, :, 64 + t0 // 16:64 + t0 // 16 + 32],
                                                            in_=pj[pb][0:64, :].rearrange("p (i r) -> p r i", r=16)), reads=[("pj", pb)], writes=["K2"])
            else:
                S.op("act", lambda e, pb=pb, t0=t0: e.copy(out=V01[0:64, 64 + t0:64 + t0 + 512], in_=pj[pb][0:64, :]), reads=[("pj", pb)], writes=["V01"])
                S.op("dve", lambda e, pb=pb, t0=t0: e.tensor_copy(out=V01[64:128, :].rearrange("p (r i) -> p r i", r=4)[:, :, 64 + t0 // 4:64 + t0 // 4 + 128],
                                                                   in_=pj[pb][64:128, :].rearrange("p (i r) -> p r i", r=4)), reads=[("pj", pb)], writes=["V01"])
    vsrc = [V01[:, m * 128:(m + 1) * 128] for m in range(36)] + [V2[:, m * 128:(m + 1) * 128] for m in range(48)]
    for t0 in range(0, 84, 8):
        n = min(8, 84 - t0)
        pb = (t0 // 8) % 2
        for j in range(n):
            S.op("pe", lambda e, src=vsrc[t0 + j], j=j, pb=pb: e.transpose(out=pT[pb][:, j * 128:(j + 1) * 128], in_=src, identity=idb[:]),
                 reads=["V01", "V2", "idb"], writes=[("pT", pb)])
        S.op("dve" if pb else "act",
             (lambda e, t0=t0, n=n, pb=pb: e.tensor_copy(out=Vt[:, t0:t0 + n, :], in_=pT[pb][:, 0:n * 128].rearrange("p (j c) -> p j c", c=128))) if pb else
             (lambda e, t0=t0, n=n, pb=pb: e.copy(out=Vt[:, t0:t0 + n, :], in_=pT[pb][:, 0:n * 128].rearrange("p (j c) -> p j c", c=128))),
             reads=[("pT", pb)], writes=["Vt"])
    Qsrc = [lambda r: Qz0[:, :], lambda r: Qz1[:, :].rearrange("p (r i) -> p r i", r=4)[:, r, :],
            lambda r: Qz2[:, :].rearrange("p (r i) -> p r i", r=16)[:, r, :]]
    Ksrc = [lambda r: K01[:, 0:4224], lambda r: K01[:, :].rearrange("p (r i) -> p r i", r=4)[:, r, :],
            lambda r: K2[:, :].rearrange("p (r i) -> p r i", r=16)[:, r, :]]
    vt_of = [lambda r, m: (m, 0), lambda r, m: (r * 9 + m, 64), lambda r, m: (36 + r * 3 + m, 64)]
    qb = 0
    for g, (window, d) in enumerate(GROUPS):
        L = S_LEN // d
        nt = L // 128 + 1
        nq = L // 128
        for r in range(d):
            q_ap, k_ap = Qsrc[g](r), Ksrc[g](r)
            for m in range(nq):
                sbk = qb % 2
                S.op("pe", lambda e, k_ap=k_ap, q_ap=q_ap, m=m, sbk=sbk: e.matmul(pS[sbk][:, 0:128], lhsT=k_ap[:, m * 128:m * 128 + 128], rhs=q_ap[:, m * 128:m * 128 + 128], start=True, stop=True),
                     reads=["Qz0", "Qz1", "Qz2", "K01", "K2"], writes=[("pS", sbk)])
                S.op("pe", lambda e, k_ap=k_ap, q_ap=q_ap, m=m, sbk=sbk: e.matmul(pS[sbk][:, 128:256], lhsT=k_ap[:, m * 128 + 128:m * 128 + 256], rhs=q_ap[:, m * 128:m * 128 + 128], start=True, stop=True),
                     reads=["Qz0", "Qz1", "Qz2", "K01", "K2"], writes=[("pS", sbk)])
                S.op("act", lambda e, sbk=sbk: e.activation(out=Pf[sbk][:], in_=pS[sbk][:, 0:256], func=AF.Exp, scale=0.125), reads=[("pS", sbk)], writes=[("Pf", sbk)])
                S.op("dve", lambda e, sbk=sbk, g=g: e.tensor_tensor(out=Pm[sbk][:], in0=Pf[sbk][:], in1=E[:, g, :], op=ALU.mult), reads=[("Pf", sbk), "E"], writes=[("Pm", sbk)])
                half = m % 2
                ob = (qb // 2) % 2
                for (pp, pname, lh, coff) in ((pOD, "pOD", None, 0), (pOD, "pOD", "ones", 256)):
                    for kt in range(2):
                        if lh is None:
                            tix, c0v = vt_of[g](r, m + kt)
                            lhs = Vt[:, tix, c0v:c0v + 64]
                        else:
                            var = 1 if (m == 0 and kt == 0) else (2 if (m == nq - 1 and kt == 1) else 0)
                            lhs = onesk[:, var, :]
                        S.op("pe", lambda e, pp=pp, lhs=lhs, kt=kt, sbk=sbk, ob=ob, half=half, coff=coff: e.matmul(pp[ob][0:64, coff + half * 128:coff + (half + 1) * 128], lhsT=lhs, rhs=Pm[sbk][:, kt * 128:(kt + 1) * 128], start=(kt == 0), stop=(kt == 1)),
                             reads=[("Pm", sbk), "Vt", "onesk"], writes=[(pname, ob)])
                if half == 1:
                    m0 = m - 1
                    dst = lambda acc, d=d, r=r, m0=m0: acc[:, :].rearrange("p (i r) -> p r i", r=d)[:, r, m0 * 128:m0 * 128 + 256]
                    if g == 0:
                        S.op("act", lambda e, ob=ob, dst=dst: e.copy(out=dst(accn), in_=pOD[ob][0:64, 0:256]), reads=[("pOD", ob)], writes=["accn"])
                        S.op("act", lambda e, ob=ob, dst=dst: e.copy(out=dst(accd), in_=pOD[ob][0:64, 256:512]), reads=[("pOD", ob)], writes=["accd"])
                    else:
                        S.op("dve", lambda e, ob=ob, dst=dst: e.tensor_tensor(out=dst(accn), in0=dst(accn), in1=pOD[ob][0:64, 0:256], op=ALU.add), reads=[("pOD", ob), "accn"], writes=["accn"])
                        S.op("dve", lambda e, ob=ob, dst=dst: e.tensor_tensor(out=dst(accd), in0=dst(accd), in1=pOD[ob][0:64, 256:512], op=ALU.add), reads=[("pOD", ob), "accd"], writes=["accd"])
                qb += 1
    S.op("dve", lambda e: e.reciprocal(out=accd[:], in_=accd[:]), reads=["accd"], writes=["accd"])
    S.op("dve", lambda e: e.tensor_tensor(out=ot[:], in0=accn[:], in1=accd[:], op=ALU.mult), reads=["accn", "accd"], writes=["ot"])
    S.op("sp", lambda e: e.dma_start(out=oT_dst, in_=ot[:]), reads=["ot"], chan="oattn")


def alloc_attention(nc, st):
    sb = lambda name, shape, dt: st.enter_context(nc.sbuf_tensor(name, shape, dt))
    ps = lambda name, shape, dt: st.enter_context(nc.psum_tensor(name, shape, dt))
    d = dict(sb=sb, ps=ps)
    d["wa"] = sb("wa_sb", [128, 16, 576], BF16)
    d["ub"] = [sb("ub%d" % i, [128, 16, 512], BF16) for i in range(2)]
    d["Qz0"] = sb("Qz0", [128, 4096], BF16)
    d["Qz1"] = sb("Qz1", [128, 4096], BF16)
    d["Qz2"] = sb("Qz2", [128, 4096], BF16)
    d["K01"] = sb("K01", [128, 4608], BF16)
    d["K2"] = sb("K2", [128, 6144], BF16)
    d["V01"] = sb("V01", [128, 4608], BF16)
    d["V2"] = sb("V2", [128, 6144], BF16)
    d["Vt"] = sb("Vt", [128, 84, 128], BF16)
    d["E"] = sb("E", [128, 3, 256], F32)
    d["Pf"] = [sb("Pf%d" % i, [128, 256], F32) for i in range(2)]
    d["Pm"] = [sb("Pm%d" % i, [128, 256], BF16) for i in range(2)]
    d["accn"] = sb("accn", [64, 4096], F32)
    d["accd"] = sb("accd", [64, 4096], F32)
    d["onesk"] = sb("onesk", [128, 3, 64], BF16)
    d["idb"] = sb("idb", [128, 128], BF16)
    d["ot"] = sb("ot", [64, 4096], BF16)
    d["pj"] = [ps("pj%d" % i, [128, 512], F32) for i in range(2)]
    d["pS"] = [ps("pS%d" % i, [128, 512], F32) for i in range(2)]
    d["pOD"] = [ps("pOD%d" % i, [128, 512], F32) for i in range(2)]
    d["pT"] = [ps("pT%d" % i, [128, 1024], BF16) for i in range(2)]
    return d


def build_b_attn():
    nc = bass.Bass("TRN2", target_bir_lowering=False)
    di = lambda name, shape, dt=F32: nc.dram_tensor(name, shape, dt, kind="ExternalInput").ap()
    uT = di("uT", [2048, 8192], BF16)
    wad = di("wa", [2048, 576])
    biasd = di("biasT", [128, 3, 256])
    identd = di("identb", [128, 128], BF16)
    oT = nc.dram_tensor("oT", [64, 8192], BF16, kind="ExternalOutput").ap()
    with contextlib.ExitStack() as st:
        S = Sched(nc)
        A = alloc_attention(nc, st)
        emit_attn_setup(S, nc, A, wad, biasd, identd)
        for b in range(2):
            emit_attention(S, nc, st, uT[:, b * 4096:(b + 1) * 4096], wad, biasd, identd, oT[:, b * 4096:(b + 1) * 4096], A)
        S.run()
    return nc


def emit_attn_setup(S, nc, A, wad, biasd, identd):
    wa, E, onesk, idb = A["wa"], A["E"], A["onesk"], A["idb"]
    src = wad.rearrange("(k p) c -> p k c", p=128)
    for ka in range(0, 16, 4):
        S.op("pool", lambda e, ka=ka: e.dma_start(out=wa[:, ka:ka + 4, :], in_=src[:, ka:ka + 4, :]), writes=["wa"], chan="wa")
    S.op("sp", lambda e: e.dma_start(out=E[:], in_=biasd), writes=["E"], chan="E")
    S.op("sp", lambda e: e.dma_start(out=idb[:], in_=identd), writes=["idb"], chan="idb")
    S.op("act", lambda e: e.activation(out=E[:], in_=E[:], func=AF.Exp), reads=["E"], writes=["E"])
    S.op("dve", lambda e: e.memset(onesk[:], 1.0), writes=["onesk"])
    S.op("dve", lambda e: e.memset(onesk[0:64, 1, :], 0.0), writes=["onesk"])
    S.op("dve", lambda e: e.memset(onesk[64:128, 2, :], 0.0), writes=["onesk"])
    for name in ("K01", "V01", "K2", "V2", "Qz0", "Qz1", "Qz2"):
        S.op("pool", lambda e, name=name: e.memset(A[name][:], 0.0), writes=[name])
R1_OUT = ("p_r", "p_k", "p_v", "kk", "a_f", "a_b", "l_f", "l_b")


def build_r1():
    nc = bass.Bass("TRN2", target_bir_lowering=False)
    di = lambda name, shape, dt=F32: nc.dram_tensor(name, shape, dt, kind="ExternalInput").ap()
    uT = di("uT", [2048, 8192], BF16)
    wrd = di("wr", [2048, 1024])
    pard = di("par", [128, 32])
    gupd = di("gup", [256, 128]); wupd = di("wup", [2, 96, 128]); aupd = di("aup", [2, 96, 128])
    bonesd = di("bones", [128, 128])
    outs = {n: nc.dram_tensor(n, [128, 8192], F32, kind="ExternalOutput").ap() for n in R1_OUT}
    g_out = nc.dram_tensor("g", [128, 8192], BF16, kind="ExternalOutput").ap()
    with contextlib.ExitStack() as st:
        sb = lambda name, shape, dt: st.enter_context(nc.sbuf_tensor(name, shape, dt))
        ps = lambda name, shape, dt: st.enter_context(nc.psum_tensor(name, shape, dt))
        S = Sched(nc)
        wr = sb("wr_sb", [128, 16, 1024], BF16)
        ub = [sb("ub%d" % i, [128, 16, 512], BF16) for i in range(2)]
        raw = [sb("raw%d" % i, [128, 4098], BF16) for i in range(9)]
        par = sb("par_sb", [128, 32], F32)
        gup = sb("gup_sb", [128, 2, 128], BF16); wup = sb("wup_sb", [96, 2, 128], BF16); aup = sb("aup_sb", [96, 2, 128], BF16)
        bones = sb("bones_sb", [128, 128], F32)
        t1 = sb("t1", [128, 4096], F32); t2 = sb("t2", [128, 4096], F32); t3 = sb("t3", [128, 4096], F32)
        nl = sb("nl", [128, 2, 4096], BF16)
        pj = [ps("pj%d" % i, [128, 512], F32) for i in range(4)]
        pq = [ps("pq%d" % i, [128, 512], F32) for i in range(4)]
        src = wrd.rearrange("(k p) c -> p k c", p=128)
        for ka in range(0, 16, 2):
            S.op("pool", lambda e, ka=ka: e.dma_start(out=wr[:, ka:ka + 2, :], in_=src[:, ka:ka + 2, :]), writes=["wr"], chan="wr")
        S.op("sp", lambda e: e.dma_start(out=par[:], in_=pard), writes=["par"], chan="par")
        S.op("sp", lambda e: e.dma_start(out=bones[:], in_=bonesd), writes=["bones"], chan="bones")
        S.op("pool", lambda e: e.dma_start(out=gup[:], in_=gupd.rearrange("(k p) c -> p k c", p=128)), writes=["gup"], chan="gup")
        S.op("pool", lambda e: e.dma_start(out=wup[:], in_=wupd.rearrange("d p c -> p d c")), writes=["wup"], chan="wup")
        S.op("pool", lambda e: e.dma_start(out=aup[:], in_=aupd.rearrange("d p c -> p d c")), writes=["aup"], chan="aup")
        S.op("dve", lambda e: e.tensor_scalar(out=par[:, 9:18], in0=par[:, 0:9], scalar1=-1.0, scalar2=1.0, op0=ALU.mult, op1=ALU.add), reads=["par"], writes=["par"])
        S.op("dve", lambda e: e.tensor_scalar(out=par[:, 18:27], in0=par[:, 0:9], scalar1=0.5, scalar2=None, op0=ALU.mult), reads=["par"], writes=["par"])
        for i in range(9):
            S.op("pool", lambda e, i=i: e.memset(raw[i][:], 0.0), writes=[("raw", i)])
        rows = [128] * 5 + [96] * 4
        uTr = uT.rearrange("(k p) t -> p k t", p=128)
        for b in range(2):
            tok0 = b * 4096
            for tblk in range(8):
                bb = tblk % 2
                S.op("sp", lambda e, tblk=tblk, bb=bb, tok0=tok0: e.dma_start(out=ub[bb][:], in_=uTr[:, :, tok0 + tblk * 512:tok0 + (tblk + 1) * 512]), writes=[("ub", bb)], chan=("ub", bb))
                for ct in range(9):
                    M = rows[ct]
                    c0 = ct * 128 if ct < 5 else 640 + (ct - 5) * 96
                    pb = ct % 4
                    for k in range(16):
                        S.op("pe", lambda e, k=k, M=M, c0=c0, pb=pb, bb=bb: e.matmul(pj[pb][0:M, :], lhsT=wr[:, k, c0:c0 + M], rhs=ub[bb][:, k, :], start=(k == 0), stop=(k == 15)),
                             reads=[("ub", bb), "wr"], writes=[("pj", pb)])
                    S.op("act" if ct % 2 else "dve",
                         (lambda e, ct=ct, M=M, pb=pb, tblk=tblk: e.copy(out=raw[ct][0:M, 1 + tblk * 512:1 + (tblk + 1) * 512], in_=pj[pb][0:M, :])) if ct % 2 else
                         (lambda e, ct=ct, M=M, pb=pb, tblk=tblk: e.tensor_copy(out=raw[ct][0:M, 1 + tblk * 512:1 + (tblk + 1) * 512], in_=pj[pb][0:M, :])),
                         reads=[("pj", pb)], writes=[("raw", ct)])
            tsl = slice(tok0, tok0 + 4096)

            def shift(ct, dst):
                M = rows[ct]
                S.op("dve", lambda e: e.tensor_tensor(out=t1[0:M, :], in0=raw[ct][0:M, 0:4096], in1=raw[ct][0:M, 2:4098], op=ALU.add), reads=[("raw", ct)], writes=["t1"])
                S.op("act", lambda e: e.activation(out=t2[0:M, :], in_=raw[ct][0:M, 1:4097], func=AF.Copy, scale=par[0:M, 9 + ct:10 + ct]), reads=[("raw", ct), "par"], writes=["t2"])
                S.op("dve", lambda e: e.scalar_tensor_tensor(out=dst[0:M, :], in0=t1[0:M, :], scalar=par[0:M, 18 + ct:19 + ct], in1=t2[0:M, :], op0=ALU.mult, op1=ALU.add),
                     reads=["t1", "t2", "par"], writes=["t3"])

            def store(name, srct, res, tsl=tsl):
                S.op("sp", lambda e: e.dma_start(out=outs[name][:, tsl], in_=srct[:]), reads=[res], chan="o_" + name)

            shift(0, t3); store("p_r", t3, "t3")
            shift(2, t3); store("p_v", t3, "t3")
            shift(1, t3); store("p_k", t3, "t3")
            S.op("dve", lambda e: e.tensor_scalar(out=t1[:], in0=t3[:], scalar1=par[:, 27:28], scalar2=None, op0=ALU.mult), reads=["t3", "par"], writes=["t1"])
            S.op("act", lambda e: e.activation(out=t2[:], in_=t1[:], func=AF.Square), reads=["t1"], writes=["t2"])
            for blk in range(8):
                pb = blk % 4
                bs = slice(blk * 512, (blk + 1) * 512)
                S.op("pe", lambda e, pb=pb, bs=bs: e.matmul(pq[pb][:], lhsT=bones[:], rhs=t2[:, bs], start=True, stop=True), reads=["t2", "bones"], writes=[("pq", pb)])
                S.op("act", lambda e, pb=pb, bs=bs: e.activation(out=t3[:, bs], in_=pq[pb][:], func=AF.Sqrt), reads=[("pq", pb)], writes=["t3"])
            S.op("dve", lambda e: e.tensor_scalar(out=t3[:], in0=t3[:], scalar1=1e-12, scalar2=None, op0=ALU.max), reads=["t3"], writes=["t3"])
            S.op("dve", lambda e: e.reciprocal(out=t3[:], in_=t3[:]), reads=["t3"], writes=["t3"])
            S.op("dve", lambda e: e.tensor_tensor(out=t3[:], in0=t3[:], in1=t1[:], op=ALU.mult), reads=["t3", "t1"], writes=["t3"])
            store("kk", t3, "t3")
            for j in range(2):
                shift(3 + j, t3)
                S.op("act", lambda e, j=j: e.activation(out=nl[:, j, :], in_=t3[:], func=AF.Sigmoid), reads=["t3"], writes=["nl"])
            for blk in range(8):
                pb = blk % 4
                bs = slice(blk * 512, (blk + 1) * 512)
                for j in range(2):
                    S.op("pe", lambda e, pb=pb, bs=bs, j=j: e.matmul(pq[pb][:], lhsT=gup[:, j, :], rhs=nl[:, j, bs], start=(j == 0), stop=(j == 1)), reads=["nl", "gup"], writes=[("pq", pb)])
                S.op("act", lambda e, pb=pb, blk=blk: e.copy(out=raw[3][:, 1 + blk * 512:1 + (blk + 1) * 512], in_=pq[pb][:]), reads=[("pq", pb)], writes=[("raw", 3)])
            S.op("sp", lambda e, tsl=tsl: e.dma_start(out=g_out[:, tsl], in_=raw[3][:, 1:4097]), reads=[("raw", 3)], chan="o_g")
            for d in range(2):
                shift(5 + d, t3)
                S.op("act", lambda e: e.activation(out=nl[0:96, 0, :], in_=t3[0:96, :], func=AF.Tanh), reads=["t3"], writes=["nl"])
                for blk in range(8):
                    pb = blk % 4
                    bs = slice(blk * 512, (blk + 1) * 512)
                    S.op("pe", lambda e, pb=pb, bs=bs, d=d: e.matmul(pq[pb][:], lhsT=wup[:, d, :], rhs=nl[0:96, 0, bs], start=True, stop=True), reads=["nl", "wup"], writes=[("pq", pb)])
                    S.op("act", lambda e, pb=pb, bs=bs, d=d: e.activation(out=t1[:, bs], in_=pq[pb][:], func=AF.Sigmoid, bias=par[:, 28 + d:29 + d]), reads=[("pq", pb), "par"], writes=["t1"])
                S.op("dve", lambda e: e.tensor_scalar(out=t1[:], in0=t1[:], scalar1=-0.6065306597126334, scalar2=None, op0=ALU.mult), reads=["t1"], writes=["t1"])
                store("l_f" if d == 0 else "l_b", t1, "t1")
                shift(7 + d, t3)
                S.op("act", lambda e: e.copy(out=nl[0:96, 1, :], in_=t3[0:96, :]), reads=["t3"], writes=["nl"])
                for blk in range(8):
                    pb = blk % 4
                    bs = slice(blk * 512, (blk + 1) * 512)
                    S.op("pe", lambda e, pb=pb, bs=bs, d=d: e.matmul(pq[pb][:], lhsT=aup[:, d, :], rhs=nl[0:96, 1, bs], start=True, stop=True), reads=["nl", "aup"], writes=[("pq", pb)])
                    S.op("act", lambda e, pb=pb, bs=bs, d=d: e.activation(out=t2[:, bs], in_=pq[pb][:], func=AF.Sigmoid, bias=par[:, 30 + d:31 + d]), reads=[("pq", pb), "par"], writes=["t2"])
                store("a_f" if d == 0 else "a_b", t2, "t2")
        S.run()
    return nc


def r1_inputs(inp, l, c):
    hs = [2 * c, 2 * c + 1]
    ch = np.concatenate([np.arange(h * 64, h * 64 + 64) for h in hs])
    cols = np.concatenate([ch, 1024 + ch, 2048 + ch, np.arange(3072, 3712)])
    wr = np.ascontiguousarray(inp["w_in"][l][:, cols])
    mu = inp["tshift_mu"][l][cols]
    par = np.zeros((128, 32), np.float32)
    for ct in range(5):
        par[:, ct] = mu[ct * 128:(ct + 1) * 128]
    for ct in range(5, 9):
        par[:96, ct] = mu[640 + (ct - 5) * 96:640 + (ct - 4) * 96]
    par[:, 27] = inp["k_k"][l][ch]
    par[:, 28] = inp["w0"][l][0][ch]; par[:, 29] = inp["w0"][l][1][ch]
    par[:, 30] = inp["a0"][l][0][ch]; par[:, 31] = inp["a0"][l][1][ch]
    bones = np.kron(np.eye(2, dtype=np.float32), np.ones((64, 64), np.float32))
    return dict(wr=wr, par=par, gup=np.ascontiguousarray(inp["g_lora_up"][l][:, ch]), wup=np.ascontiguousarray(inp["w_lora_up"][l][:, :, ch]),
                aup=np.ascontiguousarray(inp["a_lora_up"][l][:, :, ch]), bones=bones)
R2_IN = ("p_r", "p_k", "p_v", "kk", "a_f", "a_b", "l_f", "l_b")


def r2_consts():
    p = np.arange(128)
    same = (p[:, None] // 64) == (p[None, :] // 64)
    s = p[:, None] % 64
    t = p[None, :] % 64
    masks = np.zeros((128, 2, 3, 512), np.float32)
    for d in range(2):
        strict = ((s < t) if d == 0 else (s > t)) & same
        incl = ((s <= t) if d == 0 else (s >= t)) & same
        a = np.concatenate([-strict.astype(np.float32), -incl.astype(np.float32)], axis=1)
        masks[:, d, 0] = np.tile(a, (1, 2))
        masks[:, d, 1] = np.tile(-a, (1, 2))
        masks[:, d, 2] = np.tile(-strict.T.astype(np.float32), (1, 4))
    ident4 = np.tile(np.eye(128, dtype=np.float32), (1, 4))
    lvl = np.zeros((128, 6, 512), np.float32)
    for i, sz in enumerate((1, 2, 4, 8, 16, 32)):
        m = ((p[:, None] // (2 * sz)) == (p[None, :] // (2 * sz))) & ((p[:, None] // sz) != (p[None, :] // sz))
        lvl[:, i] = np.tile(m.astype(np.float32), (1, 4))
    m01 = np.ones((128, 512), np.float32)
    m01[:, ::64] = 0.0
    bones = np.kron(np.eye(2, dtype=np.float32), np.ones((64, 64), np.float32))
    return dict(masks=masks, lvl=lvl, ident4=ident4, m01=m01, bones=bones, identb=np.eye(128).astype(ml_dtypes.bfloat16))


def build_r2():
    nc = bass.Bass("TRN2", target_bir_lowering=False)
    di = lambda name, shape, dt=F32: nc.dram_tensor(name, shape, dt, kind="ExternalInput").ap()
    X = {n: di(n, [128, 8192]) for n in R2_IN}
    gd = di("g", [128, 8192], BF16)
    par2d = di("par2", [128, 8])
    masksd = di("masks", [128, 2, 3, 512]); lvld = di("lvl", [128, 6, 512]); ident4d = di("ident4", [128, 512]); m01d = di("m01", [128, 512])
    bonesd = di("bones", [128, 128]); identbd = di("identb", [128, 128], BF16)
    oT = nc.dram_tensor("oT", [128, 8192], BF16, kind="ExternalOutput").ap()
    with contextlib.ExitStack() as st:
        sb = lambda name, shape, dt: st.enter_context(nc.sbuf_tensor(name, shape, dt))
        ps = lambda name, shape, dt: st.enter_context(nc.psum_tensor(name, shape, dt))
        S = Sched(nc)
        par2 = sb("par2s", [128, 8], F32); masks = sb("maskss", [128, 2, 3, 512], F32); lvl = sb("lvls", [128, 6, 512], F32); ident4 = sb("ident4s", [128, 512], F32)
        m01 = sb("m01s", [128, 512], F32); bones = sb("boness", [128, 128], F32); idb = sb("idbs", [128, 128], BF16)
        NB = 2
        inb = [{n: sb("in_%s%d" % (n, i), [128, 512], F32) for n in ("p_r", "p_k", "p_v", "kk", "a", "l")} for i in range(NB)]
        tmp = {n: sb("tmp_" + n, [128, 512], F32) for n in ("f", "kd", "ka", "Lc", "Linc", "w1", "w2")}
        ltot = sb("ltot", [128, 8], F32); wtot = [sb("wtot%d" % i, [128, 8], F32) for i in range(NB)]
        BDn = ("KR", "BB", "KT", "BH", "KH", "VV")
        BD = [{n: sb("bd_%s%d" % (n, i), [128, 8, 256 if n == "KR" else 128], BF16) for n in BDn} for i in range(NB)]
        AB2 = [sb("AB2_%d" % i, [128, 8, 256], BF16) for i in range(NB)]
        AK2 = [sb("AK2_%d" % i, [128, 8, 256], BF16) for i in range(NB)]
        Pb = [sb("Pb%d" % i, [128, 8, 128], BF16) for i in range(2)]
        Qb = [sb("Qb%d" % i, [128, 8, 128], BF16) for i in range(2)]
        MT = [sb("MT%d" % i, [128, 8, 128], BF16) for i in range(NB)]
        Wt = sb("Wt", [128, 8, 128], BF16); Q0b = sb("Q0b", [128, 8, 128], BF16)
        TT = [{n: sb("tt_%s%d" % (n, i), [128, 8, 128], BF16) for n in ("BH", "KH", "VV")} for i in range(NB)]
        T = sb("T", [128, 128], F32); Tb = sb("Tb", [128, 128], BF16)
        Xs = sb("Xs", [128, 128], BF16); Us = sb("Us", [128, 128], BF16)
        y = sb("y", [128, 4096], F32)
        ob = {n: sb("ob_" + n, [128, 512], F32) for n in ("p_r", "p_k", "p_v", "a_f", "a_b", "t1", "t2", "t3")}
        ogb = sb("ogb", [128, 512], BF16); oo = sb("oo", [128, 512], BF16)
        pa = ps("pa", [128, 512], F32); pk = ps("pk", [128, 512], F32)
        pi = [ps("pi%d" % i, [128, 512], F32) for i in range(2)]
        ptr = ps("ptr", [128, 1024], BF16)
        pS = ps("pS", [128, 512], F32)
        pY = [ps("pY%d" % i, [128, 512], F32) for i in range(2)]
        for (t_, d_, nm) in ((par2, par2d, "par2"), (masks, masksd, "masks"), (lvl, lvld, "lvl"), (ident4, ident4d, "ident4"), (m01, m01d, "m01"), (bones, bonesd, "bones"), (idb, identbd, "idb")):
            S.op("sp", lambda e, t_=t_, d_=d_: e.dma_start(out=t_[:], in_=d_), writes=[nm], chan=nm)
        S.op("dve", lambda e: e.tensor_scalar(out=par2[:, 1:2], in0=par2[:, 0:1], scalar1=-1.0, scalar2=1.0, op0=ALU.mult, op1=ALU.add), reads=["par2"], writes=["par2"])
        S.op("dve", lambda e: e.tensor_scalar(out=par2[:, 5:6], in0=par2[:, 0:1], scalar1=-2.0, scalar2=2.0, op0=ALU.mult, op1=ALU.add), reads=["par2"], writes=["par2"])
        for i in range(NB):
            for n in BDn:
                S.op("pool", lambda e, i=i, n=n: e.memset(BD[i][n][:], 0.0), writes=[("bd", i)])

        v3 = lambda ap: ap.rearrange("p (c t) -> p c t", t=64)

        def prep_group(b, d, gi, sl):
            tok = b * 4096 + gi * 512
            I = inb[sl]
            for n in ("p_r", "p_k", "p_v", "kk"):
                S.op("sp", lambda e, n=n: e.dma_start(out=I[n][:], in_=X[n][:, tok:tok + 512]), writes=[("in", sl)], chan=("in", sl, n))
            sfx = "_f" if d == 0 else "_b"
            S.op("sp", lambda e: e.dma_start(out=I["a"][:], in_=X["a" + sfx][:, tok:tok + 512]), writes=[("in", sl)], chan=("in", sl, "a"))
            S.op("sp", lambda e: e.dma_start(out=I["l"][:], in_=X["l" + sfx][:, tok:tok + 512]), writes=[("in", sl)], chan=("in", sl, "l"))
            R = [("in", sl)]
            f, kd, ka, Lc, Linc, w1, w2 = [tmp[n] for n in ("f", "kd", "ka", "Lc", "Linc", "w1", "w2")]
            S.op("dve", lambda e: e.tensor_scalar(out=f[:], in0=I["a"][:], scalar1=par2[:, 0:1], scalar2=par2[:, 1:2], op0=ALU.mult, op1=ALU.add), reads=R + ["par2"], writes=["f"])
            S.op("dve", lambda e: e.tensor_tensor(out=kd[:], in0=I["p_k"][:], in1=f[:], op=ALU.mult), reads=R + ["f"], writes=["kd"])
            S.op("pool", lambda e: e.tensor_tensor(out=ka[:], in0=I["kk"][:], in1=I["a"][:], op=ALU.mult), reads=R, writes=["ka"])
            S.op("dve", lambda e: e.tensor_tensor_scan(out=Lc[:], data0=m01[:], data1=I["l"][:], initial=0.0, op0=ALU.mult, op1=ALU.add), reads=R + ["m01"], writes=["Lc"])
            S.op("dve", lambda e: e.tensor_copy(out=ltot[:], in_=v3(Lc[:])[:, :, 63]), reads=["Lc"], writes=["ltot"])
            lt_b = ltot[:].unsqueeze(2).to_broadcast([128, 8, 64])
            if d == 0:
                LI = Lc
                lres = "Lc"
            else:
                S.op("dve", lambda e: e.tensor_tensor(out=v3(Linc[:]), in0=lt_b, in1=v3(Lc[:]), op=ALU.subtract), reads=["Lc", "ltot"], writes=["Linc"])
                S.op("dve", lambda e: e.tensor_tensor(out=Linc[:], in0=Linc[:], in1=I["l"][:], op=ALU.add), reads=["Linc"] + R, writes=["Linc"])
                LI = Linc
                lres = "Linc"
            S.op("act", lambda e: e.activation(out=wtot[sl][:], in_=ltot[:], func=AF.Exp), reads=["ltot"], writes=[("wtot", sl)])
            Bd = BD[sl]

            def bd_write(name, c0, in0, in1, neg=False):
                for hh in range(2):
                    prt = slice(hh * 64, hh * 64 + 64)
                    o = Bd[name][prt, :, c0 + hh * 64:c0 + hh * 64 + 64]
                    if neg:
                        S.op("dve", lambda e, o=o, prt=prt: e.scalar_tensor_tensor(out=o, in0=v3(in0[prt, :]), scalar=-1.0, in1=v3(in1[prt, :]), op0=ALU.mult, op1=ALU.mult),
                             reads=R + ["ka", "kd", "w1", "w2"], writes=[("bd", sl)])
                    elif in1 is None:
                        S.op("pool", lambda e, o=o, prt=prt: e.tensor_copy(out=o, in_=v3(in0[prt, :])), reads=R, writes=[("bd", sl)])
                    else:
                        S.op("pool" if hh else "dve", lambda e, o=o, prt=prt: e.tensor_tensor(out=o, in0=v3(in0[prt, :]), in1=v3(in1[prt, :]), op=ALU.mult),
                             reads=R + ["ka", "kd", "w1", "w2"], writes=[("bd", sl)])

            S.op("act", lambda e: e.activation(out=w1[:], in_=LI[:], func=AF.Exp), reads=[lres], writes=["w1"])
            bd_write("KR", 128, I["p_r"], w1)
            S.op("dve", lambda e: e.tensor_tensor(out=w2[:], in0=LI[:], in1=I["l"][:], op=ALU.subtract), reads=[lres] + R, writes=["w2"])
            S.op("act", lambda e: e.activation(out=w2[:], in_=w2[:], func=AF.Exp), reads=["w2"], writes=["w2"])
            bd_write("KR", 0, I["kk"], w2)
            S.op("act", lambda e: e.activation(out=w1[:], in_=LI[:], func=AF.Exp, scale=-1.0), reads=[lres], writes=["w1"])
            bd_write("BB", 0, ka, w1)
            bd_write("KT", 0, kd, w1)
            S.op("dve", lambda e: e.tensor_tensor(out=v3(w2[:]), in0=lt_b, in1=v3(LI[:]), op=ALU.subtract), reads=[lres, "ltot"], writes=["w2"])
            S.op("act", lambda e: e.activation(out=w2[:], in_=w2[:], func=AF.Exp), reads=["w2"], writes=["w2"])
            bd_write("BH", 0, ka, w2, neg=True)
            bd_write("KH", 0, kd, w2)
            bd_write("VV", 0, I["p_v"], None)
            BR = [("bd", sl)]
            for c in range(8):
                o2 = (c % 2) * 256
                S.op("pe", lambda e, c=c, o2=o2: e.matmul(pa[:, o2:o2 + 256], lhsT=Bd["BB"][:, c, :], rhs=Bd["KR"][:, c, :], start=True, stop=True), reads=BR, writes=["pa"])
                S.op("pe", lambda e, c=c, o2=o2: e.matmul(pk[:, o2:o2 + 256], lhsT=Bd["KT"][:, c, :], rhs=Bd["KR"][:, c, :], start=True, stop=True), reads=BR, writes=["pk"])
                if c % 2 == 1:
                    S.op("dve", lambda e, c=c: e.tensor_tensor(out=AB2[sl][:, c - 1:c + 1, :], in0=pa[:].rearrange("p (c x) -> p c x", c=2), in1=masks[:, d, 0, :].rearrange("p (c x) -> p c x", c=2), op=ALU.mult),
                         reads=["pa", "masks"], writes=[("AB2", sl)])
                    S.op("dve", lambda e, c=c: e.tensor_tensor(out=AK2[sl][:, c - 1:c + 1, :], in0=pk[:].rearrange("p (c x) -> p c x", c=2), in1=masks[:, d, 1, :].rearrange("p (c x) -> p c x", c=2), op=ALU.mult),
                         reads=["pk", "masks"], writes=[("AK2", sl)])
            for c in range(8):
                pb = c // 4
                o4 = (c % 4) * 128
                S.op("pe", lambda e, c=c, pb=pb, o4=o4: e.matmul(pi[pb][:, o4:o4 + 128], lhsT=Bd["KR"][:, c, 0:128], rhs=Bd["BB"][:, c, :], start=True, stop=True), reads=BR, writes=[("pi", pb)])
            for pb in range(2):
                S.op("dve", lambda e, pb=pb: e.tensor_tensor(out=Q0b[:, pb * 4:pb * 4 + 4, :], in0=pi[pb][:].rearrange("p (c x) -> p c x", c=4), in1=masks[:, d, 2, :].rearrange("p (c x) -> p c x", c=4), op=ALU.mult),
                     reads=[("pi", pb), "masks"], writes=["Q0b"])
            W = MT[sl]
            NOs, NOTs, T1, T1p = Pb[0], Pb[1], Qb[0], Qb[1]
            c4 = lambda ap: ap.rearrange("p (c x) -> p c x", c=4)

            def masked(dst, dres, src, sres, lev):
                for hf in range(2):
                    S.op("pool", lambda e, hf=hf: e.tensor_tensor(out=dst[:, hf * 4:hf * 4 + 4, :], in0=src[:, hf * 4:hf * 4 + 4, :], in1=c4(lvl[:, lev, :]), op=ALU.mult),
                         reads=[sres, "lvl"], writes=[dres])

            masked(NOs, "NOs", AB2[sl][:, :, 0:128], ("AB2", sl), 0)
            masked(NOTs, "NOTs", Q0b, "Q0b", 0)
            for hf in range(2):
                S.op("dve", lambda e, hf=hf: e.tensor_tensor(out=W[:, hf * 4:hf * 4 + 4, :], in0=NOs[:, hf * 4:hf * 4 + 4, :], in1=c4(ident4[:]), op=ALU.add), reads=["NOs", "ident4"], writes=[("MT", sl)])
                S.op("dve", lambda e, hf=hf: e.tensor_tensor(out=Wt[:, hf * 4:hf * 4 + 4, :], in0=NOTs[:, hf * 4:hf * 4 + 4, :], in1=c4(ident4[:]), op=ALU.add), reads=["NOTs", "ident4"], writes=["Wt"])
            WR = ("MT", sl)

            BK0 = ([pi[0], pi[1]], [("pi", 0), ("pi", 1)])
            BK1 = ([pa, pk], ["pa", "pk"])

            def mm8(lhs, lres, rhs, rres, bk):
                for c in range(8):
                    pb = c // 4
                    o4 = (c % 4) * 128
                    S.op("pe", lambda e, c=c, pb=pb, o4=o4: e.matmul(bk[0][pb][:, o4:o4 + 128], lhsT=lhs[:, c, :], rhs=rhs[:, c, :], start=True, stop=True),
                         reads=[lres, rres], writes=[bk[1][pb]])

            for lev in range(1, 6):
                masked(NOs, "NOs", AB2[sl][:, :, 0:128], ("AB2", sl), lev)
                masked(NOTs, "NOTs", Q0b, "Q0b", lev)
                mm8(NOTs, "NOTs", W, WR, BK0)
                mm8(NOs, "NOs", Wt, "Wt", BK1)
                for pb in range(2):
                    S.op("act", lambda e, pb=pb: e.copy(out=T1[:, pb * 4:pb * 4 + 4, :], in_=c4(BK0[0][pb][:])), reads=[BK0[1][pb]], writes=["T1"])
                for pb in range(2):
                    S.op("dve", lambda e, pb=pb: e.tensor_copy(out=T1p[:, pb * 4:pb * 4 + 4, :], in_=c4(BK1[0][pb][:])), reads=[BK1[1][pb]], writes=["T1p"])
                mm8(Wt, "Wt", T1, "T1", BK0)
                mm8(W, WR, T1p, "T1p", BK1)
                for pb in range(2):
                    S.op("act", lambda e, pb=pb: e.copy(out=T1[:, pb * 4:pb * 4 + 4, :], in_=c4(BK0[0][pb][:])), reads=[BK0[1][pb]], writes=["T1"])
                for pb in range(2):
                    S.op("dve", lambda e, pb=pb: e.tensor_tensor(out=Wt[:, pb * 4:pb * 4 + 4, :], in0=Wt[:, pb * 4:pb * 4 + 4, :], in1=c4(BK1[0][pb][:]), op=ALU.add), reads=[BK1[1][pb], "Wt"], writes=["Wt"])
                for hf in range(2):
                    S.op("pool", lambda e, hf=hf: e.tensor_tensor(out=W[:, hf * 4:hf * 4 + 4, :], in0=W[:, hf * 4:hf * 4 + 4, :], in1=T1[:, hf * 4:hf * 4 + 4, :], op=ALU.add), reads=["T1", WR], writes=[WR])
            for n in ("BH", "KH", "VV"):
                for c in range(8):
                    S.op("pe", lambda e, c=c, n=n: e.transpose(out=ptr[:, c * 128:(c + 1) * 128], in_=Bd[n][:, c, :], identity=idb[:]), reads=BR + ["idb"], writes=["ptr"])
                S.op("act", lambda e, n=n: e.copy(out=TT[sl][n][:], in_=ptr[:].rearrange("p (c x) -> p c x", c=8)), reads=["ptr"], writes=[("TT", sl)])

        def scan_group(b, d, gi, sl):
            Bd = BD[sl]
            order = range(8) if d == 0 else range(7, -1, -1)
            for idx, c in enumerate(order):
                yb = idx // 4
                yo = (idx % 4) * 128
                S.op("pe", lambda e, c=c: e.matmul(pS[:, 0:128], lhsT=Bd["KR"][:, c, 0:128], rhs=Tb[:], start=True, stop=False), reads=[("bd", sl), "Tb"], writes=["pS0"])
                S.op("pe", lambda e, c=c: e.matmul(pS[:, 0:128], lhsT=AK2[sl][:, c, 0:128], rhs=TT[sl]["VV"][:, c, :], start=False, stop=True), reads=[("AK2", sl), ("TT", sl)], writes=["pS0"])
                S.op("act", lambda e: e.copy(out=Xs[:], in_=pS[:, 0:128]), reads=["pS0"], writes=["Xs"])
                S.op("pe", lambda e, c=c: e.matmul(pS[:, 128:256], lhsT=MT[sl][:, c, :], rhs=Xs[:], start=True, stop=True), reads=[("MT", sl), "Xs"], writes=["pS1"])
                S.op("dve", lambda e: e.tensor_copy(out=Us[:], in_=pS[:, 128:256]), reads=["pS1"], writes=["Us"])
                S.op("pe", lambda e, c=c: e.matmul(pS[:, 256:384], lhsT=TT[sl]["BH"][:, c, :], rhs=Us[:], start=True, stop=False), reads=[("TT", sl), "Us"], writes=["pS2"])
                S.op("pe", lambda e, c=c: e.matmul(pS[:, 256:384], lhsT=TT[sl]["KH"][:, c, :], rhs=TT[sl]["VV"][:, c, :], start=False, stop=True), reads=[("TT", sl)], writes=["pS2"])
                S.op("pe", lambda e, c=c, yb=yb, yo=yo: e.matmul(pY[yb][:, yo:yo + 128], lhsT=Tb[:], rhs=Bd["KR"][:, c, 128:256], start=True, stop=False), reads=[("bd", sl), "Tb"], writes=[("pY", yb)])
                S.op("pe", lambda e, c=c, yb=yb, yo=yo: e.matmul(pY[yb][:, yo:yo + 128], lhsT=Us[:], rhs=AB2[sl][:, c, 128:256], start=False, stop=False), reads=[("AB2", sl), "Us"], writes=[("pY", yb)])
                S.op("pe", lambda e, c=c, yb=yb, yo=yo: e.matmul(pY[yb][:, yo:yo + 128], lhsT=TT[sl]["VV"][:, c, :], rhs=AK2[sl][:, c, 128:256], start=False, stop=True), reads=[("AK2", sl), ("TT", sl)], writes=[("pY", yb)])
                S.op("dve", lambda e, c=c: e.scalar_tensor_tensor(out=T[:], in0=T[:], scalar=wtot[sl][:, c:c + 1], in1=pS[:, 256:384], op0=ALU.mult, op1=ALU.add), reads=["pS2", "T", ("wtot", sl)], writes=["T"])
                S.op("act", lambda e: e.copy(out=Tb[:], in_=T[:]), reads=["T"], writes=["Tb"])
                if idx % 4 == 3:
                    cs = sorted(list(order)[idx - 3:idx + 1])
                    c_lo = cs[0]
                    for hh in range(2):
                        prt = slice(hh * 64, hh * 64 + 64)
                        src = pY[yb][prt, :].rearrange("p (c x) -> p c x", c=4)[:, :, hh * 64:hh * 64 + 64]
                        if d == 1:
                            dsts = [(y[prt, gi * 512 + (c_lo + 3 - j) * 64:gi * 512 + (c_lo + 4 - j) * 64], pY[yb][prt, j * 128 + hh * 64:j * 128 + hh * 64 + 64]) for j in range(4)]
                            for (dd, ss) in dsts:
                                S.op("dve", lambda e, dd=dd, ss=ss: e.tensor_tensor(out=dd, in0=dd, in1=ss, op=ALU.add), reads=[("pY", yb), "y"], writes=["y"])
                        else:
                            dd = y[prt, gi * 512 + c_lo * 64:gi * 512 + (c_lo + 4) * 64].rearrange("p (c x) -> p c x", c=4)
                            S.op("act", lambda e, dd=dd, src=src: e.copy(out=dd, in_=src), reads=[("pY", yb)], writes=["y"])

        def out_block(b, blk):
            tok = b * 4096 + blk * 512
            for n in ("p_r", "p_k", "p_v", "a_f", "a_b"):
                S.op("sp", lambda e, n=n: e.dma_start(out=ob[n][:], in_=X[n][:, tok:tok + 512]), writes=[("ob", n)], chan=("ob", n))
            S.op("sp", lambda e: e.dma_start(out=ogb[:], in_=gd[:, tok:tok + 512]), writes=["ogb"], chan="ogb")
            t1, t2, t3 = ob["t1"], ob["t2"], ob["t3"]
            ys = y[:, blk * 512:(blk + 1) * 512]
            S.op("pe", lambda e: e.matmul(pa[:], lhsT=bones[:], rhs=ys, start=True, stop=True), reads=["y", "bones"], writes=["pa"])
            S.op("dve", lambda e: e.scalar_tensor_tensor(out=t1[:], in0=pa[:], scalar=-1.0 / 64, in1=ys, op0=ALU.mult, op1=ALU.add), reads=["pa", "y"], writes=["t1"])
            S.op("act", lambda e: e.activation(out=t2[:], in_=t1[:], func=AF.Square), reads=["t1"], writes=["t2"])
            S.op("pe", lambda e: e.matmul(pk[:], lhsT=bones[:], rhs=t2[:], start=True, stop=True), reads=["t2", "bones"], writes=["pk"])
            S.op("act", lambda e: e.activation(out=t2[:], in_=pk[:], func=AF.Sqrt, scale=1.0 / 64, bias=64e-5), reads=["pk"], writes=["t2"])
            S.op("dve", lambda e: e.reciprocal(out=t2[:], in_=t2[:]), reads=["t2"], writes=["t2"])
            S.op("dve", lambda e: e.tensor_tensor(out=t1[:], in0=t1[:], in1=t2[:], op=ALU.mult), reads=["t1", "t2"], writes=["t1"])
            S.op("dve", lambda e: e.tensor_scalar(out=t1[:], in0=t1[:], scalar1=par2[:, 3:4], scalar2=par2[:, 4:5], op0=ALU.mult, op1=ALU.add), reads=["t1", "par2"], writes=["t1"])
            S.op("pool", lambda e: e.tensor_tensor(out=t2[:], in0=ob["a_f"][:], in1=ob["a_b"][:], op=ALU.add), reads=[("ob", "a_f"), ("ob", "a_b")], writes=["t2"])
            S.op("dve", lambda e: e.tensor_scalar(out=t2[:], in0=t2[:], scalar1=par2[:, 0:1], scalar2=par2[:, 5:6], op0=ALU.mult, op1=ALU.add), reads=["t2", "par2"], writes=["t2"])
            S.op("pool", lambda e: e.tensor_tensor(out=t2[:], in0=t2[:], in1=ob["p_k"][:], op=ALU.mult), reads=["t2", ("ob", "p_k")], writes=["t2"])
            S.op("dve", lambda e: e.scalar_tensor_tensor(out=t3[:], in0=t2[:], scalar=par2[:, 2:3], in1=ob["p_r"][:], op0=ALU.mult, op1=ALU.mult), reads=["t2", "par2", ("ob", "p_r")], writes=["t3"])
            S.op("pe", lambda e: e.matmul(pa[:], lhsT=bones[:], rhs=t3[:], start=True, stop=True), reads=["t3", "bones"], writes=["pa"])
            S.op("dve", lambda e: e.tensor_tensor(out=t3[:], in0=pa[:], in1=ob["p_v"][:], op=ALU.mult), reads=["pa", ("ob", "p_v")], writes=["t3"])
            S.op("pool", lambda e: e.tensor_tensor(out=t3[:], in0=t3[:], in1=t1[:], op=ALU.add), reads=["t3", "t1"], writes=["t3"])
            S.op("dve", lambda e: e.tensor_tensor(out=oo[:], in0=t3[:], in1=ogb[:], op=ALU.mult), reads=["t3", "ogb"], writes=["oo"])
            S.op("sp", lambda e: e.dma_start(out=oT[:, tok:tok + 512], in_=oo[:]), reads=["oo"], chan="oo")

        def collect(fn, *args):
            buf = []
            real = S.op
            S.op = lambda *a, **k: buf.append((a, k))
            try:
                fn(*args)
            finally:
                S.op = real
            return buf

        def emit_interleaved(A, B):
            nA, nB = len(A), len(B)
            ia = 0
            for ib, (a, k) in enumerate(B):
                tgt = (ib * nA) // max(nB, 1)
                while ia < tgt:
                    S.op(*A[ia][0], **A[ia][1])
                    ia += 1
                S.op(*a, **k)
            while ia < nA:
                S.op(*A[ia][0], **A[ia][1])
                ia += 1

        for b in range(2):
            for d in range(2):
                S.op("dve", lambda e: e.memset(T[:], 0.0), writes=["T"])
                S.op("pool", lambda e: e.memset(Tb[:], 0.0), writes=["Tb"])
                gorder = list(range(8)) if d == 0 else list(range(7, -1, -1))
                prep_group(b, d, gorder[0], 0)
                for j, gi in enumerate(gorder):
                    A = collect(prep_group, b, d, gorder[j + 1], (j + 1) % 2) if j + 1 < 8 else []
                    B = collect(scan_group, b, d, gi, j % 2)
                    emit_interleaved(A, B)
            for blk in range(8):
                out_block(b, blk)
        S.run()
    return nc


def r2_inputs(inp, l, c):
    hs = [2 * c, 2 * c + 1]
    ch = np.concatenate([np.arange(h * 64, h * 64 + 64) for h in hs])
    par2 = np.zeros((128, 8), np.float32)
    par2[:, 0] = inp["k_a"][l][ch]
    par2[:, 2] = inp["r_k"][l].reshape(-1)[ch]
    par2[:, 3] = inp["gn_w"][l][ch]
    par2[:, 4] = inp["gn_b"][l][ch]
    return dict(par2=par2)
def build_p0():
    nc = bass.Bass("TRN2", target_bir_lowering=False)
    x = nc.dram_tensor("x", [1024, 2048], F32, kind="ExternalInput").ap()
    g1 = nc.dram_tensor("g1", [128, 16], F32, kind="ExternalInput").ap()
    ident = nc.dram_tensor("ident", [128, 128], F32, kind="ExternalInput").ap()
    hT = nc.dram_tensor("hT", [2048, 1024], F32, kind="ExternalOutput").ap()
    uT = nc.dram_tensor("uT", [2048, 1024], BF16, kind="ExternalOutput").ap()
    with contextlib.ExitStack() as st:
        sb = lambda name, shape, dt: st.enter_context(nc.sbuf_tensor(name, shape, dt))
        ps = lambda name, shape, dt: st.enter_context(nc.psum_tensor(name, shape, dt))
        xt = [sb("xt%d" % i, [128, 2048], F32) for i in range(2)]
        h = sb("h", [128, 16, 1024], F32)
        u = sb("u", [128, 16, 1024], BF16)
        sq = [sb("sq%d" % i, [128, 512], F32) for i in range(2)]
        rs = sb("rs", [128, 512], F32)
        g = sb("g", [128, 16], F32)
        idt = sb("idt", [128, 128], F32)
        ones = sb("ones", [128, 128], F32)
        pt = [ps("pt%d" % i, [128, 512], F32) for i in range(4)]
        pn = ps("pn", [128, 512], F32)
        S = Sched(nc)
        S.op("sp", lambda e: e.dma_start(out=g[:], in_=g1), writes=["g"], chan="g")
        S.op("sp", lambda e: e.dma_start(out=idt[:], in_=ident), writes=["idt"], chan="idt")
        S.op("dve", lambda e: e.memset(ones[:], 1.0), writes=["ones"])
        for tt in range(8):
            b = tt % 2
            S.op("sp", lambda e, tt=tt, b=b: e.dma_start(out=xt[b][:], in_=x[tt * 128:(tt + 1) * 128, :]), writes=[("xt", b)], chan=("xt", b))
            for kg in range(4):
                pb = kg
                for j in range(4):
                    k = kg * 4 + j
                    S.op("pe", lambda e, k=k, j=j, pb=pb, b=b: e.transpose(out=pt[pb][:, j * 128:(j + 1) * 128], in_=xt[b][:, k * 128:(k + 1) * 128], identity=idt[:]),
                         reads=[("xt", b), "idt"], writes=[("pt", pb)])
                if kg % 2:
                    f = lambda e, kg=kg, pb=pb, tt=tt: e.copy(out=h[:, kg * 4:(kg + 1) * 4, tt * 128:(tt + 1) * 128], in_=pt[pb][:].rearrange("p (j t) -> p j t", j=4))
                else:
                    f = lambda e, kg=kg, pb=pb, tt=tt: e.tensor_copy(out=h[:, kg * 4:(kg + 1) * 4, tt * 128:(tt + 1) * 128], in_=pt[pb][:].rearrange("p (j t) -> p j t", j=4))
                S.op("act" if kg % 2 else "dve", f, reads=[("pt", pb)], writes=[("h", tt // 4)])
        for tb in range(2):
            tsl = slice(tb * 512, (tb + 1) * 512)
            emit_rmsnorm(S, nc, h[:, :, tsl], u[:, :, tsl], g, sq, rs, ones, pn, ("h", tb), ("u", tb), "g")
        S.op("sp", lambda e: e.dma_start(out=hT.rearrange("(k p) t -> p k t", p=128), in_=h[:]), reads=[("h", 0), ("h", 1)], chan="oh")
        S.op("sp", lambda e: e.dma_start(out=uT.rearrange("(k p) t -> p k t", p=128), in_=u[:]), reads=[("u", 0), ("u", 1)], chan="ou")
        S.run()
    return nc


_NC_CACHE = {}


def _prog(name, fn):
    if name not in _NC_CACHE:
        _NC_CACHE[name] = fn()
    return _NC_CACHE[name]


def _run(nc, in_maps):
    res = run_bass_kernel_spmd(nc, in_maps, core_ids=list(range(NCORES)))
    return res.results


def kernel(x, norm1_g, w_in, tshift_mu, w0, w_lora_up, a0, a_lora_up, g_lora_up, k_k, k_a, r_k, gn_w, gn_b, rel_bias,
           w_branch_rwkv, w_branch_attn, w_out, norm2_g, w_mlp_in, w_mlp_out, final_g):
    inp = dict(x=x, norm1_g=norm1_g, w_in=w_in, tshift_mu=tshift_mu, w0=w0, w_lora_up=w_lora_up, a0=a0, a_lora_up=a_lora_up,
               g_lora_up=g_lora_up, k_k=k_k, k_a=k_a, r_k=r_k, gn_w=gn_w, gn_b=gn_b, rel_bias=rel_bias, w_branch_rwkv=w_branch_rwkv,
               w_branch_attn=w_branch_attn, w_out=w_out, norm2_g=norm2_g, w_mlp_in=w_mlp_in, w_mlp_out=w_mlp_out, final_g=final_g)
    inp = {k: np.asarray(v, dtype=np.float32) for k, v in inp.items()}
    bf = ml_dtypes.bfloat16
    depth = inp["w_in"].shape[0]
    vec = lambda v: np.ascontiguousarray(v.reshape(16, 128).T)
    xs = inp["x"].reshape(8192, 2048)
    eye = np.eye(128, dtype=np.float32)
    r = _run(_prog("p0", build_p0), [dict(x=np.ascontiguousarray(xs[c * 1024:(c + 1) * 1024]), g1=vec(inp["norm1_g"][0]), ident=eye) for c in range(NCORES)])
    hT = [np.asarray(r[c]["hT"]) for c in range(NCORES)]
    uT = [np.asarray(r[c]["uT"]) for c in range(NCORES)]
    tabs = bias_index_tables()
    consts2 = r2_consts()
    out = None
    for l in range(depth):
        uT_all = np.ascontiguousarray(np.concatenate(uT, axis=1))
        r1 = _run(_prog("r1", build_r1), [dict(r1_inputs(inp, l, c), uT=uT_all) for c in range(NCORES)])
        in2 = []
        for c in range(NCORES):
            m = dict(consts2, **r2_inputs(inp, l, c))
            for n in R2_IN:
                m[n] = np.asarray(r1[c][n])
            m["g"] = np.asarray(r1[c]["g"])
            in2.append(m)
        r2 = _run(_prog("r2", build_r2), in2)
        ina = []
        for c in range(NCORES):
            heads = [g * 8 + c for g in range(3)]
            qc = lambda h: np.arange(3712 + h * 64, 3712 + h * 64 + 64)
            kc = lambda h: np.arange(3712 + 1536 + h * 64, 3712 + 1536 + h * 64 + 64)
            vc = lambda h: np.arange(3712 + 3072 + h * 64, 3712 + 3072 + h * 64 + 64)
            cols = np.concatenate([qc(heads[0]), qc(heads[1]), kc(heads[0]), kc(heads[1]), qc(heads[2]), vc(heads[2]), kc(heads[2]), vc(heads[0]), vc(heads[1])])
            bias = np.stack([np.where(m, inp["rel_bias"][:, heads[g]][idx], np.float32(-30000.0)) for g, (idx, m) in enumerate(tabs)], axis=1).astype(np.float32)
            ina.append(dict(uT=uT_all, wa=np.ascontiguousarray(inp["w_in"][l][:, cols]), biasT=np.ascontiguousarray(bias), identb=np.eye(128).astype(bf)))
        ra = _run(_prog("battn", build_b_attn), ina)
        o_all = np.concatenate([np.asarray(r2[c]["oT"]) for c in range(NCORES)] + [np.asarray(ra[c]["oT"]) for c in range(NCORES)], axis=0)
        last = (l == depth - 1)
        common = dict(wg=np.ascontiguousarray(inp["w_in"][l][:, 8320:]), wbr=inp["w_branch_rwkv"][l], wba=inp["w_branch_attn"][l], wout=inp["w_out"][l],
                      w1=inp["w_mlp_in"][l], w2=inp["w_mlp_out"][l], g2=vec(inp["norm2_g"][l]),
                      gn=vec(inp["final_g"] if last else inp["norm1_g"][l + 1]), ident=eye)
        inc = [dict(common, hT=hT[c], uT=uT[c], oT=np.ascontiguousarray(o_all[:, c * 1024:(c + 1) * 1024])) for c in range(NCORES)]
        rc = _run(_prog("c_last" if last else "c", lambda: build_c(last)), inc)
        if last:
            out = np.concatenate([np.asarray(rc[c]["out"]) for c in range(NCORES)], axis=0)
        else:
            hT = [np.asarray(rc[c]["hTo"]) for c in range(NCORES)]
            uT = [np.asarray(rc[c]["uTo"]) for c in range(NCORES)]
    return out.reshape(inp["x"].shape).astype(np.float32)
```

```python
import contextlib
import numpy as np
import ml_dtypes
import concourse.bass as bass
import concourse.mybir as mybir
from concourse.bass_utils import run_bass_kernel_spmd

F32 = mybir.dt.float32
BF16 = mybir.dt.bfloat16
AF = mybir.ActivationFunctionType
ALU = mybir.AluOpType
NCORES = 8


class Sched:
    ENGS = ("pe", "act", "dve", "pool", "sp")

    def __init__(self, nc):
        self.nc = nc
        self.ops = []
        self.last_w = {}
        self.readers = {}
        self.chan_cnt = {}
        self.chan_order = []
        self.bar = {}

    def op(self, eng, fn, reads=(), writes=(), chan=None, inc=16):
        idx = len(self.ops)
        deps = set()
        for r in reads:
            if r in self.last_w:
                deps.add(self.last_w[r])
        for w in writes:
            if w in self.last_w:
                deps.add(self.last_w[w])
            deps.update(self.readers.get(w, ()))
        if eng in self.bar:
            deps.update(self.bar.pop(eng))
        deps.discard(idx)
        cdeps = []
        odeps = []
        for d in deps:
            o = self.ops[d]
            if o["chan"] is not None:
                cdeps.append((o["chan"], self.chan_cnt[o["chan"]]))
            else:
                odeps.append(d)
        if chan is not None:
            if chan not in self.chan_cnt:
                self.chan_cnt[chan] = 0
                self.chan_order.append(chan)
            self.chan_cnt[chan] += inc
        self.ops.append(dict(eng=eng, fn=fn, odeps=odeps, cdeps=cdeps, chan=chan, waited=False, inc=inc))
        for r in reads:
            self.readers.setdefault(r, []).append(idx)
        for w in writes:
            self.last_w[w] = idx
            self.readers[w] = []
        return idx

    def barrier(self):
        last = {}
        for i, o in enumerate(self.ops):
            last[(o["eng"], o["chan"])] = i
        deps = set(last.values())
        for e in self.ENGS:
            self.bar[e] = set(deps)

    def run(self):
        nc = self.nc
        ops = self.ops
        for i, o in enumerate(ops):
            for d in o["odeps"]:
                p = ops[d]
                if p["eng"] == "pe" and o["eng"] == "pe":
                    continue
                p["waited"] = True
        cnt = {e: 0 for e in self.ENGS}
        for o in ops:
            if o["chan"] is None and o["waited"]:
                cnt[o["eng"]] += 1
                o["val"] = cnt[o["eng"]]
        import contextlib
        with contextlib.ExitStack() as st:
            esem = {e: st.enter_context(nc.semaphore("s_" + e)) for e in self.ENGS}
            csem = {c: st.enter_context(nc.semaphore("c_%d" % i)) for i, c in enumerate(self.chan_order)}
            block = st.enter_context(nc.Block())
            final_c = {c: self.chan_cnt[c] for c in self.chan_order}

            def emit(ename):
                def body(eng):
                    waited = {}
                    for o in ops:
                        if o["eng"] != ename:
                            continue
                        need = {}
                        for d in o["odeps"]:
                            p = ops[d]
                            if p["eng"] == "pe" and ename == "pe":
                                continue
                            k = ("e", p["eng"])
                            need[k] = max(need.get(k, 0), p["val"])
                        for c, v in o["cdeps"]:
                            k = ("c", c)
                            need[k] = max(need.get(k, 0), v)
                        for k, v in need.items():
                            if waited.get(k, 0) >= v:
                                continue
                            waited[k] = v
                            eng.wait_ge(esem[k[1]] if k[0] == "e" else csem[k[1]], v)
                        ins = o["fn"](eng)
                        if o["chan"] is not None:
                            ins.then_inc(csem[o["chan"]], o["inc"])
                        elif o["waited"]:
                            ins.then_inc(esem[ename], 1)
                    if ename == "sp":
                        for c in self.chan_order:
                            eng.wait_ge(csem[c], final_c[c])
                        for e in self.ENGS:
                            if e != "sp" and cnt[e] > 0:
                                eng.wait_ge(esem[e], cnt[e])
                return body

            block.sync(emit("sp"))
            block.scalar(emit("act"))
            block.vector(emit("dve"))
            block.gpsimd(emit("pool"))
            block.tensor(emit("pe"))
        return cnt, final_c
class WRing:
    def __init__(self, S, st, nc, n=2, name="wr"):
        self.S = S
        self.n = n
        self.slots = [st.enter_context(nc.sbuf_tensor("%s%d" % (name, i), [128, 16, 512], BF16)) for i in range(n)]
        self.stg = [st.enter_context(nc.sbuf_tensor("%sstg%d" % (name, i), [128, 8, 512], F32)) for i in range(3)]
        self.i = 0
        self.j = 0
        self.name = name

    def load(self, W, k0, nk, c0, ncols):
        s = self.i % self.n
        self.i += 1
        slot = self.slots[s]
        src = W[k0 * 128:(k0 + nk) * 128, c0:c0 + ncols].rearrange("(k p) c -> p k c", p=128)
        for ka in range(0, nk, 8):
            kb = min(nk, ka + 8)
            j = self.j % 3
            eng = ("dve", "act")[self.j % 2]
            self.j += 1
            stg = self.stg[j]
            self.S.op("sp", lambda e, ka=ka, kb=kb, stg=stg, src=src: e.dma_start(out=stg[:, 0:kb - ka, 0:ncols], in_=src[:, ka:kb, :]),
                      writes=[(self.name + "stg", j)], chan=(self.name + "stg", j))
            if eng == "act":
                f = lambda e, ka=ka, kb=kb, stg=stg, slot=slot: e.copy(out=slot[:, ka:kb, 0:ncols], in_=stg[:, 0:kb - ka, 0:ncols])
            else:
                f = lambda e, ka=ka, kb=kb, stg=stg, slot=slot: e.tensor_copy(out=slot[:, ka:kb, 0:ncols], in_=stg[:, 0:kb - ka, 0:ncols])
            self.S.op(eng, f, reads=[(self.name + "stg", j)], writes=[(self.name, s)])
        return slot, (self.name, s)


def emit_rmsnorm(S, nc, h, u, g, sq, rs, ones, pn, hres, ures, gres, out_fp32_inplace=False):
    for k in range(16):
        b = k % 2
        S.op("act", lambda e, k=k, b=b: e.activation(out=sq[b][:], in_=h[:, k, :], func=AF.Square),
             reads=[hres], writes=[("sq", b)])
        S.op("pe", lambda e, k=k, b=b: e.matmul(pn[:], lhsT=ones[:], rhs=sq[b][:], start=(k == 0), stop=(k == 15)),
             reads=[("sq", b), "ones"], writes=["pn"])
    S.op("act", lambda e: e.activation(out=rs[:], in_=pn[:], func=AF.Sqrt, scale=1.0 / 2048, bias=1e-6),
         reads=["pn"], writes=["rs"])
    S.op("dve", lambda e: e.reciprocal(out=rs[:], in_=rs[:]), reads=["rs"], writes=["rs"])
    for k in range(16):
        S.op("dve", lambda e, k=k: e.scalar_tensor_tensor(out=u[:, k, :], in0=h[:, k, :], scalar=g[:, k:k + 1], in1=rs[:], op0=ALU.mult, op1=ALU.mult),
             reads=[hres, gres, "rs"], writes=[ures])


def build_c(last):
    nc = bass.Bass("TRN2", target_bir_lowering=False)
    di = lambda name, shape, dt=F32: nc.dram_tensor(name, shape, dt, kind="ExternalInput").ap()
    hT = di("hT", [2048, 1024]); uT = di("uT", [2048, 1024], BF16); oT = di("oT", [1536, 1024], BF16)
    wg = di("wg", [2048, 4096]); wbr = di("wbr", [1024, 2048]); wba = di("wba", [512, 2048]); wout = di("wout", [2048, 2048])
    w1 = di("w1", [2048, 8192]); w2 = di("w2", [8192, 2048]); g2d = di("g2", [128, 16]); gnd = di("gn", [128, 16])
    ident = di("ident", [128, 128])
    if last:
        outd = nc.dram_tensor("out", [1024, 2048], F32, kind="ExternalOutput").ap()
    else:
        hTo = nc.dram_tensor("hTo", [2048, 1024], F32, kind="ExternalOutput").ap()
        uTo = nc.dram_tensor("uTo", [2048, 1024], BF16, kind="ExternalOutput").ap()
    with contextlib.ExitStack() as st:
        sb = lambda name, shape, dt: st.enter_context(nc.sbuf_tensor(name, shape, dt))
        ps = lambda name, shape, dt: st.enter_context(nc.psum_tensor(name, shape, dt))
        S = Sched(nc)
        h = sb("h", [128, 16, 512], F32); u = sb("u", [128, 16, 512], BF16); o = sb("o", [128, 12, 512], BF16)
        mg = sb("mg", [128, 16, 512], BF16); hid = sb("hid", [128, 16, 512], BF16)
        gt = [sb("gt%d" % i, [128, 4, 512], BF16) for i in range(2)]
        t1 = sb("t1", [128, 4, 512], F32); t2 = sb("t2", [128, 512], F32)
        rl = [sb("rl%d" % i, [128, 512], F32) for i in range(2)]
        sq = [sb("sq%d" % i, [128, 512], F32) for i in range(2)]
        rs = sb("rs", [128, 512], F32)
        g2 = sb("g2s", [128, 16], F32); gn = sb("gns", [128, 16], F32)
        ones = sb("ones", [128, 128], F32); idt = sb("idt", [128, 128], F32)
        ring = WRing(S, st, nc, 2)
        pA = [ps("pA%d" % i, [128, 512], F32) for i in range(4)]
        pB = [ps("pB%d" % i, [128, 512], F32) for i in range(3)]
        pn = ps("pn", [128, 512], F32)
        S.op("sp", lambda e: e.dma_start(out=g2[:], in_=g2d), writes=["g2"], chan="g2")
        S.op("sp", lambda e: e.dma_start(out=gn[:], in_=gnd), writes=["gn"], chan="gn")
        S.op("sp", lambda e: e.dma_start(out=idt[:], in_=ident), writes=["idt"], chan="idt")
        S.op("dve", lambda e: e.memset(ones[:], 1.0), writes=["ones"])
        hTr = hT.rearrange("(k p) t -> p k t", p=128); uTr = uT.rearrange("(k p) t -> p k t", p=128)
        oTr = oT.rearrange("(k p) t -> p k t", p=128)

        PBR = [("pB", 0), ("pB", 1), ("pB", 2), "pn"]
        PAR = [("pA", j) for j in range(4)]

        def mm_group(pbanks, pres, slot, sres, nk, rhs, rres, first=True, lastk=True, ncol=4):
            for j in range(ncol):
                for k in range(nk):
                    S.op("pe", lambda e, j=j, k=k: e.matmul(pbanks[j][:], lhsT=slot[:, k, j * 128:(j + 1) * 128], rhs=rhs[:, k, :],
                                                            start=(first and k == 0), stop=(lastk and k == nk - 1)),
                         reads=[sres, rres], writes=[pres[j]])

        for tb in range(2):
            tsl = slice(tb * 512, (tb + 1) * 512)
            S.op("sp", lambda e, tsl=tsl: e.dma_start(out=h[:], in_=hTr[:, :, tsl]), writes=["h"], chan="h")
            S.op("sp", lambda e, tsl=tsl: e.dma_start(out=u[:], in_=uTr[:, :, tsl]), writes=["u"], chan="u")
            S.op("sp", lambda e, tsl=tsl: e.dma_start(out=o[:], in_=oTr[:, :, tsl]), writes=["o"], chan="o")
            for cg in range(4):
                slot, sres = ring.load(wg, 0, 16, cg * 512, 512)
                mm_group(pA, PAR, slot, sres, 16, u, "u")
                for j in range(4):
                    S.op("act", lambda e, j=j: e.activation(out=gt[0][:, j, :], in_=pA[j][:], func=AF.Sigmoid),
                         reads=[("pA", j)], writes=[("gt0", j)])
                slot, sres = ring.load(wbr, 0, 8, cg * 512, 512)
                pBx = [pB[0], pB[1], pB[2], pn]
                mm_group(pBx, PBR, slot, sres, 8, o, "o")
                for j in range(4):
                    S.op("dve", lambda e, j=j, pBx=pBx: e.tensor_tensor(out=t1[:, j, :], in0=pBx[j][:], in1=gt[0][:, j, :], op=ALU.mult),
                         reads=[PBR[j], ("gt0", j)], writes=[("t1", j)])
                slot, sres = ring.load(wg, 0, 16, 2048 + cg * 512, 512)
                mm_group(pA, PAR, slot, sres, 16, u, "u")
                for j in range(4):
                    S.op("act", lambda e, j=j: e.activation(out=gt[1][:, j, :], in_=pA[j][:], func=AF.Sigmoid),
                         reads=[("pA", j)], writes=[("gt1", j)])
                slot, sres = ring.load(wba, 0, 4, cg * 512, 512)
                osub = o[:, 8:12, :]
                mm_group(pBx, PBR, slot, sres, 4, osub, "o")
                for j in range(4):
                    S.op("dve", lambda e, j=j, pBx=pBx: e.tensor_tensor(out=t2[:], in0=pBx[j][:], in1=gt[1][:, j, :], op=ALU.mult),
                         reads=[PBR[j], ("gt1", j)], writes=["t2"])
                    S.op("pool", lambda e, j=j, cg=cg: e.tensor_tensor(out=mg[:, cg * 4 + j, :], in0=t1[:, j, :], in1=t2[:], op=ALU.add),
                         reads=[("t1", j), "t2"], writes=["mg"])
            for cg in range(4):
                slot, sres = ring.load(wout, 0, 16, cg * 512, 512)
                mm_group(pA, PAR, slot, sres, 16, mg, "mg")
                for j in range(4):
                    S.op("dve", lambda e, j=j, cg=cg: e.tensor_tensor(out=h[:, cg * 4 + j, :], in0=h[:, cg * 4 + j, :], in1=pA[j][:], op=ALU.add),
                         reads=[("pA", j), "h"], writes=["h"])
            emit_rmsnorm(S, nc, h, u, g2, sq, rs, ones, pn, "h", "u", "g2")
            for qt in range(4):
                for cg in range(4):
                    slot, sres = ring.load(w1, 0, 16, (qt * 4 + cg) * 512, 512)
                    mm_group(pA, PAR, slot, sres, 16, u, "u")
                    for j in range(4):
                        b = j % 2
                        S.op("act", lambda e, j=j, b=b: e.activation(out=rl[b][:], in_=pA[j][:], func=AF.Relu),
                             reads=[("pA", j)], writes=[("rl", b)])
                        S.op("pool" if j % 2 else "dve", lambda e, j=j, b=b, cg=cg: e.tensor_tensor(out=hid[:, cg * 4 + j, :], in0=rl[b][:], in1=rl[b][:], op=ALU.mult),
                             reads=[("rl", b)], writes=["hid"])
                for og in range(4):
                    pBx = [pB[0], pB[1], pB[2], pn]
                    slot, sres = ring.load(w2, qt * 16, 16, og * 512, 512)
                    mm_group(pBx, PBR, slot, sres, 16, hid, "hid")
                    for j in range(4):
                        S.op("dve", lambda e, j=j, og=og, pBx=pBx: e.tensor_tensor(out=h[:, og * 4 + j, :], in0=h[:, og * 4 + j, :], in1=pBx[j][:], op=ALU.add),
                             reads=[PBR[j], "h"], writes=["h"])
            if not last:
                emit_rmsnorm(S, nc, h, u, gn, sq, rs, ones, pn, "h", "u", "gn")
                S.op("sp", lambda e, tsl=tsl: e.dma_start(out=hTo.rearrange("(k p) t -> p k t", p=128)[:, :, tsl], in_=h[:]), reads=["h"], chan="oh")
                S.op("sp", lambda e, tsl=tsl: e.dma_start(out=uTo.rearrange("(k p) t -> p k t", p=128)[:, :, tsl], in_=u[:]), reads=["u"], chan="ou")
            else:
                for k in range(16):
                    b = k % 2
                    S.op("act", lambda e, k=k, b=b: e.activation(out=sq[b][:], in_=h[:, k, :], func=AF.Square), reads=["h"], writes=[("sq", b)])
                    S.op("pe", lambda e, k=k, b=b: e.matmul(pn[:], lhsT=ones[:], rhs=sq[b][:], start=(k == 0), stop=(k == 15)),
                         reads=[("sq", b), "ones"], writes=["pn"])
                S.op("act", lambda e: e.activation(out=rs[:], in_=pn[:], func=AF.Sqrt, scale=1.0 / 2048, bias=1e-6), reads=["pn"], writes=["rs"])
                S.op("dve", lambda e: e.reciprocal(out=rs[:], in_=rs[:]), reads=["rs"], writes=["rs"])
                for k in range(16):
                    S.op("dve", lambda e, k=k: e.scalar_tensor_tensor(out=h[:, k, :], in0=h[:, k, :], scalar=gn[:, k:k + 1], in1=rs[:], op0=ALU.mult, op1=ALU.mult),
                         reads=["h", "gn", "rs"], writes=["h"])
                otv = t1[:].rearrange("p a b -> p (a b)")
                T1R = [("t1", j) for j in range(4)]
                for tt in range(4):
                    for kg in range(4):
                        for j in range(4):
                            k = kg * 4 + j
                            S.op("pe", lambda e, k=k, j=j, kg=kg, tt=tt: e.transpose(out=pA[kg][:, j * 128:(j + 1) * 128], in_=h[:, k, tt * 128:(tt + 1) * 128], identity=idt[:]),
                                 reads=["h", "idt"], writes=[("pA", kg)])
                        S.op("act" if kg % 2 else "dve",
                             (lambda e, kg=kg: e.copy(out=otv[:, kg * 512:(kg + 1) * 512], in_=pA[kg][:])) if kg % 2 else
                             (lambda e, kg=kg: e.tensor_copy(out=otv[:, kg * 512:(kg + 1) * 512], in_=pA[kg][:])),
                             reads=[("pA", kg)], writes=[("t1", kg)])
                    r0 = tb * 512 + tt * 128
                    S.op("sp", lambda e, r0=r0: e.dma_start(out=outd[r0:r0 + 128, :], in_=otv), reads=T1R, chan="oo")
        S.run()
    return nc
GROUPS = ((128, 1), (512, 4), (2048, 16))
S_LEN = 4096


def t5_bucket_np(rel):
    nb = 16
    max_exact = 8
    ret = np.where(rel > 0, nb, 0)
    n = np.abs(rel)
    nf = np.maximum(n, 1).astype(np.float32)
    large = max_exact + (np.log(nf / np.float32(max_exact)) / np.float32(np.log(1024 / max_exact)) * np.float32(nb - max_exact)).astype(np.int32)
    large = np.minimum(large, nb - 1)
    return ret + np.where(n < max_exact, n, large)


def bias_index_tables():
    kap = np.arange(128)[:, None]
    qi = np.arange(128)[None, :]
    out = []
    for (window, d) in GROUPS:
        da = kap - 64 - qi
        db = kap + 64 - qi
        delta = np.concatenate([da, db], axis=1)
        out.append((t5_bucket_np(delta * d), np.abs(delta) <= 64))
    return out


def emit_attention(S, nc, st, uTb, Wa, biasd, identd, oT_dst, sbufs):
    sb, ps = sbufs["sb"], sbufs["ps"]
    wa, ub, Qz0, Qz1, Qz2, K01, K2, V01, V2, Vt, E, Pf, Pm, accn, accd, onesk, idb, ot = [sbufs[k] for k in
        ("wa", "ub", "Qz0", "Qz1", "Qz2", "K01", "K2", "V01", "V2", "Vt", "E", "Pf", "Pm", "accn", "accd", "onesk", "idb", "ot")]
    pj, pS, pOD, pT = sbufs["pj"], sbufs["pS"], sbufs["pOD"], sbufs["pT"]
    uTr = uTb.rearrange("(k p) t -> p k t", p=128)
    for tblk in range(8):
        b = tblk % 2
        S.op("sp", lambda e, tblk=tblk, b=b: e.dma_start(out=ub[b][:], in_=uTr[:, :, tblk * 512:(tblk + 1) * 512]), writes=[("ub", b)], chan=("ub", b))
        for ct in range(5):
            M = 64 if ct == 3 else 128
            c0 = ct * 128 if ct < 4 else 448
            pb = (tblk * 5 + ct) % 2
            for k in range(16):
                S.op("pe", lambda e, k=k, M=M, c0=c0, pb=pb, b=b: e.matmul(pj[pb][0:M, :], lhsT=wa[:, k, c0:c0 + M], rhs=ub[b][:, k, :], start=(k == 0), stop=(k == 15)),
                     reads=[("ub", b), "wa"], writes=[("pj", pb)])
            t0 = tblk * 512
            if ct == 0:
                S.op("act", lambda e, pb=pb, t0=t0: e.copy(out=Qz0[0:64, t0:t0 + 512], in_=pj[pb][0:64, :]), reads=[("pj", pb)], writes=["Qz0"])
                S.op("dve", lambda e, pb=pb, t0=t0: e.tensor_copy(out=Qz1[64:128, :].rearrange("p (r i) -> p r i", r=4)[:, :, t0 // 4:t0 // 4 + 128],
                                                                   in_=pj[pb][64:128, :].rearrange("p (i r) -> p r i", r=4)), reads=[("pj", pb)], writes=["Qz1"])
            elif ct == 1:
                S.op("act", lambda e, pb=pb, t0=t0: e.copy(out=K01[0:64, 64 + t0:64 + t0 + 512], in_=pj[pb][0:64, :]), reads=[("pj", pb)], writes=["K01"])
                S.op("dve", lambda e, pb=pb, t0=t0: e.tensor_copy(out=K01[64:128, :].rearrange("p (r i) -> p r i", r=4)[:, :, 64 + t0 // 4:64 + t0 // 4 + 128],
                                                                   in_=pj[pb][64:128, :].rearrange("p (i r) -> p r i", r=4)), reads=[("pj", pb)], writes=["K01"])
            elif ct == 2:
                S.op("act", lambda e, pb=pb, t0=t0: e.copy(out=Qz2[0:64, :].rearrange("p (r i) -> p r i", r=16)[:, :, t0 // 16:t0 // 16 + 32],
                                                            in_=pj[pb][0:64, :].rearrange("p (i r) -> p r i", r=16)), reads=[("pj", pb)], writes=["Qz2"])
                S.op("dve", lambda e, pb=pb, t0=t0: e.tensor_copy(out=V2[64:128, :].rearrange("p (r i) -> p r i", r=16)[:, :, 64 + t0 // 16:64 + t0 // 16 + 32],
                                                                   in_=pj[pb][64:128, :].rearrange("p (i r) -> p r i", r=16)), reads=[("pj", pb)], writes=["V2"])
            elif ct == 3:
                S.op("act", lambda e, pb=pb, t0=t0: e.copy(out=K2[0:64, :].rearrange("p (r i) -> p r i", r=16)[:, :, 64 + t0 // 16:64 + t0 // 16 + 32],
                                                            in_=pj[pb][0:64, :].rearrange("p (i r) -> p r i", r=16)), reads=[("pj", pb)], writes=["K2"])
            else:
                S.op("act", lambda e, pb=pb, t0=t0: e.copy(out=V01[0:64, 64 + t0:64 + t0 + 512], in_=pj[pb][0:64, :]), reads=[("pj", pb)], writes=["V01"])
                S.op("dve", lambda e, pb=pb, t0=t0: e.tensor_copy(out=V01[64:128, :].rearrange("p (r i) -> p r i", r=4)[:, :, 64 + t0 // 4:64 + t0 // 4 + 128],
                                                                   in_=pj[pb][64:128, :].rearrange("p (i r) -> p r i", r=4)), reads=[("pj", pb)], writes=["V01"])
    vsrc = [V01[:, m * 128:(m + 1) * 128] for m in range(36)] + [V2[:, m * 128:(m + 1) * 128] for m in range(48)]
    for t0 in range(0, 84, 8):
        n = min(8, 84 - t0)
        pb = (t0 // 8) % 2
        for j in range(n):
            S.op("pe", lambda e, src=vsrc[t0 + j], j=j, pb=pb: e.transpose(out=pT[pb][:, j * 128:(j + 1) * 128], in_=src, identity=idb[:]),
                 reads=["V01", "V2", "idb"], writes=[("pT", pb)])
        S.op("dve" if pb else "act",
             (lambda e, t0=t0, n=n, pb=pb: e.tensor_copy(out=Vt[:, t0:t0 + n, :], in_=pT[pb][:, 0:n * 128].rearrange("p (j c) -> p j c", c=128))) if pb else
             (lambda e, t0=t0, n=n, pb=pb: e.copy(out=Vt[:, t0:t0 + n, :], in_=pT[pb][:, 0:n * 128].rearrange("p (j c) -> p j c", c=128))),
             reads=[("pT", pb)], writes=["Vt"])
    Qsrc = [lambda r: Qz0[:, :], lambda r: Qz1[:, :].rearrange("p (r i) -> p r i", r=4)[:, r, :],
            lambda r: Qz2[:, :].rearrange("p (r i) -> p r i", r=16)[:, r, :]]
    Ksrc = [lambda r: K01[:, 0:4224], lambda r: K01[:, :].rearrange("p (r i) -> p r i", r=4)[:, r, :],
            lambda r: K2[:, :].rearrange("p (r i) -> p r i", r=16)[:, r, :]]
    vt_of = [lambda r, m: (m, 0), lambda r, m: (r * 9 + m, 64), lambda r, m: (36 + r * 3 + m, 64)]
    qb = 0
    for g, (window, d) in enumerate(GROUPS):
        L = S_LEN // d
        nt = L // 128 + 1
        nq = L // 128
        for r in range(d):
            q_ap, k_ap = Qsrc[g](r), Ksrc[g](r)
            for m in range(nq):
                sbk = qb % 2
                S.op("pe", lambda e, k_ap=k_ap, q_ap=q_ap, m=m, sbk=sbk: e.matmul(pS[sbk][:, 0:128], lhsT=k_ap[:, m * 128:m * 128 + 128], rhs=q_ap[:, m * 128:m * 128 + 128], start=True, stop=True),
                     reads=["Qz0", "Qz1", "Qz2", "K01", "K2"], writes=[("pS", sbk)])
                S.op("pe", lambda e, k_ap=k_ap, q_ap=q_ap, m=m, sbk=sbk: e.matmul(pS[sbk][:, 128:256], lhsT=k_ap[:, m * 128 + 128:m * 128 + 256], rhs=q_ap[:, m * 128:m * 128 + 128], start=True, stop=True),
                     reads=["Qz0", "Qz1", "Qz2", "K01", "K2"], writes=[("pS", sbk)])
                S.op("act", lambda e, sbk=sbk: e.activation(out=Pf[sbk][:], in_=pS[sbk][:, 0:256], func=AF.Exp, scale=0.125), reads=[("pS", sbk)], writes=[("Pf", sbk)])
                S.op("dve", lambda e, sbk=sbk, g=g: e.tensor_tensor(out=Pm[sbk][:], in0=Pf[sbk][:], in1=E[:, g, :], op=ALU.mult), reads=[("Pf", sbk), "E"], writes=[("Pm", sbk)])
                half = m % 2
                ob = (qb // 2) % 2
                for (pp, pname, lh, coff) in ((pOD, "pOD", None, 0), (pOD, "pOD", "ones", 256)):
                    for kt in range(2):
                        if lh is None:
                            tix, c0v = vt_of[g](r, m + kt)
                            lhs = Vt[:, tix, c0v:c0v + 64]
                        else:
                            var = 1 if (m == 0 and kt == 0) else (2 if (m == nq - 1 and kt == 1) else 0)
                            lhs = onesk[:, var, :]
                        S.op("pe", lambda e, pp=pp, lhs=lhs, kt=kt, sbk=sbk, ob=ob, half=half, coff=coff: e.matmul(pp[ob][0:64, coff + half * 128:coff + (half + 1) * 128], lhsT=lhs, rhs=Pm[sbk][:, kt * 128:(kt + 1) * 128], start=(kt == 0), stop=(kt == 1)),
                             reads=[("Pm", sbk), "Vt", "onesk"], writes=[(pname, ob)])
                if half == 1:
                    m0 = m - 1
                    dst = lambda acc, d=d, r=r, m0=m0: acc[:, :].rearrange("p (i r) -> p r i", r=d)[:, r, m0 * 128:m0 * 128 + 256]
                    if g == 0:
                        S.op("act", lambda e, ob=ob, dst=dst: e.copy(out=dst(accn), in_=pOD[ob][0:64, 0:256]), reads=[("pOD", ob)], writes=["accn"])
                        S.op("act", lambda e, ob=ob, dst=dst: e.copy(out=dst(accd), in_=pOD[ob][0:64, 256:512]), reads=[("pOD", ob)], writes=["accd"])
                    else:
                        S.op("dve", lambda e, ob=ob, dst=dst: e.tensor_tensor(out=dst(accn), in0=dst(accn), in1=pOD[ob][0:64, 0:256], op=ALU.add), reads=[("pOD", ob), "accn"], writes=["accn"])
                        S.op("dve", lambda e, ob=ob, dst=dst: e.tensor_tensor(out=dst(accd), in0=dst(accd), in1=pOD[ob][0:64, 256:512], op=ALU.add), reads=[("pOD", ob), "accd"], writes=["accd"])
                qb += 1
    S.op("dve", lambda e: e.reciprocal(out=accd[:], in_=accd[:]), reads=["accd"], writes=["accd"])
    S.op("dve", lambda e: e.tensor_tensor(out=ot[:], in0=accn[:], in1=accd[:], op=ALU.mult), reads=["accn", "accd"], writes=["ot"])
    S.op("sp", lambda e: e.dma_start(out=oT_dst, in_=ot[:]), reads=["ot"], chan="oattn")


def alloc_attention(nc, st):
    sb = lambda name, shape, dt: st.enter_context(nc.sbuf_tensor(name, shape, dt))
    ps = lambda name, shape, dt: st.enter_context(nc.psum_tensor(name, shape, dt))
    d = dict(sb=sb, ps=ps)
    d["wa"] = sb("wa_sb", [128, 16, 576], BF16)
    d["ub"] = [sb("ub%d" % i, [128, 16, 512], BF16) for i in range(2)]
    d["Qz0"] = sb("Qz0", [128, 4096], BF16)
    d["Qz1"] = sb("Qz1", [128, 4096], BF16)
    d["Qz2"] = sb("Qz2", [128, 4096], BF16)
    d["K01"] = sb("K01", [128, 4608], BF16)
    d["K2"] = sb("K2", [128, 6144], BF16)
    d["V01"] = sb("V01", [128, 4608], BF16)
    d["V2"] = sb("V2", [128, 6144], BF16)
    d["Vt"] = sb("Vt", [128, 84, 128], BF16)
    d["E"] = sb("E", [128, 3, 256], F32)
    d["Pf"] = [sb("Pf%d" % i, [128, 256], F32) for i in range(2)]
    d["Pm"] = [sb("Pm%d" % i, [128, 256], BF16) for i in range(2)]
    d["accn"] = sb("accn", [64, 4096], F32)
    d["accd"] = sb("accd", [64, 4096], F32)
    d["onesk"] = sb("onesk", [128, 3, 64], BF16)
    d["idb"] = sb("idb", [128, 128], BF16)
    d["ot"] = sb("ot", [64, 4096], BF16)
    d["pj"] = [ps("pj%d" % i, [128, 512], F32) for i in range(2)]
    d["pS"] = [ps("pS%d" % i, [128, 512], F32) for i in range(2)]
    d["pOD"] = [ps("pOD%d" % i, [128, 512], F32) for i in range(2)]
    d["pT"] = [ps("pT%d" % i, [128, 1024], BF16) for i in range(2)]
    return d


def build_b_attn():
    nc = bass.Bass("TRN2", target_bir_lowering=False)
    di = lambda name, shape, dt=F32: nc.dram_tensor(name, shape, dt, kind="ExternalInput").ap()
    uT = di("uT", [2048, 8192], BF16)
    wad = di("wa", [2048, 576])
    biasd = di("biasT", [128, 3, 256])
    identd = di("identb", [128, 128], BF16)
    oT = nc.dram_tensor("oT", [64, 8192], BF16, kind="ExternalOutput").ap()
    with contextlib.ExitStack() as st:
        S = Sched(nc)
        A = alloc_attention(nc, st)
        emit_attn_setup(S, nc, A, wad, biasd, identd)
        for b in range(2):
            emit_attention(S, nc, st, uT[:, b * 4096:(b + 1) * 4096], wad, biasd, identd, oT[:, b * 4096:(b + 1) * 4096], A)
        S.run()
    return nc


def emit_attn_setup(S, nc, A, wad, biasd, identd):
    wa, E, onesk, idb = A["wa"], A["E"], A["onesk"], A["idb"]
    src = wad.rearrange("(k p) c -> p k c", p=128)
    for ka in range(0, 16, 4):
        S.op("pool", lambda e, ka=ka: e.dma_start(out=wa[:, ka:ka + 4, :], in_=src[:, ka:ka + 4, :]), writes=["wa"], chan="wa")
    S.op("sp", lambda e: e.dma_start(out=E[:], in_=biasd), writes=["E"], chan="E")
    S.op("sp", lambda e: e.dma_start(out=idb[:], in_=identd), writes=["idb"], chan="idb")
    S.op("act", lambda e: e.activation(out=E[:], in_=E[:], func=AF.Exp), reads=["E"], writes=["E"])
    S.op("dve", lambda e: e.memset(onesk[:], 1.0), writes=["onesk"])
    S.op("dve", lambda e: e.memset(onesk[0:64, 1, :], 0.0), writes=["onesk"])
    S.op("dve", lambda e: e.memset(onesk[64:128, 2, :], 0.0), writes=["onesk"])
    for name in ("K01", "V01", "K2", "V2", "Qz0", "Qz1", "Qz2"):
        S.op("pool", lambda e, name=name: e.memset(A[name][:], 0.0), writes=[name])
R1_OUT = ("p_r", "p_k", "p_v", "kk", "a_f", "a_b", "l_f", "l_b")


def build_r1():
    nc = bass.Bass("TRN2", target_bir_lowering=False)
    di = lambda name, shape, dt=F32: nc.dram_tensor(name, shape, dt, kind="ExternalInput").ap()
    uT = di("uT", [2048, 8192], BF16)
    wrd = di("wr", [2048, 1024])
    pard = di("par", [128, 32])
    gupd = di("gup", [256, 128]); wupd = di("wup", [2, 96, 128]); aupd = di("aup", [2, 96, 128])
    bonesd = di("bones", [128, 128])
    outs = {n: nc.dram_tensor(n, [128, 8192], F32, kind="ExternalOutput").ap() for n in R1_OUT}
    g_out = nc.dram_tensor("g", [128, 8192], BF16, kind="ExternalOutput").ap()
    with contextlib.ExitStack() as st:
        sb = lambda name, shape, dt: st.enter_context(nc.sbuf_tensor(name, shape, dt))
        ps = lambda name, shape, dt: st.enter_context(nc.psum_tensor(name, shape, dt))
        S = Sched(nc)
        wr = sb("wr_sb", [128, 16, 1024], BF16)
        ub = [sb("ub%d" % i, [128, 16, 512], BF16) for i in range(2)]
        raw = [sb("raw%d" % i, [128, 4098], BF16) for i in range(9)]
        par = sb("par_sb", [128, 32], F32)
        gup = sb("gup_sb", [128, 2, 128], BF16); wup = sb("wup_sb", [96, 2, 128], BF16); aup = sb("aup_sb", [96, 2, 128], BF16)
        bones = sb("bones_sb", [128, 128], F32)
        t1 = sb("t1", [128, 4096], F32); t2 = sb("t2", [128, 4096], F32); t3 = sb("t3", [128, 4096], F32)
        nl = sb("nl", [128, 2, 4096], BF16)
        pj = [ps("pj%d" % i, [128, 512], F32) for i in range(4)]
        pq = [ps("pq%d" % i, [128, 512], F32) for i in range(4)]
        src = wrd.rearrange("(k p) c -> p k c", p=128)
        for ka in range(0, 16, 2):
            S.op("pool", lambda e, ka=ka: e.dma_start(out=wr[:, ka:ka + 2, :], in_=src[:, ka:ka + 2, :]), writes=["wr"], chan="wr")
        S.op("sp", lambda e: e.dma_start(out=par[:], in_=pard), writes=["par"], chan="par")
        S.op("sp", lambda e: e.dma_start(out=bones[:], in_=bonesd), writes=["bones"], chan="bones")
        S.op("pool", lambda e: e.dma_start(out=gup[:], in_=gupd.rearrange("(k p) c -> p k c", p=128)), writes=["gup"], chan="gup")
        S.op("pool", lambda e: e.dma_start(out=wup[:], in_=wupd.rearrange("d p c -> p d c")), writes=["wup"], chan="wup")
        S.op("pool", lambda e: e.dma_start(out=aup[:], in_=aupd.rearrange("d p c -> p d c")), writes=["aup"], chan="aup")
        S.op("dve", lambda e: e.tensor_scalar(out=par[:, 9:18], in0=par[:, 0:9], scalar1=-1.0, scalar2=1.0, op0=ALU.mult, op1=ALU.add), reads=["par"], writes=["par"])
        S.op("dve", lambda e: e.tensor_scalar(out=par[:, 18:27], in0=par[:, 0:9], scalar1=0.5, scalar2=None, op0=ALU.mult), reads=["par"], writes=["par"])
        for i in range(9):
            S.op("pool", lambda e, i=i: e.memset(raw[i][:], 0.0), writes=[("raw", i)])
        rows = [128] * 5 + [96] * 4
        uTr = uT.rearrange("(k p) t -> p k t", p=128)
        for b in range(2):
            tok0 = b * 4096
            for tblk in range(8):
                bb = tblk % 2
                S.op("sp", lambda e, tblk=tblk, bb=bb, tok0=tok0: e.dma_start(out=ub[bb][:], in_=uTr[:, :, tok0 + tblk * 512:tok0 + (tblk + 1) * 512]), writes=[("ub", bb)], chan=("ub", bb))
                for ct in range(9):
                    M = rows[ct]
                    c0 = ct * 128 if ct < 5 else 640 + (ct - 5) * 96
                    pb = ct % 4
                    for k in range(16):
                        S.op("pe", lambda e, k=k, M=M, c0=c0, pb=pb, bb=bb: e.matmul(pj[pb][0:M, :], lhsT=wr[:, k, c0:c0 + M], rhs=ub[bb][:, k, :], start=(k == 0), stop=(k == 15)),
                             reads=[("ub", bb), "wr"], writes=[("pj", pb)])
                    S.op("act" if ct % 2 else "dve",
                         (lambda e, ct=ct, M=M, pb=pb, tblk=tblk: e.copy(out=raw[ct][0:M, 1 + tblk * 512:1 + (tblk + 1) * 512], in_=pj[pb][0:M, :])) if ct % 2 else
                         (lambda e, ct=ct, M=M, pb=pb, tblk=tblk: e.tensor_copy(out=raw[ct][0:M, 1 + tblk * 512:1 + (tblk + 1) * 512], in_=pj[pb][0:M, :])),
                         reads=[("pj", pb)], writes=[("raw", ct)])
            tsl = slice(tok0, tok0 + 4096)

            def shift(ct, dst):
                M = rows[ct]
                S.op("dve", lambda e: e.tensor_tensor(out=t1[0:M, :], in0=raw[ct][0:M, 0:4096], in1=raw[ct][0:M, 2:4098], op=ALU.add), reads=[("raw", ct)], writes=["t1"])
                S.op("act", lambda e: e.activation(out=t2[0:M, :], in_=raw[ct][0:M, 1:4097], func=AF.Copy, scale=par[0:M, 9 + ct:10 + ct]), reads=[("raw", ct), "par"], writes=["t2"])
                S.op("dve", lambda e: e.scalar_tensor_tensor(out=dst[0:M, :], in0=t1[0:M, :], scalar=par[0:M, 18 + ct:19 + ct], in1=t2[0:M, :], op0=ALU.mult, op1=ALU.add),
                     reads=["t1", "t2", "par"], writes=["t3"])

            def store(name, srct, res, tsl=tsl):
                S.op("sp", lambda e: e.dma_start(out=outs[name][:, tsl], in_=srct[:]), reads=[res], chan="o_" + name)

            shift(0, t3); store("p_r", t3, "t3")
            shift(2, t3); store("p_v", t3, "t3")
            shift(1, t3); store("p_k", t3, "t3")
            S.op("dve", lambda e: e.tensor_scalar(out=t1[:], in0=t3[:], scalar1=par[:, 27:28], scalar2=None, op0=ALU.mult), reads=["t3", "par"], writes=["t1"])
            S.op("act", lambda e: e.activation(out=t2[:], in_=t1[:], func=AF.Square), reads=["t1"], writes=["t2"])
            for blk in range(8):
                pb = blk % 4
                bs = slice(blk * 512, (blk + 1) * 512)
                S.op("pe", lambda e, pb=pb, bs=bs: e.matmul(pq[pb][:], lhsT=bones[:], rhs=t2[:, bs], start=True, stop=True), reads=["t2", "bones"], writes=[("pq", pb)])
                S.op("act", lambda e, pb=pb, bs=bs: e.activation(out=t3[:, bs], in_=pq[pb][:], func=AF.Sqrt), reads=[("pq", pb)], writes=["t3"])
            S.op("dve", lambda e: e.tensor_scalar(out=t3[:], in0=t3[:], scalar1=1e-12, scalar2=None, op0=ALU.max), reads=["t3"], writes=["t3"])
            S.op("dve", lambda e: e.reciprocal(out=t3[:], in_=t3[:]), reads=["t3"], writes=["t3"])
            S.op("dve", lambda e: e.tensor_tensor(out=t3[:], in0=t3[:], in1=t1[:], op=ALU.mult), reads=["t3", "t1"], writes=["t3"])
            store("kk", t3, "t3")
            for j in range(2):
                shift(3 + j, t3)
                S.op("act", lambda e, j=j: e.activation(out=nl[:, j, :], in_=t3[:], func=AF.Sigmoid), reads=["t3"], writes=["nl"])
            for blk in range(8):
                pb = blk % 4
                bs = slice(blk * 512, (blk + 1) * 512)
                for j in range(2):
                    S.op("pe", lambda e, pb=pb, bs=bs, j=j: e.matmul(pq[pb][:], lhsT=gup[:, j, :], rhs=nl[:, j, bs], start=(j == 0), stop=(j == 1)), reads=["nl", "gup"], writes=[("pq", pb)])
                S.op("act", lambda e, pb=pb, blk=blk: e.copy(out=raw[3][:, 1 + blk * 512:1 + (blk + 1) * 512], in_=pq[pb][:]), reads=[("pq", pb)], writes=[("raw", 3)])
            S.op("sp", lambda e, tsl=tsl: e.dma_start(out=g_out[:, tsl], in_=raw[3][:, 1:4097]), reads=[("raw", 3)], chan="o_g")
            for d in range(2):
                shift(5 + d, t3)
                S.op("act", lambda e: e.activation(out=nl[0:96, 0, :], in_=t3[0:96, :], func=AF.Tanh), reads=["t3"], writes=["nl"])
                for blk in range(8):
                    pb = blk % 4
                    bs = slice(blk * 512, (blk + 1) * 512)
                    S.op("pe", lambda e, pb=pb, bs=bs, d=d: e.matmul(pq[pb][:], lhsT=wup[:, d, :], rhs=nl[0:96, 0, bs], start=True, stop=True), reads=["nl", "wup"], writes=[("pq", pb)])
                    S.op("act", lambda e, pb=pb, bs=bs, d=d: e.activation(out=t1[:, bs], in_=pq[pb][:], func=AF.Sigmoid, bias=par[:, 28 + d:29 + d]), reads=[("pq", pb), "par"], writes=["t1"])
                S.op("dve", lambda e: e.tensor_scalar(out=t1[:], in0=t1[:], scalar1=-0.6065306597126334, scalar2=None, op0=ALU.mult), reads=["t1"], writes=["t1"])
                store("l_f" if d == 0 else "l_b", t1, "t1")
                shift(7 + d, t3)
                S.op("act", lambda e: e.copy(out=nl[0:96, 1, :], in_=t3[0:96, :]), reads=["t3"], writes=["nl"])
                for blk in range(8):
                    pb = blk % 4
                    bs = slice(blk * 512, (blk + 1) * 512)
                    S.op("pe", lambda e, pb=pb, bs=bs, d=d: e.matmul(pq[pb][:], lhsT=aup[:, d, :], rhs=nl[0:96, 1, bs], start=True, stop=True), reads=["nl", "aup"], writes=[("pq", pb)])
                    S.op("act", lambda e, pb=pb, bs=bs, d=d: e.activation(out=t2[:, bs], in_=pq[pb][:], func=AF.Sigmoid, bias=par[:, 30 + d:31 + d]), reads=[("pq", pb), "par"], writes=["t2"])
                store("a_f" if d == 0 else "a_b", t2, "t2")
        S.run()
    return nc


def r1_inputs(inp, l, c):
    hs = [2 * c, 2 * c + 1]
    ch = np.concatenate([np.arange(h * 64, h * 64 + 64) for h in hs])
    cols = np.concatenate([ch, 1024 + ch, 2048 + ch, np.arange(3072, 3712)])
    wr = np.ascontiguousarray(inp["w_in"][l][:, cols])
    mu = inp["tshift_mu"][l][cols]
    par = np.zeros((128, 32), np.float32)
    for ct in range(5):
        par[:, ct] = mu[ct * 128:(ct + 1) * 128]
    for ct in range(5, 9):
        par[:96, ct] = mu[640 + (ct - 5) * 96:640 + (ct - 4) * 96]
    par[:, 27] = inp["k_k"][l][ch]
    par[:, 28] = inp["w0"][l][0][ch]; par[:, 29] = inp["w0"][l][1][ch]
    par[:, 30] = inp["a0"][l][0][ch]; par[:, 31] = inp["a0"][l][1][ch]
    bones = np.kron(np.eye(2, dtype=np.float32), np.ones((64, 64), np.float32))
    return dict(wr=wr, par=par, gup=np.ascontiguousarray(inp["g_lora_up"][l][:, ch]), wup=np.ascontiguousarray(inp["w_lora_up"][l][:, :, ch]),
                aup=np.ascontiguousarray(inp["a_lora_up"][l][:, :, ch]), bones=bones)
R2_IN = ("p_r", "p_k", "p_v", "kk", "a_f", "a_b", "l_f", "l_b")


def r2_consts():
    p = np.arange(128)
    same = (p[:, None] // 64) == (p[None, :] // 64)
    s = p[:, None] % 64
    t = p[None, :] % 64
    masks = np.zeros((128, 2, 3, 512), np.float32)
    for d in range(2):
        strict = ((s < t) if d == 0 else (s > t)) & same
        incl = ((s <= t) if d == 0 else (s >= t)) & same
        a = np.concatenate([-strict.astype(np.float32), -incl.astype(np.float32)], axis=1)
        masks[:, d, 0] = np.tile(a, (1, 2))
        masks[:, d, 1] = np.tile(-a, (1, 2))
        masks[:, d, 2] = np.tile(-strict.T.astype(np.float32), (1, 4))
    ident4 = np.tile(np.eye(128, dtype=np.float32), (1, 4))
    lvl = np.zeros((128, 6, 512), np.float32)
    for i, sz in enumerate((1, 2, 4, 8, 16, 32)):
        m = ((p[:, None] // (2 * sz)) == (p[None, :] // (2 * sz))) & ((p[:, None] // sz) != (p[None, :] // sz))
        lvl[:, i] = np.tile(m.astype(np.float32), (1, 4))
    m01 = np.ones((128, 512), np.float32)
    m01[:, ::64] = 0.0
    bones = np.kron(np.eye(2, dtype=np.float32), np.ones((64, 64), np.float32))
    return dict(masks=masks, lvl=lvl, ident4=ident4, m01=m01, bones=bones, identb=np.eye(128).astype(ml_dtypes.bfloat16))


def build_r2():
    nc = bass.Bass("TRN2", target_bir_lowering=False)
    di = lambda name, shape, dt=F32: nc.dram_tensor(name, shape, dt, kind="ExternalInput").ap()
    X = {n: di(n, [128, 8192]) for n in R2_IN}
    gd = di("g", [128, 8192], BF16)
    par2d = di("par2", [128, 8])
    masksd = di("masks", [128, 2, 3, 512]); lvld = di("lvl", [128, 6, 512]); ident4d = di("ident4", [128, 512]); m01d = di("m01", [128, 512])
    bonesd = di("bones", [128, 128]); identbd = di("identb", [128, 128], BF16)
    oT = nc.dram_tensor("oT", [128, 8192], BF16, kind="ExternalOutput").ap()
    with contextlib.ExitStack() as st:
        sb = lambda name, shape, dt: st.enter_context(nc.sbuf_tensor(name, shape, dt))
        ps = lambda name, shape, dt: st.enter_context(nc.psum_tensor(name, shape, dt))
        S = Sched(nc)
        par2 = sb("par2s", [128, 8], F32); masks = sb("maskss", [128, 2, 3, 512], F32); lvl = sb("lvls", [128, 6, 512], F32); ident4 = sb("ident4s", [128, 512], F32)
        m01 = sb("m01s", [128, 512], F32); bones = sb("boness", [128, 128], F32); idb = sb("idbs", [128, 128], BF16)
        NB = 2
        inb = [{n: sb("in_%s%d" % (n, i), [128, 512], F32) for n in ("p_r", "p_k", "p_v", "kk", "a", "l")} for i in range(NB)]
        tmp = {n: sb("tmp_" + n, [128, 512], F32) for n in ("f", "kd", "ka", "Lc", "Linc", "w1", "w2")}
        ltot = sb("ltot", [128, 8], F32); wtot = [sb("wtot%d" % i, [128, 8], F32) for i in range(NB)]
        BDn = ("KR", "BB", "KT", "BH", "KH", "VV")
        BD = [{n: sb("bd_%s%d" % (n, i), [128, 8, 256 if n == "KR" else 128], BF16) for n in BDn} for i in range(NB)]
        AB2 = [sb("AB2_%d" % i, [128, 8, 256], BF16) for i in range(NB)]
        AK2 = [sb("AK2_%d" % i, [128, 8, 256], BF16) for i in range(NB)]
        Pb = [sb("Pb%d" % i, [128, 8, 128], BF16) for i in range(2)]
        Qb = [sb("Qb%d" % i, [128, 8, 128], BF16) for i in range(2)]
        MT = [sb("MT%d" % i, [128, 8, 128], BF16) for i in range(NB)]
        Wt = sb("Wt", [128, 8, 128], BF16); Q0b = sb("Q0b", [128, 8, 128], BF16)
        TT = [{n: sb("tt_%s%d" % (n, i), [128, 8, 128], BF16) for n in ("BH", "KH", "VV")} for i in range(NB)]
        T = sb("T", [128, 128], F32); Tb = sb("Tb", [128, 128], BF16)
        Xs = sb("Xs", [128, 128], BF16); Us = sb("Us", [128, 128], BF16)
        y = sb("y", [128, 4096], F32)
        ob = {n: sb("ob_" + n, [128, 512], F32) for n in ("p_r", "p_k", "p_v", "a_f", "a_b", "t1", "t2", "t3")}
        ogb = sb("ogb", [128, 512], BF16); oo = sb("oo", [128, 512], BF16)
        pa = ps("pa", [128, 512], F32); pk = ps("pk", [128, 512], F32)
        pi = [ps("pi%d" % i, [128, 512], F32) for i in range(2)]
        ptr = ps("ptr", [128, 1024], BF16)
        pS = ps("pS", [128, 512], F32)
        pY = [ps("pY%d" % i, [128, 512], F32) for i in range(2)]
        for (t_, d_, nm) in ((par2, par2d, "par2"), (masks, masksd, "masks"), (lvl, lvld, "lvl"), (ident4, ident4d, "ident4"), (m01, m01d, "m01"), (bones, bonesd, "bones"), (idb, identbd, "idb")):
            S.op("sp", lambda e, t_=t_, d_=d_: e.dma_start(out=t_[:], in_=d_), writes=[nm], chan=nm)
        S.op("dve", lambda e: e.tensor_scalar(out=par2[:, 1:2], in0=par2[:, 0:1], scalar1=-1.0, scalar2=1.0, op0=ALU.mult, op1=ALU.add), reads=["par2"], writes=["par2"])
        S.op("dve", lambda e: e.tensor_scalar(out=par2[:, 5:6], in0=par2[:, 0:1], scalar1=-2.0, scalar2=2.0, op0=ALU.mult, op1=ALU.add), reads=["par2"], writes=["par2"])
        for i in range(NB):
            for n in BDn:
                S.op("pool", lambda e, i=i, n=n: e.memset(BD[i][n][:], 0.0), writes=[("bd", i)])

        v3 = lambda ap: ap.rearrange("p (c t) -> p c t", t=64)

        def prep_group(b, d, gi, sl):
            tok = b * 4096 + gi * 512
            I = inb[sl]
            for n in ("p_r", "p_k", "p_v", "kk"):
                S.op("sp", lambda e, n=n: e.dma_start(out=I[n][:], in_=X[n][:, tok:tok + 512]), writes=[("in", sl)], chan=("in", sl, n))
            sfx = "_f" if d == 0 else "_b"
            S.op("sp", lambda e: e.dma_start(out=I["a"][:], in_=X["a" + sfx][:, tok:tok + 512]), writes=[("in", sl)], chan=("in", sl, "a"))
            S.op("sp", lambda e: e.dma_start(out=I["l"][:], in_=X["l" + sfx][:, tok:tok + 512]), writes=[("in", sl)], chan=("in", sl, "l"))
            R = [("in", sl)]
            f, kd, ka, Lc, Linc, w1, w2 = [tmp[n] for n in ("f", "kd", "ka", "Lc", "Linc", "w1", "w2")]
            S.op("dve", lambda e: e.tensor_scalar(out=f[:], in0=I["a"][:], scalar1=par2[:, 0:1], scalar2=par2[:, 1:2], op0=ALU.mult, op1=ALU.add), reads=R + ["par2"], writes=["f"])
            S.op("dve", lambda e: e.tensor_tensor(out=kd[:], in0=I["p_k"][:], in1=f[:], op=ALU.mult), reads=R + ["f"], writes=["kd"])
            S.op("pool", lambda e: e.tensor_tensor(out=ka[:], in0=I["kk"][:], in1=I["a"][:], op=ALU.mult), reads=R, writes=["ka"])
            S.op("dve", lambda e: e.tensor_tensor_scan(out=Lc[:], data0=m01[:], data1=I["l"][:], initial=0.0, op0=ALU.mult, op1=ALU.add), reads=R + ["m01"], writes=["Lc"])
            S.op("dve", lambda e: e.tensor_copy(out=ltot[:], in_=v3(Lc[:])[:, :, 63]), reads=["Lc"], writes=["ltot"])
            lt_b = ltot[:].unsqueeze(2).to_broadcast([128, 8, 64])
            if d == 0:
                LI = Lc
                lres = "Lc"
            else:
                S.op("dve", lambda e: e.tensor_tensor(out=v3(Linc[:]), in0=lt_b, in1=v3(Lc[:]), op=ALU.subtract), reads=["Lc", "ltot"], writes=["Linc"])
                S.op("dve", lambda e: e.tensor_tensor(out=Linc[:], in0=Linc[:], in1=I["l"][:], op=ALU.add), reads=["Linc"] + R, writes=["Linc"])
                LI = Linc
                lres = "Linc"
            S.op("act", lambda e: e.activation(out=wtot[sl][:], in_=ltot[:], func=AF.Exp), reads=["ltot"], writes=[("wtot", sl)])
            Bd = BD[sl]

            def bd_write(name, c0, in0, in1, neg=False):
                for hh in range(2):
                    prt = slice(hh * 64, hh * 64 + 64)
                    o = Bd[name][prt, :, c0 + hh * 64:c0 + hh * 64 + 64]
                    if neg:
                        S.op("dve", lambda e, o=o, prt=prt: e.scalar_tensor_tensor(out=o, in0=v3(in0[prt, :]), scalar=-1.0, in1=v3(in1[prt, :]), op0=ALU.mult, op1=ALU.mult),
                             reads=R + ["ka", "kd", "w1", "w2"], writes=[("bd", sl)])
                    elif in1 is None:
                        S.op("pool", lambda e, o=o, prt=prt: e.tensor_copy(out=o, in_=v3(in0[prt, :])), reads=R, writes=[("bd", sl)])
                    else:
                        S.op("pool" if hh else "dve", lambda e, o=o, prt=prt: e.tensor_tensor(out=o, in0=v3(in0[prt, :]), in1=v3(in1[prt, :]), op=ALU.mult),
                             reads=R + ["ka", "kd", "w1", "w2"], writes=[("bd", sl)])

            S.op("act", lambda e: e.activation(out=w1[:], in_=LI[:], func=AF.Exp), reads=[lres], writes=["w1"])
            bd_write("KR", 128, I["p_r"], w1)
            S.op("dve", lambda e: e.tensor_tensor(out=w2[:], in0=LI[:], in1=I["l"][:], op=ALU.subtract), reads=[lres] + R, writes=["w2"])
            S.op("act", lambda e: e.activation(out=w2[:], in_=w2[:], func=AF.Exp), reads=["w2"], writes=["w2"])
            bd_write("KR", 0, I["kk"], w2)
            S.op("act", lambda e: e.activation(out=w1[:], in_=LI[:], func=AF.Exp, scale=-1.0), reads=[lres], writes=["w1"])
            bd_write("BB", 0, ka, w1)
            bd_write("KT", 0, kd, w1)
            S.op("dve", lambda e: e.tensor_tensor(out=v3(w2[:]), in0=lt_b, in1=v3(LI[:]), op=ALU.subtract), reads=[lres, "ltot"], writes=["w2"])
            S.op("act", lambda e: e.activation(out=w2[:], in_=w2[:], func=AF.Exp), reads=["w2"], writes=["w2"])
            bd_write("BH", 0, ka, w2, neg=True)
            bd_write("KH", 0, kd, w2)
            bd_write("VV", 0, I["p_v"], None)
            BR = [("bd", sl)]
            for c in range(8):
                o2 = (c % 2) * 256
                S.op("pe", lambda e, c=c, o2=o2: e.matmul(pa[:, o2:o2 + 256], lhsT=Bd["BB"][:, c, :], rhs=Bd["KR"][:, c, :], start=True, stop=True), reads=BR, writes=["pa"])
                S.op("pe", lambda e, c=c, o2=o2: e.matmul(pk[:, o2:o2 + 256], lhsT=Bd["KT"][:, c, :], rhs=Bd["KR"][:, c, :], start=True, stop=True), reads=BR, writes=["pk"])
                if c % 2 == 1:
                    S.op("dve", lambda e, c=c: e.tensor_tensor(out=AB2[sl][:, c - 1:c + 1, :], in0=pa[:].rearrange("p (c x) -> p c x", c=2), in1=masks[:, d, 0, :].rearrange("p (c x) -> p c x", c=2), op=ALU.mult),
                         reads=["pa", "masks"], writes=[("AB2", sl)])
                    S.op("dve", lambda e, c=c: e.tensor_tensor(out=AK2[sl][:, c - 1:c + 1, :], in0=pk[:].rearrange("p (c x) -> p c x", c=2), in1=masks[:, d, 1, :].rearrange("p (c x) -> p c x", c=2), op=ALU.mult),
                         reads=["pk", "masks"], writes=[("AK2", sl)])
            for c in range(8):
                pb = c // 4
                o4 = (c % 4) * 128
                S.op("pe", lambda e, c=c, pb=pb, o4=o4: e.matmul(pi[pb][:, o4:o4 + 128], lhsT=Bd["KR"][:, c, 0:128], rhs=Bd["BB"][:, c, :], start=True, stop=True), reads=BR, writes=[("pi", pb)])
            for pb in range(2):
                S.op("dve", lambda e, pb=pb: e.tensor_tensor(out=Q0b[:, pb * 4:pb * 4 + 4, :], in0=pi[pb][:].rearrange("p (c x) -> p c x", c=4), in1=masks[:, d, 2, :].rearrange("p (c x) -> p c x", c=4), op=ALU.mult),
                     reads=[("pi", pb), "masks"], writes=["Q0b"])
            W = MT[sl]
            NOs, NOTs, T1, T1p = Pb[0], Pb[1], Qb[0], Qb[1]
            c4 = lambda ap: ap.rearrange("p (c x) -> p c x", c=4)

            def masked(dst, dres, src, sres, lev):
                for hf in range(2):
                    S.op("dve", lambda e, hf=hf: e.tensor_tensor(out=dst[:, hf * 4:hf * 4 + 4, :], in0=src[:, hf * 4:hf * 4 + 4, :], in1=c4(lvl[:, lev, :]), op=ALU.mult),
                         reads=[sres, "lvl"], writes=[dres])

            masked(NOs, "NOs", AB2[sl][:, :, 0:128], ("AB2", sl), 0)
            masked(NOTs, "NOTs", Q0b, "Q0b", 0)
            for hf in range(2):
                S.op("dve", lambda e, hf=hf: e.tensor_tensor(out=W[:, hf * 4:hf * 4 + 4, :], in0=NOs[:, hf * 4:hf * 4 + 4, :], in1=c4(ident4[:]), op=ALU.add), reads=["NOs", "ident4"], writes=[("MT", sl)])
                S.op("dve", lambda e, hf=hf: e.tensor_tensor(out=Wt[:, hf * 4:hf * 4 + 4, :], in0=NOTs[:, hf * 4:hf * 4 + 4, :], in1=c4(ident4[:]), op=ALU.add), reads=["NOTs", "ident4"], writes=["Wt"])
            WR = ("MT", sl)

            BK0 = ([pi[0], pi[1]], [("pi", 0), ("pi", 1)])
            BK1 = ([pa, pk], ["pa", "pk"])

            def mm8(lhs, lres, rhs, rres, bk):
                for c in range(8):
                    pb = c // 4
                    o4 = (c % 4) * 128
                    S.op("pe", lambda e, c=c, pb=pb, o4=o4: e.matmul(bk[0][pb][:, o4:o4 + 128], lhsT=lhs[:, c, :], rhs=rhs[:, c, :], start=True, stop=True),
                         reads=[lres, rres], writes=[bk[1][pb]])

            for lev in range(1, 6):
                masked(NOs, "NOs", AB2[sl][:, :, 0:128], ("AB2", sl), lev)
                masked(NOTs, "NOTs", Q0b, "Q0b", lev)
                mm8(NOTs, "NOTs", W, WR, BK0)
                mm8(NOs, "NOs", Wt, "Wt", BK1)
                for pb in range(2):
                    S.op("act", lambda e, pb=pb: e.copy(out=T1[:, pb * 4:pb * 4 + 4, :], in_=c4(BK0[0][pb][:])), reads=[BK0[1][pb]], writes=["T1"])
                for pb in range(2):
                    S.op("dve", lambda e, pb=pb: e.tensor_copy(out=T1p[:, pb * 4:pb * 4 + 4, :], in_=c4(BK1[0][pb][:])), reads=[BK1[1][pb]], writes=["T1p"])
                mm8(Wt, "Wt", T1, "T1", BK0)
                mm8(W, WR, T1p, "T1p", BK1)
                for pb in range(2):
                    S.op("act", lambda e, pb=pb: e.copy(out=T1[:, pb * 4:pb * 4 + 4, :], in_=c4(BK0[0][pb][:])), reads=[BK0[1][pb]], writes=["T1"])
                for pb in range(2):
                    S.op("dve", lambda e, pb=pb: e.tensor_tensor(out=Wt[:, pb * 4:pb * 4 + 4, :], in0=Wt[:, pb * 4:pb * 4 + 4, :], in1=c4(BK1[0][pb][:]), op=ALU.add), reads=[BK1[1][pb], "Wt"], writes=["Wt"])
                for hf in range(2):
                    S.op("dve", lambda e, hf=hf: e.tensor_tensor(out=W[:, hf * 4:hf * 4 + 4, :], in0=W[:, hf * 4:hf * 4 + 4, :], in1=T1[:, hf * 4:hf * 4 + 4, :], op=ALU.add), reads=["T1", WR], writes=[WR])
            for n in ("BH", "KH", "VV"):
                for c in range(8):
                    S.op("pe", lambda e, c=c, n=n: e.transpose(out=ptr[:, c * 128:(c + 1) * 128], in_=Bd[n][:, c, :], identity=idb[:]), reads=BR + ["idb"], writes=["ptr"])
                S.op("act", lambda e, n=n: e.copy(out=TT[sl][n][:], in_=ptr[:].rearrange("p (c x) -> p c x", c=8)), reads=["ptr"], writes=[("TT", sl)])

        def scan_group(b, d, gi, sl):
            Bd = BD[sl]
            order = range(8) if d == 0 else range(7, -1, -1)
            for idx, c in enumerate(order):
                yb = idx // 4
                yo = (idx % 4) * 128
                S.op("pe", lambda e, c=c: e.matmul(pS[:, 0:128], lhsT=Bd["KR"][:, c, 0:128], rhs=Tb[:], start=True, stop=False), reads=[("bd", sl), "Tb"], writes=["pS0"])
                S.op("pe", lambda e, c=c: e.matmul(pS[:, 0:128], lhsT=AK2[sl][:, c, 0:128], rhs=TT[sl]["VV"][:, c, :], start=False, stop=True), reads=[("AK2", sl), ("TT", sl)], writes=["pS0"])
                S.op("act", lambda e: e.copy(out=Xs[:], in_=pS[:, 0:128]), reads=["pS0"], writes=["Xs"])
                S.op("pe", lambda e, c=c: e.matmul(pS[:, 128:256], lhsT=MT[sl][:, c, :], rhs=Xs[:], start=True, stop=True), reads=[("MT", sl), "Xs"], writes=["pS1"])
                S.op("dve", lambda e: e.tensor_copy(out=Us[:], in_=pS[:, 128:256]), reads=["pS1"], writes=["Us"])
                S.op("pe", lambda e, c=c: e.matmul(pS[:, 256:384], lhsT=TT[sl]["BH"][:, c, :], rhs=Us[:], start=True, stop=False), reads=[("TT", sl), "Us"], writes=["pS2"])
                S.op("pe", lambda e, c=c: e.matmul(pS[:, 256:384], lhsT=TT[sl]["KH"][:, c, :], rhs=TT[sl]["VV"][:, c, :], start=False, stop=True), reads=[("TT", sl)], writes=["pS2"])
                S.op("pe", lambda e, c=c, yb=yb, yo=yo: e.matmul(pY[yb][:, yo:yo + 128], lhsT=Tb[:], rhs=Bd["KR"][:, c, 128:256], start=True, stop=False), reads=[("bd", sl), "Tb"], writes=[("pY", yb)])
                S.op("pe", lambda e, c=c, yb=yb, yo=yo: e.matmul(pY[yb][:, yo:yo + 128], lhsT=Us[:], rhs=AB2[sl][:, c, 128:256], start=False, stop=False), reads=[("AB2", sl), "Us"], writes=[("pY", yb)])
                S.op("pe", lambda e, c=c, yb=yb, yo=yo: e.matmul(pY[yb][:, yo:yo + 128], lhsT=TT[sl]["VV"][:, c, :], rhs=AK2[sl][:, c, 128:256], start=False, stop=True), reads=[("AK2", sl), ("TT", sl)], writes=[("pY", yb)])
                S.op("dve", lambda e, c=c: e.scalar_tensor_tensor(out=T[:], in0=T[:], scalar=wtot[sl][:, c:c + 1], in1=pS[:, 256:384], op0=ALU.mult, op1=ALU.add), reads=["pS2", "T", ("wtot", sl)], writes=["T"])
                S.op("act", lambda e: e.copy(out=Tb[:], in_=T[:]), reads=["T"], writes=["Tb"])
                if idx % 4 == 3:
                    cs = sorted(list(order)[idx - 3:idx + 1])
                    c_lo = cs[0]
                    for hh in range(2):
                        prt = slice(hh * 64, hh * 64 + 64)
                        src = pY[yb][prt, :].rearrange("p (c x) -> p c x", c=4)[:, :, hh * 64:hh * 64 + 64]
                        if d == 1:
                            dsts = [(y[prt, gi * 512 + (c_lo + 3 - j) * 64:gi * 512 + (c_lo + 4 - j) * 64], pY[yb][prt, j * 128 + hh * 64:j * 128 + hh * 64 + 64]) for j in range(4)]
                            for (dd, ss) in dsts:
                                S.op("dve", lambda e, dd=dd, ss=ss: e.tensor_tensor(out=dd, in0=dd, in1=ss, op=ALU.add), reads=[("pY", yb), "y"], writes=["y"])
                        else:
                            dd = y[prt, gi * 512 + c_lo * 64:gi * 512 + (c_lo + 4) * 64].rearrange("p (c x) -> p c x", c=4)
                            S.op("act", lambda e, dd=dd, src=src: e.copy(out=dd, in_=src), reads=[("pY", yb)], writes=["y"])

        def out_block(b, blk):
            tok = b * 4096 + blk * 512
            for n in ("p_r", "p_k", "p_v", "a_f", "a_b"):
                S.op("sp", lambda e, n=n: e.dma_start(out=ob[n][:], in_=X[n][:, tok:tok + 512]), writes=[("ob", n)], chan=("ob", n))
            S.op("sp", lambda e: e.dma_start(out=ogb[:], in_=gd[:, tok:tok + 512]), writes=["ogb"], chan="ogb")
            t1, t2, t3 = ob["t1"], ob["t2"], ob["t3"]
            ys = y[:, blk * 512:(blk + 1) * 512]
            S.op("pe", lambda e: e.matmul(pa[:], lhsT=bones[:], rhs=ys, start=True, stop=True), reads=["y", "bones"], writes=["pa"])
            S.op("dve", lambda e: e.scalar_tensor_tensor(out=t1[:], in0=pa[:], scalar=-1.0 / 64, in1=ys, op0=ALU.mult, op1=ALU.add), reads=["pa", "y"], writes=["t1"])
            S.op("act", lambda e: e.activation(out=t2[:], in_=t1[:], func=AF.Square), reads=["t1"], writes=["t2"])
            S.op("pe", lambda e: e.matmul(pk[:], lhsT=bones[:], rhs=t2[:], start=True, stop=True), reads=["t2", "bones"], writes=["pk"])
            S.op("act", lambda e: e.activation(out=t2[:], in_=pk[:], func=AF.Sqrt, scale=1.0 / 64, bias=64e-5), reads=["pk"], writes=["t2"])
            S.op("dve", lambda e: e.reciprocal(out=t2[:], in_=t2[:]), reads=["t2"], writes=["t2"])
            S.op("dve", lambda e: e.tensor_tensor(out=t1[:], in0=t1[:], in1=t2[:], op=ALU.mult), reads=["t1", "t2"], writes=["t1"])
            S.op("dve", lambda e: e.tensor_scalar(out=t1[:], in0=t1[:], scalar1=par2[:, 3:4], scalar2=par2[:, 4:5], op0=ALU.mult, op1=ALU.add), reads=["t1", "par2"], writes=["t1"])
            S.op("pool", lambda e: e.tensor_tensor(out=t2[:], in0=ob["a_f"][:], in1=ob["a_b"][:], op=ALU.add), reads=[("ob", "a_f"), ("ob", "a_b")], writes=["t2"])
            S.op("dve", lambda e: e.tensor_scalar(out=t2[:], in0=t2[:], scalar1=par2[:, 0:1], scalar2=par2[:, 5:6], op0=ALU.mult, op1=ALU.add), reads=["t2", "par2"], writes=["t2"])
            S.op("pool", lambda e: e.tensor_tensor(out=t2[:], in0=t2[:], in1=ob["p_k"][:], op=ALU.mult), reads=["t2", ("ob", "p_k")], writes=["t2"])
            S.op("dve", lambda e: e.scalar_tensor_tensor(out=t3[:], in0=t2[:], scalar=par2[:, 2:3], in1=ob["p_r"][:], op0=ALU.mult, op1=ALU.mult), reads=["t2", "par2", ("ob", "p_r")], writes=["t3"])
            S.op("pe", lambda e: e.matmul(pa[:], lhsT=bones[:], rhs=t3[:], start=True, stop=True), reads=["t3", "bones"], writes=["pa"])
            S.op("dve", lambda e: e.tensor_tensor(out=t3[:], in0=pa[:], in1=ob["p_v"][:], op=ALU.mult), reads=["pa", ("ob", "p_v")], writes=["t3"])
            S.op("pool", lambda e: e.tensor_tensor(out=t3[:], in0=t3[:], in1=t1[:], op=ALU.add), reads=["t3", "t1"], writes=["t3"])
            S.op("dve", lambda e: e.tensor_tensor(out=oo[:], in0=t3[:], in1=ogb[:], op=ALU.mult), reads=["t3", "ogb"], writes=["oo"])
            S.op("sp", lambda e: e.dma_start(out=oT[:, tok:tok + 512], in_=oo[:]), reads=["oo"], chan="oo")

        def collect(fn, *args):
            buf = []
            real = S.op
            S.op = lambda *a, **k: buf.append((a, k))
            try:
                fn(*args)
            finally:
                S.op = real
            return buf

        def emit_interleaved(A, B):
            nA, nB = len(A), len(B)
            ia = 0
            for ib, (a, k) in enumerate(B):
                tgt = (ib * nA) // max(nB, 1)
                while ia < tgt:
                    S.op(*A[ia][0], **A[ia][1])
                    ia += 1
                S.op(*a, **k)
            while ia < nA:
                S.op(*A[ia][0], **A[ia][1])
                ia += 1

        for b in range(2):
            for d in range(2):
                S.op("dve", lambda e: e.memset(T[:], 0.0), writes=["T"])
                S.op("pool", lambda e: e.memset(Tb[:], 0.0), writes=["Tb"])
                gorder = list(range(8)) if d == 0 else list(range(7, -1, -1))
                prep_group(b, d, gorder[0], 0)
                for j, gi in enumerate(gorder):
                    A = collect(prep_group, b, d, gorder[j + 1], (j + 1) % 2) if j + 1 < 8 else []
                    B = collect(scan_group, b, d, gi, j % 2)
                    emit_interleaved(A, B)
            for blk in range(8):
                out_block(b, blk)
        S.run()
    return nc


def r2_inputs(inp, l, c):
    hs = [2 * c, 2 * c + 1]
    ch = np.concatenate([np.arange(h * 64, h * 64 + 64) for h in hs])
    par2 = np.zeros((128, 8), np.float32)
    par2[:, 0] = inp["k_a"][l][ch]
    par2[:, 2] = inp["r_k"][l].reshape(-1)[ch]
    par2[:, 3] = inp["gn_w"][l][ch]
    par2[:, 4] = inp["gn_b"][l][ch]
    return dict(par2=par2)
def build_p0():
    nc = bass.Bass("TRN2", target_bir_lowering=False)
    x = nc.dram_tensor("x", [1024, 2048], F32, kind="ExternalInput").ap()
    g1 = nc.dram_tensor("g1", [128, 16], F32, kind="ExternalInput").ap()
    ident = nc.dram_tensor("ident", [128, 128], F32, kind="ExternalInput").ap()
    hT = nc.dram_tensor("hT", [2048, 1024], F32, kind="ExternalOutput").ap()
    uT = nc.dram_tensor("uT", [2048, 1024], BF16, kind="ExternalOutput").ap()
    with contextlib.ExitStack() as st:
        sb = lambda name, shape, dt: st.enter_context(nc.sbuf_tensor(name, shape, dt))
        ps = lambda name, shape, dt: st.enter_context(nc.psum_tensor(name, shape, dt))
        xt = [sb("xt%d" % i, [128, 2048], F32) for i in range(2)]
        h = sb("h", [128, 16, 1024], F32)
        u = sb("u", [128, 16, 1024], BF16)
        sq = [sb("sq%d" % i, [128, 512], F32) for i in range(2)]
        rs = sb("rs", [128, 512], F32)
        g = sb("g", [128, 16], F32)
        idt = sb("idt", [128, 128], F32)
        ones = sb("ones", [128, 128], F32)
        pt = [ps("pt%d" % i, [128, 512], F32) for i in range(4)]
        pn = ps("pn", [128, 512], F32)
        S = Sched(nc)
        S.op("sp", lambda e: e.dma_start(out=g[:], in_=g1), writes=["g"], chan="g")
        S.op("sp", lambda e: e.dma_start(out=idt[:], in_=ident), writes=["idt"], chan="idt")
        S.op("dve", lambda e: e.memset(ones[:], 1.0), writes=["ones"])
        for tt in range(8):
            b = tt % 2
            S.op("sp", lambda e, tt=tt, b=b: e.dma_start(out=xt[b][:], in_=x[tt * 128:(tt + 1) * 128, :]), writes=[("xt", b)], chan=("xt", b))
            for kg in range(4):
                pb = kg
                for j in range(4):
                    k = kg * 4 + j
                    S.op("pe", lambda e, k=k, j=j, pb=pb, b=b: e.transpose(out=pt[pb][:, j * 128:(j + 1) * 128], in_=xt[b][:, k * 128:(k + 1) * 128], identity=idt[:]),
                         reads=[("xt", b), "idt"], writes=[("pt", pb)])
                if kg % 2:
                    f = lambda e, kg=kg, pb=pb, tt=tt: e.copy(out=h[:, kg * 4:(kg + 1) * 4, tt * 128:(tt + 1) * 128], in_=pt[pb][:].rearrange("p (j t) -> p j t", j=4))
                else:
                    f = lambda e, kg=kg, pb=pb, tt=tt: e.tensor_copy(out=h[:, kg * 4:(kg + 1) * 4, tt * 128:(tt + 1) * 128], in_=pt[pb][:].rearrange("p (j t) -> p j t", j=4))
                S.op("act" if kg % 2 else "dve", f, reads=[("pt", pb)], writes=[("h", tt // 4)])
        for tb in range(2):
            tsl = slice(tb * 512, (tb + 1) * 512)
            emit_rmsnorm(S, nc, h[:, :, tsl], u[:, :, tsl], g, sq, rs, ones, pn, ("h", tb), ("u", tb), "g")
        S.op("sp", lambda e: e.dma_start(out=hT.rearrange("(k p) t -> p k t", p=128), in_=h[:]), reads=[("h", 0), ("h", 1)], chan="oh")
        S.op("sp", lambda e: e.dma_start(out=uT.rearrange("(k p) t -> p k t", p=128), in_=u[:]), reads=[("u", 0), ("u", 1)], chan="ou")
        S.run()
    return nc


_NC_CACHE = {}


def _prog(name, fn):
    if name not in _NC_CACHE:
        _NC_CACHE[name] = fn()
    return _NC_CACHE[name]


def _run(nc, in_maps):
    res = run_bass_kernel_spmd(nc, in_maps, core_ids=list(range(NCORES)))
    return res.results


def kernel(x, norm1_g, w_in, tshift_mu, w0, w_lora_up, a0, a_lora_up, g_lora_up, k_k, k_a, r_k, gn_w, gn_b, rel_bias,
           w_branch_rwkv, w_branch_attn, w_out, norm2_g, w_mlp_in, w_mlp_out, final_g):
    inp = dict(x=x, norm1_g=norm1_g, w_in=w_in, tshift_mu=tshift_mu, w0=w0, w_lora_up=w_lora_up, a0=a0, a_lora_up=a_lora_up,
               g_lora_up=g_lora_up, k_k=k_k, k_a=k_a, r_k=r_k, gn_w=gn_w, gn_b=gn_b, rel_bias=rel_bias, w_branch_rwkv=w_branch_rwkv,
               w_branch_attn=w_branch_attn, w_out=w_out, norm2_g=norm2_g, w_mlp_in=w_mlp_in, w_mlp_out=w_mlp_out, final_g=final_g)
    inp = {k: np.asarray(v, dtype=np.float32) for k, v in inp.items()}
    bf = ml_dtypes.bfloat16
    depth = inp["w_in"].shape[0]
    vec = lambda v: np.ascontiguousarray(v.reshape(16, 128).T)
    xs = inp["x"].reshape(8192, 2048)
    eye = np.eye(128, dtype=np.float32)
    r = _run(_prog("p0", build_p0), [dict(x=np.ascontiguousarray(xs[c * 1024:(c + 1) * 1024]), g1=vec(inp["norm1_g"][0]), ident=eye) for c in range(NCORES)])
    hT = [np.asarray(r[c]["hT"]) for c in range(NCORES)]
    uT = [np.asarray(r[c]["uT"]) for c in range(NCORES)]
    tabs = bias_index_tables()
    consts2 = r2_consts()
    out = None
    for l in range(depth):
        uT_all = np.ascontiguousarray(np.concatenate(uT, axis=1))
        r1 = _run(_prog("r1", build_r1), [dict(r1_inputs(inp, l, c), uT=uT_all) for c in range(NCORES)])
        in2 = []
        for c in range(NCORES):
            m = dict(consts2, **r2_inputs(inp, l, c))
            for n in R2_IN:
                m[n] = np.asarray(r1[c][n])
            m["g"] = np.asarray(r1[c]["g"])
            in2.append(m)
        r2 = _run(_prog("r2", build_r2), in2)
        ina = []
        for c in range(NCORES):
            heads = [g * 8 + c for g in range(3)]
            qc = lambda h: np.arange(3712 + h * 64, 3712 + h * 64 + 64)
            kc = lambda h: np.arange(3712 + 1536 + h * 64, 3712 + 1536 + h * 64 + 64)
            vc = lambda h: np.arange(3712 + 3072 + h * 64, 3712 + 3072 + h * 64 + 64)
            cols = np.concatenate([qc(heads[0]), qc(heads[1]), kc(heads[0]), kc(heads[1]), qc(heads[2]), vc(heads[2]), kc(heads[2]), vc(heads[0]), vc(heads[1])])
            bias = np.stack([np.where(m, inp["rel_bias"][:, heads[g]][idx], np.float32(-30000.0)) for g, (idx, m) in enumerate(tabs)], axis=1).astype(np.float32)
            ina.append(dict(uT=uT_all, wa=np.ascontiguousarray(inp["w_in"][l][:, cols]), biasT=np.ascontiguousarray(bias), identb=np.eye(128).astype(bf)))
        ra = _run(_prog("battn", build_b_attn), ina)
        o_all = np.concatenate([np.asarray(r2[c]["oT"]) for c in range(NCORES)] + [np.asarray(ra[c]["oT"]) for c in range(NCORES)], axis=0)
        last = (l == depth - 1)
        common = dict(wg=np.ascontiguousarray(inp["w_in"][l][:, 8320:]), wbr=inp["w_branch_rwkv"][l], wba=inp["w_branch_attn"][l], wout=inp["w_out"][l],
                      w1=inp["w_mlp_in"][l], w2=inp["w_mlp_out"][l], g2=vec(inp["norm2_g"][l]),
                      gn=vec(inp["final_g"] if last else inp["norm1_g"][l + 1]), ident=eye)
        inc = [dict(common, hT=hT[c], uT=uT[c], oT=np.ascontiguousarray(o_all[:, c * 1024:(c + 1) * 1024])) for c in range(NCORES)]
        rc = _run(_prog("c_last" if last else "c", lambda: build_c(last)), inc)
        if last:
            out = np.concatenate([np.asarray(rc[c]["out"]) for c in range(NCORES)], axis=0)
        else:
            hT = [np.asarray(rc[c]["hTo"]) for c in range(NCORES)]
            uT = [np.asarray(rc[c]["uTo"]) for c in range(NCORES)]
    return out.reshape(inp["x"].shape).astype(np.float32)
```

```python
import contextlib
import numpy as np
import ml_dtypes
import concourse.bass as bass
import concourse.mybir as mybir
from concourse.bass_utils import run_bass_kernel_spmd

F32 = mybir.dt.float32
BF16 = mybir.dt.bfloat16
AF = mybir.ActivationFunctionType
ALU = mybir.AluOpType
NCORES = 8


class Sched:
    ENGS = ("pe", "act", "dve", "pool", "sp")

    def __init__(self, nc):
        self.nc = nc
        self.ops = []
        self.last_w = {}
        self.readers = {}
        self.chan_cnt = {}
        self.chan_order = []
        self.bar = {}

    def op(self, eng, fn, reads=(), writes=(), chan=None, inc=16):
        idx = len(self.ops)
        deps = set()
        for r in reads:
            if r in self.last_w:
                deps.add(self.last_w[r])
        for w in writes:
            if w in self.last_w:
                deps.add(self.last_w[w])
            deps.update(self.readers.get(w, ()))
        if eng in self.bar:
            deps.update(self.bar.pop(eng))
        deps.discard(idx)
        cdeps = []
        odeps = []
        for d in deps:
            o = self.ops[d]
            if o["chan"] is not None:
                cdeps.append((o["chan"], self.chan_cnt[o["chan"]]))
            else:
                odeps.append(d)
        if chan is not None:
            if chan not in self.chan_cnt:
                self.chan_cnt[chan] = 0
                self.chan_order.append(chan)
            self.chan_cnt[chan] += inc
        self.ops.append(dict(eng=eng, fn=fn, odeps=odeps, cdeps=cdeps, chan=chan, waited=False, inc=inc))
        for r in reads:
            self.readers.setdefault(r, []).append(idx)
        for w in writes:
            self.last_w[w] = idx
            self.readers[w] = []
        return idx

    def barrier(self):
        last = {}
        for i, o in enumerate(self.ops):
            last[(o["eng"], o["chan"])] = i
        deps = set(last.values())
        for e in self.ENGS:
            self.bar[e] = set(deps)

    def run(self):
        nc = self.nc
        ops = self.ops
        for i, o in enumerate(ops):
            for d in o["odeps"]:
                p = ops[d]
                if p["eng"] == "pe" and o["eng"] == "pe":
                    continue
                p["waited"] = True
        cnt = {e: 0 for e in self.ENGS}
        for o in ops:
            if o["chan"] is None and o["waited"]:
                cnt[o["eng"]] += 1
                o["val"] = cnt[o["eng"]]
        import contextlib
        with contextlib.ExitStack() as st:
            esem = {e: st.enter_context(nc.semaphore("s_" + e)) for e in self.ENGS}
            csem = {c: st.enter_context(nc.semaphore("c_%d" % i)) for i, c in enumerate(self.chan_order)}
            block = st.enter_context(nc.Block())
            final_c = {c: self.chan_cnt[c] for c in self.chan_order}

            def emit(ename):
                def body(eng):
                    waited = {}
                    for o in ops:
                        if o["eng"] != ename:
                            continue
                        need = {}
                        for d in o["odeps"]:
                            p = ops[d]
                            if p["eng"] == "pe" and ename == "pe":
                                continue
                            k = ("e", p["eng"])
                            need[k] = max(need.get(k, 0), p["val"])
                        for c, v in o["cdeps"]:
                            k = ("c", c)
                            need[k] = max(need.get(k, 0), v)
                        for k, v in need.items():
                            if waited.get(k, 0) >= v:
                                continue
                            waited[k] = v
                            eng.wait_ge(esem[k[1]] if k[0] == "e" else csem[k[1]], v)
                        ins = o["fn"](eng)
                        if o["chan"] is not None:
                            ins.then_inc(csem[o["chan"]], o["inc"])
                        elif o["waited"]:
                            ins.then_inc(esem[ename], 1)
                    if ename == "sp":
                        for c in self.chan_order:
                            eng.wait_ge(csem[c], final_c[c])
                        for e in self.ENGS:
                            if e != "sp" and cnt[e] > 0:
                                eng.wait_ge(esem[e], cnt[e])
                return body

            block.sync(emit("sp"))
            block.scalar(emit("act"))
            block.vector(emit("dve"))
            block.gpsimd(emit("pool"))
            block.tensor(emit("pe"))
        return cnt, final_c
class WRing:
    def __init__(self, S, st, nc, n=2, name="wr"):
        self.S = S
        self.n = n
        self.slots = [st.enter_context(nc.sbuf_tensor("%s%d" % (name, i), [128, 16, 512], BF16)) for i in range(n)]
        self.stg = [st.enter_context(nc.sbuf_tensor("%sstg%d" % (name, i), [128, 8, 512], F32)) for i in range(3)]
        self.i = 0
        self.j = 0
        self.name = name

    def load(self, W, k0, nk, c0, ncols):
        s = self.i % self.n
        self.i += 1
        slot = self.slots[s]
        src = W[k0 * 128:(k0 + nk) * 128, c0:c0 + ncols].rearrange("(k p) c -> p k c", p=128)
        for ka in range(0, nk, 8):
            kb = min(nk, ka + 8)
            j = self.j % 3
            eng = ("dve", "act")[self.j % 2]
            self.j += 1
            stg = self.stg[j]
            self.S.op("sp", lambda e, ka=ka, kb=kb, stg=stg, src=src: e.dma_start(out=stg[:, 0:kb - ka, 0:ncols], in_=src[:, ka:kb, :]),
                      writes=[(self.name + "stg", j)], chan=(self.name + "stg", j))
            if eng == "act":
                f = lambda e, ka=ka, kb=kb, stg=stg, slot=slot: e.copy(out=slot[:, ka:kb, 0:ncols], in_=stg[:, 0:kb - ka, 0:ncols])
            else:
                f = lambda e, ka=ka, kb=kb, stg=stg, slot=slot: e.tensor_copy(out=slot[:, ka:kb, 0:ncols], in_=stg[:, 0:kb - ka, 0:ncols])
            self.S.op(eng, f, reads=[(self.name + "stg", j)], writes=[(self.name, s)])
        return slot, (self.name, s)


def emit_rmsnorm(S, nc, h, u, g, sq, rs, ones, pn, hres, ures, gres, out_fp32_inplace=False):
    for k in range(16):
        b = k % 2
        S.op("act", lambda e, k=k, b=b: e.activation(out=sq[b][:], in_=h[:, k, :], func=AF.Square),
             reads=[hres], writes=[("sq", b)])
        S.op("pe", lambda e, k=k, b=b: e.matmul(pn[:], lhsT=ones[:], rhs=sq[b][:], start=(k == 0), stop=(k == 15)),
             reads=[("sq", b), "ones"], writes=["pn"])
    S.op("act", lambda e: e.activation(out=rs[:], in_=pn[:], func=AF.Sqrt, scale=1.0 / 2048, bias=1e-6),
         reads=["pn"], writes=["rs"])
    S.op("dve", lambda e: e.reciprocal(out=rs[:], in_=rs[:]), reads=["rs"], writes=["rs"])
    for k in range(16):
        S.op("dve", lambda e, k=k: e.scalar_tensor_tensor(out=u[:, k, :], in0=h[:, k, :], scalar=g[:, k:k + 1], in1=rs[:], op0=ALU.mult, op1=ALU.mult),
             reads=[hres, gres, "rs"], writes=[ures])


def build_c(last):
    nc = bass.Bass("TRN2", target_bir_lowering=False)
    di = lambda name, shape, dt=F32: nc.dram_tensor(name, shape, dt, kind="ExternalInput").ap()
    hT = di("hT", [2048, 1024]); uT = di("uT", [2048, 1024], BF16); oT = di("oT", [1536, 1024], BF16)
    wg = di("wg", [2048, 4096]); wbr = di("wbr", [1024, 2048]); wba = di("wba", [512, 2048]); wout = di("wout", [2048, 2048])
    w1 = di("w1", [2048, 8192]); w2 = di("w2", [8192, 2048]); g2d = di("g2", [128, 16]); gnd = di("gn", [128, 16])
    ident = di("ident", [128, 128])
    if last:
        outd = nc.dram_tensor("out", [1024, 2048], F32, kind="ExternalOutput").ap()
    else:
        hTo = nc.dram_tensor("hTo", [2048, 1024], F32, kind="ExternalOutput").ap()
        uTo = nc.dram_tensor("uTo", [2048, 1024], BF16, kind="ExternalOutput").ap()
    with contextlib.ExitStack() as st:
        sb = lambda name, shape, dt: st.enter_context(nc.sbuf_tensor(name, shape, dt))
        ps = lambda name, shape, dt: st.enter_context(nc.psum_tensor(name, shape, dt))
        S = Sched(nc)
        h = sb("h", [128, 16, 512], F32); u = sb("u", [128, 16, 512], BF16); o = sb("o", [128, 12, 512], BF16)
        mg = sb("mg", [128, 16, 512], BF16); hid = sb("hid", [128, 16, 512], BF16)
        gt = [sb("gt%d" % i, [128, 4, 512], BF16) for i in range(2)]
        t1 = sb("t1", [128, 4, 512], F32); t2 = sb("t2", [128, 512], F32)
        rl = [sb("rl%d" % i, [128, 512], F32) for i in range(2)]
        sq = [sb("sq%d" % i, [128, 512], F32) for i in range(2)]
        rs = sb("rs", [128, 512], F32)
        g2 = sb("g2s", [128, 16], F32); gn = sb("gns", [128, 16], F32)
        ones = sb("ones", [128, 128], F32); idt = sb("idt", [128, 128], F32)
        ring = WRing(S, st, nc, 2)
        pA = [ps("pA%d" % i, [128, 512], F32) for i in range(4)]
        pB = [ps("pB%d" % i, [128, 512], F32) for i in range(3)]
        pn = ps("pn", [128, 512], F32)
        S.op("sp", lambda e: e.dma_start(out=g2[:], in_=g2d), writes=["g2"], chan="g2")
        S.op("sp", lambda e: e.dma_start(out=gn[:], in_=gnd), writes=["gn"], chan="gn")
        S.op("sp", lambda e: e.dma_start(out=idt[:], in_=ident), writes=["idt"], chan="idt")
        S.op("dve", lambda e: e.memset(ones[:], 1.0), writes=["ones"])
        hTr = hT.rearrange("(k p) t -> p k t", p=128); uTr = uT.rearrange("(k p) t -> p k t", p=128)
        oTr = oT.rearrange("(k p) t -> p k t", p=128)

        PBR = [("pB", 0), ("pB", 1), ("pB", 2), "pn"]
        PAR = [("pA", j) for j in range(4)]

        def mm_group(pbanks, pres, slot, sres, nk, rhs, rres, first=True, lastk=True, ncol=4):
            for j in range(ncol):
                for k in range(nk):
                    S.op("pe", lambda e, j=j, k=k: e.matmul(pbanks[j][:], lhsT=slot[:, k, j * 128:(j + 1) * 128], rhs=rhs[:, k, :],
                                                            start=(first and k == 0), stop=(lastk and k == nk - 1)),
                         reads=[sres, rres], writes=[pres[j]])

        for tb in range(2):
            tsl = slice(tb * 512, (tb + 1) * 512)
            S.op("sp", lambda e, tsl=tsl: e.dma_start(out=h[:], in_=hTr[:, :, tsl]), writes=["h"], chan="h")
            S.op("sp", lambda e, tsl=tsl: e.dma_start(out=u[:], in_=uTr[:, :, tsl]), writes=["u"], chan="u")
            S.op("sp", lambda e, tsl=tsl: e.dma_start(out=o[:], in_=oTr[:, :, tsl]), writes=["o"], chan="o")
            for cg in range(4):
                slot, sres = ring.load(wg, 0, 16, cg * 512, 512)
                mm_group(pA, PAR, slot, sres, 16, u, "u")
                for j in range(4):
                    S.op("act", lambda e, j=j: e.activation(out=gt[0][:, j, :], in_=pA[j][:], func=AF.Sigmoid),
                         reads=[("pA", j)], writes=[("gt0", j)])
                slot, sres = ring.load(wbr, 0, 8, cg * 512, 512)
                pBx = [pB[0], pB[1], pB[2], pn]
                mm_group(pBx, PBR, slot, sres, 8, o, "o")
                for j in range(4):
                    S.op("dve", lambda e, j=j, pBx=pBx: e.tensor_tensor(out=t1[:, j, :], in0=pBx[j][:], in1=gt[0][:, j, :], op=ALU.mult),
                         reads=[PBR[j], ("gt0", j)], writes=[("t1", j)])
                slot, sres = ring.load(wg, 0, 16, 2048 + cg * 512, 512)
                mm_group(pA, PAR, slot, sres, 16, u, "u")
                for j in range(4):
                    S.op("act", lambda e, j=j: e.activation(out=gt[1][:, j, :], in_=pA[j][:], func=AF.Sigmoid),
                         reads=[("pA", j)], writes=[("gt1", j)])
                slot, sres = ring.load(wba, 0, 4, cg * 512, 512)
                osub = o[:, 8:12, :]
                mm_group(pBx, PBR, slot, sres, 4, osub, "o")
                for j in range(4):
                    S.op("dve", lambda e, j=j, pBx=pBx: e.tensor_tensor(out=t2[:], in0=pBx[j][:], in1=gt[1][:, j, :], op=ALU.mult),
                         reads=[PBR[j], ("gt1", j)], writes=["t2"])
                    S.op("pool", lambda e, j=j, cg=cg: e.tensor_tensor(out=mg[:, cg * 4 + j, :], in0=t1[:, j, :], in1=t2[:], op=ALU.add),
                         reads=[("t1", j), "t2"], writes=["mg"])
            for cg in range(4):
                slot, sres = ring.load(wout, 0, 16, cg * 512, 512)
                mm_group(pA, PAR, slot, sres, 16, mg, "mg")
                for j in range(4):
                    S.op("dve", lambda e, j=j, cg=cg: e.tensor_tensor(out=h[:, cg * 4 + j, :], in0=h[:, cg * 4 + j, :], in1=pA[j][:], op=ALU.add),
                         reads=[("pA", j), "h"], writes=["h"])
            emit_rmsnorm(S, nc, h, u, g2, sq, rs, ones, pn, "h", "u", "g2")
            for qt in range(4):
                for cg in range(4):
                    slot, sres = ring.load(w1, 0, 16, (qt * 4 + cg) * 512, 512)
                    mm_group(pA, PAR, slot, sres, 16, u, "u")
                    for j in range(4):
                        b = j % 2
                        S.op("act", lambda e, j=j, b=b: e.activation(out=rl[b][:], in_=pA[j][:], func=AF.Relu),
                             reads=[("pA", j)], writes=[("rl", b)])
                        S.op("pool" if j % 2 else "dve", lambda e, j=j, b=b, cg=cg: e.tensor_tensor(out=hid[:, cg * 4 + j, :], in0=rl[b][:], in1=rl[b][:], op=ALU.mult),
                             reads=[("rl", b)], writes=["hid"])
                for og in range(4):
                    pBx = [pB[0], pB[1], pB[2], pn]
                    slot, sres = ring.load(w2, qt * 16, 16, og * 512, 512)
                    mm_group(pBx, PBR, slot, sres, 16, hid, "hid")
                    for j in range(4):
                        S.op("dve", lambda e, j=j, og=og, pBx=pBx: e.tensor_tensor(out=h[:, og * 4 + j, :], in0=h[:, og * 4 + j, :], in1=pBx[j][:], op=ALU.add),
                             reads=[PBR[j], "h"], writes=["h"])
            if not last:
                emit_rmsnorm(S, nc, h, u, gn, sq, rs, ones, pn, "h", "u", "gn")
                S.op("sp", lambda e, tsl=tsl: e.dma_start(out=hTo.rearrange("(k p) t -> p k t", p=128)[:, :, tsl], in_=h[:]), reads=["h"], chan="oh")
                S.op("sp", lambda e, tsl=tsl: e.dma_start(out=uTo.rearrange("(k p) t -> p k t", p=128)[:, :, tsl], in_=u[:]), reads=["u"], chan="ou")
            else:
                for k in range(16):
                    b = k % 2
                    S.op("act", lambda e, k=k, b=b: e.activation(out=sq[b][:], in_=h[:, k, :], func=AF.Square), reads=["h"], writes=[("sq", b)])
                    S.op("pe", lambda e, k=k, b=b: e.matmul(pn[:], lhsT=ones[:], rhs=sq[b][:], start=(k == 0), stop=(k == 15)),
                         reads=[("sq", b), "ones"], writes=["pn"])
                S.op("act", lambda e: e.activation(out=rs[:], in_=pn[:], func=AF.Sqrt, scale=1.0 / 2048, bias=1e-6), reads=["pn"], writes=["rs"])
                S.op("dve", lambda e: e.reciprocal(out=rs[:], in_=rs[:]), reads=["rs"], writes=["rs"])
                for k in range(16):
                    S.op("dve", lambda e, k=k: e.scalar_tensor_tensor(out=h[:, k, :], in0=h[:, k, :], scalar=gn[:, k:k + 1], in1=rs[:], op0=ALU.mult, op1=ALU.mult),
                         reads=["h", "gn", "rs"], writes=["h"])
                otv = t1[:].rearrange("p a b -> p (a b)")
                T1R = [("t1", j) for j in range(4)]
                for tt in range(4):
                    for kg in range(4):
                        for j in range(4):
                            k = kg * 4 + j
                            S.op("pe", lambda e, k=k, j=j, kg=kg, tt=tt: e.transpose(out=pA[kg][:, j * 128:(j + 1) * 128], in_=h[:, k, tt * 128:(tt + 1) * 128], identity=idt[:]),
                                 reads=["h", "idt"], writes=[("pA", kg)])
                        S.op("act" if kg % 2 else "dve",
                             (lambda e, kg=kg: e.copy(out=otv[:, kg * 512:(kg + 1) * 512], in_=pA[kg][:])) if kg % 2 else
                             (lambda e, kg=kg: e.tensor_copy(out=otv[:, kg * 512:(kg + 1) * 512], in_=pA[kg][:])),
                             reads=[("pA", kg)], writes=[("t1", kg)])
                    r0 = tb * 512 + tt * 128
                    S.op("sp", lambda e, r0=r0: e.dma_start(out=outd[r0:r0 + 128, :], in_=otv), reads=T1R, chan="oo")
        S.run()
    return nc
GROUPS = ((128, 1), (512, 4), (2048, 16))
S_LEN = 4096


def t5_bucket_np(rel):
    nb = 16
    max_exact = 8
    ret = np.where(rel > 0, nb, 0)
    n = np.abs(rel)
    nf = np.maximum(n, 1).astype(np.float32)
    large = max_exact + (np.log(nf / np.float32(max_exact)) / np.float32(np.log(1024 / max_exact)) * np.float32(nb - max_exact)).astype(np.int32)
    large = np.minimum(large, nb - 1)
    return ret + np.where(n < max_exact, n, large)


def bias_index_tables():
    kap = np.arange(128)[:, None]
    qi = np.arange(128)[None, :]
    out = []
    for (window, d) in GROUPS:
        da = kap - 64 - qi
        db = kap + 64 - qi
        delta = np.concatenate([da, db], axis=1)
        out.append((t5_bucket_np(delta * d), np.abs(delta) <= 64))
    return out


def emit_attention(S, nc, st, uTb, Wa, biasd, identd, oT_dst, sbufs):
    sb, ps = sbufs["sb"], sbufs["ps"]
    wa, ub, Qz0, Qz1, Qz2, K01, K2, V01, V2, Vt, E, Pf, Pm, accn, accd, onesk, idb, ot = [sbufs[k] for k in
        ("wa", "ub", "Qz0", "Qz1", "Qz2", "K01", "K2", "V01", "V2", "Vt", "E", "Pf", "Pm", "accn", "accd", "onesk", "idb", "ot")]
    pj, pS, pOD, pT = sbufs["pj"], sbufs["pS"], sbufs["pOD"], sbufs["pT"]
    uTr = uTb.rearrange("(k p) t -> p k t", p=128)
    for tblk in range(8):
        b = tblk % 2
        S.op("sp", lambda e, tblk=tblk, b=b: e.dma_start(out=ub[b][:], in_=uTr[:, :, tblk * 512:(tblk + 1) * 512]), writes=[("ub", b)], chan=("ub", b))
        for ct in range(5):
            M = 64 if ct == 3 else 128
            c0 = ct * 128 if ct < 4 else 448
            pb = (tblk * 5 + ct) % 2
            for k in range(16):
                S.op("pe", lambda e, k=k, M=M, c0=c0, pb=pb, b=b: e.matmul(pj[pb][0:M, :], lhsT=wa[:, k, c0:c0 + M], rhs=ub[b][:, k, :], start=(k == 0), stop=(k == 15)),
                     reads=[("ub", b), "wa"], writes=[("pj", pb)])
            t0 = tblk * 512
            if ct == 0:
                S.op("act", lambda e, pb=pb, t0=t0: e.copy(out=Qz0[0:64, t0:t0 + 512], in_=pj[pb][0:64, :]), reads=[("pj", pb)], writes=["Qz0"])
                S.op("dve", lambda e, pb=pb, t0=t0: e.tensor_copy(out=Qz1[64:128, :].rearrange("p (r i) -> p r i", r=4)[:, :, t0 // 4:t0 // 4 + 128],
                                                                   in_=pj[pb][64:128, :].rearrange("p (i r) -> p r i", r=4)), reads=[("pj", pb)], writes=["Qz1"])
            elif ct == 1:
                S.op("act", lambda e, pb=pb, t0=t0: e.copy(out=K01[0:64, 64 + t0:64 + t0 + 512], in_=pj[pb][0:64, :]), reads=[("pj", pb)], writes=["K01"])
                S.op("dve", lambda e, pb=pb, t0=t0: e.tensor_copy(out=K01[64:128, :].rearrange("p (r i) -> p r i", r=4)[:, :, 64 + t0 // 4:64 + t0 // 4 + 128],
                                                                   in_=pj[pb][64:128, :].rearrange("p (i r) -> p r i", r=4)), reads=[("pj", pb)], writes=["K01"])
            elif ct == 2:
                S.op("act", lambda e, pb=pb, t0=t0: e.copy(out=Qz2[0:64, :].rearrange("p (r i) -> p r i", r=16)[:, :, t0 // 16:t0 // 16 + 32],
                                                            in_=pj[pb][0:64, :].rearrange("p (i r) -> p r i", r=16)), reads=[("pj", pb)], writes=["Qz2"])
                S.op("dve", lambda e, pb=pb, t0=t0: e.tensor_copy(out=V2[64:128, :].rearrange("p (r i) -> p r i", r=16)[:, :, 64 + t0 // 16:64 + t0 // 16 + 32],
                                                                   in_=pj[pb][64:128, :].rearrange("p (i r) -> p r i", r=16)), reads=[("pj", pb)], writes=["V2"])
            elif ct == 3:
                S.op("act", lambda e, pb=pb, t0=t0: e.copy(out=K2[0:64, :].rearrange("p (r i) -> p r i", r=16)[:, :, 64 + t0 // 16:64 + t0 // 16 + 32],
                                                            in_=pj[pb][0:64, :].rearrange("p (i r) -> p r i", r=16)), reads=[("pj", pb)], writes=["K2"])
            else:
                S.op("act", lambda e, pb=pb, t0=t0: e.copy(out=V01[0:64, 64 + t0:64 + t0 + 512], in_=pj[pb][0:64, :]), reads=[("pj", pb)], writes=["V01"])
                S.op("dve", lambda e, pb=pb, t0=t0: e.tensor_copy(out=V01[64:128, :].rearrange("p (r i) -> p r i", r=4)[:, :, 64 + t0 // 4:64 + t0 // 4 + 128],
                                                                   in_=pj[pb][64:128, :].rearrange("p (i r) -> p r i", r=4)), reads=[("pj", pb)], writes=["V01"])
    vsrc = [V01[:, m * 128:(m + 1) * 128] for m in range(36)] + [V2[:, m * 128:(m + 1) * 128] for m in range(48)]
    for t0 in range(0, 84, 8):
        n = min(8, 84 - t0)
        pb = (t0 // 8) % 2
        for j in range(n):
            S.op("pe", lambda e, src=vsrc[t0 + j], j=j, pb=pb: e.transpose(out=pT[pb][:, j * 128:(j + 1) * 128], in_=src, identity=idb[:]),
                 reads=["V01", "V2", "idb"], writes=[("pT", pb)])
        S.op("dve" if pb else "act",
             (lambda e, t0=t0, n=n, pb=pb: e.tensor_copy(out=Vt[:, t0:t0 + n, :], in_=pT[pb][:, 0:n * 128].rearrange("p (j c) -> p j c", c=128))) if pb else
             (lambda e, t0=t0, n=n, pb=pb: e.copy(out=Vt[:, t0:t0 + n, :], in_=pT[pb][:, 0:n * 128].rearrange("p (j c) -> p j c", c=128))),
             reads=[("pT", pb)], writes=["Vt"])
    Qsrc = [lambda r: Qz0[:, :], lambda r: Qz1[:, :].rearrange("p (r i) -> p r i", r=4)[:, r, :],
            lambda r: Qz2[:, :].rearrange("p (r i) -> p r i", r=16)[:, r, :]]
    Ksrc = [lambda r: K01[:, 0:4224], lambda r: K01[:, :].rearrange("p (r i) -> p r i", r=4)[:, r, :],
            lambda r: K2[:, :].rearrange("p (r i) -> p r i", r=16)[:, r, :]]
    vt_of = [lambda r, m: (m, 0), lambda r, m: (r * 9 + m, 64), lambda r, m: (36 + r * 3 + m, 64)]
    qb = 0
    for g, (window, d) in enumerate(GROUPS):
        L = S_LEN // d
        nt = L // 128 + 1
        nq = L // 128
        for r in range(d):
            q_ap, k_ap = Qsrc[g](r), Ksrc[g](r)
            for m in range(nq):
                sbk = qb % 2
                S.op("pe", lambda e, k_ap=k_ap, q_ap=q_ap, m=m, sbk=sbk: e.matmul(pS[sbk][:, 0:128], lhsT=k_ap[:, m * 128:m * 128 + 128], rhs=q_ap[:, m * 128:m * 128 + 128], start=True, stop=True),
                     reads=["Qz0", "Qz1", "Qz2", "K01", "K2"], writes=[("pS", sbk)])
                S.op("pe", lambda e, k_ap=k_ap, q_ap=q_ap, m=m, sbk=sbk: e.matmul(pS[sbk][:, 128:256], lhsT=k_ap[:, m * 128 + 128:m * 128 + 256], rhs=q_ap[:, m * 128:m * 128 + 128], start=True, stop=True),
                     reads=["Qz0", "Qz1", "Qz2", "K01", "K2"], writes=[("pS", sbk)])
                S.op("act", lambda e, sbk=sbk: e.activation(out=Pf[sbk][:], in_=pS[sbk][:, 0:256], func=AF.Exp, scale=0.125), reads=[("pS", sbk)], writes=[("Pf", sbk)])
                S.op("dve", lambda e, sbk=sbk, g=g: e.tensor_tensor(out=Pm[sbk][:], in0=Pf[sbk][:], in1=E[:, g, :], op=ALU.mult), reads=[("Pf", sbk), "E"], writes=[("Pm", sbk)])
                half = m % 2
                ob = (qb // 2) % 2
                for (pp, pname, lh, coff) in ((pOD, "pOD", None, 0), (pOD, "pOD", "ones", 256)):
                    for kt in range(2):
                        if lh is None:
                            tix, c0v = vt_of[g](r, m + kt)
                            lhs = Vt[:, tix, c0v:c0v + 64]
                        else:
                            var = 1 if (m == 0 and kt == 0) else (2 if (m == nq - 1 and kt == 1) else 0)
                            lhs = onesk[:, var, :]
                        S.op("pe", lambda e, pp=pp, lhs=lhs, kt=kt, sbk=sbk, ob=ob, half=half, coff=coff: e.matmul(pp[ob][0:64, coff + half * 128:coff + (half + 1) * 128], lhsT=lhs, rhs=Pm[sbk][:, kt * 128:(kt + 1) * 128], start=(kt == 0), stop=(kt == 1)),
                             reads=[("Pm", sbk), "Vt", "onesk"], writes=[(pname, ob)])
                if half == 1:
                    m0 = m - 1
                    dst = lambda acc, d=d, r=r, m0=m0: acc[:, :].rearrange("p (i r) -> p r i", r=d)[:, r, m0 * 128:m0 * 128 + 256]
                    if g == 0:
                        S.op("act", lambda e, ob=ob, dst=dst: e.copy(out=dst(accn), in_=pOD[ob][0:64, 0:256]), reads=[("pOD", ob)], writes=["accn"])
                        S.op("act", lambda e, ob=ob, dst=dst: e.copy(out=dst(accd), in_=pOD[ob][0:64, 256:512]), reads=[("pOD", ob)], writes=["accd"])
                    else:
                        S.op("dve", lambda e, ob=ob, dst=dst: e.tensor_tensor(out=dst(accn), in0=dst(accn), in1=pOD[ob][0:64, 0:256], op=ALU.add), reads=[("pOD", ob), "accn"], writes=["accn"])
                        S.op("dve", lambda e, ob=ob, dst=dst: e.tensor_tensor(out=dst(accd), in0=dst(accd), in1=pOD[ob][0:64, 256:512], op=ALU.add), reads=[("pOD", ob), "accd"], writes=["accd"])
                qb += 1
    S.op("dve", lambda e: e.reciprocal(out=accd[:], in_=accd[:]), reads=["accd"], writes=["accd"])
    S.op("dve", lambda e: e.tensor_tensor(out=ot[:], in0=accn[:], in1=accd[:], op=ALU.mult), reads=["accn", "accd"], writes=["ot"])
    S.op("sp", lambda e: e.dma_start(out=oT_dst, in_=ot[:]), reads=["ot"], chan="oattn")


def alloc_attention(nc, st):
    sb = lambda name, shape, dt: st.enter_context(nc.sbuf_tensor(name, shape, dt))
    ps = lambda name, shape, dt: st.enter_context(nc.psum_tensor(name, shape, dt))
    d = dict(sb=sb, ps=ps)
    d["wa"] = sb("wa_sb", [128, 16, 576], BF16)
    d["ub"] = [sb("ub%d" % i, [128, 16, 512], BF16) for i in range(2)]
    d["Qz0"] = sb("Qz0", [128, 4096], BF16)
    d["Qz1"] = sb("Qz1", [128, 4096], BF16)
    d["Qz2"] = sb("Qz2", [128, 4096], BF16)
    d["K01"] = sb("K01", [128, 4608], BF16)
    d["K2"] = sb("K2", [128, 6144], BF16)
    d["V01"] = sb("V01", [128, 4608], BF16)
    d["V2"] = sb("V2", [128, 6144], BF16)
    d["Vt"] = sb("Vt", [128, 84, 128], BF16)
    d["E"] = sb("E", [128, 3, 256], F32)
    d["Pf"] = [sb("Pf%d" % i, [128, 256], F32) for i in range(2)]
    d["Pm"] = [sb("Pm%d" % i, [128, 256], BF16) for i in range(2)]
    d["accn"] = sb("accn", [64, 4096], F32)
    d["accd"] = sb("accd", [64, 4096], F32)
    d["onesk"] = sb("onesk", [128, 3, 64], BF16)
    d["idb"] = sb("idb", [128, 128], BF16)
    d["ot"] = sb("ot", [64, 4096], BF16)
    d["pj"] = [ps("pj%d" % i, [128, 512], F32) for i in range(2)]
    d["pS"] = [ps("pS%d" % i, [128, 512], F32) for i in range(2)]
    d["pOD"] = [ps("pOD%d" % i, [128, 512], F32) for i in range(2)]
    d["pT"] = [ps("pT%d" % i, [128, 1024], BF16) for i in range(2)]
    return d


def build_b_attn():
    nc = bass.Bass("TRN2", target_bir_lowering=False)
    di = lambda name, shape, dt=F32: nc.dram_tensor(name, shape, dt, kind="ExternalInput").ap()
    uT = di("uT", [2048, 8192], BF16)
    wad = di("wa", [2048, 576])
    biasd = di("biasT", [128, 3, 256])
    identd = di("identb", [128, 128], BF16)
    oT = nc.dram_tensor("oT", [64, 8192], BF16, kind="ExternalOutput").ap()
    with contextlib.ExitStack() as st:
        S = Sched(nc)
        A = alloc_attention(nc, st)
        emit_attn_setup(S, nc, A, wad, biasd, identd)
        for b in range(2):
            emit_attention(S, nc, st, uT[:, b * 4096:(b + 1) * 4096], wad, biasd, identd, oT[:, b * 4096:(b + 1) * 4096], A)
        S.run()
    return nc


def emit_attn_setup(S, nc, A, wad, biasd, identd):
    wa, E, onesk, idb = A["wa"], A["E"], A["onesk"], A["idb"]
    src = wad.rearrange("(k p) c -> p k c", p=128)
    for ka in range(0, 16, 4):
        S.op("pool", lambda e, ka=ka: e.dma_start(out=wa[:, ka:ka + 4, :], in_=src[:, ka:ka + 4, :]), writes=["wa"], chan="wa")
    S.op("sp", lambda e: e.dma_start(out=E[:], in_=biasd), writes=["E"], chan="E")
    S.op("sp", lambda e: e.dma_start(out=idb[:], in_=identd), writes=["idb"], chan="idb")
    S.op("act", lambda e: e.activation(out=E[:], in_=E[:], func=AF.Exp), reads=["E"], writes=["E"])
    S.op("dve", lambda e: e.memset(onesk[:], 1.0), writes=["onesk"])
    S.op("dve", lambda e: e.memset(onesk[0:64, 1, :], 0.0), writes=["onesk"])
    S.op("dve", lambda e: e.memset(onesk[64:128, 2, :], 0.0), writes=["onesk"])
    for name in ("K01", "V01", "K2", "V2", "Qz0", "Qz1", "Qz2"):
        S.op("pool", lambda e, name=name: e.memset(A[name][:], 0.0), writes=[name])
R1_OUT = ("p_r", "p_k", "p_v", "kk", "a_f", "a_b", "l_f", "l_b")


def build_r1():
    nc = bass.Bass("TRN2", target_bir_lowering=False)
    di = lambda name, shape, dt=F32: nc.dram_tensor(name, shape, dt, kind="ExternalInput").ap()
    uT = di("uT", [2048, 8192], BF16)
    wrd = di("wr", [2048, 1024])
    pard = di("par", [128, 32])
    gupd = di("gup", [256, 128]); wupd = di("wup", [2, 96, 128]); aupd = di("aup", [2, 96, 128])
    bonesd = di("bones", [128, 128])
    outs = {n: nc.dram_tensor(n, [128, 8192], F32, kind="ExternalOutput").ap() for n in R1_OUT}
    g_out = nc.dram_tensor("g", [128, 8192], BF16, kind="ExternalOutput").ap()
    with contextlib.ExitStack() as st:
        sb = lambda name, shape, dt: st.enter_context(nc.sbuf_tensor(name, shape, dt))
        ps = lambda name, shape, dt: st.enter_context(nc.psum_tensor(name, shape, dt))
        S = Sched(nc)
        wr = sb("wr_sb", [128, 16, 1024], BF16)
        ub = [sb("ub%d" % i, [128, 16, 512], BF16) for i in range(2)]
        raw = [sb("raw%d" % i, [128, 4098], BF16) for i in range(9)]
        par = sb("par_sb", [128, 32], F32)
        gup = sb("gup_sb", [128, 2, 128], BF16); wup = sb("wup_sb", [96, 2, 128], BF16); aup = sb("aup_sb", [96, 2, 128], BF16)
        bones = sb("bones_sb", [128, 128], F32)
        t1 = sb("t1", [128, 4096], F32); t2 = sb("t2", [128, 4096], F32); t3 = sb("t3", [128, 4096], F32)
        nl = sb("nl", [128, 2, 4096], BF16)
        pj = [ps("pj%d" % i, [128, 512], F32) for i in range(4)]
        pq = [ps("pq%d" % i, [128, 512], F32) for i in range(4)]
        src = wrd.rearrange("(k p) c -> p k c", p=128)
        for ka in range(0, 16, 2):
            S.op("pool", lambda e, ka=ka: e.dma_start(out=wr[:, ka:ka + 2, :], in_=src[:, ka:ka + 2, :]), writes=["wr"], chan="wr")
        S.op("sp", lambda e: e.dma_start(out=par[:], in_=pard), writes=["par"], chan="par")
        S.op("sp", lambda e: e.dma_start(out=bones[:], in_=bonesd), writes=["bones"], chan="bones")
        S.op("pool", lambda e: e.dma_start(out=gup[:], in_=gupd.rearrange("(k p) c -> p k c", p=128)), writes=["gup"], chan="gup")
        S.op("pool", lambda e: e.dma_start(out=wup[:], in_=wupd.rearrange("d p c -> p d c")), writes=["wup"], chan="wup")
        S.op("pool", lambda e: e.dma_start(out=aup[:], in_=aupd.rearrange("d p c -> p d c")), writes=["aup"], chan="aup")
        S.op("dve", lambda e: e.tensor_scalar(out=par[:, 9:18], in0=par[:, 0:9], scalar1=-1.0, scalar2=1.0, op0=ALU.mult, op1=ALU.add), reads=["par"], writes=["par"])
        S.op("dve", lambda e: e.tensor_scalar(out=par[:, 18:27], in0=par[:, 0:9], scalar1=0.5, scalar2=None, op0=ALU.mult), reads=["par"], writes=["par"])
        for i in range(9):
            S.op("pool", lambda e, i=i: e.memset(raw[i][:], 0.0), writes=[("raw", i)])
        rows = [128] * 5 + [96] * 4
        uTr = uT.rearrange("(k p) t -> p k t", p=128)
        for b in range(2):
            tok0 = b * 4096
            for tblk in range(8):
                bb = tblk % 2
                S.op("sp", lambda e, tblk=tblk, bb=bb, tok0=tok0: e.dma_start(out=ub[bb][:], in_=uTr[:, :, tok0 + tblk * 512:tok0 + (tblk + 1) * 512]), writes=[("ub", bb)], chan=("ub", bb))
                for ct in range(9):
                    M = rows[ct]
                    c0 = ct * 128 if ct < 5 else 640 + (ct - 5) * 96
                    pb = ct % 4
                    for k in range(16):
                        S.op("pe", lambda e, k=k, M=M, c0=c0, pb=pb, bb=bb: e.matmul(pj[pb][0:M, :], lhsT=wr[:, k, c0:c0 + M], rhs=ub[bb][:, k, :], start=(k == 0), stop=(k == 15)),
                             reads=[("ub", bb), "wr"], writes=[("pj", pb)])
                    S.op("act" if ct % 2 else "dve",
                         (lambda e, ct=ct, M=M, pb=pb, tblk=tblk: e.copy(out=raw[ct][0:M, 1 + tblk * 512:1 + (tblk + 1) * 512], in_=pj[pb][0:M, :])) if ct % 2 else
                         (lambda e, ct=ct, M=M, pb=pb, tblk=tblk: e.tensor_copy(out=raw[ct][0:M, 1 + tblk * 512:1 + (tblk + 1) * 512], in_=pj[pb][0:M, :])),
                         reads=[("pj", pb)], writes=[("raw", ct)])
            tsl = slice(tok0, tok0 + 4096)

            def shift(ct, dst):
                M = rows[ct]
                S.op("dve", lambda e: e.tensor_tensor(out=t1[0:M, :], in0=raw[ct][0:M, 0:4096], in1=raw[ct][0:M, 2:4098], op=ALU.add), reads=[("raw", ct)], writes=["t1"])
                S.op("act", lambda e: e.activation(out=t2[0:M, :], in_=raw[ct][0:M, 1:4097], func=AF.Copy, scale=par[0:M, 9 + ct:10 + ct]), reads=[("raw", ct), "par"], writes=["t2"])
                S.op("dve", lambda e: e.scalar_tensor_tensor(out=dst[0:M, :], in0=t1[0:M, :], scalar=par[0:M, 18 + ct:19 + ct], in1=t2[0:M, :], op0=ALU.mult, op1=ALU.add),
                     reads=["t1", "t2", "par"], writes=["t3"])

            def store(name, srct, res, tsl=tsl):
                S.op("sp", lambda e: e.dma_start(out=outs[name][:, tsl], in_=srct[:]), reads=[res], chan="o_" + name)

            shift(0, t3); store("p_r", t3, "t3")
            shift(2, t3); store("p_v", t3, "t3")
            shift(1, t3); store("p_k", t3, "t3")
            S.op("dve", lambda e: e.tensor_scalar(out=t1[:], in0=t3[:], scalar1=par[:, 27:28], scalar2=None, op0=ALU.mult), reads=["t3", "par"], writes=["t1"])
            S.op("act", lambda e: e.activation(out=t2[:], in_=t1[:], func=AF.Square), reads=["t1"], writes=["t2"])
            for blk in range(8):
                pb = blk % 4
                bs = slice(blk * 512, (blk + 1) * 512)
                S.op("pe", lambda e, pb=pb, bs=bs: e.matmul(pq[pb][:], lhsT=bones[:], rhs=t2[:, bs], start=True, stop=True), reads=["t2", "bones"], writes=[("pq", pb)])
                S.op("act", lambda e, pb=pb, bs=bs: e.activation(out=t3[:, bs], in_=pq[pb][:], func=AF.Sqrt), reads=[("pq", pb)], writes=["t3"])
            S.op("dve", lambda e: e.tensor_scalar(out=t3[:], in0=t3[:], scalar1=1e-12, scalar2=None, op0=ALU.max), reads=["t3"], writes=["t3"])
            S.op("dve", lambda e: e.reciprocal(out=t3[:], in_=t3[:]), reads=["t3"], writes=["t3"])
            S.op("dve", lambda e: e.tensor_tensor(out=t3[:], in0=t3[:], in1=t1[:], op=ALU.mult), reads=["t3", "t1"], writes=["t3"])
            store("kk", t3, "t3")
            for j in range(2):
                shift(3 + j, t3)
                S.op("act", lambda e, j=j: e.activation(out=nl[:, j, :], in_=t3[:], func=AF.Sigmoid), reads=["t3"], writes=["nl"])
            for blk in range(8):
                pb = blk % 4
                bs = slice(blk * 512, (blk + 1) * 512)
                for j in range(2):
                    S.op("pe", lambda e, pb=pb, bs=bs, j=j: e.matmul(pq[pb][:], lhsT=gup[:, j, :], rhs=nl[:, j, bs], start=(j == 0), stop=(j == 1)), reads=["nl", "gup"], writes=[("pq", pb)])
                S.op("act", lambda e, pb=pb, blk=blk: e.copy(out=raw[3][:, 1 + blk * 512:1 + (blk + 1) * 512], in_=pq[pb][:]), reads=[("pq", pb)], writes=[("raw", 3)])
            S.op("sp", lambda e, tsl=tsl: e.dma_start(out=g_out[:, tsl], in_=raw[3][:, 1:4097]), reads=[("raw", 3)], chan="o_g")
            for d in range(2):
                shift(5 + d, t3)
                S.op("act", lambda e: e.activation(out=nl[0:96, 0, :], in_=t3[0:96, :], func=AF.Tanh), reads=["t3"], writes=["nl"])
                for blk in range(8):
                    pb = blk % 4
                    bs = slice(blk * 512, (blk + 1) * 512)
                    S.op("pe", lambda e, pb=pb, bs=bs, d=d: e.matmul(pq[pb][:], lhsT=wup[:, d, :], rhs=nl[0:96, 0, bs], start=True, stop=True), reads=["nl", "wup"], writes=[("pq", pb)])
                    S.op("act", lambda e, pb=pb, bs=bs, d=d: e.activation(out=t1[:, bs], in_=pq[pb][:], func=AF.Sigmoid, bias=par[:, 28 + d:29 + d]), reads=[("pq", pb), "par"], writes=["t1"])
                S.op("dve", lambda e: e.tensor_scalar(out=t1[:], in0=t1[:], scalar1=-0.6065306597126334, scalar2=None, op0=ALU.mult), reads=["t1"], writes=["t1"])
                store("l_f" if d == 0 else "l_b", t1, "t1")
                shift(7 + d, t3)
                S.op("act", lambda e: e.copy(out=nl[0:96, 1, :], in_=t3[0:96, :]), reads=["t3"], writes=["nl"])
                for blk in range(8):
                    pb = blk % 4
                    bs = slice(blk * 512, (blk + 1) * 512)
                    S.op("pe", lambda e, pb=pb, bs=bs, d=d: e.matmul(pq[pb][:], lhsT=aup[:, d, :], rhs=nl[0:96, 1, bs], start=True, stop=True), reads=["nl", "aup"], writes=[("pq", pb)])
                    S.op("act", lambda e, pb=pb, bs=bs, d=d: e.activation(out=t2[:, bs], in_=pq[pb][:], func=AF.Sigmoid, bias=par[:, 30 + d:31 + d]), reads=[("pq", pb), "par"], writes=["t2"])
                store("a_f" if d == 0 else "a_b", t2, "t2")
        S.run()
    return nc


def r1_inputs(inp, l, c):
    hs = [2 * c, 2 * c + 1]
    ch = np.concatenate([np.arange(h * 64, h * 64 + 64) for h in hs])
    cols = np.concatenate([ch, 1024 + ch, 2048 + ch, np.arange(3072, 3712)])
    wr = np.ascontiguousarray(inp["w_in"][l][:, cols])
    mu = inp["tshift_mu"][l][cols]
    par = np.zeros((128, 32), np.float32)
    for ct in range(5):
        par[:, ct] = mu[ct * 128:(ct + 1) * 128]
    for ct in range(5, 9):
        par[:96, ct] = mu[640 + (ct - 5) * 96:640 + (ct - 4) * 96]
    par[:, 27] = inp["k_k"][l][ch]
    par[:, 28] = inp["w0"][l][0][ch]; par[:, 29] = inp["w0"][l][1][ch]
    par[:, 30] = inp["a0"][l][0][ch]; par[:, 31] = inp["a0"][l][1][ch]
    bones = np.kron(np.eye(2, dtype=np.float32), np.ones((64, 64), np.float32))
    return dict(wr=wr, par=par, gup=np.ascontiguousarray(inp["g_lora_up"][l][:, ch]), wup=np.ascontiguousarray(inp["w_lora_up"][l][:, :, ch]),
                aup=np.ascontiguousarray(inp["a_lora_up"][l][:, :, ch]), bones=bones)
R2_IN = ("p_r", "p_k", "p_v", "kk", "a_f", "a_b", "l_f", "l_b")


def r2_consts():
    p = np.arange(128)
    same = (p[:, None] // 64) == (p[None, :] // 64)
    s = p[:, None] % 64
    t = p[None, :] % 64
    masks = np.zeros((128, 2, 3, 512), np.float32)
    for d in range(2):
        strict = ((s < t) if d == 0 else (s > t)) & same
        incl = ((s <= t) if d == 0 else (s >= t)) & same
        a = np.concatenate([-strict.astype(np.float32), -incl.astype(np.float32)], axis=1)
        masks[:, d, 0] = np.tile(a, (1, 2))
        masks[:, d, 1] = np.tile(-a, (1, 2))
        masks[:, d, 2] = np.tile(-strict.T.astype(np.float32), (1, 4))
    ident4 = np.tile(np.eye(128, dtype=np.float32), (1, 4))
    lvl = np.zeros((128, 6, 512), np.float32)
    for i, sz in enumerate((1, 2, 4, 8, 16, 32)):
        m = ((p[:, None] // (2 * sz)) == (p[None, :] // (2 * sz))) & ((p[:, None] // sz) != (p[None, :] // sz))
        lvl[:, i] = np.tile(m.astype(np.float32), (1, 4))
    m01 = np.ones((128, 512), np.float32)
    m01[:, ::64] = 0.0
    bones = np.kron(np.eye(2, dtype=np.float32), np.ones((64, 64), np.float32))
    return dict(masks=masks, lvl=lvl, ident4=ident4, m01=m01, bones=bones, identb=np.eye(128).astype(ml_dtypes.bfloat16))


def build_r2():
    nc = bass.Bass("TRN2", target_bir_lowering=False)
    di = lambda name, shape, dt=F32: nc.dram_tensor(name, shape, dt, kind="ExternalInput").ap()
    X = {n: di(n, [128, 8192]) for n in R2_IN}
    gd = di("g", [128, 8192], BF16)
    par2d = di("par2", [128, 8])
    masksd = di("masks", [128, 2, 3, 512]); lvld = di("lvl", [128, 6, 512]); ident4d = di("ident4", [128, 512]); m01d = di("m01", [128, 512])
    bonesd = di("bones", [128, 128]); identbd = di("identb", [128, 128], BF16)
    oT = nc.dram_tensor("oT", [128, 8192], BF16, kind="ExternalOutput").ap()
    with contextlib.ExitStack() as st:
        sb = lambda name, shape, dt: st.enter_context(nc.sbuf_tensor(name, shape, dt))
        ps = lambda name, shape, dt: st.enter_context(nc.psum_tensor(name, shape, dt))
        S = Sched(nc)
        par2 = sb("par2s", [128, 8], F32); masks = sb("maskss", [128, 2, 3, 512], F32); lvl = sb("lvls", [128, 6, 512], F32); ident4 = sb("ident4s", [128, 512], F32)
        m01 = sb("m01s", [128, 512], F32); bones = sb("boness", [128, 128], F32); idb = sb("idbs", [128, 128], BF16)
        NB = 2
        inb = [{n: sb("in_%s%d" % (n, i), [128, 512], F32) for n in ("p_r", "p_k", "p_v", "kk", "a", "l")} for i in range(NB)]
        tmp = {n: sb("tmp_" + n, [128, 512], F32) for n in ("f", "kd", "ka", "Lc", "Linc", "w1", "w2")}
        ltot = sb("ltot", [128, 8], F32); wtot = [sb("wtot%d" % i, [128, 8], F32) for i in range(NB)]
        BDn = ("KR", "BB", "KT", "BH", "KH", "VV")
        BD = [{n: sb("bd_%s%d" % (n, i), [128, 8, 256 if n == "KR" else 128], BF16) for n in BDn} for i in range(NB)]
        AB2 = [sb("AB2_%d" % i, [128, 8, 256], BF16) for i in range(NB)]
        AK2 = [sb("AK2_%d" % i, [128, 8, 256], BF16) for i in range(NB)]
        Pb = [sb("Pb%d" % i, [128, 8, 128], BF16) for i in range(2)]
        Qb = [sb("Qb%d" % i, [128, 8, 128], BF16) for i in range(2)]
        MT = [sb("MT%d" % i, [128, 8, 128], BF16) for i in range(NB)]
        Wt = sb("Wt", [128, 8, 128], BF16); Q0b = sb("Q0b", [128, 8, 128], BF16)
        TT = [{n: sb("tt_%s%d" % (n, i), [128, 8, 128], BF16) for n in ("BH", "KH", "VV")} for i in range(NB)]
        T = sb("T", [128, 128], F32); Tb = sb("Tb", [128, 128], BF16)
        Xs = sb("Xs", [128, 128], BF16); Us = sb("Us", [128, 128], BF16)
        y = sb("y", [128, 4096], F32)
        ob = {n: sb("ob_" + n, [128, 512], F32) for n in ("p_r", "p_k", "p_v", "a_f", "a_b", "t1", "t2", "t3")}
        ogb = sb("ogb", [128, 512], BF16); oo = sb("oo", [128, 512], BF16)
        pa = ps("pa", [128, 512], F32); pk = ps("pk", [128, 512], F32)
        pi = [ps("pi%d" % i, [128, 512], F32) for i in range(2)]
        ptr = ps("ptr", [128, 1024], BF16)
        pS = ps("pS", [128, 512], F32)
        pY = [ps("pY%d" % i, [128, 512], F32) for i in range(2)]
        for (t_, d_, nm) in ((par2, par2d, "par2"), (masks, masksd, "masks"), (lvl, lvld, "lvl"), (ident4, ident4d, "ident4"), (m01, m01d, "m01"), (bones, bonesd, "bones"), (idb, identbd, "idb")):
            S.op("sp", lambda e, t_=t_, d_=d_: e.dma_start(out=t_[:], in_=d_), writes=[nm], chan=nm)
        S.op("dve", lambda e: e.tensor_scalar(out=par2[:, 1:2], in0=par2[:, 0:1], scalar1=-1.0, scalar2=1.0, op0=ALU.mult, op1=ALU.add), reads=["par2"], writes=["par2"])
        S.op("dve", lambda e: e.tensor_scalar(out=par2[:, 5:6], in0=par2[:, 0:1], scalar1=-2.0, scalar2=2.0, op0=ALU.mult, op1=ALU.add), reads=["par2"], writes=["par2"])
        for i in range(NB):
            for n in BDn:
                S.op("pool", lambda e, i=i, n=n: e.memset(BD[i][n][:], 0.0), writes=[("bd", i)])

        v3 = lambda ap: ap.rearrange("p (c t) -> p c t", t=64)

        def prep_group(b, d, gi, sl):
            tok = b * 4096 + gi * 512
            I = inb[sl]
            for n in ("p_r", "p_k", "p_v", "kk"):
                S.op("sp", lambda e, n=n: e.dma_start(out=I[n][:], in_=X[n][:, tok:tok + 512]), writes=[("in", sl)], chan=("in", sl, n))
            sfx = "_f" if d == 0 else "_b"
            S.op("sp", lambda e: e.dma_start(out=I["a"][:], in_=X["a" + sfx][:, tok:tok + 512]), writes=[("in", sl)], chan=("in", sl, "a"))
            S.op("sp", lambda e: e.dma_start(out=I["l"][:], in_=X["l" + sfx][:, tok:tok + 512]), writes=[("in", sl)], chan=("in", sl, "l"))
            R = [("in", sl)]
            f, kd, ka, Lc, Linc, w1, w2 = [tmp[n] for n in ("f", "kd", "ka", "Lc", "Linc", "w1", "w2")]
            S.op("dve", lambda e: e.tensor_scalar(out=f[:], in0=I["a"][:], scalar1=par2[:, 0:1], scalar2=par2[:, 1:2], op0=ALU.mult, op1=ALU.add), reads=R + ["par2"], writes=["f"])
            S.op("dve", lambda e: e.tensor_tensor(out=kd[:], in0=I["p_k"][:], in1=f[:], op=ALU.mult), reads=R + ["f"], writes=["kd"])
            S.op("pool", lambda e: e.tensor_tensor(out=ka[:], in0=I["kk"][:], in1=I["a"][:], op=ALU.mult), reads=R, writes=["ka"])
            S.op("dve", lambda e: e.tensor_tensor_scan(out=Lc[:], data0=m01[:], data1=I["l"][:], initial=0.0, op0=ALU.mult, op1=ALU.add), reads=R + ["m01"], writes=["Lc"])
            S.op("dve", lambda e: e.tensor_copy(out=ltot[:], in_=v3(Lc[:])[:, :, 63]), reads=["Lc"], writes=["ltot"])
            lt_b = ltot[:].unsqueeze(2).to_broadcast([128, 8, 64])
            if d == 0:
                LI = Lc
                lres = "Lc"
            else:
                S.op("dve", lambda e: e.tensor_tensor(out=v3(Linc[:]), in0=lt_b, in1=v3(Lc[:]), op=ALU.subtract), reads=["Lc", "ltot"], writes=["Linc"])
                S.op("dve", lambda e: e.tensor_tensor(out=Linc[:], in0=Linc[:], in1=I["l"][:], op=ALU.add), reads=["Linc"] + R, writes=["Linc"])
                LI = Linc
                lres = "Linc"
            S.op("act", lambda e: e.activation(out=wtot[sl][:], in_=ltot[:], func=AF.Exp), reads=["ltot"], writes=[("wtot", sl)])
            Bd = BD[sl]

            def bd_write(name, c0, in0, in1, neg=False):
                for hh in range(2):
                    prt = slice(hh * 64, hh * 64 + 64)
                    o = Bd[name][prt, :, c0 + hh * 64:c0 + hh * 64 + 64]
                    if neg:
                        S.op("dve", lambda e, o=o, prt=prt: e.scalar_tensor_tensor(out=o, in0=v3(in0[prt, :]), scalar=-1.0, in1=v3(in1[prt, :]), op0=ALU.mult, op1=ALU.mult),
                             reads=R + ["ka", "kd", "w1", "w2"], writes=[("bd", sl)])
                    elif in1 is None:
                        S.op("pool", lambda e, o=o, prt=prt: e.tensor_copy(out=o, in_=v3(in0[prt, :])), reads=R, writes=[("bd", sl)])
                    else:
                        S.op("dve", lambda e, o=o, prt=prt: e.tensor_tensor(out=o, in0=v3(in0[prt, :]), in1=v3(in1[prt, :]), op=ALU.mult),
                             reads=R + ["ka", "kd", "w1", "w2"], writes=[("bd", sl)])

            S.op("act", lambda e: e.activation(out=w1[:], in_=LI[:], func=AF.Exp), reads=[lres], writes=["w1"])
            bd_write("KR", 128, I["p_r"], w1)
            S.op("dve", lambda e: e.tensor_tensor(out=w2[:], in0=LI[:], in1=I["l"][:], op=ALU.subtract), reads=[lres] + R, writes=["w2"])
            S.op("act", lambda e: e.activation(out=w2[:], in_=w2[:], func=AF.Exp), reads=["w2"], writes=["w2"])
            bd_write("KR", 0, I["kk"], w2)
            S.op("act", lambda e: e.activation(out=w1[:], in_=LI[:], func=AF.Exp, scale=-1.0), reads=[lres], writes=["w1"])
            bd_write("BB", 0, ka, w1)
            bd_write("KT", 0, kd, w1)
            S.op("dve", lambda e: e.tensor_tensor(out=v3(w2[:]), in0=lt_b, in1=v3(LI[:]), op=ALU.subtract), reads=[lres, "ltot"], writes=["w2"])
            S.op("act", lambda e: e.activation(out=w2[:], in_=w2[:], func=AF.Exp), reads=["w2"], writes=["w2"])
            bd_write("BH", 0, ka, w2, neg=True)
            bd_write("KH", 0, kd, w2)
            bd_write("VV", 0, I["p_v"], None)
            BR = [("bd", sl)]
            for c in range(8):
                o2 = (c % 2) * 256
                S.op("pe", lambda e, c=c, o2=o2: e.matmul(pa[:, o2:o2 + 256], lhsT=Bd["BB"][:, c, :], rhs=Bd["KR"][:, c, :], start=True, stop=True), reads=BR, writes=["pa"])
                S.op("pe", lambda e, c=c, o2=o2: e.matmul(pk[:, o2:o2 + 256], lhsT=Bd["KT"][:, c, :], rhs=Bd["KR"][:, c, :], start=True, stop=True), reads=BR, writes=["pk"])
                if c % 2 == 1:
                    S.op("dve", lambda e, c=c: e.tensor_tensor(out=AB2[sl][:, c - 1:c + 1, :], in0=pa[:].rearrange("p (c x) -> p c x", c=2), in1=masks[:, d, 0, :].rearrange("p (c x) -> p c x", c=2), op=ALU.mult),
                         reads=["pa", "masks"], writes=[("AB2", sl)])
                    S.op("dve", lambda e, c=c: e.tensor_tensor(out=AK2[sl][:, c - 1:c + 1, :], in0=pk[:].rearrange("p (c x) -> p c x", c=2), in1=masks[:, d, 1, :].rearrange("p (c x) -> p c x", c=2), op=ALU.mult),
                         reads=["pk", "masks"], writes=[("AK2", sl)])
            for c in range(8):
                pb = c // 4
                o4 = (c % 4) * 128
                S.op("pe", lambda e, c=c, pb=pb, o4=o4: e.matmul(pi[pb][:, o4:o4 + 128], lhsT=Bd["KR"][:, c, 0:128], rhs=Bd["BB"][:, c, :], start=True, stop=True), reads=BR, writes=[("pi", pb)])
            for pb in range(2):
                S.op("dve", lambda e, pb=pb: e.tensor_tensor(out=Q0b[:, pb * 4:pb * 4 + 4, :], in0=pi[pb][:].rearrange("p (c x) -> p c x", c=4), in1=masks[:, d, 2, :].rearrange("p (c x) -> p c x", c=4), op=ALU.mult),
                     reads=[("pi", pb), "masks"], writes=["Q0b"])
            for n in ("BH", "KH", "VV"):
                for c in range(8):
                    S.op("pe", lambda e, c=c, n=n: e.transpose(out=ptr[:, c * 128:(c + 1) * 128], in_=Bd[n][:, c, :], identity=idb[:]), reads=BR + ["idb"], writes=["ptr"])
                S.op("act", lambda e, n=n: e.copy(out=TT[sl][n][:], in_=ptr[:].rearrange("p (c x) -> p c x", c=8)), reads=["ptr"], writes=[("TT", sl)])
            W = MT[sl]
            NOs, NOTs, T1, T1p = Pb[0], Pb[1], Qb[0], Qb[1]
            c4 = lambda ap: ap.rearrange("p (c x) -> p c x", c=4)

            def masked(dst, dres, src, sres, lev):
                for hf in range(2):
                    S.op("dve", lambda e, hf=hf: e.tensor_tensor(out=dst[:, hf * 4:hf * 4 + 4, :], in0=src[:, hf * 4:hf * 4 + 4, :], in1=c4(lvl[:, lev, :]), op=ALU.mult),
                         reads=[sres, "lvl"], writes=[dres])

            masked(NOs, "NOs", AB2[sl][:, :, 0:128], ("AB2", sl), 0)
            masked(NOTs, "NOTs", Q0b, "Q0b", 0)
            for hf in range(2):
                S.op("dve", lambda e, hf=hf: e.tensor_tensor(out=W[:, hf * 4:hf * 4 + 4, :], in0=NOs[:, hf * 4:hf * 4 + 4, :], in1=c4(ident4[:]), op=ALU.add), reads=["NOs", "ident4"], writes=[("MT", sl)])
                S.op("dve", lambda e, hf=hf: e.tensor_tensor(out=Wt[:, hf * 4:hf * 4 + 4, :], in0=NOTs[:, hf * 4:hf * 4 + 4, :], in1=c4(ident4[:]), op=ALU.add), reads=["NOTs", "ident4"], writes=["Wt"])
            WR = ("MT", sl)

            BK0 = ([pi[0], pi[1]], [("pi", 0), ("pi", 1)])
            BK1 = ([pa, pk], ["pa", "pk"])

            def mm8(lhs, lres, rhs, rres, bk):
                for c in range(8):
                    pb = c // 4
                    o4 = (c % 4) * 128
                    S.op("pe", lambda e, c=c, pb=pb, o4=o4: e.matmul(bk[0][pb][:, o4:o4 + 128], lhsT=lhs[:, c, :], rhs=rhs[:, c, :], start=True, stop=True),
                         reads=[lres, rres], writes=[bk[1][pb]])

            for lev in range(1, 6):
                masked(NOs, "NOs", AB2[sl][:, :, 0:128], ("AB2", sl), lev)
                masked(NOTs, "NOTs", Q0b, "Q0b", lev)
                mm8(NOTs, "NOTs", W, WR, BK0)
                mm8(NOs, "NOs", Wt, "Wt", BK1)
                for pb in range(2):
                    S.op("act", lambda e, pb=pb: e.copy(out=T1[:, pb * 4:pb * 4 + 4, :], in_=c4(BK0[0][pb][:])), reads=[BK0[1][pb]], writes=["T1"])
                for pb in range(2):
                    S.op("dve", lambda e, pb=pb: e.tensor_copy(out=T1p[:, pb * 4:pb * 4 + 4, :], in_=c4(BK1[0][pb][:])), reads=[BK1[1][pb]], writes=["T1p"])
                mm8(Wt, "Wt", T1, "T1", BK0)
                mm8(W, WR, T1p, "T1p", BK1)
                for pb in range(2):
                    S.op("act", lambda e, pb=pb: e.copy(out=T1[:, pb * 4:pb * 4 + 4, :], in_=c4(BK0[0][pb][:])), reads=[BK0[1][pb]], writes=["T1"])
                for pb in range(2):
                    S.op("dve", lambda e, pb=pb: e.tensor_tensor(out=Wt[:, pb * 4:pb * 4 + 4, :], in0=Wt[:, pb * 4:pb * 4 + 4, :], in1=c4(BK1[0][pb][:]), op=ALU.add), reads=[BK1[1][pb], "Wt"], writes=["Wt"])
                for hf in range(2):
                    S.op("dve", lambda e, hf=hf: e.tensor_tensor(out=W[:, hf * 4:hf * 4 + 4, :], in0=W[:, hf * 4:hf * 4 + 4, :], in1=T1[:, hf * 4:hf * 4 + 4, :], op=ALU.add), reads=["T1", WR], writes=[WR])
        def scan_group(b, d, gi, sl):
            Bd = BD[sl]
            order = range(8) if d == 0 else range(7, -1, -1)
            for idx, c in enumerate(order):
                yb = idx // 4
                yo = (idx % 4) * 128
                S.op("pe", lambda e, c=c: e.matmul(pS[:, 0:128], lhsT=Bd["KR"][:, c, 0:128], rhs=Tb[:], start=True, stop=False), reads=[("bd", sl), "Tb"], writes=["pS0"])
                S.op("pe", lambda e, c=c: e.matmul(pS[:, 0:128], lhsT=AK2[sl][:, c, 0:128], rhs=TT[sl]["VV"][:, c, :], start=False, stop=True), reads=[("AK2", sl), ("TT", sl)], writes=["pS0"])
                S.op("act", lambda e: e.copy(out=Xs[:], in_=pS[:, 0:128]), reads=["pS0"], writes=["Xs"])
                S.op("pe", lambda e, c=c: e.matmul(pS[:, 128:256], lhsT=MT[sl][:, c, :], rhs=Xs[:], start=True, stop=True), reads=[("MT", sl), "Xs"], writes=["pS1"])
                S.op("dve", lambda e: e.tensor_copy(out=Us[:], in_=pS[:, 128:256]), reads=["pS1"], writes=["Us"])
                S.op("pe", lambda e, c=c: e.matmul(pS[:, 256:384], lhsT=TT[sl]["BH"][:, c, :], rhs=Us[:], start=True, stop=False), reads=[("TT", sl), "Us"], writes=["pS2"])
                S.op("pe", lambda e, c=c: e.matmul(pS[:, 256:384], lhsT=TT[sl]["KH"][:, c, :], rhs=TT[sl]["VV"][:, c, :], start=False, stop=True), reads=[("TT", sl)], writes=["pS2"])
                S.op("pe", lambda e, c=c, yb=yb, yo=yo: e.matmul(pY[yb][:, yo:yo + 128], lhsT=Tb[:], rhs=Bd["KR"][:, c, 128:256], start=True, stop=False), reads=[("bd", sl), "Tb"], writes=[("pY", yb)])
                S.op("pe", lambda e, c=c, yb=yb, yo=yo: e.matmul(pY[yb][:, yo:yo + 128], lhsT=Us[:], rhs=AB2[sl][:, c, 128:256], start=False, stop=False), reads=[("AB2", sl), "Us"], writes=[("pY", yb)])
                S.op("pe", lambda e, c=c, yb=yb, yo=yo: e.matmul(pY[yb][:, yo:yo + 128], lhsT=TT[sl]["VV"][:, c, :], rhs=AK2[sl][:, c, 128:256], start=False, stop=True), reads=[("AK2", sl), ("TT", sl)], writes=[("pY", yb)])
                S.op("dve", lambda e, c=c: e.scalar_tensor_tensor(out=T[:], in0=T[:], scalar=wtot[sl][:, c:c + 1], in1=pS[:, 256:384], op0=ALU.mult, op1=ALU.add), reads=["pS2", "T", ("wtot", sl)], writes=["T"])
                S.op("act", lambda e: e.copy(out=Tb[:], in_=T[:]), reads=["T"], writes=["Tb"])
                if idx % 4 == 3:
                    cs = sorted(list(order)[idx - 3:idx + 1])
                    c_lo = cs[0]
                    for hh in range(2):
                        prt = slice(hh * 64, hh * 64 + 64)
                        src = pY[yb][prt, :].rearrange("p (c x) -> p c x", c=4)[:, :, hh * 64:hh * 64 + 64]
                        if d == 1:
                            dsts = [(y[prt, gi * 512 + (c_lo + 3 - j) * 64:gi * 512 + (c_lo + 4 - j) * 64], pY[yb][prt, j * 128 + hh * 64:j * 128 + hh * 64 + 64]) for j in range(4)]
                            for (dd, ss) in dsts:
                                S.op("dve", lambda e, dd=dd, ss=ss: e.tensor_tensor(out=dd, in0=dd, in1=ss, op=ALU.add), reads=[("pY", yb), "y"], writes=["y"])
                        else:
                            dd = y[prt, gi * 512 + c_lo * 64:gi * 512 + (c_lo + 4) * 64].rearrange("p (c x) -> p c x", c=4)
                            S.op("act", lambda e, dd=dd, src=src: e.copy(out=dd, in_=src), reads=[("pY", yb)], writes=["y"])

        def out_block(b, blk):
            tok = b * 4096 + blk * 512
            for n in ("p_r", "p_k", "p_v", "a_f", "a_b"):
                S.op("sp", lambda e, n=n: e.dma_start(out=ob[n][:], in_=X[n][:, tok:tok + 512]), writes=[("ob", n)], chan=("ob", n))
            S.op("sp", lambda e: e.dma_start(out=ogb[:], in_=gd[:, tok:tok + 512]), writes=["ogb"], chan="ogb")
            t1, t2, t3 = ob["t1"], ob["t2"], ob["t3"]
            ys = y[:, blk * 512:(blk + 1) * 512]
            S.op("pe", lambda e: e.matmul(pa[:], lhsT=bones[:], rhs=ys, start=True, stop=True), reads=["y", "bones"], writes=["pa"])
            S.op("dve", lambda e: e.scalar_tensor_tensor(out=t1[:], in0=pa[:], scalar=-1.0 / 64, in1=ys, op0=ALU.mult, op1=ALU.add), reads=["pa", "y"], writes=["t1"])
            S.op("act", lambda e: e.activation(out=t2[:], in_=t1[:], func=AF.Square), reads=["t1"], writes=["t2"])
            S.op("pe", lambda e: e.matmul(pk[:], lhsT=bones[:], rhs=t2[:], start=True, stop=True), reads=["t2", "bones"], writes=["pk"])
            S.op("act", lambda e: e.activation(out=t2[:], in_=pk[:], func=AF.Sqrt, scale=1.0 / 64, bias=64e-5), reads=["pk"], writes=["t2"])
            S.op("dve", lambda e: e.reciprocal(out=t2[:], in_=t2[:]), reads=["t2"], writes=["t2"])
            S.op("dve", lambda e: e.tensor_tensor(out=t1[:], in0=t1[:], in1=t2[:], op=ALU.mult), reads=["t1", "t2"], writes=["t1"])
            S.op("dve", lambda e: e.tensor_scalar(out=t1[:], in0=t1[:], scalar1=par2[:, 3:4], scalar2=par2[:, 4:5], op0=ALU.mult, op1=ALU.add), reads=["t1", "par2"], writes=["t1"])
            S.op("pool", lambda e: e.tensor_tensor(out=t2[:], in0=ob["a_f"][:], in1=ob["a_b"][:], op=ALU.add), reads=[("ob", "a_f"), ("ob", "a_b")], writes=["t2"])
            S.op("dve", lambda e: e.tensor_scalar(out=t2[:], in0=t2[:], scalar1=par2[:, 0:1], scalar2=par2[:, 5:6], op0=ALU.mult, op1=ALU.add), reads=["t2", "par2"], writes=["t2"])
            S.op("pool", lambda e: e.tensor_tensor(out=t2[:], in0=t2[:], in1=ob["p_k"][:], op=ALU.mult), reads=["t2", ("ob", "p_k")], writes=["t2"])
            S.op("dve", lambda e: e.scalar_tensor_tensor(out=t3[:], in0=t2[:], scalar=par2[:, 2:3], in1=ob["p_r"][:], op0=ALU.mult, op1=ALU.mult), reads=["t2", "par2", ("ob", "p_r")], writes=["t3"])
            S.op("pe", lambda e: e.matmul(pa[:], lhsT=bones[:], rhs=t3[:], start=True, stop=True), reads=["t3", "bones"], writes=["pa"])
            S.op("dve", lambda e: e.tensor_tensor(out=t3[:], in0=pa[:], in1=ob["p_v"][:], op=ALU.mult), reads=["pa", ("ob", "p_v")], writes=["t3"])
            S.op("pool", lambda e: e.tensor_tensor(out=t3[:], in0=t3[:], in1=t1[:], op=ALU.add), reads=["t3", "t1"], writes=["t3"])
            S.op("dve", lambda e: e.tensor_tensor(out=oo[:], in0=t3[:], in1=ogb[:], op=ALU.mult), reads=["t3", "ogb"], writes=["oo"])
            S.op("sp", lambda e: e.dma_start(out=oT[:, tok:tok + 512], in_=oo[:]), reads=["oo"], chan="oo")

        def collect(fn, *args):
            buf = []
            real = S.op
            S.op = lambda *a, **k: buf.append((a, k))
            try:
                fn(*args)
            finally:
                S.op = real
            return buf

        def emit_interleaved(A, B):
            nA, nB = len(A), len(B)
            ia = 0
            for ib, (a, k) in enumerate(B):
                tgt = (ib * nA) // max(nB, 1)
                while ia < tgt:
                    S.op(*A[ia][0], **A[ia][1])
                    ia += 1
                S.op(*a, **k)
            while ia < nA:
                S.op(*A[ia][0], **A[ia][1])
                ia += 1

        for b in range(2):
            for d in range(2):
                S.op("dve", lambda e: e.memset(T[:], 0.0), writes=["T"])
                S.op("pool", lambda e: e.memset(Tb[:], 0.0), writes=["Tb"])
                gorder = list(range(8)) if d == 0 else list(range(7, -1, -1))
                prep_group(b, d, gorder[0], 0)
                for j, gi in enumerate(gorder):
                    A = collect(prep_group, b, d, gorder[j + 1], (j + 1) % 2) if j + 1 < 8 else []
                    B = collect(scan_group, b, d, gi, j % 2)
                    emit_interleaved(A, B)
            for blk in range(8):
                out_block(b, blk)
        S.run()
    return nc


def r2_inputs(inp, l, c):
    hs = [2 * c, 2 * c + 1]
    ch = np.concatenate([np.arange(h * 64, h * 64 + 64) for h in hs])
    par2 = np.zeros((128, 8), np.float32)
    par2[:, 0] = inp["k_a"][l][ch]
    par2[:, 2] = inp["r_k"][l].reshape(-1)[ch]
    par2[:, 3] = inp["gn_w"][l][ch]
    par2[:, 4] = inp["gn_b"][l][ch]
    return dict(par2=par2)
def build_p0():
    nc = bass.Bass("TRN2", target_bir_lowering=False)
    x = nc.dram_tensor("x", [1024, 2048], F32, kind="ExternalInput").ap()
    g1 = nc.dram_tensor("g1", [128, 16], F32, kind="ExternalInput").ap()
    ident = nc.dram_tensor("ident", [128, 128], F32, kind="ExternalInput").ap()
    hT = nc.dram_tensor("hT", [2048, 1024], F32, kind="ExternalOutput").ap()
    uT = nc.dram_tensor("uT", [2048, 1024], BF16, kind="ExternalOutput").ap()
    with contextlib.ExitStack() as st:
        sb = lambda name, shape, dt: st.enter_context(nc.sbuf_tensor(name, shape, dt))
        ps = lambda name, shape, dt: st.enter_context(nc.psum_tensor(name, shape, dt))
        xt = [sb("xt%d" % i, [128, 2048], F32) for i in range(2)]
        h = sb("h", [128, 16, 1024], F32)
        u = sb("u", [128, 16, 1024], BF16)
        sq = [sb("sq%d" % i, [128, 512], F32) for i in range(2)]
        rs = sb("rs", [128, 512], F32)
        g = sb("g", [128, 16], F32)
        idt = sb("idt", [128, 128], F32)
        ones = sb("ones", [128, 128], F32)
        pt = [ps("pt%d" % i, [128, 512], F32) for i in range(4)]
        pn = ps("pn", [128, 512], F32)
        S = Sched(nc)
        S.op("sp", lambda e: e.dma_start(out=g[:], in_=g1), writes=["g"], chan="g")
        S.op("sp", lambda e: e.dma_start(out=idt[:], in_=ident), writes=["idt"], chan="idt")
        S.op("dve", lambda e: e.memset(ones[:], 1.0), writes=["ones"])
        for tt in range(8):
            b = tt % 2
            S.op("sp", lambda e, tt=tt, b=b: e.dma_start(out=xt[b][:], in_=x[tt * 128:(tt + 1) * 128, :]), writes=[("xt", b)], chan=("xt", b))
            for kg in range(4):
                pb = kg
                for j in range(4):
                    k = kg * 4 + j
                    S.op("pe", lambda e, k=k, j=j, pb=pb, b=b: e.transpose(out=pt[pb][:, j * 128:(j + 1) * 128], in_=xt[b][:, k * 128:(k + 1) * 128], identity=idt[:]),
                         reads=[("xt", b), "idt"], writes=[("pt", pb)])
                if kg % 2:
                    f = lambda e, kg=kg, pb=pb, tt=tt: e.copy(out=h[:, kg * 4:(kg + 1) * 4, tt * 128:(tt + 1) * 128], in_=pt[pb][:].rearrange("p (j t) -> p j t", j=4))
                else:
                    f = lambda e, kg=kg, pb=pb, tt=tt: e.tensor_copy(out=h[:, kg * 4:(kg + 1) * 4, tt * 128:(tt + 1) * 128], in_=pt[pb][:].rearrange("p (j t) -> p j t", j=4))
                S.op("act" if kg % 2 else "dve", f, reads=[("pt", pb)], writes=[("h", tt // 4)])
        for tb in range(2):
            tsl = slice(tb * 512, (tb + 1) * 512)
            emit_rmsnorm(S, nc, h[:, :, tsl], u[:, :, tsl], g, sq, rs, ones, pn, ("h", tb), ("u", tb), "g")
        S.op("sp", lambda e: e.dma_start(out=hT.rearrange("(k p) t -> p k t", p=128), in_=h[:]), reads=[("h", 0), ("h", 1)], chan="oh")
        S.op("sp", lambda e: e.dma_start(out=uT.rearrange("(k p) t -> p k t", p=128), in_=u[:]), reads=[("u", 0), ("u", 1)], chan="ou")
        S.run()
    return nc


_NC_CACHE = {}


def _prog(name, fn):
    if name not in _NC_CACHE:
        _NC_CACHE[name] = fn()
    return _NC_CACHE[name]


def _run(nc, in_maps):
    res = run_bass_kernel_spmd(nc, in_maps, core_ids=list(range(NCORES)))
    return res.results


def kernel(x, norm1_g, w_in, tshift_mu, w0, w_lora_up, a0, a_lora_up, g_lora_up, k_k, k_a, r_k, gn_w, gn_b, rel_bias,
           w_branch_rwkv, w_branch_attn, w_out, norm2_g, w_mlp_in, w_mlp_out, final_g):
    inp = dict(x=x, norm1_g=norm1_g, w_in=w_in, tshift_mu=tshift_mu, w0=w0, w_lora_up=w_lora_up, a0=a0, a_lora_up=a_lora_up,
               g_lora_up=g_lora_up, k_k=k_k, k_a=k_a, r_k=r_k, gn_w=gn_w, gn_b=gn_b, rel_bias=rel_bias, w_branch_rwkv=w_branch_rwkv,
               w_branch_attn=w_branch_attn, w_out=w_out, norm2_g=norm2_g, w_mlp_in=w_mlp_in, w_mlp_out=w_mlp_out, final_g=final_g)
    inp = {k: np.asarray(v, dtype=np.float32) for k, v in inp.items()}
    bf = ml_dtypes.bfloat16
    depth = inp["w_in"].shape[0]
    vec = lambda v: np.ascontiguousarray(v.reshape(16, 128).T)
    xs = inp["x"].reshape(8192, 2048)
    eye = np.eye(128, dtype=np.float32)
    r = _run(_prog("p0", build_p0), [dict(x=np.ascontiguousarray(xs[c * 1024:(c + 1) * 1024]), g1=vec(inp["norm1_g"][0]), ident=eye) for c in range(NCORES)])
    hT = [np.asarray(r[c]["hT"]) for c in range(NCORES)]
    uT = [np.asarray(r[c]["uT"]) for c in range(NCORES)]
    tabs = bias_index_tables()
    consts2 = r2_consts()
    out = None
    for l in range(depth):
        uT_all = np.ascontiguousarray(np.concatenate(uT, axis=1))
        r1 = _run(_prog("r1", build_r1), [dict(r1_inputs(inp, l, c), uT=uT_all) for c in range(NCORES)])
        in2 = []
        for c in range(NCORES):
            m = dict(consts2, **r2_inputs(inp, l, c))
            for n in R2_IN:
                m[n] = np.asarray(r1[c][n])
            m["g"] = np.asarray(r1[c]["g"])
            in2.append(m)
        r2 = _run(_prog("r2", build_r2), in2)
        ina = []
        for c in range(NCORES):
            heads = [g * 8 + c for g in range(3)]
            qc = lambda h: np.arange(3712 + h * 64, 3712 + h * 64 + 64)
            kc = lambda h: np.arange(3712 + 1536 + h * 64, 3712 + 1536 + h * 64 + 64)
            vc = lambda h: np.arange(3712 + 3072 + h * 64, 3712 + 3072 + h * 64 + 64)
            cols = np.concatenate([qc(heads[0]), qc(heads[1]), kc(heads[0]), kc(heads[1]), qc(heads[2]), vc(heads[2]), kc(heads[2]), vc(heads[0]), vc(heads[1])])
            bias = np.stack([np.where(m, inp["rel_bias"][:, heads[g]][idx], np.float32(-30000.0)) for g, (idx, m) in enumerate(tabs)], axis=1).astype(np.float32)
            ina.append(dict(uT=uT_all, wa=np.ascontiguousarray(inp["w_in"][l][:, cols]), biasT=np.ascontiguousarray(bias), identb=np.eye(128).astype(bf)))
        ra = _run(_prog("battn", build_b_attn), ina)
        o_all = np.concatenate([np.asarray(r2[c]["oT"]) for c in range(NCORES)] + [np.asarray(ra[c]["oT"]) for c in range(NCORES)], axis=0)
        last = (l == depth - 1)
        common = dict(wg=np.ascontiguousarray(inp["w_in"][l][:, 8320:]), wbr=inp["w_branch_rwkv"][l], wba=inp["w_branch_attn"][l], wout=inp["w_out"][l],
                      w1=inp["w_mlp_in"][l], w2=inp["w_mlp_out"][l], g2=vec(inp["norm2_g"][l]),
                      gn=vec(inp["final_g"] if last else inp["norm1_g"][l + 1]), ident=eye)
        inc = [dict(common, hT=hT[c], uT=uT[c], oT=np.ascontiguousarray(o_all[:, c * 1024:(c + 1) * 1024])) for c in range(NCORES)]
        rc = _run(_prog("c_last" if last else "c", lambda: build_c(last)), inc)
        if last:
            out = np.concatenate([np.asarray(rc[c]["out"]) for c in range(NCORES)], axis=0)
        else:
            hT = [np.asarray(rc[c]["hTo"]) for c in range(NCORES)]
            uT = [np.asarray(rc[c]["uTo"]) for c in range(NCORES)]
    return out.reshape(inp["x"].shape).astype(np.float32)
```

```python
import contextlib
import numpy as np
import ml_dtypes
import concourse.bass as bass
import concourse.mybir as mybir
from concourse.bass_utils import run_bass_kernel_spmd

F32 = mybir.dt.float32
BF16 = mybir.dt.bfloat16
AF = mybir.ActivationFunctionType
ALU = mybir.AluOpType
NCORES = 8


class Sched:
    ENGS = ("pe", "act", "dve", "pool", "sp")

    def __init__(self, nc):
        self.nc = nc
        self.ops = []
        self.last_w = {}
        self.readers = {}
        self.chan_cnt = {}
        self.chan_order = []
        self.bar = {}

    def op(self, eng, fn, reads=(), writes=(), chan=None, inc=16):
        idx = len(self.ops)
        deps = set()
        for r in reads:
            if r in self.last_w:
                deps.add(self.last_w[r])
        for w in writes:
            if w in self.last_w:
                deps.add(self.last_w[w])
            deps.update(self.readers.get(w, ()))
        if eng in self.bar:
            deps.update(self.bar.pop(eng))
        deps.discard(idx)
        cdeps = []
        odeps = []
        for d in deps:
            o = self.ops[d]
            if o["chan"] is not None:
                cdeps.append((o["chan"], self.chan_cnt[o["chan"]]))
            else:
                odeps.append(d)
        if chan is not None:
            if chan not in self.chan_cnt:
                self.chan_cnt[chan] = 0
                self.chan_order.append(chan)
            self.chan_cnt[chan] += inc
        self.ops.append(dict(eng=eng, fn=fn, odeps=odeps, cdeps=cdeps, chan=chan, waited=False, inc=inc))
        for r in reads:
            self.readers.setdefault(r, []).append(idx)
        for w in writes:
            self.last_w[w] = idx
            self.readers[w] = []
        return idx

    def barrier(self):
        last = {}
        for i, o in enumerate(self.ops):
            last[(o["eng"], o["chan"])] = i
        deps = set(last.values())
        for e in self.ENGS:
            self.bar[e] = set(deps)

    _uid = [0]

    def run(self):
        nc = self.nc
        Sched._uid[0] += 1
        u = "p%d_" % Sched._uid[0]
        ops = self.ops
        for i, o in enumerate(ops):
            for d in o["odeps"]:
                p = ops[d]
                if p["eng"] == "pe" and o["eng"] == "pe":
                    continue
                p["waited"] = True
        cnt = {e: 0 for e in self.ENGS}
        for o in ops:
            if o["chan"] is None and o["waited"]:
                cnt[o["eng"]] += 1
                o["val"] = cnt[o["eng"]]
        import contextlib
        with contextlib.ExitStack() as st:
            esem = {e: st.enter_context(nc.semaphore(u + "s_" + e)) for e in self.ENGS}
            csem = {c: st.enter_context(nc.semaphore(u + "c_%d" % i)) for i, c in enumerate(self.chan_order)}
            block = st.enter_context(nc.Block())
            final_c = {c: self.chan_cnt[c] for c in self.chan_order}

            def emit(ename):
                def body(eng):
                    waited = {}
                    for o in ops:
                        if o["eng"] != ename:
                            continue
                        need = {}
                        for d in o["odeps"]:
                            p = ops[d]
                            if p["eng"] == "pe" and ename == "pe":
                                continue
                            k = ("e", p["eng"])
                            need[k] = max(need.get(k, 0), p["val"])
                        for c, v in o["cdeps"]:
                            k = ("c", c)
                            need[k] = max(need.get(k, 0), v)
                        for k, v in need.items():
                            if waited.get(k, 0) >= v:
                                continue
                            waited[k] = v
                            eng.wait_ge(esem[k[1]] if k[0] == "e" else csem[k[1]], v)
                        ins = o["fn"](eng)
                        if o["chan"] is not None:
                            ins.then_inc(csem[o["chan"]], o["inc"])
                        elif o["waited"]:
                            ins.then_inc(esem[ename], 1)
                    if ename == "sp":
                        for c in self.chan_order:
                            eng.wait_ge(csem[c], final_c[c])
                        for e in self.ENGS:
                            if e != "sp" and cnt[e] > 0:
                                eng.wait_ge(esem[e], cnt[e])
                return body

            block.sync(emit("sp"))
            block.scalar(emit("act"))
            block.vector(emit("dve"))
            block.gpsimd(emit("pool"))
            block.tensor(emit("pe"))
        return cnt, final_c
class WRing:
    def __init__(self, S, st, nc, n=2, name="wr"):
        self.S = S
        self.n = n
        self.slots = [st.enter_context(nc.sbuf_tensor("%s%d" % (name, i), [128, 16, 512], BF16)) for i in range(n)]
        self.stg = [st.enter_context(nc.sbuf_tensor("%sstg%d" % (name, i), [128, 8, 512], F32)) for i in range(3)]
        self.i = 0
        self.j = 0
        self.name = name

    def load(self, W, k0, nk, c0, ncols):
        s = self.i % self.n
        self.i += 1
        slot = self.slots[s]
        src = W[k0 * 128:(k0 + nk) * 128, c0:c0 + ncols].rearrange("(k p) c -> p k c", p=128)
        for ka in range(0, nk, 8):
            kb = min(nk, ka + 8)
            j = self.j % 3
            eng = ("dve", "act")[self.j % 2]
            self.j += 1
            stg = self.stg[j]
            self.S.op("sp", lambda e, ka=ka, kb=kb, stg=stg, src=src: e.dma_start(out=stg[:, 0:kb - ka, 0:ncols], in_=src[:, ka:kb, :]),
                      writes=[(self.name + "stg", j)], chan=(self.name + "stg", j))
            if eng == "act":
                f = lambda e, ka=ka, kb=kb, stg=stg, slot=slot: e.copy(out=slot[:, ka:kb, 0:ncols], in_=stg[:, 0:kb - ka, 0:ncols])
            else:
                f = lambda e, ka=ka, kb=kb, stg=stg, slot=slot: e.tensor_copy(out=slot[:, ka:kb, 0:ncols], in_=stg[:, 0:kb - ka, 0:ncols])
            self.S.op(eng, f, reads=[(self.name + "stg", j)], writes=[(self.name, s)])
        return slot, (self.name, s)


def emit_rmsnorm(S, nc, h, u, g, sq, rs, ones, pn, hres, ures, gres, out_fp32_inplace=False):
    for k in range(16):
        b = k % 2
        S.op("act", lambda e, k=k, b=b: e.activation(out=sq[b][:], in_=h[:, k, :], func=AF.Square),
             reads=[hres], writes=[("sq", b)])
        S.op("pe", lambda e, k=k, b=b: e.matmul(pn[:], lhsT=ones[:], rhs=sq[b][:], start=(k == 0), stop=(k == 15)),
             reads=[("sq", b), "ones"], writes=["pn"])
    S.op("act", lambda e: e.activation(out=rs[:], in_=pn[:], func=AF.Sqrt, scale=1.0 / 2048, bias=1e-6),
         reads=["pn"], writes=["rs"])
    S.op("dve", lambda e: e.reciprocal(out=rs[:], in_=rs[:]), reads=["rs"], writes=["rs"])
    for k in range(16):
        S.op("dve", lambda e, k=k: e.scalar_tensor_tensor(out=u[:, k, :], in0=h[:, k, :], scalar=g[:, k:k + 1], in1=rs[:], op0=ALU.mult, op1=ALU.mult),
             reads=[hres, gres, "rs"], writes=[ures])


def build_c(last):
    nc = bass.Bass("TRN2", target_bir_lowering=False)
    di = lambda name, shape, dt=F32: nc.dram_tensor(name, shape, dt, kind="ExternalInput").ap()
    hT = di("hT", [2048, 1024]); uT = di("uT", [2048, 1024], BF16); oT = di("oT", [1536, 1024], BF16)
    wg = di("wg", [2048, 4096]); wbr = di("wbr", [1024, 2048]); wba = di("wba", [512, 2048]); wout = di("wout", [2048, 2048])
    w1 = di("w1", [2048, 8192]); w2 = di("w2", [8192, 2048]); g2d = di("g2", [128, 16]); gnd = di("gn", [128, 16])
    ident = di("ident", [128, 128])
    if last:
        outd = nc.dram_tensor("out", [1024, 2048], F32, kind="ExternalOutput").ap()
    else:
        hTo = nc.dram_tensor("hTo", [2048, 1024], F32, kind="ExternalOutput").ap()
        uTo = nc.dram_tensor("uTo", [2048, 1024], BF16, kind="ExternalOutput").ap()
    with contextlib.ExitStack() as st:
        sb = lambda name, shape, dt: st.enter_context(nc.sbuf_tensor(name, shape, dt))
        ps = lambda name, shape, dt: st.enter_context(nc.psum_tensor(name, shape, dt))
        S = Sched(nc)
        h = sb("h", [128, 16, 512], F32); u = sb("u", [128, 16, 512], BF16); o = sb("o", [128, 12, 512], BF16)
        mg = sb("mg", [128, 16, 512], BF16); hid = sb("hid", [128, 16, 512], BF16)
        gt = [sb("gt%d" % i, [128, 4, 512], BF16) for i in range(2)]
        t1 = sb("t1", [128, 4, 512], F32); t2 = sb("t2", [128, 512], F32)
        rl = [sb("rl%d" % i, [128, 512], F32) for i in range(2)]
        sq = [sb("sq%d" % i, [128, 512], F32) for i in range(2)]
        rs = sb("rs", [128, 512], F32)
        g2 = sb("g2s", [128, 16], F32); gn = sb("gns", [128, 16], F32)
        ones = sb("ones", [128, 128], F32); idt = sb("idt", [128, 128], F32)
        ring = WRing(S, st, nc, 2)
        pA = [ps("pA%d" % i, [128, 512], F32) for i in range(4)]
        pB = [ps("pB%d" % i, [128, 512], F32) for i in range(3)]
        pn = ps("pn", [128, 512], F32)
        S.op("sp", lambda e: e.dma_start(out=g2[:], in_=g2d), writes=["g2"], chan="g2")
        S.op("sp", lambda e: e.dma_start(out=gn[:], in_=gnd), writes=["gn"], chan="gn")
        S.op("sp", lambda e: e.dma_start(out=idt[:], in_=ident), writes=["idt"], chan="idt")
        S.op("dve", lambda e: e.memset(ones[:], 1.0), writes=["ones"])
        hTr = hT.rearrange("(k p) t -> p k t", p=128); uTr = uT.rearrange("(k p) t -> p k t", p=128)
        oTr = oT.rearrange("(k p) t -> p k t", p=128)

        PBR = [("pB", 0), ("pB", 1), ("pB", 2), "pn"]
        PAR = [("pA", j) for j in range(4)]

        def mm_group(pbanks, pres, slot, sres, nk, rhs, rres, first=True, lastk=True, ncol=4):
            for j in range(ncol):
                for k in range(nk):
                    S.op("pe", lambda e, j=j, k=k: e.matmul(pbanks[j][:], lhsT=slot[:, k, j * 128:(j + 1) * 128], rhs=rhs[:, k, :],
                                                            start=(first and k == 0), stop=(lastk and k == nk - 1)),
                         reads=[sres, rres], writes=[pres[j]])

        for tb in range(2):
            tsl = slice(tb * 512, (tb + 1) * 512)
            S.op("sp", lambda e, tsl=tsl: e.dma_start(out=h[:], in_=hTr[:, :, tsl]), writes=["h"], chan="h")
            S.op("sp", lambda e, tsl=tsl: e.dma_start(out=u[:], in_=uTr[:, :, tsl]), writes=["u"], chan="u")
            S.op("sp", lambda e, tsl=tsl: e.dma_start(out=o[:], in_=oTr[:, :, tsl]), writes=["o"], chan="o")
            for cg in range(4):
                slot, sres = ring.load(wg, 0, 16, cg * 512, 512)
                mm_group(pA, PAR, slot, sres, 16, u, "u")
                for j in range(4):
                    S.op("act", lambda e, j=j: e.activation(out=gt[0][:, j, :], in_=pA[j][:], func=AF.Sigmoid),
                         reads=[("pA", j)], writes=[("gt0", j)])
                slot, sres = ring.load(wbr, 0, 8, cg * 512, 512)
                pBx = [pB[0], pB[1], pB[2], pn]
                mm_group(pBx, PBR, slot, sres, 8, o, "o")
                for j in range(4):
                    S.op("dve", lambda e, j=j, pBx=pBx: e.tensor_tensor(out=t1[:, j, :], in0=pBx[j][:], in1=gt[0][:, j, :], op=ALU.mult),
                         reads=[PBR[j], ("gt0", j)], writes=[("t1", j)])
                slot, sres = ring.load(wg, 0, 16, 2048 + cg * 512, 512)
                mm_group(pA, PAR, slot, sres, 16, u, "u")
                for j in range(4):
                    S.op("act", lambda e, j=j: e.activation(out=gt[1][:, j, :], in_=pA[j][:], func=AF.Sigmoid),
                         reads=[("pA", j)], writes=[("gt1", j)])
                slot, sres = ring.load(wba, 0, 4, cg * 512, 512)
                osub = o[:, 8:12, :]
                mm_group(pBx, PBR, slot, sres, 4, osub, "o")
                for j in range(4):
                    S.op("dve", lambda e, j=j, pBx=pBx: e.tensor_tensor(out=t2[:], in0=pBx[j][:], in1=gt[1][:, j, :], op=ALU.mult),
                         reads=[PBR[j], ("gt1", j)], writes=["t2"])
                    S.op("pool", lambda e, j=j, cg=cg: e.tensor_tensor(out=mg[:, cg * 4 + j, :], in0=t1[:, j, :], in1=t2[:], op=ALU.add),
                         reads=[("t1", j), "t2"], writes=["mg"])
            for cg in range(4):
                slot, sres = ring.load(wout, 0, 16, cg * 512, 512)
                mm_group(pA, PAR, slot, sres, 16, mg, "mg")
                for j in range(4):
                    S.op("dve", lambda e, j=j, cg=cg: e.tensor_tensor(out=h[:, cg * 4 + j, :], in0=h[:, cg * 4 + j, :], in1=pA[j][:], op=ALU.add),
                         reads=[("pA", j), "h"], writes=["h"])
            emit_rmsnorm(S, nc, h, u, g2, sq, rs, ones, pn, "h", "u", "g2")
            for qt in range(4):
                for cg in range(4):
                    slot, sres = ring.load(w1, 0, 16, (qt * 4 + cg) * 512, 512)
                    mm_group(pA, PAR, slot, sres, 16, u, "u")
                    for j in range(4):
                        b = j % 2
                        S.op("act", lambda e, j=j, b=b: e.activation(out=rl[b][:], in_=pA[j][:], func=AF.Relu),
                             reads=[("pA", j)], writes=[("rl", b)])
                        S.op("pool" if j % 2 else "dve", lambda e, j=j, b=b, cg=cg: e.tensor_tensor(out=hid[:, cg * 4 + j, :], in0=rl[b][:], in1=rl[b][:], op=ALU.mult),
                             reads=[("rl", b)], writes=["hid"])
                for og in range(4):
                    pBx = [pB[0], pB[1], pB[2], pn]
                    slot, sres = ring.load(w2, qt * 16, 16, og * 512, 512)
                    mm_group(pBx, PBR, slot, sres, 16, hid, "hid")
                    for j in range(4):
                        S.op("dve", lambda e, j=j, og=og, pBx=pBx: e.tensor_tensor(out=h[:, og * 4 + j, :], in0=h[:, og * 4 + j, :], in1=pBx[j][:], op=ALU.add),
                             reads=[PBR[j], "h"], writes=["h"])
            if not last:
                emit_rmsnorm(S, nc, h, u, gn, sq, rs, ones, pn, "h", "u", "gn")
                S.op("sp", lambda e, tsl=tsl: e.dma_start(out=hTo.rearrange("(k p) t -> p k t", p=128)[:, :, tsl], in_=h[:]), reads=["h"], chan="oh")
                S.op("sp", lambda e, tsl=tsl: e.dma_start(out=uTo.rearrange("(k p) t -> p k t", p=128)[:, :, tsl], in_=u[:]), reads=["u"], chan="ou")
            else:
                for k in range(16):
                    b = k % 2
                    S.op("act", lambda e, k=k, b=b: e.activation(out=sq[b][:], in_=h[:, k, :], func=AF.Square), reads=["h"], writes=[("sq", b)])
                    S.op("pe", lambda e, k=k, b=b: e.matmul(pn[:], lhsT=ones[:], rhs=sq[b][:], start=(k == 0), stop=(k == 15)),
                         reads=[("sq", b), "ones"], writes=["pn"])
                S.op("act", lambda e: e.activation(out=rs[:], in_=pn[:], func=AF.Sqrt, scale=1.0 / 2048, bias=1e-6), reads=["pn"], writes=["rs"])
                S.op("dve", lambda e: e.reciprocal(out=rs[:], in_=rs[:]), reads=["rs"], writes=["rs"])
                for k in range(16):
                    S.op("dve", lambda e, k=k: e.scalar_tensor_tensor(out=h[:, k, :], in0=h[:, k, :], scalar=gn[:, k:k + 1], in1=rs[:], op0=ALU.mult, op1=ALU.mult),
                         reads=["h", "gn", "rs"], writes=["h"])
                otv = t1[:].rearrange("p a b -> p (a b)")
                T1R = [("t1", j) for j in range(4)]
                for tt in range(4):
                    for kg in range(4):
                        for j in range(4):
                            k = kg * 4 + j
                            S.op("pe", lambda e, k=k, j=j, kg=kg, tt=tt: e.transpose(out=pA[kg][:, j * 128:(j + 1) * 128], in_=h[:, k, tt * 128:(tt + 1) * 128], identity=idt[:]),
                                 reads=["h", "idt"], writes=[("pA", kg)])
                        S.op("act" if kg % 2 else "dve",
                             (lambda e, kg=kg: e.copy(out=otv[:, kg * 512:(kg + 1) * 512], in_=pA[kg][:])) if kg % 2 else
                             (lambda e, kg=kg: e.tensor_copy(out=otv[:, kg * 512:(kg + 1) * 512], in_=pA[kg][:])),
                             reads=[("pA", kg)], writes=[("t1", kg)])
                    r0 = tb * 512 + tt * 128
                    S.op("sp", lambda e, r0=r0: e.dma_start(out=outd[r0:r0 + 128, :], in_=otv), reads=T1R, chan="oo")
        S.run()
    return nc
GROUPS = ((128, 1), (512, 4), (2048, 16))
S_LEN = 4096


def t5_bucket_np(rel):
    nb = 16
    max_exact = 8
    ret = np.where(rel > 0, nb, 0)
    n = np.abs(rel)
    nf = np.maximum(n, 1).astype(np.float32)
    large = max_exact + (np.log(nf / np.float32(max_exact)) / np.float32(np.log(1024 / max_exact)) * np.float32(nb - max_exact)).astype(np.int32)
    large = np.minimum(large, nb - 1)
    return ret + np.where(n < max_exact, n, large)


def bias_index_tables():
    kap = np.arange(128)[:, None]
    qi = np.arange(128)[None, :]
    out = []
    for (window, d) in GROUPS:
        da = kap - 64 - qi
        db = kap + 64 - qi
        delta = np.concatenate([da, db], axis=1)
        out.append((t5_bucket_np(delta * d), np.abs(delta) <= 64))
    return out


def emit_attention(S, nc, st, uTb, Wa, biasd, identd, oT_dst, sbufs):
    sb, ps = sbufs["sb"], sbufs["ps"]
    wa, ub, Qz0, Qz1, Qz2, K01, K2, V01, V2, Vt, E, Pf, Pm, accn, accd, onesk, idb, ot = [sbufs[k] for k in
        ("wa", "ub", "Qz0", "Qz1", "Qz2", "K01", "K2", "V01", "V2", "Vt", "E", "Pf", "Pm", "accn", "accd", "onesk", "idb", "ot")]
    pj, pS, pOD, pT = sbufs["pj"], sbufs["pS"], sbufs["pOD"], sbufs["pT"]
    uTr = uTb.rearrange("(k p) t -> p k t", p=128)
    for tblk in range(8):
        b = tblk % 2
        S.op("sp", lambda e, tblk=tblk, b=b: e.dma_start(out=ub[b][:], in_=uTr[:, :, tblk * 512:(tblk + 1) * 512]), writes=[("ub", b)], chan=("ub", b))
        for ct in range(5):
            M = 64 if ct == 3 else 128
            c0 = ct * 128 if ct < 4 else 448
            pb = (tblk * 5 + ct) % 2
            for k in range(16):
                S.op("pe", lambda e, k=k, M=M, c0=c0, pb=pb, b=b: e.matmul(pj[pb][0:M, :], lhsT=wa[:, k, c0:c0 + M], rhs=ub[b][:, k, :], start=(k == 0), stop=(k == 15)),
                     reads=[("ub", b), "wa"], writes=[("pj", pb)])
            t0 = tblk * 512
            if ct == 0:
                S.op("act", lambda e, pb=pb, t0=t0: e.copy(out=Qz0[0:64, t0:t0 + 512], in_=pj[pb][0:64, :]), reads=[("pj", pb)], writes=["Qz0"])
                S.op("dve", lambda e, pb=pb, t0=t0: e.tensor_copy(out=Qz1[64:128, :].rearrange("p (r i) -> p r i", r=4)[:, :, t0 // 4:t0 // 4 + 128],
                                                                   in_=pj[pb][64:128, :].rearrange("p (i r) -> p r i", r=4)), reads=[("pj", pb)], writes=["Qz1"])
            elif ct == 1:
                S.op("act", lambda e, pb=pb, t0=t0: e.copy(out=K01[0:64, 64 + t0:64 + t0 + 512], in_=pj[pb][0:64, :]), reads=[("pj", pb)], writes=["K01"])
                S.op("dve", lambda e, pb=pb, t0=t0: e.tensor_copy(out=K01[64:128, :].rearrange("p (r i) -> p r i", r=4)[:, :, 64 + t0 // 4:64 + t0 // 4 + 128],
                                                                   in_=pj[pb][64:128, :].rearrange("p (i r) -> p r i", r=4)), reads=[("pj", pb)], writes=["K01"])
            elif ct == 2:
                S.op("act", lambda e, pb=pb, t0=t0: e.copy(out=Qz2[0:64, :].rearrange("p (r i) -> p r i", r=16)[:, :, t0 // 16:t0 // 16 + 32],
                                                            in_=pj[pb][0:64, :].rearrange("p (i r) -> p r i", r=16)), reads=[("pj", pb)], writes=["Qz2"])
                S.op("dve", lambda e, pb=pb, t0=t0: e.tensor_copy(out=V2[64:128, :].rearrange("p (r i) -> p r i", r=16)[:, :, 64 + t0 // 16:64 + t0 // 16 + 32],
                                                                   in_=pj[pb][64:128, :].rearrange("p (i r) -> p r i", r=16)), reads=[("pj", pb)], writes=["V2"])
            elif ct == 3:
                S.op("act", lambda e, pb=pb, t0=t0: e.copy(out=K2[0:64, :].rearrange("p (r i) -> p r i", r=16)[:, :, 64 + t0 // 16:64 + t0 // 16 + 32],
                                                            in_=pj[pb][0:64, :].rearrange("p (i r) -> p r i", r=16)), reads=[("pj", pb)], writes=["K2"])
            else:
                S.op("act", lambda e, pb=pb, t0=t0: e.copy(out=V01[0:64, 64 + t0:64 + t0 + 512], in_=pj[pb][0:64, :]), reads=[("pj", pb)], writes=["V01"])
                S.op("dve", lambda e, pb=pb, t0=t0: e.tensor_copy(out=V01[64:128, :].rearrange("p (r i) -> p r i", r=4)[:, :, 64 + t0 // 4:64 + t0 // 4 + 128],
                                                                   in_=pj[pb][64:128, :].rearrange("p (i r) -> p r i", r=4)), reads=[("pj", pb)], writes=["V01"])
    vsrc = [V01[:, m * 128:(m + 1) * 128] for m in range(36)] + [V2[:, m * 128:(m + 1) * 128] for m in range(48)]
    for t0 in range(0, 84, 8):
        n = min(8, 84 - t0)
        pb = (t0 // 8) % 2
        for j in range(n):
            S.op("pe", lambda e, src=vsrc[t0 + j], j=j, pb=pb: e.transpose(out=pT[pb][:, j * 128:(j + 1) * 128], in_=src, identity=idb[:]),
                 reads=["V01", "V2", "idb"], writes=[("pT", pb)])
        S.op("dve" if pb else "act",
             (lambda e, t0=t0, n=n, pb=pb: e.tensor_copy(out=Vt[:, t0:t0 + n, :], in_=pT[pb][:, 0:n * 128].rearrange("p (j c) -> p j c", c=128))) if pb else
             (lambda e, t0=t0, n=n, pb=pb: e.copy(out=Vt[:, t0:t0 + n, :], in_=pT[pb][:, 0:n * 128].rearrange("p (j c) -> p j c", c=128))),
             reads=[("pT", pb)], writes=["Vt"])
    Qsrc = [lambda r: Qz0[:, :], lambda r: Qz1[:, :].rearrange("p (r i) -> p r i", r=4)[:, r, :],
            lambda r: Qz2[:, :].rearrange("p (r i) -> p r i", r=16)[:, r, :]]
    Ksrc = [lambda r: K01[:, 0:4224], lambda r: K01[:, :].rearrange("p (r i) -> p r i", r=4)[:, r, :],
            lambda r: K2[:, :].rearrange("p (r i) -> p r i", r=16)[:, r, :]]
    vt_of = [lambda r, m: (m, 0), lambda r, m: (r * 9 + m, 64), lambda r, m: (36 + r * 3 + m, 64)]
    qb = 0
    for g, (window, d) in enumerate(GROUPS):
        L = S_LEN // d
        nt = L // 128 + 1
        nq = L // 128
        for r in range(d):
            q_ap, k_ap = Qsrc[g](r), Ksrc[g](r)
            for m in range(nq):
                sbk = qb % 2
                S.op("pe", lambda e, k_ap=k_ap, q_ap=q_ap, m=m, sbk=sbk: e.matmul(pS[sbk][:, 0:128], lhsT=k_ap[:, m * 128:m * 128 + 128], rhs=q_ap[:, m * 128:m * 128 + 128], start=True, stop=True),
                     reads=["Qz0", "Qz1", "Qz2", "K01", "K2"], writes=[("pS", sbk)])
                S.op("pe", lambda e, k_ap=k_ap, q_ap=q_ap, m=m, sbk=sbk: e.matmul(pS[sbk][:, 128:256], lhsT=k_ap[:, m * 128 + 128:m * 128 + 256], rhs=q_ap[:, m * 128:m * 128 + 128], start=True, stop=True),
                     reads=["Qz0", "Qz1", "Qz2", "K01", "K2"], writes=[("pS", sbk)])
                S.op("act", lambda e, sbk=sbk: e.activation(out=Pf[sbk][:], in_=pS[sbk][:, 0:256], func=AF.Exp, scale=0.125), reads=[("pS", sbk)], writes=[("Pf", sbk)])
                S.op("dve", lambda e, sbk=sbk, g=g: e.tensor_tensor(out=Pm[sbk][:], in0=Pf[sbk][:], in1=E[:, g, :], op=ALU.mult), reads=[("Pf", sbk), "E"], writes=[("Pm", sbk)])
                half = m % 2
                ob = (qb // 2) % 2
                for (pp, pname, lh, coff) in ((pOD, "pOD", None, 0), (pOD, "pOD", "ones", 256)):
                    for kt in range(2):
                        if lh is None:
                            tix, c0v = vt_of[g](r, m + kt)
                            lhs = Vt[:, tix, c0v:c0v + 64]
                        else:
                            var = 1 if (m == 0 and kt == 0) else (2 if (m == nq - 1 and kt == 1) else 0)
                            lhs = onesk[:, var, :]
                        S.op("pe", lambda e, pp=pp, lhs=lhs, kt=kt, sbk=sbk, ob=ob, half=half, coff=coff: e.matmul(pp[ob][0:64, coff + half * 128:coff + (half + 1) * 128], lhsT=lhs, rhs=Pm[sbk][:, kt * 128:(kt + 1) * 128], start=(kt == 0), stop=(kt == 1)),
                             reads=[("Pm", sbk), "Vt", "onesk"], writes=[(pname, ob)])
                if half == 1:
                    m0 = m - 1
                    dst = lambda acc, d=d, r=r, m0=m0: acc[:, :].rearrange("p (i r) -> p r i", r=d)[:, r, m0 * 128:m0 * 128 + 256]
                    if g == 0:
                        S.op("act", lambda e, ob=ob, dst=dst: e.copy(out=dst(accn), in_=pOD[ob][0:64, 0:256]), reads=[("pOD", ob)], writes=["accn"])
                        S.op("act", lambda e, ob=ob, dst=dst: e.copy(out=dst(accd), in_=pOD[ob][0:64, 256:512]), reads=[("pOD", ob)], writes=["accd"])
                    else:
                        S.op("dve", lambda e, ob=ob, dst=dst: e.tensor_tensor(out=dst(accn), in0=dst(accn), in1=pOD[ob][0:64, 0:256], op=ALU.add), reads=[("pOD", ob), "accn"], writes=["accn"])
                        S.op("dve", lambda e, ob=ob, dst=dst: e.tensor_tensor(out=dst(accd), in0=dst(accd), in1=pOD[ob][0:64, 256:512], op=ALU.add), reads=[("pOD", ob), "accd"], writes=["accd"])
                qb += 1
    S.op("dve", lambda e: e.reciprocal(out=accd[:], in_=accd[:]), reads=["accd"], writes=["accd"])
    S.op("dve", lambda e: e.tensor_tensor(out=ot[:], in0=accn[:], in1=accd[:], op=ALU.mult), reads=["accn", "accd"], writes=["ot"])
    S.op("sp", lambda e: e.dma_start(out=oT_dst, in_=ot[:]), reads=["ot"], chan="oattn")


def alloc_attention(nc, st):
    sb = lambda name, shape, dt: st.enter_context(nc.sbuf_tensor(name, shape, dt))
    ps = lambda name, shape, dt: st.enter_context(nc.psum_tensor(name, shape, dt))
    d = dict(sb=sb, ps=ps)
    d["wa"] = sb("wa_sb", [128, 16, 576], BF16)
    d["ub"] = [sb("ub%d" % i, [128, 16, 512], BF16) for i in range(2)]
    d["Qz0"] = sb("Qz0", [128, 4096], BF16)
    d["Qz1"] = sb("Qz1", [128, 4096], BF16)
    d["Qz2"] = sb("Qz2", [128, 4096], BF16)
    d["K01"] = sb("K01", [128, 4608], BF16)
    d["K2"] = sb("K2", [128, 6144], BF16)
    d["V01"] = sb("V01", [128, 4608], BF16)
    d["V2"] = sb("V2", [128, 6144], BF16)
    d["Vt"] = sb("Vt", [128, 84, 128], BF16)
    d["E"] = sb("E", [128, 3, 256], F32)
    d["Pf"] = [sb("Pf%d" % i, [128, 256], F32) for i in range(2)]
    d["Pm"] = [sb("Pm%d" % i, [128, 256], BF16) for i in range(2)]
    d["accn"] = sb("accn", [64, 4096], F32)
    d["accd"] = sb("accd", [64, 4096], F32)
    d["onesk"] = sb("onesk", [128, 3, 64], BF16)
    d["idb"] = sb("idb", [128, 128], BF16)
    d["ot"] = sb("ot", [64, 4096], BF16)
    d["pj"] = [ps("pj%d" % i, [128, 512], F32) for i in range(2)]
    d["pS"] = [ps("pS%d" % i, [128, 512], F32) for i in range(2)]
    d["pOD"] = [ps("pOD%d" % i, [128, 512], F32) for i in range(2)]
    d["pT"] = [ps("pT%d" % i, [128, 1024], BF16) for i in range(2)]
    return d


def build_b_attn():
    nc = bass.Bass("TRN2", target_bir_lowering=False)
    di = lambda name, shape, dt=F32: nc.dram_tensor(name, shape, dt, kind="ExternalInput").ap()
    uT = di("uT", [2048, 8192], BF16)
    wad = di("wa", [2048, 576])
    biasd = di("biasT", [128, 3, 256])
    identd = di("identb", [128, 128], BF16)
    oT = nc.dram_tensor("oT", [64, 8192], BF16, kind="ExternalOutput").ap()
    with contextlib.ExitStack() as st:
        S = Sched(nc)
        A = alloc_attention(nc, st)
        emit_attn_setup(S, nc, A, wad, biasd, identd)
        for b in range(2):
            emit_attention(S, nc, st, uT[:, b * 4096:(b + 1) * 4096], wad, biasd, identd, oT[:, b * 4096:(b + 1) * 4096], A)
        S.run()
    return nc


def emit_attn_setup(S, nc, A, wad, biasd, identd):
    wa, E, onesk, idb = A["wa"], A["E"], A["onesk"], A["idb"]
    src = wad.rearrange("(k p) c -> p k c", p=128)
    for ka in range(0, 16, 4):
        S.op("pool", lambda e, ka=ka: e.dma_start(out=wa[:, ka:ka + 4, :], in_=src[:, ka:ka + 4, :]), writes=["wa"], chan="wa")
    S.op("sp", lambda e: e.dma_start(out=E[:], in_=biasd), writes=["E"], chan="E")
    S.op("sp", lambda e: e.dma_start(out=idb[:], in_=identd), writes=["idb"], chan="idb")
    S.op("act", lambda e: e.activation(out=E[:], in_=E[:], func=AF.Exp), reads=["E"], writes=["E"])
    S.op("dve", lambda e: e.memset(onesk[:], 1.0), writes=["onesk"])
    S.op("dve", lambda e: e.memset(onesk[0:64, 1, :], 0.0), writes=["onesk"])
    S.op("dve", lambda e: e.memset(onesk[64:128, 2, :], 0.0), writes=["onesk"])
    for name in ("K01", "V01", "K2", "V2", "Qz0", "Qz1", "Qz2"):
        S.op("pool", lambda e, name=name: e.memset(A[name][:], 0.0), writes=[name])
R1_OUT = ("p_r", "p_k", "p_v", "kk", "a_f", "a_b", "l_f", "l_b")


def build_r1(nc=None, merged=False):
    if nc is None:
        nc = bass.Bass("TRN2", target_bir_lowering=False)
    di = lambda name, shape, dt=F32: nc.dram_tensor(name, shape, dt, kind="ExternalInput").ap()
    uT = di("uT", [2048, 8192], BF16)
    wrd = di("wr", [2048, 1024])
    pard = di("par", [128, 32])
    gupd = di("gup", [256, 128]); wupd = di("wup", [2, 96, 128]); aupd = di("aup", [2, 96, 128])
    bonesd = di("bones", [128, 128])
    if merged:
        outs = {n: nc.dram_tensor("scr_" + n, [128, 8192], F32).ap() for n in R1_OUT}
        g_out = nc.dram_tensor("scr_g", [128, 8192], BF16).ap()
    else:
        outs = {n: nc.dram_tensor(n, [128, 8192], F32, kind="ExternalOutput").ap() for n in R1_OUT}
        g_out = nc.dram_tensor("g", [128, 8192], BF16, kind="ExternalOutput").ap()
    with contextlib.ExitStack() as st:
        sb = lambda name, shape, dt: st.enter_context(nc.sbuf_tensor(name, shape, dt))
        ps = lambda name, shape, dt: st.enter_context(nc.psum_tensor(name, shape, dt))
        S = Sched(nc)
        wr = sb("wr_sb", [128, 16, 1024], BF16)
        ub = [sb("ub%d" % i, [128, 16, 512], BF16) for i in range(2)]
        raw = [sb("raw%d" % i, [128, 4098], BF16) for i in range(9)]
        par = sb("par_sb", [128, 32], F32)
        gup = sb("gup_sb", [128, 2, 128], BF16); wup = sb("wup_sb", [96, 2, 128], BF16); aup = sb("aup_sb", [96, 2, 128], BF16)
        bones = sb("bones_sb", [128, 128], F32)
        t1 = sb("t1", [128, 4096], F32); t2 = sb("t2", [128, 4096], F32); t3 = sb("t3", [128, 4096], F32)
        nl = sb("nl", [128, 2, 4096], BF16)
        pj = [ps("pj%d" % i, [128, 512], F32) for i in range(4)]
        pq = [ps("pq%d" % i, [128, 512], F32) for i in range(4)]
        src = wrd.rearrange("(k p) c -> p k c", p=128)
        for ka in range(0, 16, 2):
            S.op("pool", lambda e, ka=ka: e.dma_start(out=wr[:, ka:ka + 2, :], in_=src[:, ka:ka + 2, :]), writes=["wr"], chan="wr")
        S.op("sp", lambda e: e.dma_start(out=par[:], in_=pard), writes=["par"], chan="par")
        S.op("sp", lambda e: e.dma_start(out=bones[:], in_=bonesd), writes=["bones"], chan="bones")
        S.op("pool", lambda e: e.dma_start(out=gup[:], in_=gupd.rearrange("(k p) c -> p k c", p=128)), writes=["gup"], chan="gup")
        S.op("pool", lambda e: e.dma_start(out=wup[:], in_=wupd.rearrange("d p c -> p d c")), writes=["wup"], chan="wup")
        S.op("pool", lambda e: e.dma_start(out=aup[:], in_=aupd.rearrange("d p c -> p d c")), writes=["aup"], chan="aup")
        S.op("dve", lambda e: e.tensor_scalar(out=par[:, 9:18], in0=par[:, 0:9], scalar1=-1.0, scalar2=1.0, op0=ALU.mult, op1=ALU.add), reads=["par"], writes=["par"])
        S.op("dve", lambda e: e.tensor_scalar(out=par[:, 18:27], in0=par[:, 0:9], scalar1=0.5, scalar2=None, op0=ALU.mult), reads=["par"], writes=["par"])
        for i in range(9):
            S.op("pool", lambda e, i=i: e.memset(raw[i][:], 0.0), writes=[("raw", i)])
        rows = [128] * 5 + [96] * 4
        uTr = uT.rearrange("(k p) t -> p k t", p=128)
        for b in range(2):
            tok0 = b * 4096
            for tblk in range(8):
                bb = tblk % 2
                S.op("sp", lambda e, tblk=tblk, bb=bb, tok0=tok0: e.dma_start(out=ub[bb][:], in_=uTr[:, :, tok0 + tblk * 512:tok0 + (tblk + 1) * 512]), writes=[("ub", bb)], chan=("ub", bb))
                for ct in range(9):
                    M = rows[ct]
                    c0 = ct * 128 if ct < 5 else 640 + (ct - 5) * 96
                    pb = ct % 4
                    for k in range(16):
                        S.op("pe", lambda e, k=k, M=M, c0=c0, pb=pb, bb=bb: e.matmul(pj[pb][0:M, :], lhsT=wr[:, k, c0:c0 + M], rhs=ub[bb][:, k, :], start=(k == 0), stop=(k == 15)),
                             reads=[("ub", bb), "wr"], writes=[("pj", pb)])
                    S.op("act" if ct % 2 else "dve",
                         (lambda e, ct=ct, M=M, pb=pb, tblk=tblk: e.copy(out=raw[ct][0:M, 1 + tblk * 512:1 + (tblk + 1) * 512], in_=pj[pb][0:M, :])) if ct % 2 else
                         (lambda e, ct=ct, M=M, pb=pb, tblk=tblk: e.tensor_copy(out=raw[ct][0:M, 1 + tblk * 512:1 + (tblk + 1) * 512], in_=pj[pb][0:M, :])),
                         reads=[("pj", pb)], writes=[("raw", ct)])
            tsl = slice(tok0, tok0 + 4096)

            def shift(ct, dst):
                M = rows[ct]
                S.op("dve", lambda e: e.tensor_tensor(out=t1[0:M, :], in0=raw[ct][0:M, 0:4096], in1=raw[ct][0:M, 2:4098], op=ALU.add), reads=[("raw", ct)], writes=["t1"])
                S.op("act", lambda e: e.activation(out=t2[0:M, :], in_=raw[ct][0:M, 1:4097], func=AF.Copy, scale=par[0:M, 9 + ct:10 + ct]), reads=[("raw", ct), "par"], writes=["t2"])
                S.op("dve", lambda e: e.scalar_tensor_tensor(out=dst[0:M, :], in0=t1[0:M, :], scalar=par[0:M, 18 + ct:19 + ct], in1=t2[0:M, :], op0=ALU.mult, op1=ALU.add),
                     reads=["t1", "t2", "par"], writes=["t3"])

            def store(name, srct, res, tsl=tsl):
                S.op("sp", lambda e: e.dma_start(out=outs[name][:, tsl], in_=srct[:]), reads=[res], chan="o_" + name)

            shift(0, t3); store("p_r", t3, "t3")
            shift(2, t3); store("p_v", t3, "t3")
            shift(1, t3); store("p_k", t3, "t3")
            S.op("dve", lambda e: e.tensor_scalar(out=t1[:], in0=t3[:], scalar1=par[:, 27:28], scalar2=None, op0=ALU.mult), reads=["t3", "par"], writes=["t1"])
            S.op("act", lambda e: e.activation(out=t2[:], in_=t1[:], func=AF.Square), reads=["t1"], writes=["t2"])
            for blk in range(8):
                pb = blk % 4
                bs = slice(blk * 512, (blk + 1) * 512)
                S.op("pe", lambda e, pb=pb, bs=bs: e.matmul(pq[pb][:], lhsT=bones[:], rhs=t2[:, bs], start=True, stop=True), reads=["t2", "bones"], writes=[("pq", pb)])
                S.op("act", lambda e, pb=pb, bs=bs: e.activation(out=t3[:, bs], in_=pq[pb][:], func=AF.Sqrt), reads=[("pq", pb)], writes=["t3"])
            S.op("dve", lambda e: e.tensor_scalar(out=t3[:], in0=t3[:], scalar1=1e-12, scalar2=None, op0=ALU.max), reads=["t3"], writes=["t3"])
            S.op("dve", lambda e: e.reciprocal(out=t3[:], in_=t3[:]), reads=["t3"], writes=["t3"])
            S.op("dve", lambda e: e.tensor_tensor(out=t3[:], in0=t3[:], in1=t1[:], op=ALU.mult), reads=["t3", "t1"], writes=["t3"])
            store("kk", t3, "t3")
            for j in range(2):
                shift(3 + j, t3)
                S.op("act", lambda e, j=j: e.activation(out=nl[:, j, :], in_=t3[:], func=AF.Sigmoid), reads=["t3"], writes=["nl"])
            for blk in range(8):
                pb = blk % 4
                bs = slice(blk * 512, (blk + 1) * 512)
                for j in range(2):
                    S.op("pe", lambda e, pb=pb, bs=bs, j=j: e.matmul(pq[pb][:], lhsT=gup[:, j, :], rhs=nl[:, j, bs], start=(j == 0), stop=(j == 1)), reads=["nl", "gup"], writes=[("pq", pb)])
                S.op("act", lambda e, pb=pb, blk=blk: e.copy(out=raw[3][:, 1 + blk * 512:1 + (blk + 1) * 512], in_=pq[pb][:]), reads=[("pq", pb)], writes=[("raw", 3)])
            S.op("sp", lambda e, tsl=tsl: e.dma_start(out=g_out[:, tsl], in_=raw[3][:, 1:4097]), reads=[("raw", 3)], chan="o_g")
            for d in range(2):
                shift(5 + d, t3)
                S.op("act", lambda e: e.activation(out=nl[0:96, 0, :], in_=t3[0:96, :], func=AF.Tanh), reads=["t3"], writes=["nl"])
                for blk in range(8):
                    pb = blk % 4
                    bs = slice(blk * 512, (blk + 1) * 512)
                    S.op("pe", lambda e, pb=pb, bs=bs, d=d: e.matmul(pq[pb][:], lhsT=wup[:, d, :], rhs=nl[0:96, 0, bs], start=True, stop=True), reads=["nl", "wup"], writes=[("pq", pb)])
                    S.op("act", lambda e, pb=pb, bs=bs, d=d: e.activation(out=t1[:, bs], in_=pq[pb][:], func=AF.Sigmoid, bias=par[:, 28 + d:29 + d]), reads=[("pq", pb), "par"], writes=["t1"])
                S.op("dve", lambda e: e.tensor_scalar(out=t1[:], in0=t1[:], scalar1=-0.6065306597126334, scalar2=None, op0=ALU.mult), reads=["t1"], writes=["t1"])
                store("l_f" if d == 0 else "l_b", t1, "t1")
                shift(7 + d, t3)
                S.op("act", lambda e: e.copy(out=nl[0:96, 1, :], in_=t3[0:96, :]), reads=["t3"], writes=["nl"])
                for blk in range(8):
                    pb = blk % 4
                    bs = slice(blk * 512, (blk + 1) * 512)
                    S.op("pe", lambda e, pb=pb, bs=bs, d=d: e.matmul(pq[pb][:], lhsT=aup[:, d, :], rhs=nl[0:96, 1, bs], start=True, stop=True), reads=["nl", "aup"], writes=[("pq", pb)])
                    S.op("act", lambda e, pb=pb, bs=bs, d=d: e.activation(out=t2[:, bs], in_=pq[pb][:], func=AF.Sigmoid, bias=par[:, 30 + d:31 + d]), reads=[("pq", pb), "par"], writes=["t2"])
                store("a_f" if d == 0 else "a_b", t2, "t2")
        S.run()
    if merged:
        return nc, outs, g_out
    return nc


def r1_inputs(inp, l, c):
    hs = [2 * c, 2 * c + 1]
    ch = np.concatenate([np.arange(h * 64, h * 64 + 64) for h in hs])
    cols = np.concatenate([ch, 1024 + ch, 2048 + ch, np.arange(3072, 3712)])
    wr = np.ascontiguousarray(inp["w_in"][l][:, cols])
    mu = inp["tshift_mu"][l][cols]
    par = np.zeros((128, 32), np.float32)
    for ct in range(5):
        par[:, ct] = mu[ct * 128:(ct + 1) * 128]
    for ct in range(5, 9):
        par[:96, ct] = mu[640 + (ct - 5) * 96:640 + (ct - 4) * 96]
    par[:, 27] = inp["k_k"][l][ch]
    par[:, 28] = inp["w0"][l][0][ch]; par[:, 29] = inp["w0"][l][1][ch]
    par[:, 30] = inp["a0"][l][0][ch]; par[:, 31] = inp["a0"][l][1][ch]
    bones = np.kron(np.eye(2, dtype=np.float32), np.ones((64, 64), np.float32))
    return dict(wr=wr, par=par, gup=np.ascontiguousarray(inp["g_lora_up"][l][:, ch]), wup=np.ascontiguousarray(inp["w_lora_up"][l][:, :, ch]),
                aup=np.ascontiguousarray(inp["a_lora_up"][l][:, :, ch]), bones=bones)
R2_IN = ("p_r", "p_k", "p_v", "kk", "a_f", "a_b", "l_f", "l_b")


def r2_consts():
    p = np.arange(128)
    same = (p[:, None] // 64) == (p[None, :] // 64)
    s = p[:, None] % 64
    t = p[None, :] % 64
    masks = np.zeros((128, 2, 3, 512), np.float32)
    for d in range(2):
        strict = ((s < t) if d == 0 else (s > t)) & same
        incl = ((s <= t) if d == 0 else (s >= t)) & same
        a = np.concatenate([-strict.astype(np.float32), -incl.astype(np.float32)], axis=1)
        masks[:, d, 0] = np.tile(a, (1, 2))
        masks[:, d, 1] = np.tile(-a, (1, 2))
        masks[:, d, 2] = np.tile(-strict.T.astype(np.float32), (1, 4))
    ident4 = np.tile(np.eye(128, dtype=np.float32), (1, 4))
    lvl = np.zeros((128, 6, 512), np.float32)
    for i, sz in enumerate((1, 2, 4, 8, 16, 32)):
        m = ((p[:, None] // (2 * sz)) == (p[None, :] // (2 * sz))) & ((p[:, None] // sz) != (p[None, :] // sz))
        lvl[:, i] = np.tile(m.astype(np.float32), (1, 4))
    m01 = np.ones((128, 512), np.float32)
    m01[:, ::64] = 0.0
    bones = np.kron(np.eye(2, dtype=np.float32), np.ones((64, 64), np.float32))
    return dict(masks=masks, lvl=lvl, ident4=ident4, m01=m01, bones=bones, identb=np.eye(128).astype(ml_dtypes.bfloat16))


def build_r2(nc=None, X=None, gd=None):
    merged = nc is not None
    if nc is None:
        nc = bass.Bass("TRN2", target_bir_lowering=False)
    di = lambda name, shape, dt=F32: nc.dram_tensor(name, shape, dt, kind="ExternalInput").ap()
    if not merged:
        X = {n: di(n, [128, 8192]) for n in R2_IN}
        gd = di("g", [128, 8192], BF16)
    par2d = di("par2", [128, 8])
    masksd = di("masks", [128, 2, 3, 512]); lvld = di("lvl", [128, 6, 512]); ident4d = di("ident4", [128, 512]); m01d = di("m01", [128, 512])
    bonesd = di("bones2" if merged else "bones", [128, 128]); identbd = di("identb", [128, 128], BF16)
    oT = nc.dram_tensor("oT", [128, 8192], BF16, kind="ExternalOutput").ap()
    with contextlib.ExitStack() as st:
        sb = lambda name, shape, dt: st.enter_context(nc.sbuf_tensor(name, shape, dt))
        ps = lambda name, shape, dt: st.enter_context(nc.psum_tensor(name, shape, dt))
        S = Sched(nc)
        par2 = sb("par2s", [128, 8], F32); masks = sb("maskss", [128, 2, 3, 512], F32); lvl = sb("lvls", [128, 6, 512], F32); ident4 = sb("ident4s", [128, 512], F32)
        m01 = sb("m01s", [128, 512], F32); bones = sb("boness", [128, 128], F32); idb = sb("idbs", [128, 128], BF16)
        NB = 2
        inb = [{n: sb("in_%s%d" % (n, i), [128, 512], F32) for n in ("p_r", "p_k", "p_v", "kk", "a", "l")} for i in range(NB)]
        tmp = {n: sb("tmp_" + n, [128, 512], F32) for n in ("f", "kd", "ka", "Lc", "Linc", "w1", "w2")}
        ltot = sb("ltot", [128, 8], F32); wtot = [sb("wtot%d" % i, [128, 8], F32) for i in range(NB)]
        BDn = ("KR", "BB", "KT", "BH", "KH", "VV")
        BD = [{n: sb("bd_%s%d" % (n, i), [128, 8, 256 if n == "KR" else 128], BF16) for n in BDn} for i in range(NB)]
        AB2 = [sb("AB2_%d" % i, [128, 8, 256], BF16) for i in range(NB)]
        AK2 = [sb("AK2_%d" % i, [128, 8, 256], BF16) for i in range(NB)]
        Pb = [sb("Pb%d" % i, [128, 8, 128], BF16) for i in range(2)]
        Qb = [sb("Qb%d" % i, [128, 8, 128], BF16) for i in range(2)]
        MT = [sb("MT%d" % i, [128, 8, 128], BF16) for i in range(NB)]
        Wt = sb("Wt", [128, 8, 128], BF16); Q0b = sb("Q0b", [128, 8, 128], BF16)
        TT = [{n: sb("tt_%s%d" % (n, i), [128, 8, 128], BF16) for n in ("BH", "KH", "VV")} for i in range(NB)]
        T = sb("T", [128, 128], F32); Tb = sb("Tb", [128, 128], BF16)
        Xs = sb("Xs", [128, 128], BF16); Us = sb("Us", [128, 128], BF16)
        y = sb("y", [128, 4096], F32)
        ob = {n: sb("ob_" + n, [128, 512], F32) for n in ("p_r", "p_k", "p_v", "a_f", "a_b", "t1", "t2", "t3")}
        ogb = sb("ogb", [128, 512], BF16); oo = sb("oo", [128, 512], BF16)
        pa = ps("pa", [128, 512], F32); pk = ps("pk", [128, 512], F32)
        pi = [ps("pi%d" % i, [128, 512], F32) for i in range(2)]
        ptr = ps("ptr", [128, 1024], BF16)
        pS = ps("pS", [128, 512], F32)
        pY = [ps("pY%d" % i, [128, 512], F32) for i in range(2)]
        for (t_, d_, nm) in ((par2, par2d, "par2"), (masks, masksd, "masks"), (lvl, lvld, "lvl"), (ident4, ident4d, "ident4"), (m01, m01d, "m01"), (bones, bonesd, "bones"), (idb, identbd, "idb")):
            S.op("sp", lambda e, t_=t_, d_=d_: e.dma_start(out=t_[:], in_=d_), writes=[nm], chan=nm)
        S.op("dve", lambda e: e.tensor_scalar(out=par2[:, 1:2], in0=par2[:, 0:1], scalar1=-1.0, scalar2=1.0, op0=ALU.mult, op1=ALU.add), reads=["par2"], writes=["par2"])
        S.op("dve", lambda e: e.tensor_scalar(out=par2[:, 5:6], in0=par2[:, 0:1], scalar1=-2.0, scalar2=2.0, op0=ALU.mult, op1=ALU.add), reads=["par2"], writes=["par2"])
        for i in range(NB):
            for n in BDn:
                S.op("pool", lambda e, i=i, n=n: e.memset(BD[i][n][:], 0.0), writes=[("bd", i)])

        v3 = lambda ap: ap.rearrange("p (c t) -> p c t", t=64)

        def prep_group(b, d, gi, sl):
            tok = b * 4096 + gi * 512
            I = inb[sl]
            for n in ("p_r", "p_k", "p_v", "kk"):
                S.op("sp", lambda e, n=n: e.dma_start(out=I[n][:], in_=X[n][:, tok:tok + 512]), writes=[("in", sl)], chan=("in", sl, n))
            sfx = "_f" if d == 0 else "_b"
            S.op("sp", lambda e: e.dma_start(out=I["a"][:], in_=X["a" + sfx][:, tok:tok + 512]), writes=[("in", sl)], chan=("in", sl, "a"))
            S.op("sp", lambda e: e.dma_start(out=I["l"][:], in_=X["l" + sfx][:, tok:tok + 512]), writes=[("in", sl)], chan=("in", sl, "l"))
            R = [("in", sl)]
            f, kd, ka, Lc, Linc, w1, w2 = [tmp[n] for n in ("f", "kd", "ka", "Lc", "Linc", "w1", "w2")]
            S.op("dve", lambda e: e.tensor_scalar(out=f[:], in0=I["a"][:], scalar1=par2[:, 0:1], scalar2=par2[:, 1:2], op0=ALU.mult, op1=ALU.add), reads=R + ["par2"], writes=["f"])
            S.op("dve", lambda e: e.tensor_tensor(out=kd[:], in0=I["p_k"][:], in1=f[:], op=ALU.mult), reads=R + ["f"], writes=["kd"])
            S.op("pool", lambda e: e.tensor_tensor(out=ka[:], in0=I["kk"][:], in1=I["a"][:], op=ALU.mult), reads=R, writes=["ka"])
            S.op("dve", lambda e: e.tensor_tensor_scan(out=Lc[:], data0=m01[:], data1=I["l"][:], initial=0.0, op0=ALU.mult, op1=ALU.add), reads=R + ["m01"], writes=["Lc"])
            S.op("dve", lambda e: e.tensor_copy(out=ltot[:], in_=v3(Lc[:])[:, :, 63]), reads=["Lc"], writes=["ltot"])
            lt_b = ltot[:].unsqueeze(2).to_broadcast([128, 8, 64])
            if d == 0:
                LI = Lc
                lres = "Lc"
            else:
                S.op("dve", lambda e: e.tensor_tensor(out=v3(Linc[:]), in0=lt_b, in1=v3(Lc[:]), op=ALU.subtract), reads=["Lc", "ltot"], writes=["Linc"])
                S.op("dve", lambda e: e.tensor_tensor(out=Linc[:], in0=Linc[:], in1=I["l"][:], op=ALU.add), reads=["Linc"] + R, writes=["Linc"])
                LI = Linc
                lres = "Linc"
            S.op("act", lambda e: e.activation(out=wtot[sl][:], in_=ltot[:], func=AF.Exp), reads=["ltot"], writes=[("wtot", sl)])
            Bd = BD[sl]

            def bd_write(name, c0, in0, in1, neg=False):
                for hh in range(2):
                    prt = slice(hh * 64, hh * 64 + 64)
                    o = Bd[name][prt, :, c0 + hh * 64:c0 + hh * 64 + 64]
                    if neg:
                        S.op("dve", lambda e, o=o, prt=prt: e.scalar_tensor_tensor(out=o, in0=v3(in0[prt, :]), scalar=-1.0, in1=v3(in1[prt, :]), op0=ALU.mult, op1=ALU.mult),
                             reads=R + ["ka", "kd", "w1", "w2"], writes=[("bd", sl)])
                    elif in1 is None:
                        S.op("pool", lambda e, o=o, prt=prt: e.tensor_copy(out=o, in_=v3(in0[prt, :])), reads=R, writes=[("bd", sl)])
                    else:
                        S.op("dve", lambda e, o=o, prt=prt: e.tensor_tensor(out=o, in0=v3(in0[prt, :]), in1=v3(in1[prt, :]), op=ALU.mult),
                             reads=R + ["ka", "kd", "w1", "w2"], writes=[("bd", sl)])

            S.op("act", lambda e: e.activation(out=w1[:], in_=LI[:], func=AF.Exp), reads=[lres], writes=["w1"])
            bd_write("KR", 128, I["p_r"], w1)
            S.op("dve", lambda e: e.tensor_tensor(out=w2[:], in0=LI[:], in1=I["l"][:], op=ALU.subtract), reads=[lres] + R, writes=["w2"])
            S.op("act", lambda e: e.activation(out=w2[:], in_=w2[:], func=AF.Exp), reads=["w2"], writes=["w2"])
            bd_write("KR", 0, I["kk"], w2)
            S.op("act", lambda e: e.activation(out=w1[:], in_=LI[:], func=AF.Exp, scale=-1.0), reads=[lres], writes=["w1"])
            bd_write("BB", 0, ka, w1)
            bd_write("KT", 0, kd, w1)
            S.op("dve", lambda e: e.tensor_tensor(out=v3(w2[:]), in0=lt_b, in1=v3(LI[:]), op=ALU.subtract), reads=[lres, "ltot"], writes=["w2"])
            S.op("act", lambda e: e.activation(out=w2[:], in_=w2[:], func=AF.Exp), reads=["w2"], writes=["w2"])
            bd_write("BH", 0, ka, w2, neg=True)
            bd_write("KH", 0, kd, w2)
            bd_write("VV", 0, I["p_v"], None)
            BR = [("bd", sl)]
            for c in range(8):
                o2 = (c % 2) * 256
                S.op("pe", lambda e, c=c, o2=o2: e.matmul(pa[:, o2:o2 + 256], lhsT=Bd["BB"][:, c, :], rhs=Bd["KR"][:, c, :], start=True, stop=True), reads=BR, writes=["pa"])
                S.op("pe", lambda e, c=c, o2=o2: e.matmul(pk[:, o2:o2 + 256], lhsT=Bd["KT"][:, c, :], rhs=Bd["KR"][:, c, :], start=True, stop=True), reads=BR, writes=["pk"])
                if c % 2 == 1:
                    S.op("dve", lambda e, c=c: e.tensor_tensor(out=AB2[sl][:, c - 1:c + 1, :], in0=pa[:].rearrange("p (c x) -> p c x", c=2), in1=masks[:, d, 0, :].rearrange("p (c x) -> p c x", c=2), op=ALU.mult),
                         reads=["pa", "masks"], writes=[("AB2", sl)])
                    S.op("dve", lambda e, c=c: e.tensor_tensor(out=AK2[sl][:, c - 1:c + 1, :], in0=pk[:].rearrange("p (c x) -> p c x", c=2), in1=masks[:, d, 1, :].rearrange("p (c x) -> p c x", c=2), op=ALU.mult),
                         reads=["pk", "masks"], writes=[("AK2", sl)])
            for c in range(8):
                pb = c // 4
                o4 = (c % 4) * 128
                S.op("pe", lambda e, c=c, pb=pb, o4=o4: e.matmul(pi[pb][:, o4:o4 + 128], lhsT=Bd["KR"][:, c, 0:128], rhs=Bd["BB"][:, c, :], start=True, stop=True), reads=BR, writes=[("pi", pb)])
            for pb in range(2):
                S.op("dve", lambda e, pb=pb: e.tensor_tensor(out=Q0b[:, pb * 4:pb * 4 + 4, :], in0=pi[pb][:].rearrange("p (c x) -> p c x", c=4), in1=masks[:, d, 2, :].rearrange("p (c x) -> p c x", c=4), op=ALU.mult),
                     reads=[("pi", pb), "masks"], writes=["Q0b"])
            for n in ("BH", "KH", "VV"):
                for c in range(8):
                    S.op("pe", lambda e, c=c, n=n: e.transpose(out=ptr[:, c * 128:(c + 1) * 128], in_=Bd[n][:, c, :], identity=idb[:]), reads=BR + ["idb"], writes=["ptr"])
                S.op("act", lambda e, n=n: e.copy(out=TT[sl][n][:], in_=ptr[:].rearrange("p (c x) -> p c x", c=8)), reads=["ptr"], writes=[("TT", sl)])
            W = MT[sl]
            NOs, NOTs, T1, T1p = Pb[0], Pb[1], Qb[0], Qb[1]
            c4 = lambda ap: ap.rearrange("p (c x) -> p c x", c=4)

            def masked(dst, dres, src, sres, lev):
                for hf in range(2):
                    S.op("dve", lambda e, hf=hf: e.tensor_tensor(out=dst[:, hf * 4:hf * 4 + 4, :], in0=src[:, hf * 4:hf * 4 + 4, :], in1=c4(lvl[:, lev, :]), op=ALU.mult),
                         reads=[sres, "lvl"], writes=[dres])

            masked(NOs, "NOs", AB2[sl][:, :, 0:128], ("AB2", sl), 0)
            masked(NOTs, "NOTs", Q0b, "Q0b", 0)
            for hf in range(2):
                S.op("dve", lambda e, hf=hf: e.tensor_tensor(out=W[:, hf * 4:hf * 4 + 4, :], in0=NOs[:, hf * 4:hf * 4 + 4, :], in1=c4(ident4[:]), op=ALU.add), reads=["NOs", "ident4"], writes=[("MT", sl)])
                S.op("dve", lambda e, hf=hf: e.tensor_tensor(out=Wt[:, hf * 4:hf * 4 + 4, :], in0=NOTs[:, hf * 4:hf * 4 + 4, :], in1=c4(ident4[:]), op=ALU.add), reads=["NOTs", "ident4"], writes=["Wt"])
            WR = ("MT", sl)

            BK0 = ([pi[0], pi[1]], [("pi", 0), ("pi", 1)])
            BK1 = ([pa, pk], ["pa", "pk"])

            def mm8(lhs, lres, rhs, rres, bk):
                for c in range(8):
                    pb = c // 4
                    o4 = (c % 4) * 128
                    S.op("pe", lambda e, c=c, pb=pb, o4=o4: e.matmul(bk[0][pb][:, o4:o4 + 128], lhsT=lhs[:, c, :], rhs=rhs[:, c, :], start=True, stop=True),
                         reads=[lres, rres], writes=[bk[1][pb]])

            for lev in range(1, 6):
                masked(NOs, "NOs", AB2[sl][:, :, 0:128], ("AB2", sl), lev)
                masked(NOTs, "NOTs", Q0b, "Q0b", lev)
                mm8(NOTs, "NOTs", W, WR, BK0)
                mm8(NOs, "NOs", Wt, "Wt", BK1)
                for pb in range(2):
                    S.op("act", lambda e, pb=pb: e.copy(out=T1[:, pb * 4:pb * 4 + 4, :], in_=c4(BK0[0][pb][:])), reads=[BK0[1][pb]], writes=["T1"])
                for pb in range(2):
                    S.op("dve", lambda e, pb=pb: e.tensor_copy(out=T1p[:, pb * 4:pb * 4 + 4, :], in_=c4(BK1[0][pb][:])), reads=[BK1[1][pb]], writes=["T1p"])
                mm8(Wt, "Wt", T1, "T1", BK0)
                mm8(W, WR, T1p, "T1p", BK1)
                for pb in range(2):
                    S.op("act", lambda e, pb=pb: e.copy(out=T1[:, pb * 4:pb * 4 + 4, :], in_=c4(BK0[0][pb][:])), reads=[BK0[1][pb]], writes=["T1"])
                for pb in range(2):
                    S.op("dve", lambda e, pb=pb: e.tensor_tensor(out=Wt[:, pb * 4:pb * 4 + 4, :], in0=Wt[:, pb * 4:pb * 4 + 4, :], in1=c4(BK1[0][pb][:]), op=ALU.add), reads=[BK1[1][pb], "Wt"], writes=["Wt"])
                for hf in range(2):
                    S.op("dve", lambda e, hf=hf: e.tensor_tensor(out=W[:, hf * 4:hf * 4 + 4, :], in0=W[:, hf * 4:hf * 4 + 4, :], in1=T1[:, hf * 4:hf * 4 + 4, :], op=ALU.add), reads=["T1", WR], writes=[WR])
        def scan_group(b, d, gi, sl):
            Bd = BD[sl]
            order = range(8) if d == 0 else range(7, -1, -1)
            for idx, c in enumerate(order):
                yb = idx // 4
                yo = (idx % 4) * 128
                S.op("pe", lambda e, c=c: e.matmul(pS[:, 0:128], lhsT=Bd["KR"][:, c, 0:128], rhs=Tb[:], start=True, stop=False), reads=[("bd", sl), "Tb"], writes=["pS0"])
                S.op("pe", lambda e, c=c: e.matmul(pS[:, 0:128], lhsT=AK2[sl][:, c, 0:128], rhs=TT[sl]["VV"][:, c, :], start=False, stop=True), reads=[("AK2", sl), ("TT", sl)], writes=["pS0"])
                S.op("act", lambda e: e.copy(out=Xs[:], in_=pS[:, 0:128]), reads=["pS0"], writes=["Xs"])
                S.op("pe", lambda e, c=c: e.matmul(pS[:, 128:256], lhsT=MT[sl][:, c, :], rhs=Xs[:], start=True, stop=True), reads=[("MT", sl), "Xs"], writes=["pS1"])
                S.op("dve", lambda e: e.tensor_copy(out=Us[:], in_=pS[:, 128:256]), reads=["pS1"], writes=["Us"])
                S.op("pe", lambda e, c=c: e.matmul(pS[:, 256:384], lhsT=TT[sl]["BH"][:, c, :], rhs=Us[:], start=True, stop=False), reads=[("TT", sl), "Us"], writes=["pS2"])
                S.op("pe", lambda e, c=c: e.matmul(pS[:, 256:384], lhsT=TT[sl]["KH"][:, c, :], rhs=TT[sl]["VV"][:, c, :], start=False, stop=True), reads=[("TT", sl)], writes=["pS2"])
                S.op("pe", lambda e, c=c, yb=yb, yo=yo: e.matmul(pY[yb][:, yo:yo + 128], lhsT=Tb[:], rhs=Bd["KR"][:, c, 128:256], start=True, stop=False), reads=[("bd", sl), "Tb"], writes=[("pY", yb)])
                S.op("pe", lambda e, c=c, yb=yb, yo=yo: e.matmul(pY[yb][:, yo:yo + 128], lhsT=Us[:], rhs=AB2[sl][:, c, 128:256], start=False, stop=False), reads=[("AB2", sl), "Us"], writes=[("pY", yb)])
                S.op("pe", lambda e, c=c, yb=yb, yo=yo: e.matmul(pY[yb][:, yo:yo + 128], lhsT=TT[sl]["VV"][:, c, :], rhs=AK2[sl][:, c, 128:256], start=False, stop=True), reads=[("AK2", sl), ("TT", sl)], writes=[("pY", yb)])
                S.op("dve", lambda e, c=c: e.scalar_tensor_tensor(out=T[:], in0=T[:], scalar=wtot[sl][:, c:c + 1], in1=pS[:, 256:384], op0=ALU.mult, op1=ALU.add), reads=["pS2", "T", ("wtot", sl)], writes=["T"])
                S.op("act", lambda e: e.copy(out=Tb[:], in_=T[:]), reads=["T"], writes=["Tb"])
                if idx % 4 == 3:
                    cs = sorted(list(order)[idx - 3:idx + 1])
                    c_lo = cs[0]
                    for hh in range(2):
                        prt = slice(hh * 64, hh * 64 + 64)
                        src = pY[yb][prt, :].rearrange("p (c x) -> p c x", c=4)[:, :, hh * 64:hh * 64 + 64]
                        if d == 1:
                            dsts = [(y[prt, gi * 512 + (c_lo + 3 - j) * 64:gi * 512 + (c_lo + 4 - j) * 64], pY[yb][prt, j * 128 + hh * 64:j * 128 + hh * 64 + 64]) for j in range(4)]
                            for (dd, ss) in dsts:
                                S.op("dve", lambda e, dd=dd, ss=ss: e.tensor_tensor(out=dd, in0=dd, in1=ss, op=ALU.add), reads=[("pY", yb), "y"], writes=["y"])
                        else:
                            dd = y[prt, gi * 512 + c_lo * 64:gi * 512 + (c_lo + 4) * 64].rearrange("p (c x) -> p c x", c=4)
                            S.op("act", lambda e, dd=dd, src=src: e.copy(out=dd, in_=src), reads=[("pY", yb)], writes=["y"])

        def out_block(b, blk):
            tok = b * 4096 + blk * 512
            for n in ("p_r", "p_k", "p_v", "a_f", "a_b"):
                S.op("sp", lambda e, n=n: e.dma_start(out=ob[n][:], in_=X[n][:, tok:tok + 512]), writes=[("ob", n)], chan=("ob", n))
            S.op("sp", lambda e: e.dma_start(out=ogb[:], in_=gd[:, tok:tok + 512]), writes=["ogb"], chan="ogb")
            t1, t2, t3 = ob["t1"], ob["t2"], ob["t3"]
            ys = y[:, blk * 512:(blk + 1) * 512]
            S.op("pe", lambda e: e.matmul(pa[:], lhsT=bones[:], rhs=ys, start=True, stop=True), reads=["y", "bones"], writes=["pa"])
            S.op("dve", lambda e: e.scalar_tensor_tensor(out=t1[:], in0=pa[:], scalar=-1.0 / 64, in1=ys, op0=ALU.mult, op1=ALU.add), reads=["pa", "y"], writes=["t1"])
            S.op("act", lambda e: e.activation(out=t2[:], in_=t1[:], func=AF.Square), reads=["t1"], writes=["t2"])
            S.op("pe", lambda e: e.matmul(pk[:], lhsT=bones[:], rhs=t2[:], start=True, stop=True), reads=["t2", "bones"], writes=["pk"])
            S.op("act", lambda e: e.activation(out=t2[:], in_=pk[:], func=AF.Sqrt, scale=1.0 / 64, bias=64e-5), reads=["pk"], writes=["t2"])
            S.op("dve", lambda e: e.reciprocal(out=t2[:], in_=t2[:]), reads=["t2"], writes=["t2"])
            S.op("dve", lambda e: e.tensor_tensor(out=t1[:], in0=t1[:], in1=t2[:], op=ALU.mult), reads=["t1", "t2"], writes=["t1"])
            S.op("dve", lambda e: e.tensor_scalar(out=t1[:], in0=t1[:], scalar1=par2[:, 3:4], scalar2=par2[:, 4:5], op0=ALU.mult, op1=ALU.add), reads=["t1", "par2"], writes=["t1"])
            S.op("pool", lambda e: e.tensor_tensor(out=t2[:], in0=ob["a_f"][:], in1=ob["a_b"][:], op=ALU.add), reads=[("ob", "a_f"), ("ob", "a_b")], writes=["t2"])
            S.op("dve", lambda e: e.tensor_scalar(out=t2[:], in0=t2[:], scalar1=par2[:, 0:1], scalar2=par2[:, 5:6], op0=ALU.mult, op1=ALU.add), reads=["t2", "par2"], writes=["t2"])
            S.op("pool", lambda e: e.tensor_tensor(out=t2[:], in0=t2[:], in1=ob["p_k"][:], op=ALU.mult), reads=["t2", ("ob", "p_k")], writes=["t2"])
            S.op("dve", lambda e: e.scalar_tensor_tensor(out=t3[:], in0=t2[:], scalar=par2[:, 2:3], in1=ob["p_r"][:], op0=ALU.mult, op1=ALU.mult), reads=["t2", "par2", ("ob", "p_r")], writes=["t3"])
            S.op("pe", lambda e: e.matmul(pa[:], lhsT=bones[:], rhs=t3[:], start=True, stop=True), reads=["t3", "bones"], writes=["pa"])
            S.op("dve", lambda e: e.tensor_tensor(out=t3[:], in0=pa[:], in1=ob["p_v"][:], op=ALU.mult), reads=["pa", ("ob", "p_v")], writes=["t3"])
            S.op("pool", lambda e: e.tensor_tensor(out=t3[:], in0=t3[:], in1=t1[:], op=ALU.add), reads=["t3", "t1"], writes=["t3"])
            S.op("dve", lambda e: e.tensor_tensor(out=oo[:], in0=t3[:], in1=ogb[:], op=ALU.mult), reads=["t3", "ogb"], writes=["oo"])
            S.op("sp", lambda e: e.dma_start(out=oT[:, tok:tok + 512], in_=oo[:]), reads=["oo"], chan="oo")

        def collect(fn, *args):
            buf = []
            real = S.op
            S.op = lambda *a, **k: buf.append((a, k))
            try:
                fn(*args)
            finally:
                S.op = real
            return buf

        def emit_interleaved(A, B):
            nA, nB = len(A), len(B)
            ia = 0
            for ib, (a, k) in enumerate(B):
                tgt = (ib * nA) // max(nB, 1)
                while ia < tgt:
                    S.op(*A[ia][0], **A[ia][1])
                    ia += 1
                S.op(*a, **k)
            while ia < nA:
                S.op(*A[ia][0], **A[ia][1])
                ia += 1

        for b in range(2):
            for d in range(2):
                S.op("dve", lambda e: e.memset(T[:], 0.0), writes=["T"])
                S.op("pool", lambda e: e.memset(Tb[:], 0.0), writes=["Tb"])
                gorder = list(range(8)) if d == 0 else list(range(7, -1, -1))
                prep_group(b, d, gorder[0], 0)
                for j, gi in enumerate(gorder):
                    A = collect(prep_group, b, d, gorder[j + 1], (j + 1) % 2) if j + 1 < 8 else []
                    B = collect(scan_group, b, d, gi, j % 2)
                    emit_interleaved(A, B)
            for blk in range(8):
                out_block(b, blk)
        S.run()
    return nc


def build_r12():
    nc = bass.Bass("TRN2", target_bir_lowering=False)
    nc, outs, g_out = build_r1(nc, merged=True)
    nc.all_engine_barrier()
    build_r2(nc, X=outs, gd=g_out)
    return nc


def r2_inputs(inp, l, c):
    hs = [2 * c, 2 * c + 1]
    ch = np.concatenate([np.arange(h * 64, h * 64 + 64) for h in hs])
    par2 = np.zeros((128, 8), np.float32)
    par2[:, 0] = inp["k_a"][l][ch]
    par2[:, 2] = inp["r_k"][l].reshape(-1)[ch]
    par2[:, 3] = inp["gn_w"][l][ch]
    par2[:, 4] = inp["gn_b"][l][ch]
    return dict(par2=par2)
def build_p0():
    nc = bass.Bass("TRN2", target_bir_lowering=False)
    x = nc.dram_tensor("x", [1024, 2048], F32, kind="ExternalInput").ap()
    g1 = nc.dram_tensor("g1", [128, 16], F32, kind="ExternalInput").ap()
    ident = nc.dram_tensor("ident", [128, 128], F32, kind="ExternalInput").ap()
    hT = nc.dram_tensor("hT", [2048, 1024], F32, kind="ExternalOutput").ap()
    uT = nc.dram_tensor("uT", [2048, 1024], BF16, kind="ExternalOutput").ap()
    with contextlib.ExitStack() as st:
        sb = lambda name, shape, dt: st.enter_context(nc.sbuf_tensor(name, shape, dt))
        ps = lambda name, shape, dt: st.enter_context(nc.psum_tensor(name, shape, dt))
        xt = [sb("xt%d" % i, [128, 2048], F32) for i in range(2)]
        h = sb("h", [128, 16, 1024], F32)
        u = sb("u", [128, 16, 1024], BF16)
        sq = [sb("sq%d" % i, [128, 512], F32) for i in range(2)]
        rs = sb("rs", [128, 512], F32)
        g = sb("g", [128, 16], F32)
        idt = sb("idt", [128, 128], F32)
        ones = sb("ones", [128, 128], F32)
        pt = [ps("pt%d" % i, [128, 512], F32) for i in range(4)]
        pn = ps("pn", [128, 512], F32)
        S = Sched(nc)
        S.op("sp", lambda e: e.dma_start(out=g[:], in_=g1), writes=["g"], chan="g")
        S.op("sp", lambda e: e.dma_start(out=idt[:], in_=ident), writes=["idt"], chan="idt")
        S.op("dve", lambda e: e.memset(ones[:], 1.0), writes=["ones"])
        for tt in range(8):
            b = tt % 2
            S.op("sp", lambda e, tt=tt, b=b: e.dma_start(out=xt[b][:], in_=x[tt * 128:(tt + 1) * 128, :]), writes=[("xt", b)], chan=("xt", b))
            for kg in range(4):
                pb = kg
                for j in range(4):
                    k = kg * 4 + j
                    S.op("pe", lambda e, k=k, j=j, pb=pb, b=b: e.transpose(out=pt[pb][:, j * 128:(j + 1) * 128], in_=xt[b][:, k * 128:(k + 1) * 128], identity=idt[:]),
                         reads=[("xt", b), "idt"], writes=[("pt", pb)])
                if kg % 2:
                    f = lambda e, kg=kg, pb=pb, tt=tt: e.copy(out=h[:, kg * 4:(kg + 1) * 4, tt * 128:(tt + 1) * 128], in_=pt[pb][:].rearrange("p (j t) -> p j t", j=4))
                else:
                    f = lambda e, kg=kg, pb=pb, tt=tt: e.tensor_copy(out=h[:, kg * 4:(kg + 1) * 4, tt * 128:(tt + 1) * 128], in_=pt[pb][:].rearrange("p (j t) -> p j t", j=4))
                S.op("act" if kg % 2 else "dve", f, reads=[("pt", pb)], writes=[("h", tt // 4)])
        for tb in range(2):
            tsl = slice(tb * 512, (tb + 1) * 512)
            emit_rmsnorm(S, nc, h[:, :, tsl], u[:, :, tsl], g, sq, rs, ones, pn, ("h", tb), ("u", tb), "g")
        S.op("sp", lambda e: e.dma_start(out=hT.rearrange("(k p) t -> p k t", p=128), in_=h[:]), reads=[("h", 0), ("h", 1)], chan="oh")
        S.op("sp", lambda e: e.dma_start(out=uT.rearrange("(k p) t -> p k t", p=128), in_=u[:]), reads=[("u", 0), ("u", 1)], chan="ou")
        S.run()
    return nc


_NC_CACHE = {}


def _prog(name, fn):
    if name not in _NC_CACHE:
        _NC_CACHE[name] = fn()
    return _NC_CACHE[name]


def _run(nc, in_maps):
    res = run_bass_kernel_spmd(nc, in_maps, core_ids=list(range(NCORES)))
    return res.results


def kernel(x, norm1_g, w_in, tshift_mu, w0, w_lora_up, a0, a_lora_up, g_lora_up, k_k, k_a, r_k, gn_w, gn_b, rel_bias,
           w_branch_rwkv, w_branch_attn, w_out, norm2_g, w_mlp_in, w_mlp_out, final_g):
    inp = dict(x=x, norm1_g=norm1_g, w_in=w_in, tshift_mu=tshift_mu, w0=w0, w_lora_up=w_lora_up, a0=a0, a_lora_up=a_lora_up,
               g_lora_up=g_lora_up, k_k=k_k, k_a=k_a, r_k=r_k, gn_w=gn_w, gn_b=gn_b, rel_bias=rel_bias, w_branch_rwkv=w_branch_rwkv,
               w_branch_attn=w_branch_attn, w_out=w_out, norm2_g=norm2_g, w_mlp_in=w_mlp_in, w_mlp_out=w_mlp_out, final_g=final_g)
    inp = {k: np.asarray(v, dtype=np.float32) for k, v in inp.items()}
    bf = ml_dtypes.bfloat16
    depth = inp["w_in"].shape[0]
    vec = lambda v: np.ascontiguousarray(v.reshape(16, 128).T)
    xs = inp["x"].reshape(8192, 2048)
    eye = np.eye(128, dtype=np.float32)
    r = _run(_prog("p0", build_p0), [dict(x=np.ascontiguousarray(xs[c * 1024:(c + 1) * 1024]), g1=vec(inp["norm1_g"][0]), ident=eye) for c in range(NCORES)])
    hT = [np.asarray(r[c]["hT"]) for c in range(NCORES)]
    uT = [np.asarray(r[c]["uT"]) for c in range(NCORES)]
    tabs = bias_index_tables()
    consts2 = r2_consts()
    out = None
    for l in range(depth):
        uT_all = np.ascontiguousarray(np.concatenate(uT, axis=1))
        in12 = []
        for c in range(NCORES):
            m = dict(r1_inputs(inp, l, c), uT=uT_all)
            m.update({("bones2" if k == "bones" else k): v for k, v in consts2.items()})
            m.update(r2_inputs(inp, l, c))
            in12.append(m)
        r2 = _run(_prog("r12", build_r12), in12)
        ina = []
        for c in range(NCORES):
            heads = [g * 8 + c for g in range(3)]
            qc = lambda h: np.arange(3712 + h * 64, 3712 + h * 64 + 64)
            kc = lambda h: np.arange(3712 + 1536 + h * 64, 3712 + 1536 + h * 64 + 64)
            vc = lambda h: np.arange(3712 + 3072 + h * 64, 3712 + 3072 + h * 64 + 64)
            cols = np.concatenate([qc(heads[0]), qc(heads[1]), kc(heads[0]), kc(heads[1]), qc(heads[2]), vc(heads[2]), kc(heads[2]), vc(heads[0]), vc(heads[1])])
            bias = np.stack([np.where(m, inp["rel_bias"][:, heads[g]][idx], np.float32(-30000.0)) for g, (idx, m) in enumerate(tabs)], axis=1).astype(np.float32)
            ina.append(dict(uT=uT_all, wa=np.ascontiguousarray(inp["w_in"][l][:, cols]), biasT=np.ascontiguousarray(bias), identb=np.eye(128).astype(bf)))
        ra = _run(_prog("battn", build_b_attn), ina)
        o_all = np.concatenate([np.asarray(r2[c]["oT"]) for c in range(NCORES)] + [np.asarray(ra[c]["oT"]) for c in range(NCORES)], axis=0)
        last = (l == depth - 1)
        common = dict(wg=np.ascontiguousarray(inp["w_in"][l][:, 8320:]), wbr=inp["w_branch_rwkv"][l], wba=inp["w_branch_attn"][l], wout=inp["w_out"][l],
                      w1=inp["w_mlp_in"][l], w2=inp["w_mlp_out"][l], g2=vec(inp["norm2_g"][l]),
                      gn=vec(inp["final_g"] if last else inp["norm1_g"][l + 1]), ident=eye)
        inc = [dict(common, hT=hT[c], uT=uT[c], oT=np.ascontiguousarray(o_all[:, c * 1024:(c + 1) * 1024])) for c in range(NCORES)]
        rc = _run(_prog("c_last" if last else "c", lambda: build_c(last)), inc)
        if last:
            out = np.concatenate([np.asarray(rc[c]["out"]) for c in range(NCORES)], axis=0)
        else:
            hT = [np.asarray(rc[c]["hTo"]) for c in range(NCORES)]
            uT = [np.asarray(rc[c]["uTo"]) for c in range(NCORES)]
    return out.reshape(inp["x"].shape).astype(np.float32)
```
